# Optimizing a Trainium2 kernel written in Bass

```python
import math
import jax, jax.numpy as jnp
from jax import lax
import numpy as np

D_MODEL = 1024
BATCH = 8
SEQ = 2048
DEPTH = 4
DEC_BATCH = 128
DEC_SEQ = 8
PAST_LEN = 16384
PAGE_SIZE = 128

MIX_DIM = 2 * D_MODEL
CONV_WIDTH = 4
SSD_DIM = D_MODEL
SSD_HEADDIM = 64
SSD_HEADS = SSD_DIM // SSD_HEADDIM
SSD_GROUPS = 4
SSD_HPG = SSD_HEADS // SSD_GROUPS
SSD_STATE = 128
SSD_CONV_DIM = SSD_DIM + 2 * SSD_GROUPS * SSD_STATE
SSD_CHUNK = 128
LRU_DIM = D_MODEL // 2
LRU_BLOCKS = 8
LRU_BLOCK_DIM = LRU_DIM // LRU_BLOCKS
LRU_C = 8.0
S5_DIM = D_MODEL // 2
S5_GROUP = 16
S5_NGROUPS = S5_DIM // S5_GROUP
S5_STATE = 64
EPS = 1e-6

OFF_XBC = SSD_DIM
OFF_DT = OFF_XBC + SSD_CONV_DIM
OFF_LRU = OFF_DT + SSD_HEADS
OFF_LRU_G = OFF_LRU + LRU_DIM
OFF_S5 = OFF_LRU_G + LRU_DIM
OFF_S5_G = OFF_S5 + S5_DIM
IN_DIM = OFF_S5_G + S5_DIM
SPLITS = (OFF_XBC, OFF_DT, OFF_LRU, OFF_LRU_G, OFF_S5, OFF_S5_G)

kernel_name = "hymba_ssd_rglru_s5_step"


def rmsnorm(x, g):
    xf = x.astype(jnp.float32)
    return xf * lax.rsqrt(jnp.mean(xf * xf, axis=-1, keepdims=True) + EPS) * g


def causal_conv(x, prev, w, b):
    l = x.shape[1]
    xp = jnp.concatenate([prev, x], axis=1)
    y = b + sum(w[k] * xp[:, k:k + l] for k in range(CONV_WIDTH))
    return y, xp[:, l:]


def ssd_scan(x, dt, a, bmat, cmat, h0):
    bsz, l = x.shape[:2]
    q = l if l <= SSD_CHUNK else math.gcd(l, SSD_CHUNK)
    nc = l // q
    xc = x.reshape(bsz, nc, q, SSD_GROUPS, SSD_HPG, SSD_HEADDIM)
    dtc = dt.reshape(bsz, nc, q, SSD_GROUPS, SSD_HPG)
    bc = bmat.reshape(bsz, nc, q, SSD_GROUPS, SSD_STATE)
    cc = cmat.reshape(bsz, nc, q, SSD_GROUPS, SSD_STATE)
    acum = jnp.moveaxis(jnp.cumsum(dtc * a.reshape(SSD_GROUPS, SSD_HPG), axis=2), 2, -1)
    dt_t = jnp.moveaxis(dtc, 2, -1)
    diff = acum[..., :, None] - acum[..., None, :]
    causal = jnp.tril(jnp.ones((q, q), dtype=bool))
    decay = jnp.where(causal, jnp.exp(jnp.where(causal, diff, 0.0)), 0.0)
    scores = jnp.einsum('bcqgn,bcsgn->bcgqs', cc, bc)
    y_diag = jnp.einsum('bcgqs,bcgrqs,bcgrs,bcsgrp->bcqgrp', scores, decay, dt_t, xc)
    to_end = jnp.exp(acum[..., -1:] - acum)
    chunk_states = jnp.einsum('bcsgn,bcgrs,bcsgrp->bcgrpn', bc, to_end * dt_t, xc)
    chunk_decay = jnp.exp(acum[..., -1])

    def step(h, inp):
        dec, st = inp
        return dec[..., None, None] * h + st, h

    h_last, h_prev = lax.scan(
        step, h0.reshape(bsz, SSD_GROUPS, SSD_HPG, SSD_HEADDIM, SSD_STATE),
        (jnp.moveaxis(chunk_decay, 1, 0), jnp.moveaxis(chunk_states, 1, 0)))
    h_prev = jnp.moveaxis(h_prev, 0, 1)
    y_off = jnp.einsum('bcqgn,bcgrpn,bcgrq->bcqgrp', cc, h_prev, jnp.exp(acum))
    y = (y_diag + y_off).reshape(bsz, l, SSD_HEADS, SSD_HEADDIM)
    return y, h_last.reshape(bsz, SSD_HEADS, SSD_HEADDIM, SSD_STATE)


def linear_scan(a, b, h0):
    b = b.at[:, 0].add(a[:, 0] * h0)

    def combine(e1, e2):
        a1, b1 = e1
        a2, b2 = e2
        return a1 * a2, a2 * b1 + b2

    _, h = lax.associative_scan(combine, (a, b), axis=1)
    return h


def complex_linear_scan(ar, ai, br, bi, h0r, h0i):
    br = br.at[:, 0].add(ar[:, 0] * h0r - ai[:, 0] * h0i)
    bi = bi.at[:, 0].add(ar[:, 0] * h0i + ai[:, 0] * h0r)

    def combine(e1, e2):
        ar1, ai1, br1, bi1 = e1
        ar2, ai2, br2, bi2 = e2
        return (ar1 * ar2 - ai1 * ai2, ar1 * ai2 + ai1 * ar2,
                ar2 * br1 - ai2 * bi1 + br2, ar2 * bi1 + ai2 * br1 + bi2)

    _, _, hr, hi = lax.associative_scan(combine, (ar, ai, br, bi), axis=1)
    return hr, hi


def s5_mixer(u, lam_re, lam_im, log_dt, b_re, b_im, c_re, c_im, d, h0r, h0i):
    bsz, l, _ = u.shape
    ug = u.reshape(bsz, l, S5_NGROUPS, S5_GROUP)
    delta = jnp.exp(log_dt)[:, None]
    mag = jnp.exp(lam_re * delta)
    abar_re = mag * jnp.cos(lam_im * delta)
    abar_im = mag * jnp.sin(lam_im * delta)
    denom = lam_re * lam_re + lam_im * lam_im
    nr = abar_re - 1.0
    ni = abar_im
    coef_re = (nr * lam_re + ni * lam_im) / denom
    coef_im = (ni * lam_re - nr * lam_im) / denom
    bbar_re = coef_re[..., None] * b_re - coef_im[..., None] * b_im
    bbar_im = coef_re[..., None] * b_im + coef_im[..., None] * b_re
    bu_re = jnp.einsum('blgh,gph->blgp', ug, bbar_re)
    bu_im = jnp.einsum('blgh,gph->blgp', ug, bbar_im)
    ar = jnp.broadcast_to(abar_re, bu_re.shape)
    ai = jnp.broadcast_to(abar_im, bu_re.shape)
    hr, hi = complex_linear_scan(ar, ai, bu_re, bu_im, h0r, h0i)
    y = jnp.einsum('blgp,ghp->blgh', hr, c_re) - jnp.einsum('blgp,ghp->blgh', hi, c_im)
    y = y.reshape(bsz, l, S5_DIM) + d * u
    return y, hr[:, -1], hi[:, -1]


def mixer_layer(x, states, p):
    ssd_h0, ssd_conv0, lru_h0, lru_conv0, s5_h0r, s5_h0i = states
    bsz, l, _ = x.shape
    h = rmsnorm(x, p['norm_g'])
    proj = h @ p['w_in']
    z, xbc, dt_raw, lru_x, lru_gate, s5_u, s5_gate = jnp.split(proj, SPLITS, axis=-1)

    xbc, ssd_conv_new = causal_conv(xbc, ssd_conv0, p['ssd_conv_w'], p['ssd_conv_b'])
    xbc = jax.nn.silu(xbc)
    xs, bm, cm = jnp.split(xbc, (SSD_DIM, SSD_DIM + SSD_GROUPS * SSD_STATE), axis=-1)
    dt = jax.nn.softplus(dt_raw + p['ssd_dt_bias'])
    a = -jnp.exp(p['ssd_a_log'])
    xh = xs.reshape(bsz, l, SSD_HEADS, SSD_HEADDIM)
    y, ssd_h = ssd_scan(xh, dt, a, bm.reshape(bsz, l, SSD_GROUPS, SSD_STATE),
                        cm.reshape(bsz, l, SSD_GROUPS, SSD_STATE), ssd_h0)
    y = y + p['ssd_d'][:, None] * xh
    y_ssd = rmsnorm(y.reshape(bsz, l, SSD_DIM) * jax.nn.silu(z), p['ssd_norm_g'])

    xr, lru_conv_new = causal_conv(lru_x, lru_conv0, p['lru_conv_w'], p['lru_conv_b'])
    xb = xr.reshape(bsz, l, LRU_BLOCKS, LRU_BLOCK_DIM)
    r = jax.nn.sigmoid(jnp.einsum('blki,kij->blkj', xb, p['lru_wa']).reshape(bsz, l, LRU_DIM) + p['lru_ba'])
    gi = jax.nn.sigmoid(jnp.einsum('blki,kij->blkj', xb, p['lru_wx']).reshape(bsz, l, LRU_DIM) + p['lru_bx'])
    log_a = -LRU_C * r * jax.nn.softplus(-p['lru_lambda'])
    a_t = jnp.exp(log_a)
    gain = jnp.sqrt(jnp.maximum(-jnp.expm1(2.0 * log_a), 0.0))
    hs = linear_scan(a_t, gain * gi * xr, lru_h0)
    y_lru = hs * jax.nn.silu(lru_gate)

    ys5, s5_hr, s5_hi = s5_mixer(s5_u, p['s5_lambda_re'], p['s5_lambda_im'], p['s5_log_dt'],
                                 p['s5_b_re'], p['s5_b_im'], p['s5_c_re'], p['s5_c_im'],
                                 p['s5_d'], s5_h0r, s5_h0i)
    ys5 = jax.nn.gelu(ys5)
    ys5 = ys5 * jax.nn.sigmoid(ys5 @ p['s5_glu_w'] + p['s5_glu_b'])
    y_s5 = ys5 * jax.nn.silu(s5_gate)

    out = jnp.concatenate([y_ssd, y_lru, y_s5], axis=-1) @ p['w_out']
    return x + out, (ssd_h, ssd_conv_new, hs[:, -1], lru_conv_new, s5_hr, s5_hi)


def setup_inputs(seed: int = 0) -> dict:
    key = jax.random.key(seed)
    ks = iter(jax.random.split(key, 48))
    f32 = jnp.float32

    def nrm(shape, scale):
        return scale * jax.random.normal(next(ks), shape, f32)

    def uni(shape, lo, hi):
        return jax.random.uniform(next(ks), shape, f32, lo, hi)

    x_prompt = nrm((BATCH, SEQ, D_MODEL), 1.0)
    x_sample = nrm((DEC_BATCH, DEC_SEQ, D_MODEL), 1.0)
    state_ssd = nrm((DEPTH, DEC_BATCH, SSD_HEADS, SSD_HEADDIM, SSD_STATE), 0.1)
    state_ssd_conv = nrm((DEPTH, DEC_BATCH, CONV_WIDTH - 1, SSD_CONV_DIM), 1.0)
    state_lru = nrm((DEPTH, DEC_BATCH, LRU_DIM), 0.5)
    state_lru_conv = nrm((DEPTH, DEC_BATCH, CONV_WIDTH - 1, LRU_DIM), 1.0)
    state_s5_re = nrm((DEPTH, DEC_BATCH, S5_NGROUPS, S5_STATE), 0.5)
    state_s5_im = nrm((DEPTH, DEC_BATCH, S5_NGROUPS, S5_STATE), 0.5)

    norm_g = 1.0 + nrm((DEPTH, D_MODEL), 0.02)
    w_in = nrm((DEPTH, D_MODEL, IN_DIM), D_MODEL ** -0.5)
    ssd_conv_w = nrm((DEPTH, CONV_WIDTH, SSD_CONV_DIM), CONV_WIDTH ** -0.5)
    ssd_conv_b = nrm((DEPTH, SSD_CONV_DIM), 0.02)
    dt0 = jnp.exp(uni((DEPTH, SSD_HEADS), math.log(1e-3), math.log(1e-1)))
    ssd_dt_bias = dt0 + jnp.log(-jnp.expm1(-dt0))
    ssd_a_log = jnp.log(uni((DEPTH, SSD_HEADS), 1.0, 16.0))
    ssd_d = 1.0 + nrm((DEPTH, SSD_HEADS), 0.02)
    ssd_norm_g = 1.0 + nrm((DEPTH, SSD_DIM), 0.02)
    lru_conv_w = nrm((DEPTH, CONV_WIDTH, LRU_DIM), CONV_WIDTH ** -0.5)
    lru_conv_b = nrm((DEPTH, LRU_DIM), 0.02)
    lru_wa = nrm((DEPTH, LRU_BLOCKS, LRU_BLOCK_DIM, LRU_BLOCK_DIM), LRU_BLOCK_DIM ** -0.5)
    lru_ba = nrm((DEPTH, LRU_DIM), 0.02)
    lru_wx = nrm((DEPTH, LRU_BLOCKS, LRU_BLOCK_DIM, LRU_BLOCK_DIM), LRU_BLOCK_DIM ** -0.5)
    lru_bx = nrm((DEPTH, LRU_DIM), 0.02)
    u_a = uni((DEPTH, LRU_DIM), 0.9, 0.999)
    lru_lambda = jnp.log(u_a) - jnp.log1p(-u_a)
    s5_lambda_re = -0.5 + nrm((DEPTH, S5_NGROUPS, S5_STATE), 0.01)
    s5_lambda_im = jnp.pi * jnp.arange(S5_STATE, dtype=f32) + nrm((DEPTH, S5_NGROUPS, S5_STATE), 0.01)
    s5_log_dt = uni((DEPTH, S5_NGROUPS), math.log(1e-3), math.log(1e-1))
    s5_b_re = nrm((DEPTH, S5_NGROUPS, S5_STATE, S5_GROUP), (2 * S5_GROUP) ** -0.5)
    s5_b_im = nrm((DEPTH, S5_NGROUPS, S5_STATE, S5_GROUP), (2 * S5_GROUP) ** -0.5)
    s5_c_re = nrm((DEPTH, S5_NGROUPS, S5_GROUP, S5_STATE), S5_STATE ** -0.5)
    s5_c_im = nrm((DEPTH, S5_NGROUPS, S5_GROUP, S5_STATE), S5_STATE ** -0.5)
    s5_d = nrm((DEPTH, S5_DIM), 1.0)
    s5_glu_w = nrm((DEPTH, S5_DIM, S5_DIM), S5_DIM ** -0.5)
    s5_glu_b = nrm((DEPTH, S5_DIM), 0.02)
    w_out = nrm((DEPTH, MIX_DIM, D_MODEL), MIX_DIM ** -0.5)
    final_norm_g = 1.0 + nrm((D_MODEL,), 0.02)
    return {
        "x_prompt": x_prompt, "x_sample": x_sample,
        "state_ssd": state_ssd, "state_ssd_conv": state_ssd_conv,
        "state_lru": state_lru, "state_lru_conv": state_lru_conv,
        "state_s5_re": state_s5_re, "state_s5_im": state_s5_im,
        "norm_g": norm_g, "w_in": w_in,
        "ssd_conv_w": ssd_conv_w, "ssd_conv_b": ssd_conv_b, "ssd_dt_bias": ssd_dt_bias,
        "ssd_a_log": ssd_a_log, "ssd_d": ssd_d, "ssd_norm_g": ssd_norm_g,
        "lru_conv_w": lru_conv_w, "lru_conv_b": lru_conv_b, "lru_wa": lru_wa, "lru_ba": lru_ba,
        "lru_wx": lru_wx, "lru_bx": lru_bx, "lru_lambda": lru_lambda,
        "s5_lambda_re": s5_lambda_re, "s5_lambda_im": s5_lambda_im, "s5_log_dt": s5_log_dt,
        "s5_b_re": s5_b_re, "s5_b_im": s5_b_im, "s5_c_re": s5_c_re, "s5_c_im": s5_c_im,
        "s5_d": s5_d, "s5_glu_w": s5_glu_w, "s5_glu_b": s5_glu_b,
        "w_out": w_out, "final_norm_g": final_norm_g,
    }


def reference(x_prompt, x_sample, state_ssd, state_ssd_conv, state_lru, state_lru_conv,
              state_s5_re, state_s5_im, norm_g, w_in, ssd_conv_w, ssd_conv_b, ssd_dt_bias,
              ssd_a_log, ssd_d, ssd_norm_g, lru_conv_w, lru_conv_b, lru_wa, lru_ba, lru_wx,
              lru_bx, lru_lambda, s5_lambda_re, s5_lambda_im, s5_log_dt, s5_b_re, s5_b_im,
              s5_c_re, s5_c_im, s5_d, s5_glu_w, s5_glu_b, w_out, final_norm_g):
    f32 = jnp.float32
    bp = x_prompt.shape[0]
    xp = x_prompt.astype(f32)
    xs = x_sample.astype(f32)
    zero_states = (
        jnp.zeros((bp, SSD_HEADS, SSD_HEADDIM, SSD_STATE), f32),
        jnp.zeros((bp, CONV_WIDTH - 1, SSD_CONV_DIM), f32),
        jnp.zeros((bp, LRU_DIM), f32),
        jnp.zeros((bp, CONV_WIDTH - 1, LRU_DIM), f32),
        jnp.zeros((bp, S5_NGROUPS, S5_STATE), f32),
        jnp.zeros((bp, S5_NGROUPS, S5_STATE), f32),
    )
    new_p = ([], [], [], [], [], [])
    new_s = ([], [], [], [], [], [])
    for i in range(DEPTH):
        layer = {
            'norm_g': norm_g[i], 'w_in': w_in[i],
            'ssd_conv_w': ssd_conv_w[i], 'ssd_conv_b': ssd_conv_b[i], 'ssd_dt_bias': ssd_dt_bias[i],
            'ssd_a_log': ssd_a_log[i], 'ssd_d': ssd_d[i], 'ssd_norm_g': ssd_norm_g[i],
            'lru_conv_w': lru_conv_w[i], 'lru_conv_b': lru_conv_b[i], 'lru_wa': lru_wa[i],
            'lru_ba': lru_ba[i], 'lru_wx': lru_wx[i], 'lru_bx': lru_bx[i], 'lru_lambda': lru_lambda[i],
            's5_lambda_re': s5_lambda_re[i], 's5_lambda_im': s5_lambda_im[i], 's5_log_dt': s5_log_dt[i],
            's5_b_re': s5_b_re[i], 's5_b_im': s5_b_im[i], 's5_c_re': s5_c_re[i], 's5_c_im': s5_c_im[i],
            's5_d': s5_d[i], 's5_glu_w': s5_glu_w[i], 's5_glu_b': s5_glu_b[i], 'w_out': w_out[i],
        }
        p = {k: v.astype(f32) for k, v in layer.items()}
        sample_states = (state_ssd[i].astype(f32), state_ssd_conv[i].astype(f32),
                         state_lru[i].astype(f32), state_lru_conv[i].astype(f32),
                         state_s5_re[i].astype(f32), state_s5_im[i].astype(f32))
        xp, sp = mixer_layer(xp, zero_states, p)
        xs, ss = mixer_layer(xs, sample_states, p)
        for j in range(6):
            new_p[j].append(sp[j])
            new_s[j].append(ss[j])
    fg = final_norm_g.astype(f32)
    y_prompt = rmsnorm(xp, fg).astype(x_prompt.dtype)
    y_sample = rmsnorm(xs, fg).astype(x_sample.dtype)
    ssd_p = jnp.stack(new_p[0]).astype(state_ssd.dtype)
    ssd_s = jnp.stack(new_s[0]).astype(state_ssd.dtype)
    ssd_conv_p = jnp.stack(new_p[1]).astype(state_ssd_conv.dtype)
    ssd_conv_s = jnp.stack(new_s[1]).astype(state_ssd_conv.dtype)
    lru_p = jnp.stack(new_p[2]).astype(state_lru.dtype)
    lru_s = jnp.stack(new_s[2]).astype(state_lru.dtype)
    lru_conv_p = jnp.stack(new_p[3]).astype(state_lru_conv.dtype)
    lru_conv_s = jnp.stack(new_s[3]).astype(state_lru_conv.dtype)
    s5_re_p = jnp.stack(new_p[4]).astype(state_s5_re.dtype)
    s5_re_s = jnp.stack(new_s[4]).astype(state_s5_re.dtype)
    s5_im_p = jnp.stack(new_p[5]).astype(state_s5_im.dtype)
    s5_im_s = jnp.stack(new_s[5]).astype(state_s5_im.dtype)
    return (y_prompt, y_sample, ssd_p, ssd_s, ssd_conv_p, ssd_conv_s, lru_p, lru_s,
            lru_conv_p, lru_conv_s, s5_re_p, s5_re_s, s5_im_p, s5_im_s)
```

```python
import math
import os
from contextlib import ExitStack

import numpy as np
import concourse.bass as bass
import concourse.mybir as mybir
from concourse.bass_utils import run_bass_kernel_spmd

F32 = mybir.dt.float32
BF16 = mybir.dt.bfloat16
I32 = mybir.dt.int32
ALU = mybir.AluOpType
AF = mybir.ActivationFunctionType

NCORES = 8
D = 1024
NL = 4
TP = 2048
NSEQ = 16
LS = 8
TS = NSEQ * LS
TT = TP + TS
WT = 512
IN_DIM = 5136
EPS = 1e-6
TWO_PI = 2.0 * math.pi

PP_NG = 0
PP_NG2 = 8
PP_SD = 16
PP_CW = 24
PP_CB = 88
PP_LCW = 104
PP_LCB = 120
PP_LBA = 124
PP_LBX = 128
PP_LLAM = 132
PP_S5D = 136
PP_GLB = 140
PP_LRE = 144
PP_LIM = 160
PP_LDT = 176
PP_FG = 192
NPP = 200

EPOCH = 12000


class Ring:
    def __init__(self, bufs, full=None):
        self.bufs = bufs
        self.full = full
        self.i = 0

    def get(self):
        b = self.bufs[self.i % len(self.bufs)]
        self.i += 1
        return b

    def get_full(self):
        b = self.full[self.i % len(self.full)]
        self.i += 1
        return b


class KB:
    ENG = ("pe", "act", "dve", "pool", "sp")

    def __init__(self, nc, es):
        self.nc = nc
        self.es = es
        self.prog = {e: [] for e in self.ENG}
        self.cnt = {e: 0 for e in self.ENG}
        self.sem = {}
        self.nsem = 0
        for e in ("pe", "act", "dve", "pool"):
            self.sem[e] = self._newsem("c_" + e)
        self.waited = {e: {} for e in self.ENG}
        self.dead = False
        self.pending = []
        self.ncp = 0
        self.stop = int(os.environ.get("KSTOP", "-1"))
        self.lastw = {}
        self.readers = {}
        self.dsem = {}
        self.drr = {}
        for q, n in (("sp", 16), ("pool", 24), ("act", 2)):
            self.dsem[q] = [[self._newsem("d_%s%d" % (q, i)), 0] for i in range(n)]
            self.drr[q] = 0

    def _newsem(self, name):
        self.nsem += 1
        return self.es.enter_context(self.nc.semaphore("%s_%d" % (name, self.nsem)))

    slotw = {}

    def _keys(self, r):
        if isinstance(r, str):
            return [r]
        name = r.name
        w = self.slotw.get(name)
        if w is None:
            return [name]
        ap = r.ap
        off = r.offset % ap[0][0]
        hi = off + sum((c - 1) * s for s, c in ap[1:])
        return ["%s:%d" % (name, i) for i in range(off // w, hi // w + 1)]

    def defer(self, fn, depth=1):
        self.pending.append(fn)
        while len(self.pending) > depth:
            self.pending.pop(0)()

    def flush(self):
        while self.pending:
            self.pending.pop(0)()

    def cp(self, name=""):
        self.ncp += 1
        if self.stop >= 0 and self.ncp > self.stop and not self.dead:
            self.dead = True
            print("KSTOP: program truncated before checkpoint", self.ncp, name, flush=True)

    def op(self, e, fn, reads=(), writes=(), dma=False):
        if self.dead:
            return None
        waits = {}

        def need(tok, raw):
            if tok is None:
                return
            sem, val, src, isdma = tok
            if src == e and not isdma and e == "pe":
                return
            if self.waited[e].get(sem.name, 0) >= val:
                return
            if sem.name not in waits or waits[sem.name][1] < val:
                waits[sem.name] = (sem, val)

        rk = [x for r in reads for x in self._keys(r)]
        wk = [x for w in writes for x in self._keys(w)]
        for r in rk:
            need(self.lastw.get(r), True)
        for w in wk:
            need(self.lastw.get(w), False)
            for t in self.readers.get(w, {}).values():
                need(t, False)
        if dma:
            slot = self.dsem[e][self.drr[e] % len(self.dsem[e])]
            self.drr[e] += 1
            if slot[1] > 0:
                need((slot[0], slot[1], e, True), True)
            slot[1] += 16
            tok = (slot[0], slot[1], e, True)
            inc = (slot[0], 16)
        else:
            if self.cnt[e] >= EPOCH:
                self.sem[e] = self._newsem("c_" + e)
                self.cnt[e] = 0
            self.cnt[e] += 1
            tok = (self.sem[e], self.cnt[e], e, False)
            inc = (self.sem[e], 1)
        for s, v in waits.values():
            self.waited[e][s.name] = v
        self.prog[e].append((list(waits.values()), fn, inc))
        for r in rk:
            self.readers.setdefault(r, {})[tok[0].name] = tok
        for w in wk:
            self.lastw[w] = tok
            self.readers[w] = {}
        return tok

    def finish(self):
        fin = []
        for q in self.dsem:
            for sem, v in self.dsem[q]:
                if v > 0:
                    fin.append((sem, v))
        self.final_waits = fin

    def emit(self):
        nc = self.nc
        handles = {"pe": "tensor", "act": "scalar", "dve": "vector", "pool": "gpsimd", "sp": "sync"}
        with nc.Block() as block:
            for e in self.ENG:
                prog = self.prog[e]
                extra = self.final_waits if e == "sp" else []

                def body(eng, prog=prog, extra=extra):
                    for waits, fn, inc in prog:
                        for s, v in waits:
                            eng.wait_ge(s, v)
                        ins = fn(eng)
                        ins.then_inc(inc[0], inc[1])
                    for s, v in extra:
                        eng.wait_ge(s, v)

                getattr(block, handles[e])(body)

    def mm(self, out, lhsT, rhs, start=True, stop=True):
        self.op("pe", lambda t: t.matmul(out, lhsT=lhsT, rhs=rhs, start=start, stop=stop),
                [lhsT, rhs], [out])

    def tr(self, out, in_, ident):
        self.op("pe", lambda t: t.transpose(out, in_, ident), [in_, ident], [out])

    def act(self, out, in_, func, bias=None, scale=None):
        rd = [in_]
        kw = {}
        if bias is not None:
            kw["bias"] = bias
            if not isinstance(bias, (int, float)):
                rd.append(bias)
        if scale is not None:
            kw["scale"] = scale
            if not isinstance(scale, (int, float)):
                rd.append(scale)
        self.op("act", lambda a: a.activation(out=out, in_=in_, func=func, **kw), rd, [out])

    def tt(self, e, out, in0, in1, op):
        self.op(e, lambda v: v.tensor_tensor(out=out, in0=in0, in1=in1, op=op), [in0, in1], [out])

    def ts(self, e, out, in0, s1, s2, op0, op1=None):
        rd = [in0]
        for s in (s1, s2):
            if s is not None and not isinstance(s, (int, float)):
                rd.append(s)
        if op1 is None:
            self.op(e, lambda v: v.tensor_scalar(out=out, in0=in0, scalar1=s1, scalar2=None, op0=op0),
                    rd, [out])
        else:
            self.op(e, lambda v: v.tensor_scalar(out=out, in0=in0, scalar1=s1, scalar2=s2, op0=op0, op1=op1),
                    rd, [out])

    def stt(self, out, in0, scalar, in1, op0, op1):
        rd = [in0, in1]
        if not isinstance(scalar, (int, float)):
            rd.append(scalar)
        self.op("dve", lambda v: v.scalar_tensor_tensor(out=out, in0=in0, scalar=scalar, in1=in1,
                                                        op0=op0, op1=op1), rd, [out])

    def scan(self, out, d0, d1, init):
        rd = [d0, d1]
        if not isinstance(init, (int, float)):
            rd.append(init)
        self.op("dve", lambda v: v.tensor_tensor_scan(out=out, data0=d0, data1=d1, initial=init,
                                                      op0=ALU.mult, op1=ALU.add), rd, [out])

    def copy(self, e, out, in_):
        if e == "act":
            self.op("act", lambda a: a.activation(out=out, in_=in_, func=AF.Copy), [in_], [out])
        else:
            self.op(e, lambda v: v.tensor_copy(out=out, in_=in_), [in_], [out])

    def memset(self, e, out, val):
        self.op(e, lambda v: v.memset(out, val), [], [out])

    def dma(self, q, out, in_, rk=(), wk=()):
        self.op(q, lambda g: g.dma_start(out=out, in_=in_), [in_] + list(rk), [out] + list(wk), dma=True)


def build_program(nlayers=NL):
    nc = bass.Bass("TRN2", target_bir_lowering=False)

    def din(name, shape, dt=F32):
        return nc.dram_tensor(name, list(shape), dt, kind="ExternalInput").ap()

    def dout(name, shape, dt=F32):
        return nc.dram_tensor(name, list(shape), dt, kind="ExternalOutput").ap()

    xin = din("xin", [8, 128, TT])
    w_in = din("w_in_r", [NL, 128, 8, IN_DIM])
    w_out = din("w_out_r", [NL, 128, 16, D])
    glu_w = din("glu_r", [NL, 128, 4, 512])
    wa_bd = din("wa_bd", [NL, 128, 4, 128])
    wx_bd = din("wx_bd", [NL, 128, 4, 128])
    pp_d = din("pp_in", [128, NL, NPP])
    pdt_d = din("pdt_in", [128, NL, 32])
    bpad_re = din("bpadT_re", [NL, 128, 16, 128])
    bpad_im = din("bpadT_im", [NL, 128, 16, 128])
    ctpad_re = din("ctpad_re", [NL, 128, 16, 128])
    ctpad_im = din("ctpad_im", [NL, 128, 16, 128])
    h0T_d = din("h0T", [NL, NSEQ, 128, 1024])
    sconv0 = din("sconv0", [NL, 128, 16, NSEQ, 3])
    lconv0 = din("lconv0", [NL, 128, 4, NSEQ, 3])
    lru0 = din("lru0", [NL, 128, 4, NSEQ])
    s5r0 = din("s5r0", [NL, 128, 16, NSEQ])
    s5i0 = din("s5i0", [NL, 128, 16, NSEQ])
    c_f32 = din("c_f32", [128, 8, 128])
    c_bf = din("c_bf", [128, 4, 128])
    c_iota = din("c_iota", [128, WT])

    yout = dout("yout", [8, 128, TT])
    ssd_p_o = dout("ssd_p_o", [NL, 128, 1024])
    ssd_s_o = dout("ssd_s_o", [NL, NSEQ, 128, 1024])
    sconv_p_o = dout("sconv_p_o", [NL, 128, 16, 3])
    sconv_s_o = dout("sconv_s_o", [NL, 128, 16, NSEQ, 3])
    lru_p_o = dout("lru_p_o", [NL, 128, 4])
    lru_s_o = dout("lru_s_o", [NL, 128, 4, NSEQ])
    lconv_p_o = dout("lconv_p_o", [NL, 128, 4, 3])
    lconv_s_o = dout("lconv_s_o", [NL, 128, 4, NSEQ, 3])
    s5r_p_o = dout("s5r_p_o", [NL, 128, 16])
    s5r_s_o = dout("s5r_s_o", [NL, 128, 16, NSEQ])
    s5i_p_o = dout("s5i_p_o", [NL, 128, 16])
    s5i_s_o = dout("s5i_s_o", [NL, 128, 16, NSEQ])
    xsc = nc.dram_tensor("xsc", [2, 8, 128, TT], F32, kind="Internal").ap()
    wbf = nc.dram_tensor("wbf", [NL, 128, 8, IN_DIM], BF16, kind="Internal").ap()
    wobf = nc.dram_tensor("wobf", [NL, 8, 128, 16, 128], BF16, kind="Internal").ap()

    with ExitStack() as es:
        k = KB(nc, es)
        k.slotw = {"F8": 512, "zs": 512, "A16": 512, "Y": 512, "xt": 512, "hn": 512, "LT": 128,
                   "hnew": 512, "h0f": 512}

        def sb(name, shape, dt):
            return es.enter_context(nc.sbuf_tensor(name, list(shape), dt))

        def ps(name, shape, dt):
            return es.enter_context(nc.psum_tensor(name, list(shape), dt))

        xt = sb("xt", [128, 8, WT], F32)
        hn = sb("hn", [128, 8, WT], BF16)
        zs = sb("zs", [128, 8, WT], BF16)
        A16 = sb("A16", [128, 16, WT], BF16)
        F8 = sb("F8", [128, 8, WT], F32)
        Y = sb("Y", [128, 16, WT], BF16)
        LT = sb("LT", [128, 16, 128], BF16)
        x_tm = sb("x_tm", [128, 1024], BF16)
        xw_tm = sb("xw_tm", [128, 1024], BF16)
        B_tm = sb("B_tm", [128, 512], BF16)
        bmring = Ring([sb("bm%d" % i, [128, 512], BF16) for i in range(1)])
        hT = sb("hT", [128, 1024], F32)
        hT_bf = sb("hT_bf", [128, 1024], BF16)
        dt_tm = sb("dt_tm", [128, 4, 16], F32)
        dtA_tm = sb("dtA_tm", [128, 4, 16], F32)
        pre = sb("pre", [128, 7, 64], F32)
        wring = Ring([sb("wbuf%d" % i, [128, 8, 512], BF16) for i in range(2)])
        woring = Ring([sb("wobuf%d" % i, [128, 16, 128], BF16) for i in range(2)])
        glu_bf = sb("glu_bf", [128, 4, 512], BF16)
        BbT_re = sb("BbT_re", [128, 16, 128], BF16)
        BbT_im = sb("BbT_im", [128, 16, 128], BF16)
        CT_re = sb("CT_re", [128, 16, 128], BF16)
        nCT_re = sb("nCT_re", [128, 16, 128], BF16)
        nCT_im = sb("nCT_im", [128, 16, 128], BF16)
        wa_bf = sb("wa_bf", [128, 4, 128], BF16)
        wx_bf = sb("wx_bf", [128, 4, 128], BF16)
        pp = sb("pp", [128, NL, NPP], F32)
        pdt = sb("pdt", [128, NL, 32], F32)
        expA = sb("expA", [128, 16], F32)
        lp = sb("lp", [128, 8, 16], F32)
        cf = sb("cf", [128, 8, 128], F32)
        cb = sb("cb", [128, 4, 128], BF16)
        iota = sb("iota", [128, WT], F32)
        iota_c = sb("iota_c", [128, WT], F32)
        carry_s = sb("carry_s", [128, 16, NSEQ, 3], F32)
        carry_l = sb("carry_l", [128, 4, NSEQ, 3], F32)
        pcar_s = sb("pcar_s", [128, 16, 3], F32)
        pcar_l = sb("pcar_l", [128, 4, 3], F32)
        pcar_h = sb("pcar_h", [128, 4], F32)
        hcar = sb("hcar", [128, 4, NSEQ], F32)
        s5car_r = sb("s5car_r", [128, 16], F32)
        s5car_i = sb("s5car_i", [128, 16], F32)
        s5p_r = sb("s5p_r", [128, 16], F32)
        s5p_i = sb("s5p_i", [128, 16], F32)
        s5o_r = sb("s5o_r", [128, 16, NSEQ], F32)
        s5o_i = sb("s5o_i", [128, 16, NSEQ], F32)
        s5h0_r = sb("s5h0_r", [128, 16, NSEQ], F32)
        s5h0_i = sb("s5h0_i", [128, 16, NSEQ], F32)
        lruh0 = sb("lruh0", [128, 4, NSEQ], F32)
        h0f = sb("h0f", [128, 1024], F32)
        h0bring = Ring([sb("h0b%d" % i, [128, 1024], BF16) for i in range(2)])
        hnew = sb("hnew", [128, 1024], F32)
        dtA_rep = h0f
        _fr = [sb("fr%d" % i, [128, WT + 4], F32) for i in range(8)]
        fring = Ring([t_[:, 0:WT] for t_ in _fr], [t_[:, :] for t_ in _fr])
        iring = Ring([sb("ir%d" % i, [128, WT], I32) for i in range(2)])
        fring_x = [h0f[:, 0:512], h0f[:, 512:1024]]
        sring2 = [hnew[:, 0:512], hnew[:, 512:1024]]
        sring = Ring([sb("sr%d" % i, [128, 128], F32) for i in range(4)])
        tring = Ring([sb("tn%d" % i, [128, 48], F32) for i in range(10)])
        dhl = sb("dhl", [128, 2, 64], BF16)
        f8ring = Ring([F8[:, i, :] for i in range(8)])
        zring = Ring([zs[:, i, :] for i in range(8)])

        pheld = ps("pheld", [128, 512], F32)
        pheld2 = ps("pheld2", [128, 512], F32)
        pheld3 = ps("pheld3", [128, 512], F32)
        pring = Ring([ps("pb%d" % i, [128, 512], F32) for i in range(4)])
        ptb = ps("ptb", [128, 1024], BF16)

        tri_b2 = cf[:, 0, :].bitcast(BF16)
        tri_p = cf[:, 1, :]
        tri_s = cf[:, 2, :]
        ones_f = cf[:, 3, :]
        ones_s = cf[:, 4, :]
        iota_s = cf[:, 5, :]
        notstart = cf[:, 6, :]
        ind = cf[:, 7, :]
        ident_b = cb[:, 0, :]
        ones_b = cb[:, 1, :]
        neg_p = cb[:, 2, :]
        neg_s = cb[:, 3, :]

        k.dma("sp", cf[:], c_f32)
        nident_b = sb("nident_b", [128, 128], BF16)
        k.dma("pool", cb[:], c_bf)
        k.dma("sp", iota[:], c_iota)
        k.dma("sp", pp[:], pp_d)
        k.dma("sp", pdt[:], pdt_d)
        k.act(nident_b[:], ident_b, AF.Copy, scale=-1.0)

        tiles = [(i * WT, WT, "p") for i in range(TP // WT)] + [(TP, TS, "s")]
        blocks = [("z", 0, 512), ("z", 512, 512), ("x", 1024, 512), ("x", 1536, 512), ("B", 2048, 512),
                  ("C", 2560, 512), ("dt", 3072, 16), ("lg", 3600, 512), ("lx", 3088, 512),
                  ("sg", 4624, 512), ("su", 4112, 512)]
        stream = [(l, ti, bi) for l in range(nlayers) for ti in range(len(tiles)) for bi in range(len(blocks))]
        wbufs = {}
        st = {"next": 0, "item": 0}

        def prefetch_w(upto):
            while st["next"] < len(stream) and st["next"] <= upto:
                l_, ti_, bi_ = stream[st["next"]]
                _, c0_, n_ = blocks[bi_]
                buf = wring.get()
                k.dma("sp", buf[:, :, 0:n_], wbf[l_][:, :, c0_:c0_ + n_], rk=["wbf%d_%d" % (l_, bi_)])
                wbufs[st["next"]] = buf
                st["next"] += 1

        def next_block():
            prefetch_w(st["item"] + 1)
            wb = wbufs.pop(st["item"])
            st["item"] += 1
            return wb

        pringA = Ring(pring.bufs + [pheld, pheld2, pheld3])

        def proj(wb, m, W):
            pm = pringA.get()
            for kt in range(8):
                k.mm(pm[:, 0:W], wb[:, kt, m * 128:(m + 1) * 128], hn[:, kt, 0:W], start=(kt == 0), stop=(kt == 7))
            return pm

        def rmsnorm_tile(src3, ncol, gcol0, l, out_fn):
            pn = pring.get()
            for kt in range(8):
                sq = zring.get()
                k.act(sq[:, 0:ncol], src3[:, kt, 0:ncol], AF.Square)
                k.mm(pn[:, 0:ncol], ones_b, sq[:, 0:ncol], start=(kt == 0), stop=(kt == 7))
            t1 = fring.get()
            k.act(t1[:, 0:ncol], pn[:, 0:ncol], AF.Ln, bias=epsc[:, 0:1], scale=1.0 / D)
            rstd = fring.get()
            k.act(rstd[:, 0:ncol], t1[:, 0:ncol], AF.Exp, scale=-0.5)
            for kt in range(8):
                k.stt(out_fn(kt), src3[:, kt, 0:ncol], pp[:, l, gcol0 + kt:gcol0 + kt + 1], rstd[:, 0:ncol],
                      ALU.mult, ALU.mult)

        def frac_sincos(u, W, sn_out, cs_out):
            ui = iring.get()
            k.copy("dve", ui[:, 0:W], u)
            r = fring.get()
            k.tt("dve", r[:, 0:W], u, ui[:, 0:W], ALU.subtract)
            k.act(sn_out, r[:, 0:W], AF.Sin, scale=TWO_PI)
            ar = fring.get()
            k.stt(ar[:, 0:W], r[:, 0:W], -1.0, r[:, 0:W], ALU.mult, ALU.max)
            k.act(cs_out, ar[:, 0:W], AF.Sin, bias=epsc[:, 1:2], scale=-TWO_PI)

        epsc = sb("epsc", [128, 4], F32)
        k.memset("dve", epsc[:, 0:1], EPS)
        k.memset("dve", epsc[:, 1:2], math.pi / 2.0)
        k.memset("dve", epsc[:, 2:3], 1.0)

        def conv_group(l, pms, W, smp, specs):
            n = len(pms)
            accs = [fring.get() for _ in range(n)]
            raws = [fring.get_full() for _ in range(n)]
            if smp:
                rawv = [r_[:, 0:NSEQ * (LS + 3)].rearrange("p (s t) -> p s t", s=NSEQ) for r_ in raws]
                pmv = [p_[:, 0:W].rearrange("p (s t) -> p s t", s=NSEQ) for p_ in pms]
                accv = [a_[:, 0:W].rearrange("p (s t) -> p s t", s=NSEQ) for a_ in accs]
                for i in range(n):
                    k.copy("dve", rawv[i][:, :, 0:3], specs[i][3])
                for i in range(n):
                    k.copy("act", rawv[i][:, :, 3:3 + LS], pmv[i])
                    k.act(accv[i], pmv[i], AF.Identity, bias=pp[:, l, specs[i][1]:specs[i][1] + 1],
                          scale=pp[:, l, specs[i][0] + 3:specs[i][0] + 4])
                for i in range(n):
                    st3 = tring.get()
                    st3v = st3[:, 0:48].rearrange("p (s t) -> p s t", s=NSEQ)
                    k.copy("dve", st3v, rawv[i][:, :, LS:LS + 3])
                    k.dma("sp", specs[i][4], st3v)
                for kk in range(3):
                    for i in range(n):
                        k.stt(accv[i], rawv[i][:, :, kk:kk + LS], pp[:, l, specs[i][0] + kk:specs[i][0] + kk + 1],
                              accv[i], ALU.mult, ALU.add)
            else:
                for i in range(n):
                    k.copy("dve", raws[i][:, 0:3], specs[i][2])
                for i in range(n):
                    k.copy("act", raws[i][:, 3:3 + W], pms[i][:, 0:W])
                    k.act(accs[i][:, 0:W], pms[i][:, 0:W], AF.Identity, bias=pp[:, l, specs[i][1]:specs[i][1] + 1],
                          scale=pp[:, l, specs[i][0] + 3:specs[i][0] + 4])
                for kk in range(3):
                    for i in range(n):
                        k.stt(accs[i][:, 0:W], raws[i][:, kk:kk + W], pp[:, l, specs[i][0] + kk:specs[i][0] + kk + 1],
                              accs[i][:, 0:W], ALU.mult, ALU.add)
                for i in range(n):
                    k.copy("dve", specs[i][2], raws[i][:, W:W + 3])
            return accs

        def ssd_stage(l, ti, W, smp):
            xc = A16
            nch = W // 128
            TRI = tri_s if smp else tri_p
            ONESM = ones_s if smp else ones_f
            NEG = neg_s if smp else neg_p
            TRIB = tri_b2[:, 128:256] if smp else tri_b2[:, 0:128]
            for ci in range(nch):
                cs = slice(ci * 128, (ci + 1) * 128)
                dtA = dtA_tm[:, ci, :]
                dtc = dt_tm[:, ci, :]
                nacum = pre[:, 1, ci * 16:(ci + 1) * 16]
                w_tm = pre[:, 3, ci * 16:(ci + 1) * 16]
                DEC = pre[:, 4, ci * 16:(ci + 1) * 16]
                k.cp("ssd A acum")
                psc = pheld
                for g in range(4):
                    k.mm(psc[:, g * 128:(g + 1) * 128], xc[:, 8 + g, cs], xc[:, 12 + g, cs])
                k.cp("ssd B scores")
                for j in range(8):
                    k.tr(ptb[:, j * 128:(j + 1) * 128], xc[:, j, cs], ident_b)
                k.cp("T1 xtr")
                k.copy("act", x_tm[:], ptb[:, :])
                k.cp("T2 xcopy")
                k.tt("dve", hnew[:].rearrange("p (h q) -> p h q", h=16), x_tm[:].rearrange("p (h q) -> p h q", h=16),
                     w_tm[:, 0:16].unsqueeze(2).to_broadcast([128, 16, 64]), ALU.mult)
                k.copy("act", xw_tm[:], hnew[:])
                k.cp("T3 xw")
                for g in range(4):
                    k.tr(ptb[:, g * 128:(g + 1) * 128], xc[:, 8 + g, cs], ident_b)
                k.cp("T4 btr")
                k.copy("act", B_tm[:], ptb[:, 0:512])
                k.cp("ssd C transposes")
                for g in range(4):
                    pab = pring.get()
                    for r in range(4):
                        h = 4 * g + r
                        cc = ci * 16 + h
                        k.mm(pab[:, r * 128:(r + 1) * 128], dhl[:, 0, cc:cc + 1].to_broadcast([128, 128]), TRIB,
                             start=True, stop=False)
                        k.mm(pab[:, r * 128:(r + 1) * 128], dhl[:, 1, cc:cc + 1].to_broadcast([128, 128]), TRIB,
                             start=False, stop=False)
                        k.mm(pab[:, r * 128:(r + 1) * 128], ident_b, NEG, start=False, stop=True)
                    for r in range(4):
                        h = 4 * g + r
                        Dh = sring.get()
                        k.act(Dh[:], pab[:, r * 128:(r + 1) * 128], AF.Exp, bias=nacum[:, h:h + 1])
                        k.stt(LT[:, h, :], Dh[:], dt_tm[:, ci, h:h + 1], psc[:, g * 128:(g + 1) * 128],
                              ALU.mult, ALU.mult)
                k.cp("ssd D LT")
                if smp:
                    EAs = [fring.get(), fring.get()]
                    k.copy("dve", dtA_rep[:].rearrange("p (h q) -> p h q", h=16),
                           dtA_tm[:, ci, :].unsqueeze(2).to_broadcast([128, 16, 64]))
                    for j in range(8):
                        pe_ = pring.get()
                        k.mm(pe_[:, 0:128], dtA_rep[:, j * 128:(j + 1) * 128], TRI)
                        k.act(EAs[j // 4][:, (j % 4) * 128:(j % 4 + 1) * 128], pe_[:, 0:128], AF.Exp)
                    pdS = pring.get()
                    for b_ in range(NSEQ):
                        k.mm(pdS[:, b_ * 16:(b_ + 1) * 16], ind[:, b_:b_ + 1].to_broadcast([128, 128]), dtA)
                    DECS = fring.get()
                    k.act(DECS[:, 0:256], pdS[:, 0:256], AF.Exp)
                    hx = Ring([h0f, hnew])
                    bufs = [hx.get()]
                    k.dma("sp", bufs[0][:], h0T_d[l, 0])
                    for b_ in range(NSEQ):
                        hb = bufs[b_]
                        if b_ + 1 < NSEQ:
                            nb_ = hx.get()
                            bufs.append(nb_)
                            k.dma("sp", nb_[:], h0T_d[l, b_ + 1])
                        h0b = h0bring.get()
                        k.copy("act", h0b[:], hb[:])
                        for j in range(8):
                            pyo = pheld2 if j < 4 else pheld3
                            jj = j % 4
                            k.mm(pyo[:, jj * 128 + b_ * 8: jj * 128 + b_ * 8 + 8], h0b[:, j * 128:(j + 1) * 128],
                                 xc[:, 12 + j // 2, b_ * 8:(b_ + 1) * 8])
                        Bm = bmring.get()
                        k.ts("dve", Bm[:], B_tm[:], ind[:, b_:b_ + 1], None, ALU.mult)
                        pS0 = pring.get()
                        pS1 = pring.get()
                        for g in range(4):
                            pS = pS0 if g < 2 else pS1
                            k.mm(pS[:, (g % 2) * 256:(g % 2 + 1) * 256], Bm[:, g * 128:(g + 1) * 128],
                                 xw_tm[:, g * 256:(g + 1) * 256])
                        hb3 = hb[:].rearrange("p (h q) -> p h q", h=16)
                        k.tt("dve", hb3, hb3, DECS[:, b_ * 16:(b_ + 1) * 16].unsqueeze(2).to_broadcast([128, 16, 64]),
                             ALU.mult)
                        k.tt("dve", hb[:, 0:512], hb[:, 0:512], pS0[:, :], ALU.add)
                        k.tt("dve", hb[:, 512:1024], hb[:, 512:1024], pS1[:, :], ALU.add)
                        k.dma("sp", ssd_s_o[l, b_], hb[:])
                    yo0 = fring.get()
                    yo1 = fring.get()
                    k.copy("act", yo0[:], pheld2[:, :])
                    k.copy("act", yo1[:], pheld3[:, :])
                if not smp:
                    k.copy("dve", dtA_rep[:].rearrange("p (h q) -> p h q", h=16),
                           dtA_tm[:, ci, :].unsqueeze(2).to_broadcast([128, 16, 64]))
                for j in range(8):
                    py = pring.get()
                    if not smp:
                        k.mm(py[:, 256:384], dtA_rep[:, j * 128:(j + 1) * 128], TRI)
                    k.mm(py[0:64, 0:128], x_tm[:, (2 * j) * 64:(2 * j + 1) * 64], LT[:, 2 * j, :])
                    k.mm(py[64:128, 0:128], x_tm[:, (2 * j + 1) * 64:(2 * j + 2) * 64], LT[:, 2 * j + 1, :])
                    tmp = sring.get()
                    if smp:
                        yo = (yo0 if j < 4 else yo1)[:, (j % 4) * 128:(j % 4 + 1) * 128]
                        k.tt("dve", tmp[:], yo, EAs[j // 4][:, (j % 4) * 128:(j % 4 + 1) * 128], ALU.mult)
                    else:
                        k.mm(py[:, 128:256], hT_bf[:, j * 128:(j + 1) * 128], xc[:, 12 + j // 2, cs])
                        EA = sring.get()
                        k.act(EA[:], py[:, 256:384], AF.Exp)
                        k.tt("dve", tmp[:], py[:, 128:256], EA[:], ALU.mult)
                    k.defer(lambda j=j, py=py, tmp=tmp, cs=cs: k.tt("dve", F8[:, j, cs], py[:, 0:128], tmp[:], ALU.add),
                            depth=1)
                k.flush()
                k.cp("ssd E y")
                if not smp:
                    for g in range(4):
                        pS = pheld2 if g < 2 else pheld3
                        k.mm(pS[:, (g % 2) * 256:(g % 2 + 1) * 256], B_tm[:, g * 128:(g + 1) * 128],
                             xw_tm[:, g * 256:(g + 1) * 256])
                    hT3 = hT[:].rearrange("p (h q) -> p h q", h=16)
                    k.tt("dve", hT3, hT3, DEC[:, 0:16].unsqueeze(2).to_broadcast([128, 16, 64]), ALU.mult)
                    k.tt("dve", hT[:, 0:512], hT[:, 0:512], pheld2[:, :], ALU.add)
                    k.tt("dve", hT[:, 512:1024], hT[:, 512:1024], pheld3[:, :], ALU.add)
                    k.copy("act", hT_bf[:], hT[:])
            k.cp("ssd F chunks done")
            pn = pring.get()
            for j in range(8):
                k.stt(F8[:, j, 0:W], xc[:, j, 0:W], pp[:, l, PP_SD + j:PP_SD + j + 1], F8[:, j, 0:W], ALU.mult, ALU.add)
            for j in range(8):
                k.tt("dve", F8[:, j, 0:W], F8[:, j, 0:W], zs[:, j, 0:W], ALU.mult)
            for j in range(8):
                sq = fring.get()
                sqb = sq[:, 0:WT // 2].bitcast(BF16)
                k.act(sqb[:, 0:W], F8[:, j, 0:W], AF.Square)
                k.mm(pn[:, 0:W], ones_b, sqb[:, 0:W], start=(j == 0), stop=(j == 7))
            t1 = fring.get()
            k.act(t1[:, 0:W], pn[:, 0:W], AF.Ln, bias=epsc[:, 0:1], scale=1.0 / D)
            rstd = fring.get()
            k.act(rstd[:, 0:W], t1[:, 0:W], AF.Exp, scale=-0.5)
            for j in range(8):
                k.stt(Y[:, j, 0:W], F8[:, j, 0:W], pp[:, l, PP_NG2 + j:PP_NG2 + j + 1], rstd[:, 0:W], ALU.mult, ALU.mult)
            if (not smp) and ti == len(tiles) - 2:
                k.dma("sp", ssd_p_o[l], hT[:])

        def lru_block(l, j, acc, W, smp):
            xr_bf = zring.get()
            k.copy("act", xr_bf[:, 0:W], acc[:, 0:W])
            pr = pring.get()
            k.mm(pr[:, 0:W], wa_bf[:, j, :], xr_bf[:, 0:W])
            pg = pring.get()
            k.mm(pg[:, 0:W], wx_bf[:, j, :], xr_bf[:, 0:W])
            r = fring.get()
            k.act(r[:, 0:W], pr[:, 0:W], AF.Sigmoid, bias=pp[:, l, PP_LBA + j:PP_LBA + j + 1])
            gi = fring.get()
            k.act(gi[:, 0:W], pg[:, 0:W], AF.Sigmoid, bias=pp[:, l, PP_LBX + j:PP_LBX + j + 1])
            a = f8ring.get()
            k.act(a[:, 0:W], r[:, 0:W], AF.Exp, scale=lp[:, 0, j:j + 1])

            def stage2():
                k.tt("dve", r[:, 0:W], a[:, 0:W], a[:, 0:W], ALU.mult)
                k.ts("dve", r[:, 0:W], r[:, 0:W], 1.0, -1.0, ALU.min, ALU.mult)
                k.act(r[:, 0:W], r[:, 0:W], AF.Sqrt, bias=epsc[:, 2:3])
                k.tt("dve", gi[:, 0:W], gi[:, 0:W], r[:, 0:W], ALU.mult)
                k.tt("dve", gi[:, 0:W], gi[:, 0:W], acc[:, 0:W], ALU.mult)
                hs = f8ring.get()
                if smp:
                    am = f8ring.get()
                    k.tt("dve", am[:, 0:W], a[:, 0:W], notstart, ALU.mult)
                    t = tring.get()
                    k.tt("dve", t[:, 0:16], a[:, 0:W:LS], lruh0[:, j, :], ALU.mult)
                    k.tt("dve", gi[:, 0:W:LS], gi[:, 0:W:LS], t[:, 0:16], ALU.add)
                    k.scan(hs[:, 0:W], am[:, 0:W], gi[:, 0:W], 0.0)
                    k.copy("dve", hcar[:, j, :], hs[:, LS - 1:W:LS])
                else:
                    k.scan(hs[:, 0:W], a[:, 0:W], gi[:, 0:W], pcar_h[:, j:j + 1])
                    k.copy("dve", pcar_h[:, j:j + 1], hs[:, W - 1:W])
                k.tt("dve", Y[:, 8 + j, 0:W], hs[:, 0:W], A16[:, j, 0:W], ALU.mult)

            k.defer(stage2, depth=1)

        def s5_stage(l, ti, c0, W, smp):
            last_p = (not smp) and ti == len(tiles) - 2
            if not smp:
                k.ts("dve", iota_c[:, 0:W], iota[:, 0:W], float(c0), None, ALU.add)
            tsrc = iota_s if smp else iota_c[:, 0:W]
            tabring = Ring([A16[:, 0, :], A16[:, 1, :], A16[:, 2, :], A16[:, 3, :], Y[:, 14, :], Y[:, 15, :]])
            srring = Ring([F8[:, 0, :], F8[:, 2, :]])
            siring = Ring([F8[:, 1, :], F8[:, 3, :]])
            tprod = [zs[:, i, :] for i in range(4)]
            mprod = [zs[:, 4 + i, :] for i in range(4)]
            Srb = Y[:, 12, :]
            Sib = Y[:, 13, :]
            pGr = pheld2
            pGi = pheld3

            def tables(pr_):
                thc = lp[:, 1, pr_:pr_ + 1]
                u = fring.get()
                ui = iring.get()
                k.ts("dve", ui[:, 0:W], tsrc, thc, None, ALU.mult)
                k.stt(u[:, 0:W], tsrc, thc, ui[:, 0:W], ALU.mult, ALU.subtract)
                uf = fring.get()
                sn = tabring.get()
                cs = tabring.get()
                k.act(sn[:, 0:W], u[:, 0:W], AF.Sin, scale=TWO_PI)
                k.act(uf[:, 0:W], u[:, 0:W], AF.Abs)
                k.act(cs[:, 0:W], uf[:, 0:W], AF.Sin, bias=epsc[:, 1:2], scale=-TWO_PI)
                return sn, cs

            def make_tail(pr_, q, py5, sn, cs, Sr, Si):
                def tail():
                    m1, m2, m3, m4 = mprod
                    k.tt("pool", m1[:, 0:W], cs[:, 0:W], Srb[:, 0:W], ALU.mult)
                    k.tt("pool", m2[:, 0:W], sn[:, 0:W], Sib[:, 0:W], ALU.mult)
                    k.tt("pool", m3[:, 0:W], cs[:, 0:W], Sib[:, 0:W], ALU.mult)
                    k.tt("pool", m4[:, 0:W], sn[:, 0:W], Srb[:, 0:W], ALU.mult)
                    k.mm(py5[:, 0:W], CT_re[:, pr_, :], m1[:, 0:W], start=(q == 0), stop=False)
                    k.mm(py5[:, 0:W], nCT_re[:, pr_, :], m2[:, 0:W], start=False, stop=False)
                    k.mm(py5[:, 0:W], nCT_im[:, pr_, :], m3[:, 0:W], start=False, stop=False)
                    k.mm(py5[:, 0:W], nCT_im[:, pr_, :], m4[:, 0:W], start=False, stop=(q == 3))
                    if smp or last_p:
                        if smp:
                            sel = slice(LS - 1, W, LS)
                            n_ = NSEQ
                            dr = s5o_r[:, pr_, :]
                            di = s5o_i[:, pr_, :]
                        else:
                            sel = slice(W - 1, W)
                            n_ = 1
                            dr = s5p_r[:, pr_:pr_ + 1]
                            di = s5p_i[:, pr_:pr_ + 1]
                        ta = tring.get()
                        tb2 = tring.get()
                        tc2 = tring.get()
                        td2 = tring.get()
                        k.tt("dve", ta[:, 0:n_], cs[:, sel], Sr[:, sel], ALU.mult)
                        k.tt("dve", tb2[:, 0:n_], sn[:, sel], Si[:, sel], ALU.mult)
                        k.tt("dve", tc2[:, 0:n_], cs[:, sel], Si[:, sel], ALU.mult)
                        k.tt("dve", td2[:, 0:n_], sn[:, sel], Sr[:, sel], ALU.mult)
                        k.tt("dve", dr, ta[:, 0:n_], tb2[:, 0:n_], ALU.subtract)
                        k.tt("dve", di, tc2[:, 0:n_], td2[:, 0:n_], ALU.add)
                return tail

            nxt = tables(0)
            prev_tail = None
            for kt in range(4):
                py5 = pheld
                ub = A16[:, 8 + kt, 0:W]
                for q in range(4):
                    pr_ = kt * 4 + q
                    pbr = pring.get()
                    k.mm(pbr[:, 0:W], BbT_re[:, pr_, :], ub)
                    pbi = pring.get()
                    k.mm(pbi[:, 0:W], BbT_im[:, pr_, :], ub)
                    sn, cs = nxt
                    if pr_ + 1 < 16:
                        nxt = tables(pr_ + 1)
                    t1, t2, t3, t4 = tprod
                    k.tt("dve", t1[:, 0:W], pbr[:, 0:W], cs[:, 0:W], ALU.mult)
                    k.tt("dve", t2[:, 0:W], pbi[:, 0:W], sn[:, 0:W], ALU.mult)
                    k.tt("dve", t3[:, 0:W], pbi[:, 0:W], cs[:, 0:W], ALU.mult)
                    k.tt("dve", t4[:, 0:W], pbr[:, 0:W], sn[:, 0:W], ALU.mult)
                    k.mm(pGr[:, 0:W], ident_b, t1[:, 0:W], start=True, stop=False)
                    k.mm(pGr[:, 0:W], ident_b, t2[:, 0:W], start=False, stop=True)
                    k.mm(pGi[:, 0:W], ident_b, t3[:, 0:W], start=True, stop=False)
                    k.mm(pGi[:, 0:W], nident_b[:], t4[:, 0:W], start=False, stop=True)
                    if prev_tail is not None:
                        prev_tail()
                        prev_tail = None
                    Sr = srring.get()
                    Si = siring.get()
                    if smp:
                        ar_ = lp[:, 3, pr_:pr_ + 1]
                        ai_ = lp[:, 4, pr_:pr_ + 1]
                        h0r = s5h0_r[:, pr_, :]
                        h0i = s5h0_i[:, pr_, :]
                        tb = tring.get()
                        k.ts("dve", tb[:, 0:16], h0i, ai_, None, ALU.mult)
                        injr = tring.get()
                        k.stt(injr[:, 0:16], h0r, ar_, tb[:, 0:16], ALU.mult, ALU.subtract)
                        tc = tring.get()
                        k.ts("dve", tc[:, 0:16], h0r, ai_, None, ALU.mult)
                        inji = tring.get()
                        k.stt(inji[:, 0:16], h0i, ar_, tc[:, 0:16], ALU.mult, ALU.add)
                        k.tt("dve", pGr[:, 0:W:LS], pGr[:, 0:W:LS], injr[:, 0:16], ALU.add)
                        k.tt("dve", pGi[:, 0:W:LS], pGi[:, 0:W:LS], inji[:, 0:16], ALU.add)
                        magm = sring.get()
                        k.ts("dve", magm[:], notstart, lp[:, 2, pr_:pr_ + 1], None, ALU.mult)
                        k.scan(Sr[:, 0:W], magm[:], pGr[:, 0:W], 0.0)
                        k.scan(Si[:, 0:W], magm[:], pGi[:, 0:W], 0.0)
                    else:
                        magb = lp[:, 2, pr_:pr_ + 1].to_broadcast([128, W])
                        k.scan(Sr[:, 0:W], magb, pGr[:, 0:W], s5car_r[:, pr_:pr_ + 1])
                        k.scan(Si[:, 0:W], magb, pGi[:, 0:W], s5car_i[:, pr_:pr_ + 1])
                        k.copy("dve", s5car_r[:, pr_:pr_ + 1], Sr[:, W - 1:W])
                        k.copy("dve", s5car_i[:, pr_:pr_ + 1], Si[:, W - 1:W])
                    k.copy("act", Srb[:, 0:W], Sr[:, 0:W])
                    k.copy("act", Sib[:, 0:W], Si[:, 0:W])
                    prev_tail = make_tail(pr_, q, py5, sn, cs, Sr, Si)
                    if q == 3:
                        prev_tail()
                        prev_tail = None
                ys = fring.get()
                k.stt(ys[:, 0:W], ub, pp[:, l, PP_S5D + kt:PP_S5D + kt + 1], py5[:, 0:W], ALU.mult, ALU.add)
                x2 = fring.get()
                k.act(x2[:, 0:W], ys[:, 0:W], AF.Square)
                k.ts("dve", x2[:, 0:W], x2[:, 0:W], 0.044715, 1.0, ALU.mult, ALU.add)
                k.tt("dve", x2[:, 0:W], x2[:, 0:W], ys[:, 0:W], ALU.mult)
                k.act(x2[:, 0:W], x2[:, 0:W], AF.Sigmoid, scale=1.5957691216057308)
                k.tt("dve", A16[:, 12 + kt, 0:W], ys[:, 0:W], x2[:, 0:W], ALU.mult)
            for m in range(4):
                pg = pring.get()
                for kt in range(4):
                    k.mm(pg[:, 0:W], glu_bf[:, kt, m * 128:(m + 1) * 128], A16[:, 12 + kt, 0:W],
                         start=(kt == 0), stop=(kt == 3))
                sg = fring.get()
                k.act(sg[:, 0:W], pg[:, 0:W], AF.Sigmoid, bias=pp[:, l, PP_GLB + m:PP_GLB + m + 1])
                k.tt("dve", sg[:, 0:W], sg[:, 0:W], A16[:, 12 + m, 0:W], ALU.mult)
                k.tt("dve", Y[:, 12 + m, 0:W], sg[:, 0:W], A16[:, 4 + m, 0:W], ALU.mult)

        def convert_w_in(l_):
            for bi_, (_, c0_, n_) in enumerate(blocks):
                k.dma("pool", wbf[l_][:, :, c0_:c0_ + n_], w_in[l_][:, :, c0_:c0_ + n_],
                      wk=["wbf%d_%d" % (l_, bi_)])

        def convert_w_out(l_):
            for m_ in range(8):
                k.dma("pool", wobf[l_, m_], w_out[l_][:, :, m_ * 128:(m_ + 1) * 128], wk=["wobf%d_%d" % (l_, m_)])

        for l in range(nlayers):
            k.dma("pool", glu_bf[:], glu_w[l])
            k.dma("pool", wa_bf[:], wa_bd[l])
            k.dma("pool", wx_bf[:], wx_bd[l])
            k.dma("pool", CT_re[:], ctpad_re[l])
            k.dma("pool", nCT_im[:], ctpad_im[l])
            if l == 0:
                convert_w_in(0)
            k.act(nCT_re[:].rearrange("p a b -> p (a b)"), CT_re[:].rearrange("p a b -> p (a b)"), AF.Copy, scale=-1.0)
            k.act(nCT_im[:].rearrange("p a b -> p (a b)"), nCT_im[:].rearrange("p a b -> p (a b)"), AF.Copy, scale=-1.0)
            k.act(expA[:], pdt[:, l, 16:32], AF.Exp)
            t = tring.get()
            k.act(t[:, 0:4], pp[:, l, PP_LLAM:PP_LLAM + 4], AF.Exp, scale=-1.0)
            t2 = tring.get()
            k.act(t2[:, 0:4], t[:, 0:4], AF.Ln, bias=epsc[:, 2:3])
            k.ts("dve", lp[:, 0, 0:4], t2[:, 0:4], -8.0, None, ALU.mult)
            dl = tring.get()
            k.act(dl[:, 0:16], pp[:, l, PP_LDT:PP_LDT + 16], AF.Exp)
            thp = lp[:, 1, :]
            k.stt(thp, pp[:, l, PP_LIM:PP_LIM + 16], 1.0 / TWO_PI, dl[:, 0:16], ALU.mult, ALU.mult)
            lm = tring.get()
            k.tt("dve", lm[:, 0:16], pp[:, l, PP_LRE:PP_LRE + 16], dl[:, 0:16], ALU.mult)
            mag = lp[:, 2, :]
            k.act(mag, lm[:, 0:16], AF.Exp)
            sn0 = tring.get()
            cs0 = tring.get()
            frac_sincos(thp, 16, sn0[:, 0:16], cs0[:, 0:16])
            k.tt("dve", lp[:, 3, :], mag, cs0[:, 0:16], ALU.mult)
            k.tt("dve", lp[:, 4, :], mag, sn0[:, 0:16], ALU.mult)
            lre_c = pp[:, l, PP_LRE:PP_LRE + 16]
            lim_c = pp[:, l, PP_LIM:PP_LIM + 16]
            nr = tring.get()
            k.ts("dve", nr[:, 0:16], lp[:, 3, :], -1.0, None, ALU.add)
            den = tring.get()
            t_a = tring.get()
            k.tt("dve", den[:, 0:16], lre_c, lre_c, ALU.mult)
            k.tt("dve", t_a[:, 0:16], lim_c, lim_c, ALU.mult)
            k.tt("dve", den[:, 0:16], den[:, 0:16], t_a[:, 0:16], ALU.add)
            k.op("dve", lambda v, o=den[:, 0:16]: v.reciprocal(out=o, in_=o), [den], [den])
            cre = lp[:, 5, :]
            cim = lp[:, 6, :]
            t_b = tring.get()
            k.tt("dve", cre, nr[:, 0:16], lre_c, ALU.mult)
            k.tt("dve", t_b[:, 0:16], lp[:, 4, :], lim_c, ALU.mult)
            k.tt("dve", cre, cre, t_b[:, 0:16], ALU.add)
            k.tt("dve", cre, cre, den[:, 0:16], ALU.mult)
            t_c = tring.get()
            k.tt("dve", cim, lp[:, 4, :], lre_c, ALU.mult)
            k.tt("dve", t_c[:, 0:16], nr[:, 0:16], lim_c, ALU.mult)
            k.tt("dve", cim, cim, t_c[:, 0:16], ALU.subtract)
            k.tt("dve", cim, cim, den[:, 0:16], ALU.mult)
            for c4 in range(4):
                bre = fring.get()
                bim = fring.get()
                k.dma("sp", bre[:].rearrange("p (a b) -> p a b", a=4), bpad_re[l][:, c4 * 4:(c4 + 1) * 4, :])
                k.dma("sp", bim[:].rearrange("p (a b) -> p a b", a=4), bpad_im[l][:, c4 * 4:(c4 + 1) * 4, :])
                crb = cre[:, c4 * 4:(c4 + 1) * 4].unsqueeze(2).to_broadcast([128, 4, 128])
                cib = cim[:, c4 * 4:(c4 + 1) * 4].unsqueeze(2).to_broadcast([128, 4, 128])
                v3 = lambda ap_: ap_.rearrange("p (a b) -> p a b", a=4)
                m_a, m_b, m_c, m_d = [f8ring.get() for _ in range(4)]
                k.tt("dve", v3(m_a), v3(bre[:]), crb, ALU.mult)
                k.tt("dve", v3(m_b), v3(bim[:]), cib, ALU.mult)
                k.tt("dve", v3(m_c), v3(bre[:]), cib, ALU.mult)
                k.tt("dve", v3(m_d), v3(bim[:]), crb, ALU.mult)
                o_re = zring.get()
                o_im = zring.get()
                k.tt("dve", o_re, m_a, m_b, ALU.subtract)
                k.tt("dve", o_im, m_c, m_d, ALU.add)
                for i4 in range(4):
                    k.tr(ptb[:, i4 * 128:(i4 + 1) * 128], o_re[:, i4 * 128:(i4 + 1) * 128], ident_b)
                    k.tr(ptb[:, 512 + i4 * 128:512 + (i4 + 1) * 128], o_im[:, i4 * 128:(i4 + 1) * 128], ident_b)
                k.copy("act", BbT_re[:, c4 * 4:(c4 + 1) * 4, :].rearrange("p a b -> p (a b)"), ptb[:, 0:512])
                k.copy("act", BbT_im[:, c4 * 4:(c4 + 1) * 4, :].rearrange("p a b -> p (a b)"), ptb[:, 512:1024])
            k.dma("sp", carry_s[:], sconv0[l])
            k.dma("sp", carry_l[:], lconv0[l])
            k.dma("sp", lruh0[:], lru0[l])
            k.dma("sp", s5h0_r[:], s5r0[l])
            k.dma("sp", s5h0_i[:], s5i0[l])
            k.memset("dve", hT[:], 0.0)
            k.memset("dve", hT_bf[:], 0.0)
            k.memset("dve", pcar_s[:], 0.0)
            k.memset("dve", pcar_l[:], 0.0)
            k.memset("dve", pcar_h[:], 0.0)
            k.memset("dve", s5car_r[:], 0.0)
            k.memset("dve", s5car_i[:], 0.0)

            if l == 0:
                prefetch_w(1)
            k.cp("setup done l%d" % l)
            for ti, (c0, W, kind) in enumerate(tiles):
                smp = kind == "s"
                k.cp("tile start l%d t%d" % (l, ti))
                if l == 0:
                    k.dma("sp", xt[:, :, 0:W], xin.rearrange("k p t -> p k t")[:, :, c0:c0 + W])
                else:
                    k.dma("sp", xt[:, :, 0:W], xsc[(l - 1) % 2].rearrange("k p t -> p k t")[:, :, c0:c0 + W],
                          rk=["xsc%d_%d" % ((l - 1) % 2, ti)])
                rmsnorm_tile(xt, W, PP_NG, l, lambda kt: hn[:, kt, 0:W])
                k.cp("norm done")
                for zb in range(2):
                    wb = next_block()
                    for m in range(4):
                        pm = proj(wb, m, W)
                        k.act(zs[:, zb * 4 + m, 0:W], pm[:, 0:W], AF.Silu)
                for xb in range(4):
                    wb = next_block()
                    for half in range(2):
                        js = [xb * 4 + half * 2 + i for i in range(2)]
                        pms = [proj(wb, half * 2 + i, W) for i in range(2)]
                        accs = conv_group(l, pms, W, smp,
                                          [(PP_CW + 4 * j, PP_CB + j, pcar_s[:, j, :], carry_s[:, j, :, :],
                                            sconv_s_o[l][:, j, :, :]) for j in js])

                        def silus(js=js, accs=accs):
                            for j, acc in zip(js, accs):
                                k.act(A16[:, j, 0:W], acc[:, 0:W], AF.Silu)
                        k.defer(silus, depth=1)
                wb = next_block()
                k.flush()
                nchk = W // 128
                pd = pring.get()
                for ci in range(nchk):
                    for kt in range(8):
                        k.mm(pd[:, ci * 16:(ci + 1) * 16], hn[:, kt, ci * 128:(ci + 1) * 128], wb[:, kt, 0:16],
                             start=(kt == 0), stop=(kt == 7))
                dt2 = dt_tm[:, 0:nchk, :]
                dtA2 = dtA_tm[:, 0:nchk, :]
                pd3 = pd[:, 0:nchk * 16].rearrange("p (c h) -> p c h", c=nchk)
                v = pre[:, 6, 0:nchk * 16].rearrange("p (c h) -> p c h", c=nchk)
                k.tt("dve", v, pd3, pdt[:, l, 0:16].unsqueeze(1).to_broadcast([128, nchk, 16]), ALU.add)
                k.act(v, v, AF.Exp)
                k.act(dt2, v, AF.Ln, bias=epsc[:, 2:3])
                k.stt(dtA2, dt2, -1.0, expA[:].unsqueeze(1).to_broadcast([128, nchk, 16]), ALU.mult, ALU.mult)
                hi_f = pre[:, 5, 0:nchk * 16]
                k.copy("act", dhl[:, 0, 0:nchk * 16], dtA_tm[:, 0:nchk, :].rearrange("p c h -> p (c h)"))
                k.copy("act", hi_f, dhl[:, 0, 0:nchk * 16])
                k.tt("dve", hi_f, dtA_tm[:, 0:nchk, :].rearrange("p c h -> p (c h)"), hi_f, ALU.subtract)
                k.copy("act", dhl[:, 1, 0:nchk * 16], hi_f)
                TRI_ = tri_s if smp else tri_p
                ONESM_ = ones_s if smp else ones_f
                n16 = nchk * 16
                pa = pring.get()
                dtA_flat = dtA_tm[:, 0:nchk, :].rearrange("p c h -> p (c h)")
                k.mm(pa[:, 0:n16], TRI_, dtA_flat)
                k.mm(pa[:, 64:64 + n16], ONESM_, dtA_flat)
                k.copy("act", pre[:, 0, 0:n16], pa[:, 0:n16])
                k.ts("dve", pre[:, 1, 0:n16], pa[:, 0:n16], -1.0, None, ALU.mult)
                k.tt("dve", pre[:, 2, 0:n16], pa[:, 64:64 + n16], pre[:, 0, 0:n16], ALU.subtract)
                k.act(pre[:, 2, 0:n16], pre[:, 2, 0:n16], AF.Exp)
                k.tt("dve", pre[:, 3, 0:n16], pre[:, 2, 0:n16], dt_tm[:, 0:nchk, :].rearrange("p c h -> p (c h)"),
                     ALU.mult)
                k.act(pre[:, 4, 0:n16], pa[:, 64:64 + n16], AF.Exp)
                if l == 0 and ti == 0:
                    convert_w_out(0)
                k.cp("stage A done")
                ssd_stage(l, ti, W, smp)
                k.cp("ssd done")
                wb = next_block()
                for m in range(4):
                    pm = proj(wb, m, W)
                    k.act(A16[:, m, 0:W], pm[:, 0:W], AF.Silu)
                wb = next_block()
                for m in range(4):
                    pm = proj(wb, m, W)
                    acc = conv_group(l, [pm], W, smp, [(PP_LCW + 4 * m, PP_LCB + m, pcar_l[:, m, :],
                                                         carry_l[:, m, :, :], lconv_s_o[l][:, m, :, :])])[0]
                    lru_block(l, m, acc, W, smp)
                k.cp("lru done")
                k.flush()
                wb = next_block()
                for m in range(4):
                    pm = proj(wb, m, W)
                    k.act(A16[:, 4 + m, 0:W], pm[:, 0:W], AF.Silu)
                wb = next_block()
                for m in range(4):
                    pm = proj(wb, m, W)
                    k.copy("act", A16[:, 8 + m, 0:W], pm[:, 0:W])
                s5_stage(l, ti, c0, W, smp)
                if l + 1 < nlayers and ti == 0:
                    convert_w_in(l + 1)
                if l + 1 < nlayers and ti == 1:
                    convert_w_out(l + 1)
                k.cp("s5 done")
                wo = woring.get()
                k.dma("sp", wo[:], wobf[l, 0], rk=["wobf%d_0" % l])
                for m in range(8):
                    if m + 1 < 8:
                        wo_n = woring.get()
                        k.dma("sp", wo_n[:], wobf[l, m + 1], rk=["wobf%d_%d" % (l, m + 1)])
                    po = pring.get()
                    for kt in range(16):
                        k.mm(po[:, 0:W], wo[:, kt, :], Y[:, kt, 0:W], start=(kt == 0), stop=(kt == 15))
                    k.tt("dve", xt[:, m, 0:W], po[:, 0:W], xt[:, m, 0:W], ALU.add)
                    if m + 1 < 8:
                        wo = wo_n
                if l < nlayers - 1:
                    k.dma("sp", xsc[l % 2].rearrange("k p t -> p k t")[:, :, c0:c0 + W], xt[:, :, 0:W],
                          wk=["xsc%d_%d" % (l % 2, ti)])
                else:
                    rmsnorm_tile(xt, W, PP_FG, l, lambda kt: F8[:, kt, 0:W])
                    k.dma("sp", yout.rearrange("k p t -> p k t")[:, :, c0:c0 + W], F8[:, :, 0:W])
            k.dma("sp", sconv_p_o[l], pcar_s[:])
            k.dma("sp", lconv_p_o[l], pcar_l[:])
            k.dma("sp", lru_p_o[l], pcar_h[:])
            k.dma("sp", lru_s_o[l], hcar[:])
            k.dma("sp", s5r_p_o[l], s5p_r[:])
            k.dma("sp", s5i_p_o[l], s5p_i[:])
            k.dma("sp", s5r_s_o[l], s5o_r[:])
            k.dma("sp", s5i_s_o[l], s5o_i[:])
        k.finish()
        k.emit()
    return nc


def _consts():
    i = np.arange(128)
    ident = np.eye(128, dtype=np.float32)
    tri_p = (i[:, None] <= i[None, :]).astype(np.float32)
    same = (i[:, None] // LS == i[None, :] // LS)
    tri_s = (tri_p > 0) & same
    cfa = np.zeros((128, 8, 128), np.float32)
    import ml_dtypes
    trib = np.concatenate([tri_p, tri_s.astype(np.float32)], axis=1).astype(ml_dtypes.bfloat16)
    cfa[:, 0] = np.ascontiguousarray(trib).view(np.float32)
    cfa[:, 1] = tri_p
    cfa[:, 2] = tri_s
    cfa[:, 3] = 1.0
    cfa[:, 4] = same
    cfa[:, 5] = np.broadcast_to((i % LS)[None, :], (128, 128))
    cfa[:, 6] = np.broadcast_to((i % LS != 0)[None, :], (128, 128))
    cfa[:, 7, 0:NSEQ] = (i[:, None] // LS == np.arange(NSEQ)[None, :])
    cba = np.zeros((128, 4, 128), np.float32)
    cba[:, 0] = ident
    cba[:, 1] = 1.0
    cba[:, 2] = np.where(tri_p > 0, 0.0, -30000.0)
    cba[:, 3] = np.where(tri_s, 0.0, -30000.0)
    iota = np.broadcast_to(np.arange(WT, dtype=np.float32)[None, :], (128, WT)).copy()
    return cfa, cba, iota


def _fm(v, nt):
    return np.moveaxis(v.reshape(v.shape[:-1] + (nt, 128)), -1, -2)


def _prep_shared(inp):
    f = lambda a: np.ascontiguousarray(a, dtype=np.float32)
    sh = {}
    sh["w_in_r"] = f(inp["w_in"].reshape(NL, 8, 128, IN_DIM).transpose(0, 2, 1, 3))
    sh["w_out_r"] = f(inp["w_out"].reshape(NL, 16, 128, D).transpose(0, 2, 1, 3))
    sh["glu_r"] = f(inp["s5_glu_w"].reshape(NL, 4, 128, 512).transpose(0, 2, 1, 3))
    for nm, src in (("wa_bd", inp["lru_wa"]), ("wx_bd", inp["lru_wx"])):
        bd = np.zeros((NL, 128, 4, 128), np.float32)
        for m in range(4):
            for k2 in range(2):
                bd[:, k2 * 64:(k2 + 1) * 64, m, k2 * 64:(k2 + 1) * 64] = src[:, 2 * m + k2]
        sh[nm] = bd
    pp = np.zeros((128, NL, NPP), np.float32)
    for l in range(NL):
        pp[:, l, PP_NG:PP_NG + 8] = _fm(inp["norm_g"][l], 8)
        pp[:, l, PP_NG2:PP_NG2 + 8] = _fm(inp["ssd_norm_g"][l], 8)
        pp[:, l, PP_SD:PP_SD + 8] = _fm(np.repeat(inp["ssd_d"][l], 64), 8)
        pp[:, l, PP_CW:PP_CW + 64] = inp["ssd_conv_w"][l].reshape(4, 16, 128).transpose(2, 1, 0).reshape(128, 64)
        pp[:, l, PP_CB:PP_CB + 16] = _fm(inp["ssd_conv_b"][l], 16)
        pp[:, l, PP_LCW:PP_LCW + 16] = inp["lru_conv_w"][l].reshape(4, 4, 128).transpose(2, 1, 0).reshape(128, 16)
        pp[:, l, PP_LCB:PP_LCB + 4] = _fm(inp["lru_conv_b"][l], 4)
        pp[:, l, PP_LBA:PP_LBA + 4] = _fm(inp["lru_ba"][l], 4)
        pp[:, l, PP_LBX:PP_LBX + 4] = _fm(inp["lru_bx"][l], 4)
        pp[:, l, PP_LLAM:PP_LLAM + 4] = _fm(inp["lru_lambda"][l], 4)
        pp[:, l, PP_S5D:PP_S5D + 4] = _fm(inp["s5_d"][l], 4)
        pp[:, l, PP_GLB:PP_GLB + 4] = _fm(inp["s5_glu_b"][l], 4)
        pp[:, l, PP_LRE:PP_LRE + 16] = inp["s5_lambda_re"][l].reshape(16, 128).T
        pp[:, l, PP_LIM:PP_LIM + 16] = inp["s5_lambda_im"][l].reshape(16, 128).T
        pp[:, l, PP_LDT:PP_LDT + 16] = np.repeat(inp["s5_log_dt"][l], 64).reshape(16, 128).T
        pp[:, l, PP_FG:PP_FG + 8] = _fm(inp["final_norm_g"], 8)
    sh["pp_in"] = pp
    pdt = np.zeros((128, NL, 32), np.float32)
    pdt[:, :, 0:16] = inp["ssd_dt_bias"][None]
    pdt[:, :, 16:32] = inp["ssd_a_log"][None]
    sh["pdt_in"] = pdt
    bre = np.zeros((NL, 128, 16, 128), np.float32)
    bim = np.zeros((NL, 128, 16, 128), np.float32)
    cre = np.zeros((NL, 128, 16, 128), np.float32)
    cim = np.zeros((NL, 128, 16, 128), np.float32)
    for g in range(32):
        pr, g2, gl = g // 2, g % 2, g % 8
        bre[:, g2 * 64:(g2 + 1) * 64, pr, gl * 16:(gl + 1) * 16] = inp["s5_b_re"][:, g]
        bim[:, g2 * 64:(g2 + 1) * 64, pr, gl * 16:(gl + 1) * 16] = inp["s5_b_im"][:, g]
        cre[:, g2 * 64:(g2 + 1) * 64, pr, gl * 16:(gl + 1) * 16] = inp["s5_c_re"][:, g].transpose(0, 2, 1)
        cim[:, g2 * 64:(g2 + 1) * 64, pr, gl * 16:(gl + 1) * 16] = inp["s5_c_im"][:, g].transpose(0, 2, 1)
    sh["bpadT_re"], sh["bpadT_im"], sh["ctpad_re"], sh["ctpad_im"] = bre, bim, cre, cim
    cfa, cba, iota = _consts()
    sh["c_f32"], sh["c_bf"], sh["c_iota"] = cfa, cba, iota
    return sh


def _prep_core(inp, c):
    f = lambda a: np.ascontiguousarray(a, dtype=np.float32)
    sl = slice(NSEQ * c, NSEQ * (c + 1))
    m = {}
    x_tok = np.concatenate([inp["x_prompt"][c], inp["x_sample"][sl].reshape(TS, D)], axis=0)
    m["xin"] = f(x_tok.T.reshape(8, 128, TT))
    m["h0T"] = f(inp["state_ssd"][:, sl].transpose(0, 1, 4, 2, 3).reshape(NL, NSEQ, 128, 1024))
    m["sconv0"] = f(inp["state_ssd_conv"][:, sl].reshape(NL, NSEQ, 3, 16, 128).transpose(0, 4, 3, 1, 2))
    m["lconv0"] = f(inp["state_lru_conv"][:, sl].reshape(NL, NSEQ, 3, 4, 128).transpose(0, 4, 3, 1, 2))
    m["lru0"] = f(inp["state_lru"][:, sl].reshape(NL, NSEQ, 4, 128).transpose(0, 3, 2, 1))
    m["s5r0"] = f(inp["state_s5_re"][:, sl].reshape(NL, NSEQ, 16, 128).transpose(0, 3, 2, 1))
    m["s5i0"] = f(inp["state_s5_im"][:, sl].reshape(NL, NSEQ, 16, 128).transpose(0, 3, 2, 1))
    return m


_PROG = {}


def kernel(**inputs):
    inp = {k_: np.asarray(v) for k_, v in inputs.items()}
    if "nc" not in _PROG:
        _PROG["nc"] = build_program(NL)
    nc = _PROG["nc"]
    shared = _prep_shared(inp)
    in_maps = []
    for c in range(NCORES):
        m = dict(shared)
        m.update(_prep_core(inp, c))
        in_maps.append(m)
    res = run_bass_kernel_spmd(nc, in_maps, core_ids=list(range(NCORES)))
    R = res.results
    B = NCORES
    y_prompt = np.zeros((B, TP, D), np.float32)
    y_sample = np.zeros((B * NSEQ, LS, D), np.float32)
    ssd_p = np.zeros((NL, B, 16, 64, 128), np.float32)
    ssd_s = np.zeros((NL, B * NSEQ, 16, 64, 128), np.float32)
    ssd_conv_p = np.zeros((NL, B, 3, 2048), np.float32)
    ssd_conv_s = np.zeros((NL, B * NSEQ, 3, 2048), np.float32)
    lru_p = np.zeros((NL, B, 512), np.float32)
    lru_s = np.zeros((NL, B * NSEQ, 512), np.float32)
    lru_conv_p = np.zeros((NL, B, 3, 512), np.float32)
    lru_conv_s = np.zeros((NL, B * NSEQ, 3, 512), np.float32)
    s5_re_p = np.zeros((NL, B, 32, 64), np.float32)
    s5_re_s = np.zeros((NL, B * NSEQ, 32, 64), np.float32)
    s5_im_p = np.zeros((NL, B, 32, 64), np.float32)
    s5_im_s = np.zeros((NL, B * NSEQ, 32, 64), np.float32)
    for c in range(B):
        r = R[c]
        sl = slice(NSEQ * c, NSEQ * (c + 1))
        y = np.asarray(r["yout"]).reshape(D, TT).T
        y_prompt[c] = y[0:TP]
        y_sample[sl] = y[TP:].reshape(NSEQ, LS, D)
        ssd_p[:, c] = np.asarray(r["ssd_p_o"]).reshape(NL, 128, 16, 64).transpose(0, 2, 3, 1)
        ssd_s[:, sl] = np.asarray(r["ssd_s_o"]).reshape(NL, NSEQ, 128, 16, 64).transpose(0, 1, 3, 4, 2)
        ssd_conv_p[:, c] = np.asarray(r["sconv_p_o"]).transpose(0, 3, 2, 1).reshape(NL, 3, 2048)
        ssd_conv_s[:, sl] = np.asarray(r["sconv_s_o"]).transpose(0, 3, 4, 2, 1).reshape(NL, NSEQ, 3, 2048)
        lru_p[:, c] = np.asarray(r["lru_p_o"]).transpose(0, 2, 1).reshape(NL, 512)
        lru_s[:, sl] = np.asarray(r["lru_s_o"]).transpose(0, 3, 2, 1).reshape(NL, NSEQ, 512)
        lru_conv_p[:, c] = np.asarray(r["lconv_p_o"]).transpose(0, 3, 2, 1).reshape(NL, 3, 512)
        lru_conv_s[:, sl] = np.asarray(r["lconv_s_o"]).transpose(0, 3, 4, 2, 1).reshape(NL, NSEQ, 3, 512)
        s5_re_p[:, c] = np.asarray(r["s5r_p_o"]).transpose(0, 2, 1).reshape(NL, 32, 64)
        s5_im_p[:, c] = np.asarray(r["s5i_p_o"]).transpose(0, 2, 1).reshape(NL, 32, 64)
        s5_re_s[:, sl] = np.asarray(r["s5r_s_o"]).transpose(0, 3, 2, 1).reshape(NL, NSEQ, 32, 64)
        s5_im_s[:, sl] = np.asarray(r["s5i_s_o"]).transpose(0, 3, 2, 1).reshape(NL, NSEQ, 32, 64)
    return (y_prompt, y_sample, ssd_p, ssd_s, ssd_conv_p, ssd_conv_s, lru_p, lru_s,
            lru_conv_p, lru_conv_s, s5_re_p, s5_re_s, s5_im_p, s5_im_s)
```

```python
import math
import os
from contextlib import ExitStack

import numpy as np
import concourse.bass as bass
import concourse.mybir as mybir
from concourse.bass_utils import run_bass_kernel_spmd

F32 = mybir.dt.float32
BF16 = mybir.dt.bfloat16
I32 = mybir.dt.int32
ALU = mybir.AluOpType
AF = mybir.ActivationFunctionType

NCORES = 8
D = 1024
NL = 4
TP = 2048
NSEQ = 16
LS = 8
TS = NSEQ * LS
TT = TP + TS
WT = 512
IN_DIM = 5136
EPS = 1e-6
TWO_PI = 2.0 * math.pi

PP_NG = 0
PP_NG2 = 8
PP_SD = 16
PP_CW = 24
PP_CB = 88
PP_LCW = 104
PP_LCB = 120
PP_LBA = 124
PP_LBX = 128
PP_LLAM = 132
PP_S5D = 136
PP_GLB = 140
PP_LRE = 144
PP_LIM = 160
PP_LDT = 176
PP_FG = 192
NPP = 200

EPOCH = 12000


class Ring:
    def __init__(self, bufs, full=None):
        self.bufs = bufs
        self.full = full
        self.i = 0

    def get(self):
        b = self.bufs[self.i % len(self.bufs)]
        self.i += 1
        return b

    def get_full(self):
        b = self.full[self.i % len(self.full)]
        self.i += 1
        return b


class KB:
    ENG = ("pe", "act", "dve", "pool", "sp")

    def __init__(self, nc, es):
        self.nc = nc
        self.es = es
        self.prog = {e: [] for e in self.ENG}
        self.cnt = {e: 0 for e in self.ENG}
        self.sem = {}
        self.nsem = 0
        for e in ("pe", "act", "dve", "pool"):
            self.sem[e] = self._newsem("c_" + e)
        self.waited = {e: {} for e in self.ENG}
        self.dead = False
        self.pending = []
        self.ncp = 0
        self.stop = int(os.environ.get("KSTOP", "-1"))
        self.lastw = {}
        self.readers = {}
        self.dsem = {}
        self.drr = {}
        for q, n in (("sp", 16), ("pool", 24), ("act", 2)):
            self.dsem[q] = [[self._newsem("d_%s%d" % (q, i)), 0] for i in range(n)]
            self.drr[q] = 0

    def _newsem(self, name):
        self.nsem += 1
        return self.es.enter_context(self.nc.semaphore("%s_%d" % (name, self.nsem)))

    slotw = {}

    def _keys(self, r):
        if isinstance(r, str):
            return [r]
        name = r.name
        w = self.slotw.get(name)
        if w is None:
            return [name]
        ap = r.ap
        off = r.offset % ap[0][0]
        hi = off + sum((c - 1) * s for s, c in ap[1:])
        return ["%s:%d" % (name, i) for i in range(off // w, hi // w + 1)]

    def defer(self, fn, depth=1):
        self.pending.append(fn)
        while len(self.pending) > depth:
            self.pending.pop(0)()

    def flush(self):
        while self.pending:
            self.pending.pop(0)()

    def cp(self, name=""):
        self.ncp += 1
        if self.stop >= 0 and self.ncp > self.stop and not self.dead:
            self.dead = True
            print("KSTOP: program truncated before checkpoint", self.ncp, name, flush=True)

    def op(self, e, fn, reads=(), writes=(), dma=False):
        if self.dead:
            return None
        waits = {}

        def need(tok, raw):
            if tok is None:
                return
            sem, val, src, isdma = tok
            if src == e and not isdma and e == "pe":
                return
            if self.waited[e].get(sem.name, 0) >= val:
                return
            if sem.name not in waits or waits[sem.name][1] < val:
                waits[sem.name] = (sem, val)

        rk = [x for r in reads for x in self._keys(r)]
        wk = [x for w in writes for x in self._keys(w)]
        for r in rk:
            need(self.lastw.get(r), True)
        for w in wk:
            need(self.lastw.get(w), False)
            for t in self.readers.get(w, {}).values():
                need(t, False)
        if dma:
            slot = self.dsem[e][self.drr[e] % len(self.dsem[e])]
            self.drr[e] += 1
            if slot[1] > 0:
                need((slot[0], slot[1], e, True), True)
            slot[1] += 16
            tok = (slot[0], slot[1], e, True)
            inc = (slot[0], 16)
        else:
            if self.cnt[e] >= EPOCH:
                self.sem[e] = self._newsem("c_" + e)
                self.cnt[e] = 0
            self.cnt[e] += 1
            tok = (self.sem[e], self.cnt[e], e, False)
            inc = (self.sem[e], 1)
        for s, v in waits.values():
            self.waited[e][s.name] = v
        self.prog[e].append((list(waits.values()), fn, inc))
        for r in rk:
            self.readers.setdefault(r, {})[tok[0].name] = tok
        for w in wk:
            self.lastw[w] = tok
            self.readers[w] = {}
        return tok

    def finish(self):
        fin = []
        for q in self.dsem:
            for sem, v in self.dsem[q]:
                if v > 0:
                    fin.append((sem, v))
        self.final_waits = fin

    def emit(self):
        nc = self.nc
        handles = {"pe": "tensor", "act": "scalar", "dve": "vector", "pool": "gpsimd", "sp": "sync"}
        with nc.Block() as block:
            for e in self.ENG:
                prog = self.prog[e]
                extra = self.final_waits if e == "sp" else []

                def body(eng, prog=prog, extra=extra):
                    for waits, fn, inc in prog:
                        for s, v in waits:
                            eng.wait_ge(s, v)
                        ins = fn(eng)
                        ins.then_inc(inc[0], inc[1])
                    for s, v in extra:
                        eng.wait_ge(s, v)

                getattr(block, handles[e])(body)

    def mm(self, out, lhsT, rhs, start=True, stop=True):
        self.op("pe", lambda t: t.matmul(out, lhsT=lhsT, rhs=rhs, start=start, stop=stop),
                [lhsT, rhs], [out])

    def tr(self, out, in_, ident):
        self.op("pe", lambda t: t.transpose(out, in_, ident), [in_, ident], [out])

    def act(self, out, in_, func, bias=None, scale=None):
        rd = [in_]
        kw = {}
        if bias is not None:
            kw["bias"] = bias
            if not isinstance(bias, (int, float)):
                rd.append(bias)
        if scale is not None:
            kw["scale"] = scale
            if not isinstance(scale, (int, float)):
                rd.append(scale)
        self.op("act", lambda a: a.activation(out=out, in_=in_, func=func, **kw), rd, [out])

    def tt(self, e, out, in0, in1, op):
        self.op(e, lambda v: v.tensor_tensor(out=out, in0=in0, in1=in1, op=op), [in0, in1], [out])

    def ts(self, e, out, in0, s1, s2, op0, op1=None):
        rd = [in0]
        for s in (s1, s2):
            if s is not None and not isinstance(s, (int, float)):
                rd.append(s)
        if op1 is None:
            self.op(e, lambda v: v.tensor_scalar(out=out, in0=in0, scalar1=s1, scalar2=None, op0=op0),
                    rd, [out])
        else:
            self.op(e, lambda v: v.tensor_scalar(out=out, in0=in0, scalar1=s1, scalar2=s2, op0=op0, op1=op1),
                    rd, [out])

    def stt(self, out, in0, scalar, in1, op0, op1):
        rd = [in0, in1]
        if not isinstance(scalar, (int, float)):
            rd.append(scalar)
        self.op("dve", lambda v: v.scalar_tensor_tensor(out=out, in0=in0, scalar=scalar, in1=in1,
                                                        op0=op0, op1=op1), rd, [out])

    def scan(self, out, d0, d1, init):
        rd = [d0, d1]
        if not isinstance(init, (int, float)):
            rd.append(init)
        self.op("dve", lambda v: v.tensor_tensor_scan(out=out, data0=d0, data1=d1, initial=init,
                                                      op0=ALU.mult, op1=ALU.add), rd, [out])

    def copy(self, e, out, in_):
        if e == "act":
            self.op("act", lambda a: a.activation(out=out, in_=in_, func=AF.Copy), [in_], [out])
        else:
            self.op(e, lambda v: v.tensor_copy(out=out, in_=in_), [in_], [out])

    def memset(self, e, out, val):
        self.op(e, lambda v: v.memset(out, val), [], [out])

    def dma(self, q, out, in_, rk=(), wk=()):
        self.op(q, lambda g: g.dma_start(out=out, in_=in_), [in_] + list(rk), [out] + list(wk), dma=True)


def build_program(nlayers=NL):
    nc = bass.Bass("TRN2", target_bir_lowering=False)

    def din(name, shape, dt=F32):
        return nc.dram_tensor(name, list(shape), dt, kind="ExternalInput").ap()

    def dout(name, shape, dt=F32):
        return nc.dram_tensor(name, list(shape), dt, kind="ExternalOutput").ap()

    xin = din("xin", [8, 128, TT])
    w_in = din("w_in_r", [NL, 128, 8, IN_DIM])
    w_out = din("w_out_r", [NL, 128, 16, D])
    glu_w = din("glu_r", [NL, 128, 4, 512])
    wa_bd = din("wa_bd", [NL, 128, 4, 128])
    wx_bd = din("wx_bd", [NL, 128, 4, 128])
    pp_d = din("pp_in", [128, NL, NPP])
    pdt_d = din("pdt_in", [128, NL, 32])
    bpad_re = din("bpadT_re", [NL, 128, 16, 128])
    bpad_im = din("bpadT_im", [NL, 128, 16, 128])
    ctpad_re = din("ctpad_re", [NL, 128, 16, 128])
    ctpad_im = din("ctpad_im", [NL, 128, 16, 128])
    h0T_d = din("h0T", [NL, NSEQ, 128, 1024])
    sconv0 = din("sconv0", [NL, 128, 16, NSEQ, 3])
    lconv0 = din("lconv0", [NL, 128, 4, NSEQ, 3])
    lru0 = din("lru0", [NL, 128, 4, NSEQ])
    s5r0 = din("s5r0", [NL, 128, 16, NSEQ])
    s5i0 = din("s5i0", [NL, 128, 16, NSEQ])
    c_f32 = din("c_f32", [128, 8, 128])
    c_bf = din("c_bf", [128, 4, 128])
    c_iota = din("c_iota", [128, WT])

    yout = dout("yout", [8, 128, TT])
    ssd_p_o = dout("ssd_p_o", [NL, 128, 1024])
    ssd_s_o = dout("ssd_s_o", [NL, NSEQ, 128, 1024])
    sconv_p_o = dout("sconv_p_o", [NL, 128, 16, 3])
    sconv_s_o = dout("sconv_s_o", [NL, 128, 16, NSEQ, 3])
    lru_p_o = dout("lru_p_o", [NL, 128, 4])
    lru_s_o = dout("lru_s_o", [NL, 128, 4, NSEQ])
    lconv_p_o = dout("lconv_p_o", [NL, 128, 4, 3])
    lconv_s_o = dout("lconv_s_o", [NL, 128, 4, NSEQ, 3])
    s5r_p_o = dout("s5r_p_o", [NL, 128, 16])
    s5r_s_o = dout("s5r_s_o", [NL, 128, 16, NSEQ])
    s5i_p_o = dout("s5i_p_o", [NL, 128, 16])
    s5i_s_o = dout("s5i_s_o", [NL, 128, 16, NSEQ])
    xsc = nc.dram_tensor("xsc", [2, 8, 128, TT], F32, kind="Internal").ap()
    wbf = nc.dram_tensor("wbf", [NL, 128, 8, IN_DIM], BF16, kind="Internal").ap()
    wobf = nc.dram_tensor("wobf", [NL, 8, 128, 16, 128], BF16, kind="Internal").ap()

    with ExitStack() as es:
        k = KB(nc, es)
        k.slotw = {"F8": 512, "zs": 512, "A16": 512, "Y": 512, "xt": 512, "hn": 512, "LT": 128,
                   "hnew": 512, "h0f": 512}

        def sb(name, shape, dt):
            return es.enter_context(nc.sbuf_tensor(name, list(shape), dt))

        def ps(name, shape, dt):
            return es.enter_context(nc.psum_tensor(name, list(shape), dt))

        xt = sb("xt", [128, 8, WT], F32)
        hn = sb("hn", [128, 8, WT], BF16)
        zs = sb("zs", [128, 8, WT], BF16)
        A16 = sb("A16", [128, 16, WT], BF16)
        F8 = sb("F8", [128, 8, WT], F32)
        Y = sb("Y", [128, 16, WT], BF16)
        LT = sb("LT", [128, 16, 128], BF16)
        x_tm = sb("x_tm", [128, 1024], BF16)
        xw_tm = sb("xw_tm", [128, 1024], BF16)
        B_tm = sb("B_tm", [128, 512], BF16)
        bmring = Ring([sb("bm%d" % i, [128, 512], BF16) for i in range(1)])
        hT = sb("hT", [128, 1024], F32)
        hT_bf = sb("hT_bf", [128, 1024], BF16)
        dt_tm = sb("dt_tm", [128, 4, 16], F32)
        dtA_tm = sb("dtA_tm", [128, 4, 16], F32)
        pre = sb("pre", [128, 7, 64], F32)
        wring = Ring([sb("wbuf%d" % i, [128, 8, 512], BF16) for i in range(2)])
        woring = Ring([sb("wobuf%d" % i, [128, 16, 128], BF16) for i in range(2)])
        glu_bf = sb("glu_bf", [128, 4, 512], BF16)
        BbT_re = sb("BbT_re", [128, 16, 128], BF16)
        BbT_im = sb("BbT_im", [128, 16, 128], BF16)
        CT_re = sb("CT_re", [128, 16, 128], BF16)
        nCT_re = sb("nCT_re", [128, 16, 128], BF16)
        nCT_im = sb("nCT_im", [128, 16, 128], BF16)
        wa_bf = sb("wa_bf", [128, 4, 128], BF16)
        wx_bf = sb("wx_bf", [128, 4, 128], BF16)
        pp = sb("pp", [128, NL, NPP], F32)
        pdt = sb("pdt", [128, NL, 32], F32)
        expA = sb("expA", [128, 16], F32)
        lp = sb("lp", [128, 8, 16], F32)
        cf = sb("cf", [128, 8, 128], F32)
        cb = sb("cb", [128, 4, 128], BF16)
        iota = sb("iota", [128, WT], F32)
        iota_c = sb("iota_c", [128, WT], F32)
        carry_s = sb("carry_s", [128, 16, NSEQ, 3], F32)
        carry_l = sb("carry_l", [128, 4, NSEQ, 3], F32)
        pcar_s = sb("pcar_s", [128, 16, 3], F32)
        pcar_l = sb("pcar_l", [128, 4, 3], F32)
        pcar_h = sb("pcar_h", [128, 4], F32)
        hcar = sb("hcar", [128, 4, NSEQ], F32)
        s5car_r = sb("s5car_r", [128, 16], F32)
        s5car_i = sb("s5car_i", [128, 16], F32)
        s5p_r = sb("s5p_r", [128, 16], F32)
        s5p_i = sb("s5p_i", [128, 16], F32)
        s5o_r = sb("s5o_r", [128, 16, NSEQ], F32)
        s5o_i = sb("s5o_i", [128, 16, NSEQ], F32)
        s5h0_r = sb("s5h0_r", [128, 16, NSEQ], F32)
        s5h0_i = sb("s5h0_i", [128, 16, NSEQ], F32)
        lruh0 = sb("lruh0", [128, 4, NSEQ], F32)
        h0f = sb("h0f", [128, 1024], F32)
        h0bring = Ring([sb("h0b%d" % i, [128, 1024], BF16) for i in range(2)])
        hnew = sb("hnew", [128, 1024], F32)
        dtA_rep = h0f
        _fr = [sb("fr%d" % i, [128, WT + 4], F32) for i in range(8)]
        fring = Ring([t_[:, 0:WT] for t_ in _fr], [t_[:, :] for t_ in _fr])
        iring = Ring([sb("ir%d" % i, [128, WT], I32) for i in range(2)])
        fring_x = [h0f[:, 0:512], h0f[:, 512:1024]]
        sring2 = [hnew[:, 0:512], hnew[:, 512:1024]]
        sring = Ring([sb("sr%d" % i, [128, 128], F32) for i in range(4)])
        tring = Ring([sb("tn%d" % i, [128, 48], F32) for i in range(10)])
        dhl = sb("dhl", [128, 2, 64], BF16)
        f8ring = Ring([F8[:, i, :] for i in range(8)])
        zring = Ring([zs[:, i, :] for i in range(8)])

        pheld = ps("pheld", [128, 512], F32)
        pheld2 = ps("pheld2", [128, 512], F32)
        pheld3 = ps("pheld3", [128, 512], F32)
        pring = Ring([ps("pb%d" % i, [128, 512], F32) for i in range(4)])
        ptb = ps("ptb", [128, 1024], BF16)

        tri_b2 = cf[:, 0, :].bitcast(BF16)
        tri_p = cf[:, 1, :]
        tri_s = cf[:, 2, :]
        ones_f = cf[:, 3, :]
        ones_s = cf[:, 4, :]
        iota_s = cf[:, 5, :]
        notstart = cf[:, 6, :]
        ind = cf[:, 7, :]
        ident_b = cb[:, 0, :]
        ones_b = cb[:, 1, :]
        neg_p = cb[:, 2, :]
        neg_s = cb[:, 3, :]

        k.dma("sp", cf[:], c_f32)
        nident_b = sb("nident_b", [128, 128], BF16)
        k.dma("pool", cb[:], c_bf)
        k.dma("sp", iota[:], c_iota)
        k.dma("sp", pp[:], pp_d)
        k.dma("sp", pdt[:], pdt_d)
        k.act(nident_b[:], ident_b, AF.Copy, scale=-1.0)

        tiles = [(i * WT, WT, "p") for i in range(TP // WT)] + [(TP, TS, "s")]
        blocks = [("z", 0, 512), ("z", 512, 512), ("x", 1024, 512), ("x", 1536, 512), ("B", 2048, 512),
                  ("C", 2560, 512), ("dt", 3072, 16), ("lg", 3600, 512), ("lx", 3088, 512),
                  ("sg", 4624, 512), ("su", 4112, 512)]
        stream = [(l, ti, bi) for l in range(nlayers) for ti in range(len(tiles)) for bi in range(len(blocks))]
        wbufs = {}
        st = {"next": 0, "item": 0}

        def prefetch_w(upto):
            while st["next"] < len(stream) and st["next"] <= upto:
                l_, ti_, bi_ = stream[st["next"]]
                _, c0_, n_ = blocks[bi_]
                buf = wring.get()
                k.dma("sp", buf[:, :, 0:n_], wbf[l_][:, :, c0_:c0_ + n_], rk=["wbf%d_%d" % (l_, bi_)])
                wbufs[st["next"]] = buf
                st["next"] += 1

        def next_block():
            prefetch_w(st["item"] + 1)
            wb = wbufs.pop(st["item"])
            st["item"] += 1
            return wb

        pringA = Ring(pring.bufs + [pheld, pheld2, pheld3])

        def proj(wb, m, W):
            pm = pringA.get()
            for kt in range(8):
                k.mm(pm[:, 0:W], wb[:, kt, m * 128:(m + 1) * 128], hn[:, kt, 0:W], start=(kt == 0), stop=(kt == 7))
            return pm

        def rmsnorm_tile(src3, ncol, gcol0, l, out_fn):
            pn = pring.get()
            for kt in range(8):
                sq = zring.get()
                k.act(sq[:, 0:ncol], src3[:, kt, 0:ncol], AF.Square)
                k.mm(pn[:, 0:ncol], ones_b, sq[:, 0:ncol], start=(kt == 0), stop=(kt == 7))
            t1 = fring.get()
            k.act(t1[:, 0:ncol], pn[:, 0:ncol], AF.Ln, bias=epsc[:, 0:1], scale=1.0 / D)
            rstd = fring.get()
            k.act(rstd[:, 0:ncol], t1[:, 0:ncol], AF.Exp, scale=-0.5)
            for kt in range(8):
                k.stt(out_fn(kt), src3[:, kt, 0:ncol], pp[:, l, gcol0 + kt:gcol0 + kt + 1], rstd[:, 0:ncol],
                      ALU.mult, ALU.mult)

        def frac_sincos(u, W, sn_out, cs_out):
            ui = iring.get()
            k.copy("dve", ui[:, 0:W], u)
            r = fring.get()
            k.tt("dve", r[:, 0:W], u, ui[:, 0:W], ALU.subtract)
            k.act(sn_out, r[:, 0:W], AF.Sin, scale=TWO_PI)
            ar = fring.get()
            k.stt(ar[:, 0:W], r[:, 0:W], -1.0, r[:, 0:W], ALU.mult, ALU.max)
            k.act(cs_out, ar[:, 0:W], AF.Sin, bias=epsc[:, 1:2], scale=-TWO_PI)

        epsc = sb("epsc", [128, 4], F32)
        k.memset("dve", epsc[:, 0:1], EPS)
        k.memset("dve", epsc[:, 1:2], math.pi / 2.0)
        k.memset("dve", epsc[:, 2:3], 1.0)

        def conv_group(l, pms, W, smp, specs):
            n = len(pms)
            accs = [fring.get() for _ in range(n)]
            raws = [fring.get_full() for _ in range(n)]
            if smp:
                rawv = [r_[:, 0:NSEQ * (LS + 3)].rearrange("p (s t) -> p s t", s=NSEQ) for r_ in raws]
                pmv = [p_[:, 0:W].rearrange("p (s t) -> p s t", s=NSEQ) for p_ in pms]
                accv = [a_[:, 0:W].rearrange("p (s t) -> p s t", s=NSEQ) for a_ in accs]
                for i in range(n):
                    k.copy("dve", rawv[i][:, :, 0:3], specs[i][3])
                for i in range(n):
                    k.copy("act", rawv[i][:, :, 3:3 + LS], pmv[i])
                    k.act(accv[i], pmv[i], AF.Identity, bias=pp[:, l, specs[i][1]:specs[i][1] + 1],
                          scale=pp[:, l, specs[i][0] + 3:specs[i][0] + 4])
                for i in range(n):
                    st3 = tring.get()
                    st3v = st3[:, 0:48].rearrange("p (s t) -> p s t", s=NSEQ)
                    k.copy("dve", st3v, rawv[i][:, :, LS:LS + 3])
                    k.dma("sp", specs[i][4], st3v)
                for kk in range(3):
                    for i in range(n):
                        k.stt(accv[i], rawv[i][:, :, kk:kk + LS], pp[:, l, specs[i][0] + kk:specs[i][0] + kk + 1],
                              accv[i], ALU.mult, ALU.add)
            else:
                for i in range(n):
                    k.copy("dve", raws[i][:, 0:3], specs[i][2])
                for i in range(n):
                    k.copy("act", raws[i][:, 3:3 + W], pms[i][:, 0:W])
                    k.act(accs[i][:, 0:W], pms[i][:, 0:W], AF.Identity, bias=pp[:, l, specs[i][1]:specs[i][1] + 1],
                          scale=pp[:, l, specs[i][0] + 3:specs[i][0] + 4])
                for kk in range(3):
                    for i in range(n):
                        k.stt(accs[i][:, 0:W], raws[i][:, kk:kk + W], pp[:, l, specs[i][0] + kk:specs[i][0] + kk + 1],
                              accs[i][:, 0:W], ALU.mult, ALU.add)
                for i in range(n):
                    k.copy("dve", specs[i][2], raws[i][:, W:W + 3])
            return accs

        def ssd_stage(l, ti, W, smp):
            xc = A16
            nch = W // 128
            TRI = tri_s if smp else tri_p
            ONESM = ones_s if smp else ones_f
            NEG = neg_s if smp else neg_p
            TRIB = tri_b2[:, 128:256] if smp else tri_b2[:, 0:128]
            for ci in range(nch):
                cs = slice(ci * 128, (ci + 1) * 128)
                dtA = dtA_tm[:, ci, :]
                dtc = dt_tm[:, ci, :]
                nacum = pre[:, 1, ci * 16:(ci + 1) * 16]
                w_tm = pre[:, 3, ci * 16:(ci + 1) * 16]
                DEC = pre[:, 4, ci * 16:(ci + 1) * 16]
                k.cp("ssd A acum")
                psc = pheld
                for g in range(4):
                    k.mm(psc[:, g * 128:(g + 1) * 128], xc[:, 8 + g, cs], xc[:, 12 + g, cs])
                k.cp("ssd B scores")
                for j in range(8):
                    k.tr(ptb[:, j * 128:(j + 1) * 128], xc[:, j, cs], ident_b)
                k.cp("T1 xtr")
                k.copy("act", x_tm[:], ptb[:, :])
                k.cp("T2 xcopy")
                k.tt("dve", hnew[:].rearrange("p (h q) -> p h q", h=16), x_tm[:].rearrange("p (h q) -> p h q", h=16),
                     w_tm[:, 0:16].unsqueeze(2).to_broadcast([128, 16, 64]), ALU.mult)
                k.copy("act", xw_tm[:], hnew[:])
                k.cp("T3 xw")
                for g in range(4):
                    k.tr(ptb[:, g * 128:(g + 1) * 128], xc[:, 8 + g, cs], ident_b)
                k.cp("T4 btr")
                k.copy("act", B_tm[:], ptb[:, 0:512])
                k.cp("ssd C transposes")
                for g in range(4):
                    pab = pring.get()
                    for r in range(4):
                        h = 4 * g + r
                        cc = ci * 16 + h
                        k.mm(pab[:, r * 128:(r + 1) * 128], dhl[:, 0, cc:cc + 1].to_broadcast([128, 128]), TRIB,
                             start=True, stop=False)
                        k.mm(pab[:, r * 128:(r + 1) * 128], dhl[:, 1, cc:cc + 1].to_broadcast([128, 128]), TRIB,
                             start=False, stop=False)
                        k.mm(pab[:, r * 128:(r + 1) * 128], ident_b, NEG, start=False, stop=True)
                    for r in range(4):
                        h = 4 * g + r
                        Dh = sring.get()
                        k.act(Dh[:], pab[:, r * 128:(r + 1) * 128], AF.Exp, bias=nacum[:, h:h + 1])
                        k.stt(LT[:, h, :], Dh[:], dt_tm[:, ci, h:h + 1], psc[:, g * 128:(g + 1) * 128],
                              ALU.mult, ALU.mult)
                k.cp("ssd D LT")
                if smp:
                    EAs = [fring.get(), fring.get()]
                    k.copy("dve", dtA_rep[:].rearrange("p (h q) -> p h q", h=16),
                           dtA_tm[:, ci, :].unsqueeze(2).to_broadcast([128, 16, 64]))
                    for j in range(8):
                        pe_ = pring.get()
                        k.mm(pe_[:, 0:128], dtA_rep[:, j * 128:(j + 1) * 128], TRI)
                        k.act(EAs[j // 4][:, (j % 4) * 128:(j % 4 + 1) * 128], pe_[:, 0:128], AF.Exp)
                    pdS = pring.get()
                    for b_ in range(NSEQ):
                        k.mm(pdS[:, b_ * 16:(b_ + 1) * 16], ind[:, b_:b_ + 1].to_broadcast([128, 128]), dtA)
                    DECS = fring.get()
                    k.act(DECS[:, 0:256], pdS[:, 0:256], AF.Exp)
                    hx = Ring([h0f, hnew])
                    bufs = [hx.get()]
                    k.dma("sp", bufs[0][:], h0T_d[l, 0])
                    for b_ in range(NSEQ):
                        hb = bufs[b_]
                        if b_ + 1 < NSEQ:
                            nb_ = hx.get()
                            bufs.append(nb_)
                            k.dma("sp", nb_[:], h0T_d[l, b_ + 1])
                        h0b = h0bring.get()
                        k.copy("act", h0b[:], hb[:])
                        for j in range(8):
                            pyo = pheld2 if j < 4 else pheld3
                            jj = j % 4
                            k.mm(pyo[:, jj * 128 + b_ * 8: jj * 128 + b_ * 8 + 8], h0b[:, j * 128:(j + 1) * 128],
                                 xc[:, 12 + j // 2, b_ * 8:(b_ + 1) * 8])
                        Bm = bmring.get()
                        k.ts("dve", Bm[:], B_tm[:], ind[:, b_:b_ + 1], None, ALU.mult)
                        pS0 = pring.get()
                        pS1 = pring.get()
                        for g in range(4):
                            pS = pS0 if g < 2 else pS1
                            k.mm(pS[:, (g % 2) * 256:(g % 2 + 1) * 256], Bm[:, g * 128:(g + 1) * 128],
                                 xw_tm[:, g * 256:(g + 1) * 256])
                        hb3 = hb[:].rearrange("p (h q) -> p h q", h=16)
                        k.tt("dve", hb3, hb3, DECS[:, b_ * 16:(b_ + 1) * 16].unsqueeze(2).to_broadcast([128, 16, 64]),
                             ALU.mult)
                        k.tt("dve", hb[:, 0:512], hb[:, 0:512], pS0[:, :], ALU.add)
                        k.tt("dve", hb[:, 512:1024], hb[:, 512:1024], pS1[:, :], ALU.add)
                        k.dma("sp", ssd_s_o[l, b_], hb[:])
                    yo0 = fring.get()
                    yo1 = fring.get()
                    k.copy("act", yo0[:], pheld2[:, :])
                    k.copy("act", yo1[:], pheld3[:, :])
                if not smp:
                    k.copy("dve", dtA_rep[:].rearrange("p (h q) -> p h q", h=16),
                           dtA_tm[:, ci, :].unsqueeze(2).to_broadcast([128, 16, 64]))
                for j in range(8):
                    py = pring.get()
                    if not smp:
                        k.mm(py[:, 256:384], dtA_rep[:, j * 128:(j + 1) * 128], TRI)
                    k.mm(py[0:64, 0:128], x_tm[:, (2 * j) * 64:(2 * j + 1) * 64], LT[:, 2 * j, :])
                    k.mm(py[64:128, 0:128], x_tm[:, (2 * j + 1) * 64:(2 * j + 2) * 64], LT[:, 2 * j + 1, :])
                    tmp = sring.get()
                    if smp:
                        yo = (yo0 if j < 4 else yo1)[:, (j % 4) * 128:(j % 4 + 1) * 128]
                        k.tt("dve", tmp[:], yo, EAs[j // 4][:, (j % 4) * 128:(j % 4 + 1) * 128], ALU.mult)
                    else:
                        k.mm(py[:, 128:256], hT_bf[:, j * 128:(j + 1) * 128], xc[:, 12 + j // 2, cs])
                        EA = sring.get()
                        k.act(EA[:], py[:, 256:384], AF.Exp)
                        k.tt("dve", tmp[:], py[:, 128:256], EA[:], ALU.mult)
                    k.defer(lambda j=j, py=py, tmp=tmp, cs=cs: k.tt("dve", F8[:, j, cs], py[:, 0:128], tmp[:], ALU.add),
                            depth=1)
                k.flush()
                k.cp("ssd E y")
                if not smp:
                    for g in range(4):
                        pS = pheld2 if g < 2 else pheld3
                        k.mm(pS[:, (g % 2) * 256:(g % 2 + 1) * 256], B_tm[:, g * 128:(g + 1) * 128],
                             xw_tm[:, g * 256:(g + 1) * 256])
                    hT3 = hT[:].rearrange("p (h q) -> p h q", h=16)
                    k.tt("dve", hT3, hT3, DEC[:, 0:16].unsqueeze(2).to_broadcast([128, 16, 64]), ALU.mult)
                    k.tt("dve", hT[:, 0:512], hT[:, 0:512], pheld2[:, :], ALU.add)
                    k.tt("dve", hT[:, 512:1024], hT[:, 512:1024], pheld3[:, :], ALU.add)
                    k.copy("act", hT_bf[:], hT[:])
            k.cp("ssd F chunks done")
            pn = pring.get()
            for j in range(8):
                k.stt(F8[:, j, 0:W], xc[:, j, 0:W], pp[:, l, PP_SD + j:PP_SD + j + 1], F8[:, j, 0:W], ALU.mult, ALU.add)
            for j in range(8):
                k.tt("dve", F8[:, j, 0:W], F8[:, j, 0:W], zs[:, j, 0:W], ALU.mult)
            for j in range(8):
                sq = fring.get()
                sqb = sq[:, 0:WT // 2].bitcast(BF16)
                k.act(sqb[:, 0:W], F8[:, j, 0:W], AF.Square)
                k.mm(pn[:, 0:W], ones_b, sqb[:, 0:W], start=(j == 0), stop=(j == 7))
            t1 = fring.get()
            k.act(t1[:, 0:W], pn[:, 0:W], AF.Ln, bias=epsc[:, 0:1], scale=1.0 / D)
            rstd = fring.get()
            k.act(rstd[:, 0:W], t1[:, 0:W], AF.Exp, scale=-0.5)
            for j in range(8):
                k.stt(Y[:, j, 0:W], F8[:, j, 0:W], pp[:, l, PP_NG2 + j:PP_NG2 + j + 1], rstd[:, 0:W], ALU.mult, ALU.mult)
            if (not smp) and ti == len(tiles) - 2:
                k.dma("sp", ssd_p_o[l], hT[:])

        def lru_group(l, js, accs, W, smp):
            n = len(js)
            xrs, prs, pgs, rs, gis, as_ = [], [], [], [], [], []
            for i in range(n):
                xr_bf = zring.get()
                k.copy("act", xr_bf[:, 0:W], accs[i][:, 0:W])
                xrs.append(xr_bf)
            for i in range(n):
                pr = pring.get()
                k.mm(pr[:, 0:W], wa_bf[:, js[i], :], xrs[i][:, 0:W])
                pg = pring.get()
                k.mm(pg[:, 0:W], wx_bf[:, js[i], :], xrs[i][:, 0:W])
                prs.append(pr)
                pgs.append(pg)
            for i in range(n):
                j = js[i]
                r = fring.get()
                k.act(r[:, 0:W], prs[i][:, 0:W], AF.Sigmoid, bias=pp[:, l, PP_LBA + j:PP_LBA + j + 1])
                gi = fring.get()
                k.act(gi[:, 0:W], pgs[i][:, 0:W], AF.Sigmoid, bias=pp[:, l, PP_LBX + j:PP_LBX + j + 1])
                rs.append(r)
                gis.append(gi)
            for i in range(n):
                a = f8ring.get()
                k.act(a[:, 0:W], rs[i][:, 0:W], AF.Exp, scale=lp[:, 0, js[i]:js[i] + 1])
                as_.append(a)
            for i in range(n):
                k.tt("dve", rs[i][:, 0:W], as_[i][:, 0:W], as_[i][:, 0:W], ALU.mult)
            for i in range(n):
                k.ts("dve", rs[i][:, 0:W], rs[i][:, 0:W], 1.0, -1.0, ALU.min, ALU.mult)
            for i in range(n):
                k.act(rs[i][:, 0:W], rs[i][:, 0:W], AF.Sqrt, bias=epsc[:, 2:3])
            for i in range(n):
                k.tt("dve", gis[i][:, 0:W], gis[i][:, 0:W], rs[i][:, 0:W], ALU.mult)
            for i in range(n):
                k.tt("dve", gis[i][:, 0:W], gis[i][:, 0:W], accs[i][:, 0:W], ALU.mult)
            hss = []
            for i in range(n):
                j = js[i]
                a = as_[i]
                gi = gis[i]
                hs = f8ring.get()
                if smp:
                    am = f8ring.get()
                    k.tt("dve", am[:, 0:W], a[:, 0:W], notstart, ALU.mult)
                    t = tring.get()
                    k.tt("dve", t[:, 0:16], a[:, 0:W:LS], lruh0[:, j, :], ALU.mult)
                    k.tt("dve", gi[:, 0:W:LS], gi[:, 0:W:LS], t[:, 0:16], ALU.add)
                    k.scan(hs[:, 0:W], am[:, 0:W], gi[:, 0:W], 0.0)
                else:
                    k.scan(hs[:, 0:W], a[:, 0:W], gi[:, 0:W], pcar_h[:, j:j + 1])
                hss.append(hs)
            for i in range(n):
                j = js[i]
                if smp:
                    k.copy("dve", hcar[:, j, :], hss[i][:, LS - 1:W:LS])
                else:
                    k.copy("dve", pcar_h[:, j:j + 1], hss[i][:, W - 1:W])
                k.tt("dve", Y[:, 8 + j, 0:W], hss[i][:, 0:W], A16[:, j, 0:W], ALU.mult)

        def s5_stage(l, ti, c0, W, smp):
            last_p = (not smp) and ti == len(tiles) - 2
            if not smp:
                k.ts("dve", iota_c[:, 0:W], iota[:, 0:W], float(c0), None, ALU.add)
            tsrc = iota_s if smp else iota_c[:, 0:W]
            tabring = Ring([A16[:, 0, :], A16[:, 1, :], A16[:, 2, :], A16[:, 3, :], Y[:, 14, :], Y[:, 15, :]])
            srring = Ring([F8[:, 0, :], F8[:, 2, :]])
            siring = Ring([F8[:, 1, :], F8[:, 3, :]])
            tprod = [zs[:, i, :] for i in range(4)]
            mprod = [zs[:, 4 + i, :] for i in range(4)]
            Srb = Y[:, 12, :]
            Sib = Y[:, 13, :]
            pGr = pheld2
            pGi = pheld3

            def tables(pr_):
                thc = lp[:, 1, pr_:pr_ + 1]
                u = fring.get()
                ui = iring.get()
                k.ts("dve", ui[:, 0:W], tsrc, thc, None, ALU.mult)
                k.stt(u[:, 0:W], tsrc, thc, ui[:, 0:W], ALU.mult, ALU.subtract)
                uf = fring.get()
                sn = tabring.get()
                cs = tabring.get()
                k.act(sn[:, 0:W], u[:, 0:W], AF.Sin, scale=TWO_PI)
                k.act(uf[:, 0:W], u[:, 0:W], AF.Abs)
                k.act(cs[:, 0:W], uf[:, 0:W], AF.Sin, bias=epsc[:, 1:2], scale=-TWO_PI)
                return sn, cs

            def make_tail(pr_, q, py5, sn, cs, Sr, Si):
                def tail():
                    m1, m2, m3, m4 = mprod
                    k.tt("pool", m1[:, 0:W], cs[:, 0:W], Srb[:, 0:W], ALU.mult)
                    k.tt("pool", m2[:, 0:W], sn[:, 0:W], Sib[:, 0:W], ALU.mult)
                    k.tt("pool", m3[:, 0:W], cs[:, 0:W], Sib[:, 0:W], ALU.mult)
                    k.tt("pool", m4[:, 0:W], sn[:, 0:W], Srb[:, 0:W], ALU.mult)
                    k.mm(py5[:, 0:W], CT_re[:, pr_, :], m1[:, 0:W], start=(q == 0), stop=False)
                    k.mm(py5[:, 0:W], nCT_re[:, pr_, :], m2[:, 0:W], start=False, stop=False)
                    k.mm(py5[:, 0:W], nCT_im[:, pr_, :], m3[:, 0:W], start=False, stop=False)
                    k.mm(py5[:, 0:W], nCT_im[:, pr_, :], m4[:, 0:W], start=False, stop=(q == 3))
                    if smp or last_p:
                        if smp:
                            sel = slice(LS - 1, W, LS)
                            n_ = NSEQ
                            dr = s5o_r[:, pr_, :]
                            di = s5o_i[:, pr_, :]
                        else:
                            sel = slice(W - 1, W)
                            n_ = 1
                            dr = s5p_r[:, pr_:pr_ + 1]
                            di = s5p_i[:, pr_:pr_ + 1]
                        ta = tring.get()
                        tb2 = tring.get()
                        tc2 = tring.get()
                        td2 = tring.get()
                        k.tt("dve", ta[:, 0:n_], cs[:, sel], Sr[:, sel], ALU.mult)
                        k.tt("dve", tb2[:, 0:n_], sn[:, sel], Si[:, sel], ALU.mult)
                        k.tt("dve", tc2[:, 0:n_], cs[:, sel], Si[:, sel], ALU.mult)
                        k.tt("dve", td2[:, 0:n_], sn[:, sel], Sr[:, sel], ALU.mult)
                        k.tt("dve", dr, ta[:, 0:n_], tb2[:, 0:n_], ALU.subtract)
                        k.tt("dve", di, tc2[:, 0:n_], td2[:, 0:n_], ALU.add)
                return tail

            nxt = tables(0)
            prev_tail = None
            for kt in range(4):
                py5 = pheld
                ub = A16[:, 8 + kt, 0:W]
                for q in range(4):
                    pr_ = kt * 4 + q
                    pbr = pring.get()
                    k.mm(pbr[:, 0:W], BbT_re[:, pr_, :], ub)
                    pbi = pring.get()
                    k.mm(pbi[:, 0:W], BbT_im[:, pr_, :], ub)
                    sn, cs = nxt
                    if pr_ + 1 < 16:
                        nxt = tables(pr_ + 1)
                    t1, t2, t3, t4 = tprod
                    k.tt("dve", t1[:, 0:W], pbr[:, 0:W], cs[:, 0:W], ALU.mult)
                    k.tt("dve", t2[:, 0:W], pbi[:, 0:W], sn[:, 0:W], ALU.mult)
                    k.tt("dve", t3[:, 0:W], pbi[:, 0:W], cs[:, 0:W], ALU.mult)
                    k.tt("dve", t4[:, 0:W], pbr[:, 0:W], sn[:, 0:W], ALU.mult)
                    k.mm(pGr[:, 0:W], ident_b, t1[:, 0:W], start=True, stop=False)
                    k.mm(pGr[:, 0:W], ident_b, t2[:, 0:W], start=False, stop=True)
                    k.mm(pGi[:, 0:W], ident_b, t3[:, 0:W], start=True, stop=False)
                    k.mm(pGi[:, 0:W], nident_b[:], t4[:, 0:W], start=False, stop=True)
                    if prev_tail is not None:
                        prev_tail()
                        prev_tail = None
                    Sr = srring.get()
                    Si = siring.get()
                    if smp:
                        ar_ = lp[:, 3, pr_:pr_ + 1]
                        ai_ = lp[:, 4, pr_:pr_ + 1]
                        h0r = s5h0_r[:, pr_, :]
                        h0i = s5h0_i[:, pr_, :]
                        tb = tring.get()
                        k.ts("dve", tb[:, 0:16], h0i, ai_, None, ALU.mult)
                        injr = tring.get()
                        k.stt(injr[:, 0:16], h0r, ar_, tb[:, 0:16], ALU.mult, ALU.subtract)
                        tc = tring.get()
                        k.ts("dve", tc[:, 0:16], h0r, ai_, None, ALU.mult)
                        inji = tring.get()
                        k.stt(inji[:, 0:16], h0i, ar_, tc[:, 0:16], ALU.mult, ALU.add)
                        k.tt("dve", pGr[:, 0:W:LS], pGr[:, 0:W:LS], injr[:, 0:16], ALU.add)
                        k.tt("dve", pGi[:, 0:W:LS], pGi[:, 0:W:LS], inji[:, 0:16], ALU.add)
                        magm = sring.get()
                        k.ts("dve", magm[:], notstart, lp[:, 2, pr_:pr_ + 1], None, ALU.mult)
                        k.scan(Sr[:, 0:W], magm[:], pGr[:, 0:W], 0.0)
                        k.scan(Si[:, 0:W], magm[:], pGi[:, 0:W], 0.0)
                    else:
                        magb = lp[:, 2, pr_:pr_ + 1].to_broadcast([128, W])
                        k.scan(Sr[:, 0:W], magb, pGr[:, 0:W], s5car_r[:, pr_:pr_ + 1])
                        k.scan(Si[:, 0:W], magb, pGi[:, 0:W], s5car_i[:, pr_:pr_ + 1])
                        k.copy("dve", s5car_r[:, pr_:pr_ + 1], Sr[:, W - 1:W])
                        k.copy("dve", s5car_i[:, pr_:pr_ + 1], Si[:, W - 1:W])
                    k.copy("act", Srb[:, 0:W], Sr[:, 0:W])
                    k.copy("act", Sib[:, 0:W], Si[:, 0:W])
                    prev_tail = make_tail(pr_, q, py5, sn, cs, Sr, Si)
                    if q == 3:
                        prev_tail()
                        prev_tail = None
                ys = fring.get()
                k.stt(ys[:, 0:W], ub, pp[:, l, PP_S5D + kt:PP_S5D + kt + 1], py5[:, 0:W], ALU.mult, ALU.add)
                x2 = fring.get()
                k.act(x2[:, 0:W], ys[:, 0:W], AF.Square)
                k.ts("dve", x2[:, 0:W], x2[:, 0:W], 0.044715, 1.0, ALU.mult, ALU.add)
                k.tt("dve", x2[:, 0:W], x2[:, 0:W], ys[:, 0:W], ALU.mult)
                k.act(x2[:, 0:W], x2[:, 0:W], AF.Sigmoid, scale=1.5957691216057308)
                k.tt("dve", A16[:, 12 + kt, 0:W], ys[:, 0:W], x2[:, 0:W], ALU.mult)
            for m in range(4):
                pg = pring.get()
                for kt in range(4):
                    k.mm(pg[:, 0:W], glu_bf[:, kt, m * 128:(m + 1) * 128], A16[:, 12 + kt, 0:W],
                         start=(kt == 0), stop=(kt == 3))
                sg = fring.get()
                k.act(sg[:, 0:W], pg[:, 0:W], AF.Sigmoid, bias=pp[:, l, PP_GLB + m:PP_GLB + m + 1])
                k.tt("dve", sg[:, 0:W], sg[:, 0:W], A16[:, 12 + m, 0:W], ALU.mult)
                k.tt("dve", Y[:, 12 + m, 0:W], sg[:, 0:W], A16[:, 4 + m, 0:W], ALU.mult)

        def convert_w_in(l_):
            for bi_, (_, c0_, n_) in enumerate(blocks):
                k.dma("pool", wbf[l_][:, :, c0_:c0_ + n_], w_in[l_][:, :, c0_:c0_ + n_],
                      wk=["wbf%d_%d" % (l_, bi_)])

        def convert_w_out(l_):
            for m_ in range(8):
                k.dma("pool", wobf[l_, m_], w_out[l_][:, :, m_ * 128:(m_ + 1) * 128], wk=["wobf%d_%d" % (l_, m_)])

        for l in range(nlayers):
            k.dma("pool", glu_bf[:], glu_w[l])
            k.dma("pool", wa_bf[:], wa_bd[l])
            k.dma("pool", wx_bf[:], wx_bd[l])
            k.dma("pool", CT_re[:], ctpad_re[l])
            k.dma("pool", nCT_im[:], ctpad_im[l])
            if l == 0:
                convert_w_in(0)
            k.act(nCT_re[:].rearrange("p a b -> p (a b)"), CT_re[:].rearrange("p a b -> p (a b)"), AF.Copy, scale=-1.0)
            k.act(nCT_im[:].rearrange("p a b -> p (a b)"), nCT_im[:].rearrange("p a b -> p (a b)"), AF.Copy, scale=-1.0)
            k.act(expA[:], pdt[:, l, 16:32], AF.Exp)
            t = tring.get()
            k.act(t[:, 0:4], pp[:, l, PP_LLAM:PP_LLAM + 4], AF.Exp, scale=-1.0)
            t2 = tring.get()
            k.act(t2[:, 0:4], t[:, 0:4], AF.Ln, bias=epsc[:, 2:3])
            k.ts("dve", lp[:, 0, 0:4], t2[:, 0:4], -8.0, None, ALU.mult)
            dl = tring.get()
            k.act(dl[:, 0:16], pp[:, l, PP_LDT:PP_LDT + 16], AF.Exp)
            thp = lp[:, 1, :]
            k.stt(thp, pp[:, l, PP_LIM:PP_LIM + 16], 1.0 / TWO_PI, dl[:, 0:16], ALU.mult, ALU.mult)
            lm = tring.get()
            k.tt("dve", lm[:, 0:16], pp[:, l, PP_LRE:PP_LRE + 16], dl[:, 0:16], ALU.mult)
            mag = lp[:, 2, :]
            k.act(mag, lm[:, 0:16], AF.Exp)
            sn0 = tring.get()
            cs0 = tring.get()
            frac_sincos(thp, 16, sn0[:, 0:16], cs0[:, 0:16])
            k.tt("dve", lp[:, 3, :], mag, cs0[:, 0:16], ALU.mult)
            k.tt("dve", lp[:, 4, :], mag, sn0[:, 0:16], ALU.mult)
            lre_c = pp[:, l, PP_LRE:PP_LRE + 16]
            lim_c = pp[:, l, PP_LIM:PP_LIM + 16]
            nr = tring.get()
            k.ts("dve", nr[:, 0:16], lp[:, 3, :], -1.0, None, ALU.add)
            den = tring.get()
            t_a = tring.get()
            k.tt("dve", den[:, 0:16], lre_c, lre_c, ALU.mult)
            k.tt("dve", t_a[:, 0:16], lim_c, lim_c, ALU.mult)
            k.tt("dve", den[:, 0:16], den[:, 0:16], t_a[:, 0:16], ALU.add)
            k.op("dve", lambda v, o=den[:, 0:16]: v.reciprocal(out=o, in_=o), [den], [den])
            cre = lp[:, 5, :]
            cim = lp[:, 6, :]
            t_b = tring.get()
            k.tt("dve", cre, nr[:, 0:16], lre_c, ALU.mult)
            k.tt("dve", t_b[:, 0:16], lp[:, 4, :], lim_c, ALU.mult)
            k.tt("dve", cre, cre, t_b[:, 0:16], ALU.add)
            k.tt("dve", cre, cre, den[:, 0:16], ALU.mult)
            t_c = tring.get()
            k.tt("dve", cim, lp[:, 4, :], lre_c, ALU.mult)
            k.tt("dve", t_c[:, 0:16], nr[:, 0:16], lim_c, ALU.mult)
            k.tt("dve", cim, cim, t_c[:, 0:16], ALU.subtract)
            k.tt("dve", cim, cim, den[:, 0:16], ALU.mult)
            for c4 in range(4):
                bre = fring.get()
                bim = fring.get()
                k.dma("sp", bre[:].rearrange("p (a b) -> p a b", a=4), bpad_re[l][:, c4 * 4:(c4 + 1) * 4, :])
                k.dma("sp", bim[:].rearrange("p (a b) -> p a b", a=4), bpad_im[l][:, c4 * 4:(c4 + 1) * 4, :])
                crb = cre[:, c4 * 4:(c4 + 1) * 4].unsqueeze(2).to_broadcast([128, 4, 128])
                cib = cim[:, c4 * 4:(c4 + 1) * 4].unsqueeze(2).to_broadcast([128, 4, 128])
                v3 = lambda ap_: ap_.rearrange("p (a b) -> p a b", a=4)
                m_a, m_b, m_c, m_d = [f8ring.get() for _ in range(4)]
                k.tt("dve", v3(m_a), v3(bre[:]), crb, ALU.mult)
                k.tt("dve", v3(m_b), v3(bim[:]), cib, ALU.mult)
                k.tt("dve", v3(m_c), v3(bre[:]), cib, ALU.mult)
                k.tt("dve", v3(m_d), v3(bim[:]), crb, ALU.mult)
                o_re = zring.get()
                o_im = zring.get()
                k.tt("dve", o_re, m_a, m_b, ALU.subtract)
                k.tt("dve", o_im, m_c, m_d, ALU.add)
                for i4 in range(4):
                    k.tr(ptb[:, i4 * 128:(i4 + 1) * 128], o_re[:, i4 * 128:(i4 + 1) * 128], ident_b)
                    k.tr(ptb[:, 512 + i4 * 128:512 + (i4 + 1) * 128], o_im[:, i4 * 128:(i4 + 1) * 128], ident_b)
                k.copy("act", BbT_re[:, c4 * 4:(c4 + 1) * 4, :].rearrange("p a b -> p (a b)"), ptb[:, 0:512])
                k.copy("act", BbT_im[:, c4 * 4:(c4 + 1) * 4, :].rearrange("p a b -> p (a b)"), ptb[:, 512:1024])
            k.dma("sp", carry_s[:], sconv0[l])
            k.dma("sp", carry_l[:], lconv0[l])
            k.dma("sp", lruh0[:], lru0[l])
            k.dma("sp", s5h0_r[:], s5r0[l])
            k.dma("sp", s5h0_i[:], s5i0[l])
            k.memset("dve", hT[:], 0.0)
            k.memset("dve", hT_bf[:], 0.0)
            k.memset("dve", pcar_s[:], 0.0)
            k.memset("dve", pcar_l[:], 0.0)
            k.memset("dve", pcar_h[:], 0.0)
            k.memset("dve", s5car_r[:], 0.0)
            k.memset("dve", s5car_i[:], 0.0)

            if l == 0:
                prefetch_w(1)
            k.cp("setup done l%d" % l)
            for ti, (c0, W, kind) in enumerate(tiles):
                smp = kind == "s"
                k.cp("tile start l%d t%d" % (l, ti))
                if l == 0:
                    k.dma("sp", xt[:, :, 0:W], xin.rearrange("k p t -> p k t")[:, :, c0:c0 + W])
                else:
                    k.dma("sp", xt[:, :, 0:W], xsc[(l - 1) % 2].rearrange("k p t -> p k t")[:, :, c0:c0 + W],
                          rk=["xsc%d_%d" % ((l - 1) % 2, ti)])
                rmsnorm_tile(xt, W, PP_NG, l, lambda kt: hn[:, kt, 0:W])
                k.cp("norm done")
                for zb in range(2):
                    wb = next_block()
                    for m in range(4):
                        pm = proj(wb, m, W)
                        k.act(zs[:, zb * 4 + m, 0:W], pm[:, 0:W], AF.Silu)
                for xb in range(4):
                    wb = next_block()
                    for half in range(2):
                        js = [xb * 4 + half * 2 + i for i in range(2)]
                        pms = [proj(wb, half * 2 + i, W) for i in range(2)]
                        accs = conv_group(l, pms, W, smp,
                                          [(PP_CW + 4 * j, PP_CB + j, pcar_s[:, j, :], carry_s[:, j, :, :],
                                            sconv_s_o[l][:, j, :, :]) for j in js])

                        def silus(js=js, accs=accs):
                            for j, acc in zip(js, accs):
                                k.act(A16[:, j, 0:W], acc[:, 0:W], AF.Silu)
                        k.defer(silus, depth=1)
                wb = next_block()
                k.flush()
                nchk = W // 128
                pd = pring.get()
                for ci in range(nchk):
                    for kt in range(8):
                        k.mm(pd[:, ci * 16:(ci + 1) * 16], hn[:, kt, ci * 128:(ci + 1) * 128], wb[:, kt, 0:16],
                             start=(kt == 0), stop=(kt == 7))
                dt2 = dt_tm[:, 0:nchk, :]
                dtA2 = dtA_tm[:, 0:nchk, :]
                pd3 = pd[:, 0:nchk * 16].rearrange("p (c h) -> p c h", c=nchk)
                v = pre[:, 6, 0:nchk * 16].rearrange("p (c h) -> p c h", c=nchk)
                k.tt("dve", v, pd3, pdt[:, l, 0:16].unsqueeze(1).to_broadcast([128, nchk, 16]), ALU.add)
                k.act(v, v, AF.Exp)
                k.act(dt2, v, AF.Ln, bias=epsc[:, 2:3])
                k.stt(dtA2, dt2, -1.0, expA[:].unsqueeze(1).to_broadcast([128, nchk, 16]), ALU.mult, ALU.mult)
                hi_f = pre[:, 5, 0:nchk * 16]
                k.copy("act", dhl[:, 0, 0:nchk * 16], dtA_tm[:, 0:nchk, :].rearrange("p c h -> p (c h)"))
                k.copy("act", hi_f, dhl[:, 0, 0:nchk * 16])
                k.tt("dve", hi_f, dtA_tm[:, 0:nchk, :].rearrange("p c h -> p (c h)"), hi_f, ALU.subtract)
                k.copy("act", dhl[:, 1, 0:nchk * 16], hi_f)
                TRI_ = tri_s if smp else tri_p
                ONESM_ = ones_s if smp else ones_f
                n16 = nchk * 16
                pa = pring.get()
                dtA_flat = dtA_tm[:, 0:nchk, :].rearrange("p c h -> p (c h)")
                k.mm(pa[:, 0:n16], TRI_, dtA_flat)
                k.mm(pa[:, 64:64 + n16], ONESM_, dtA_flat)
                k.copy("act", pre[:, 0, 0:n16], pa[:, 0:n16])
                k.ts("dve", pre[:, 1, 0:n16], pa[:, 0:n16], -1.0, None, ALU.mult)
                k.tt("dve", pre[:, 2, 0:n16], pa[:, 64:64 + n16], pre[:, 0, 0:n16], ALU.subtract)
                k.act(pre[:, 2, 0:n16], pre[:, 2, 0:n16], AF.Exp)
                k.tt("dve", pre[:, 3, 0:n16], pre[:, 2, 0:n16], dt_tm[:, 0:nchk, :].rearrange("p c h -> p (c h)"),
                     ALU.mult)
                k.act(pre[:, 4, 0:n16], pa[:, 64:64 + n16], AF.Exp)
                if l == 0 and ti == 0:
                    convert_w_out(0)
                k.cp("stage A done")
                ssd_stage(l, ti, W, smp)
                k.cp("ssd done")
                wb = next_block()
                for m in range(4):
                    pm = proj(wb, m, W)
                    k.act(A16[:, m, 0:W], pm[:, 0:W], AF.Silu)
                wb = next_block()
                for half in range(2):
                    js = [half * 2, half * 2 + 1]
                    pms = [proj(wb, j, W) for j in js]
                    accs = conv_group(l, pms, W, smp, [(PP_LCW + 4 * j, PP_LCB + j, pcar_l[:, j, :],
                                                         carry_l[:, j, :, :], lconv_s_o[l][:, j, :, :]) for j in js])
                    lru_group(l, js, accs, W, smp)
                k.cp("lru done")
                k.flush()
                wb = next_block()
                for m in range(4):
                    pm = proj(wb, m, W)
                    k.act(A16[:, 4 + m, 0:W], pm[:, 0:W], AF.Silu)
                wb = next_block()
                for m in range(4):
                    pm = proj(wb, m, W)
                    k.copy("act", A16[:, 8 + m, 0:W], pm[:, 0:W])
                s5_stage(l, ti, c0, W, smp)
                if l + 1 < nlayers and ti == 0:
                    convert_w_in(l + 1)
                if l + 1 < nlayers and ti == 1:
                    convert_w_out(l + 1)
                k.cp("s5 done")
                wo = woring.get()
                k.dma("sp", wo[:], wobf[l, 0], rk=["wobf%d_0" % l])
                for m in range(8):
                    if m + 1 < 8:
                        wo_n = woring.get()
                        k.dma("sp", wo_n[:], wobf[l, m + 1], rk=["wobf%d_%d" % (l, m + 1)])
                    po = pring.get()
                    for kt in range(16):
                        k.mm(po[:, 0:W], wo[:, kt, :], Y[:, kt, 0:W], start=(kt == 0), stop=(kt == 15))
                    k.tt("dve", xt[:, m, 0:W], po[:, 0:W], xt[:, m, 0:W], ALU.add)
                    if m + 1 < 8:
                        wo = wo_n
                if l < nlayers - 1:
                    k.dma("sp", xsc[l % 2].rearrange("k p t -> p k t")[:, :, c0:c0 + W], xt[:, :, 0:W],
                          wk=["xsc%d_%d" % (l % 2, ti)])
                else:
                    rmsnorm_tile(xt, W, PP_FG, l, lambda kt: F8[:, kt, 0:W])
                    k.dma("sp", yout.rearrange("k p t -> p k t")[:, :, c0:c0 + W], F8[:, :, 0:W])
            k.dma("sp", sconv_p_o[l], pcar_s[:])
            k.dma("sp", lconv_p_o[l], pcar_l[:])
            k.dma("sp", lru_p_o[l], pcar_h[:])
            k.dma("sp", lru_s_o[l], hcar[:])
            k.dma("sp", s5r_p_o[l], s5p_r[:])
            k.dma("sp", s5i_p_o[l], s5p_i[:])
            k.dma("sp", s5r_s_o[l], s5o_r[:])
            k.dma("sp", s5i_s_o[l], s5o_i[:])
        k.finish()
        k.emit()
    return nc


def _consts():
    i = np.arange(128)
    ident = np.eye(128, dtype=np.float32)
    tri_p = (i[:, None] <= i[None, :]).astype(np.float32)
    same = (i[:, None] // LS == i[None, :] // LS)
    tri_s = (tri_p > 0) & same
    cfa = np.zeros((128, 8, 128), np.float32)
    import ml_dtypes
    trib = np.concatenate([tri_p, tri_s.astype(np.float32)], axis=1).astype(ml_dtypes.bfloat16)
    cfa[:, 0] = np.ascontiguousarray(trib).view(np.float32)
    cfa[:, 1] = tri_p
    cfa[:, 2] = tri_s
    cfa[:, 3] = 1.0
    cfa[:, 4] = same
    cfa[:, 5] = np.broadcast_to((i % LS)[None, :], (128, 128))
    cfa[:, 6] = np.broadcast_to((i % LS != 0)[None, :], (128, 128))
    cfa[:, 7, 0:NSEQ] = (i[:, None] // LS == np.arange(NSEQ)[None, :])
    cba = np.zeros((128, 4, 128), np.float32)
    cba[:, 0] = ident
    cba[:, 1] = 1.0
    cba[:, 2] = np.where(tri_p > 0, 0.0, -30000.0)
    cba[:, 3] = np.where(tri_s, 0.0, -30000.0)
    iota = np.broadcast_to(np.arange(WT, dtype=np.float32)[None, :], (128, WT)).copy()
    return cfa, cba, iota


def _fm(v, nt):
    return np.moveaxis(v.reshape(v.shape[:-1] + (nt, 128)), -1, -2)


def _prep_shared(inp):
    f = lambda a: np.ascontiguousarray(a, dtype=np.float32)
    sh = {}
    sh["w_in_r"] = f(inp["w_in"].reshape(NL, 8, 128, IN_DIM).transpose(0, 2, 1, 3))
    sh["w_out_r"] = f(inp["w_out"].reshape(NL, 16, 128, D).transpose(0, 2, 1, 3))
    sh["glu_r"] = f(inp["s5_glu_w"].reshape(NL, 4, 128, 512).transpose(0, 2, 1, 3))
    for nm, src in (("wa_bd", inp["lru_wa"]), ("wx_bd", inp["lru_wx"])):
        bd = np.zeros((NL, 128, 4, 128), np.float32)
        for m in range(4):
            for k2 in range(2):
                bd[:, k2 * 64:(k2 + 1) * 64, m, k2 * 64:(k2 + 1) * 64] = src[:, 2 * m + k2]
        sh[nm] = bd
    pp = np.zeros((128, NL, NPP), np.float32)
    for l in range(NL):
        pp[:, l, PP_NG:PP_NG + 8] = _fm(inp["norm_g"][l], 8)
        pp[:, l, PP_NG2:PP_NG2 + 8] = _fm(inp["ssd_norm_g"][l], 8)
        pp[:, l, PP_SD:PP_SD + 8] = _fm(np.repeat(inp["ssd_d"][l], 64), 8)
        pp[:, l, PP_CW:PP_CW + 64] = inp["ssd_conv_w"][l].reshape(4, 16, 128).transpose(2, 1, 0).reshape(128, 64)
        pp[:, l, PP_CB:PP_CB + 16] = _fm(inp["ssd_conv_b"][l], 16)
        pp[:, l, PP_LCW:PP_LCW + 16] = inp["lru_conv_w"][l].reshape(4, 4, 128).transpose(2, 1, 0).reshape(128, 16)
        pp[:, l, PP_LCB:PP_LCB + 4] = _fm(inp["lru_conv_b"][l], 4)
        pp[:, l, PP_LBA:PP_LBA + 4] = _fm(inp["lru_ba"][l], 4)
        pp[:, l, PP_LBX:PP_LBX + 4] = _fm(inp["lru_bx"][l], 4)
        pp[:, l, PP_LLAM:PP_LLAM + 4] = _fm(inp["lru_lambda"][l], 4)
        pp[:, l, PP_S5D:PP_S5D + 4] = _fm(inp["s5_d"][l], 4)
        pp[:, l, PP_GLB:PP_GLB + 4] = _fm(inp["s5_glu_b"][l], 4)
        pp[:, l, PP_LRE:PP_LRE + 16] = inp["s5_lambda_re"][l].reshape(16, 128).T
        pp[:, l, PP_LIM:PP_LIM + 16] = inp["s5_lambda_im"][l].reshape(16, 128).T
        pp[:, l, PP_LDT:PP_LDT + 16] = np.repeat(inp["s5_log_dt"][l], 64).reshape(16, 128).T
        pp[:, l, PP_FG:PP_FG + 8] = _fm(inp["final_norm_g"], 8)
    sh["pp_in"] = pp
    pdt = np.zeros((128, NL, 32), np.float32)
    pdt[:, :, 0:16] = inp["ssd_dt_bias"][None]
    pdt[:, :, 16:32] = inp["ssd_a_log"][None]
    sh["pdt_in"] = pdt
    bre = np.zeros((NL, 128, 16, 128), np.float32)
    bim = np.zeros((NL, 128, 16, 128), np.float32)
    cre = np.zeros((NL, 128, 16, 128), np.float32)
    cim = np.zeros((NL, 128, 16, 128), np.float32)
    for g in range(32):
        pr, g2, gl = g // 2, g % 2, g % 8
        bre[:, g2 * 64:(g2 + 1) * 64, pr, gl * 16:(gl + 1) * 16] = inp["s5_b_re"][:, g]
        bim[:, g2 * 64:(g2 + 1) * 64, pr, gl * 16:(gl + 1) * 16] = inp["s5_b_im"][:, g]
        cre[:, g2 * 64:(g2 + 1) * 64, pr, gl * 16:(gl + 1) * 16] = inp["s5_c_re"][:, g].transpose(0, 2, 1)
        cim[:, g2 * 64:(g2 + 1) * 64, pr, gl * 16:(gl + 1) * 16] = inp["s5_c_im"][:, g].transpose(0, 2, 1)
    sh["bpadT_re"], sh["bpadT_im"], sh["ctpad_re"], sh["ctpad_im"] = bre, bim, cre, cim
    cfa, cba, iota = _consts()
    sh["c_f32"], sh["c_bf"], sh["c_iota"] = cfa, cba, iota
    return sh


def _prep_core(inp, c):
    f = lambda a: np.ascontiguousarray(a, dtype=np.float32)
    sl = slice(NSEQ * c, NSEQ * (c + 1))
    m = {}
    x_tok = np.concatenate([inp["x_prompt"][c], inp["x_sample"][sl].reshape(TS, D)], axis=0)
    m["xin"] = f(x_tok.T.reshape(8, 128, TT))
    m["h0T"] = f(inp["state_ssd"][:, sl].transpose(0, 1, 4, 2, 3).reshape(NL, NSEQ, 128, 1024))
    m["sconv0"] = f(inp["state_ssd_conv"][:, sl].reshape(NL, NSEQ, 3, 16, 128).transpose(0, 4, 3, 1, 2))
    m["lconv0"] = f(inp["state_lru_conv"][:, sl].reshape(NL, NSEQ, 3, 4, 128).transpose(0, 4, 3, 1, 2))
    m["lru0"] = f(inp["state_lru"][:, sl].reshape(NL, NSEQ, 4, 128).transpose(0, 3, 2, 1))
    m["s5r0"] = f(inp["state_s5_re"][:, sl].reshape(NL, NSEQ, 16, 128).transpose(0, 3, 2, 1))
    m["s5i0"] = f(inp["state_s5_im"][:, sl].reshape(NL, NSEQ, 16, 128).transpose(0, 3, 2, 1))
    return m


_PROG = {}


def kernel(**inputs):
    inp = {k_: np.asarray(v) for k_, v in inputs.items()}
    if "nc" not in _PROG:
        _PROG["nc"] = build_program(NL)
    nc = _PROG["nc"]
    shared = _prep_shared(inp)
    in_maps = []
    for c in range(NCORES):
        m = dict(shared)
        m.update(_prep_core(inp, c))
        in_maps.append(m)
    res = run_bass_kernel_spmd(nc, in_maps, core_ids=list(range(NCORES)))
    R = res.results
    B = NCORES
    y_prompt = np.zeros((B, TP, D), np.float32)
    y_sample = np.zeros((B * NSEQ, LS, D), np.float32)
    ssd_p = np.zeros((NL, B, 16, 64, 128), np.float32)
    ssd_s = np.zeros((NL, B * NSEQ, 16, 64, 128), np.float32)
    ssd_conv_p = np.zeros((NL, B, 3, 2048), np.float32)
    ssd_conv_s = np.zeros((NL, B * NSEQ, 3, 2048), np.float32)
    lru_p = np.zeros((NL, B, 512), np.float32)
    lru_s = np.zeros((NL, B * NSEQ, 512), np.float32)
    lru_conv_p = np.zeros((NL, B, 3, 512), np.float32)
    lru_conv_s = np.zeros((NL, B * NSEQ, 3, 512), np.float32)
    s5_re_p = np.zeros((NL, B, 32, 64), np.float32)
    s5_re_s = np.zeros((NL, B * NSEQ, 32, 64), np.float32)
    s5_im_p = np.zeros((NL, B, 32, 64), np.float32)
    s5_im_s = np.zeros((NL, B * NSEQ, 32, 64), np.float32)
    for c in range(B):
        r = R[c]
        sl = slice(NSEQ * c, NSEQ * (c + 1))
        y = np.asarray(r["yout"]).reshape(D, TT).T
        y_prompt[c] = y[0:TP]
        y_sample[sl] = y[TP:].reshape(NSEQ, LS, D)
        ssd_p[:, c] = np.asarray(r["ssd_p_o"]).reshape(NL, 128, 16, 64).transpose(0, 2, 3, 1)
        ssd_s[:, sl] = np.asarray(r["ssd_s_o"]).reshape(NL, NSEQ, 128, 16, 64).transpose(0, 1, 3, 4, 2)
        ssd_conv_p[:, c] = np.asarray(r["sconv_p_o"]).transpose(0, 3, 2, 1).reshape(NL, 3, 2048)
        ssd_conv_s[:, sl] = np.asarray(r["sconv_s_o"]).transpose(0, 3, 4, 2, 1).reshape(NL, NSEQ, 3, 2048)
        lru_p[:, c] = np.asarray(r["lru_p_o"]).transpose(0, 2, 1).reshape(NL, 512)
        lru_s[:, sl] = np.asarray(r["lru_s_o"]).transpose(0, 3, 2, 1).reshape(NL, NSEQ, 512)
        lru_conv_p[:, c] = np.asarray(r["lconv_p_o"]).transpose(0, 3, 2, 1).reshape(NL, 3, 512)
        lru_conv_s[:, sl] = np.asarray(r["lconv_s_o"]).transpose(0, 3, 4, 2, 1).reshape(NL, NSEQ, 3, 512)
        s5_re_p[:, c] = np.asarray(r["s5r_p_o"]).transpose(0, 2, 1).reshape(NL, 32, 64)
        s5_im_p[:, c] = np.asarray(r["s5i_p_o"]).transpose(0, 2, 1).reshape(NL, 32, 64)
        s5_re_s[:, sl] = np.asarray(r["s5r_s_o"]).transpose(0, 3, 2, 1).reshape(NL, NSEQ, 32, 64)
        s5_im_s[:, sl] = np.asarray(r["s5i_s_o"]).transpose(0, 3, 2, 1).reshape(NL, NSEQ, 32, 64)
    return (y_prompt, y_sample, ssd_p, ssd_s, ssd_conv_p, ssd_conv_s, lru_p, lru_s,
            lru_conv_p, lru_conv_s, s5_re_p, s5_re_s, s5_im_p, s5_im_s)
```

```python
import math
import os
from contextlib import ExitStack

import numpy as np
import concourse.bass as bass
import concourse.mybir as mybir
from concourse.bass_utils import run_bass_kernel_spmd

F32 = mybir.dt.float32
BF16 = mybir.dt.bfloat16
I32 = mybir.dt.int32
ALU = mybir.AluOpType
AF = mybir.ActivationFunctionType

NCORES = 8
D = 1024
NL = 4
TP = 2048
NSEQ = 16
LS = 8
TS = NSEQ * LS
TT = TP + TS
WT = 512
IN_DIM = 5136
EPS = 1e-6
TWO_PI = 2.0 * math.pi

PP_NG = 0
PP_NG2 = 8
PP_SD = 16
PP_CW = 24
PP_CB = 88
PP_LCW = 104
PP_LCB = 120
PP_LBA = 124
PP_LBX = 128
PP_LLAM = 132
PP_S5D = 136
PP_GLB = 140
PP_LRE = 144
PP_LIM = 160
PP_LDT = 176
PP_FG = 192
NPP = 200

EPOCH = 12000


class Ring:
    def __init__(self, bufs, full=None):
        self.bufs = bufs
        self.full = full
        self.i = 0

    def get(self):
        b = self.bufs[self.i % len(self.bufs)]
        self.i += 1
        return b

    def get_full(self):
        b = self.full[self.i % len(self.full)]
        self.i += 1
        return b


class KB:
    ENG = ("pe", "act", "dve", "pool", "sp")

    def __init__(self, nc, es):
        self.nc = nc
        self.es = es
        self.prog = {e: [] for e in self.ENG}
        self.cnt = {e: 0 for e in self.ENG}
        self.sem = {}
        self.nsem = 0
        for e in ("pe", "act", "dve", "pool"):
            self.sem[e] = self._newsem("c_" + e)
        self.waited = {e: {} for e in self.ENG}
        self.dead = False
        self.pending = []
        self.ncp = 0
        self.stop = int(os.environ.get("KSTOP", "-1"))
        self.lastw = {}
        self.readers = {}
        self.dsem = {}
        self.drr = {}
        for q, n in (("sp", 16), ("pool", 24), ("act", 2)):
            self.dsem[q] = [[self._newsem("d_%s%d" % (q, i)), 0] for i in range(n)]
            self.drr[q] = 0

    def _newsem(self, name):
        self.nsem += 1
        return self.es.enter_context(self.nc.semaphore("%s_%d" % (name, self.nsem)))

    slotw = {}

    def _keys(self, r):
        if isinstance(r, str):
            return [r]
        name = r.name
        w = self.slotw.get(name)
        if w is None:
            return [name]
        ap = r.ap
        off = r.offset % ap[0][0]
        hi = off + sum((c - 1) * s for s, c in ap[1:])
        return ["%s:%d" % (name, i) for i in range(off // w, hi // w + 1)]

    def defer(self, fn, depth=1):
        self.pending.append(fn)
        while len(self.pending) > depth:
            self.pending.pop(0)()

    def flush(self):
        while self.pending:
            self.pending.pop(0)()

    def cp(self, name=""):
        self.ncp += 1
        if self.stop >= 0 and self.ncp > self.stop and not self.dead:
            self.dead = True
            print("KSTOP: program truncated before checkpoint", self.ncp, name, flush=True)

    def op(self, e, fn, reads=(), writes=(), dma=False):
        if self.dead:
            return None
        waits = {}

        def need(tok, raw):
            if tok is None:
                return
            sem, val, src, isdma = tok
            if src == e and not isdma and e == "pe":
                return
            if self.waited[e].get(sem.name, 0) >= val:
                return
            if sem.name not in waits or waits[sem.name][1] < val:
                waits[sem.name] = (sem, val)

        rk = [x for r in reads for x in self._keys(r)]
        wk = [x for w in writes for x in self._keys(w)]
        for r in rk:
            need(self.lastw.get(r), True)
        for w in wk:
            need(self.lastw.get(w), False)
            for t in self.readers.get(w, {}).values():
                need(t, False)
        if dma:
            slot = self.dsem[e][self.drr[e] % len(self.dsem[e])]
            self.drr[e] += 1
            if slot[1] > 0:
                need((slot[0], slot[1], e, True), True)
            slot[1] += 16
            tok = (slot[0], slot[1], e, True)
            inc = (slot[0], 16)
        else:
            if self.cnt[e] >= EPOCH:
                self.sem[e] = self._newsem("c_" + e)
                self.cnt[e] = 0
            self.cnt[e] += 1
            tok = (self.sem[e], self.cnt[e], e, False)
            inc = (self.sem[e], 1)
        for s, v in waits.values():
            self.waited[e][s.name] = v
        self.prog[e].append((list(waits.values()), fn, inc))
        for r in rk:
            self.readers.setdefault(r, {})[tok[0].name] = tok
        for w in wk:
            self.lastw[w] = tok
            self.readers[w] = {}
        return tok

    def finish(self):
        fin = []
        for q in self.dsem:
            for sem, v in self.dsem[q]:
                if v > 0:
                    fin.append((sem, v))
        self.final_waits = fin

    def emit(self):
        nc = self.nc
        handles = {"pe": "tensor", "act": "scalar", "dve": "vector", "pool": "gpsimd", "sp": "sync"}
        with nc.Block() as block:
            for e in self.ENG:
                prog = self.prog[e]
                extra = self.final_waits if e == "sp" else []

                def body(eng, prog=prog, extra=extra):
                    for waits, fn, inc in prog:
                        for s, v in waits:
                            eng.wait_ge(s, v)
                        ins = fn(eng)
                        ins.then_inc(inc[0], inc[1])
                    for s, v in extra:
                        eng.wait_ge(s, v)

                getattr(block, handles[e])(body)

    def mm(self, out, lhsT, rhs, start=True, stop=True):
        self.op("pe", lambda t: t.matmul(out, lhsT=lhsT, rhs=rhs, start=start, stop=stop),
                [lhsT, rhs], [out])

    def tr(self, out, in_, ident):
        self.op("pe", lambda t: t.transpose(out, in_, ident), [in_, ident], [out])

    def act(self, out, in_, func, bias=None, scale=None):
        rd = [in_]
        kw = {}
        if bias is not None:
            kw["bias"] = bias
            if not isinstance(bias, (int, float)):
                rd.append(bias)
        if scale is not None:
            kw["scale"] = scale
            if not isinstance(scale, (int, float)):
                rd.append(scale)
        self.op("act", lambda a: a.activation(out=out, in_=in_, func=func, **kw), rd, [out])

    def tt(self, e, out, in0, in1, op):
        self.op(e, lambda v: v.tensor_tensor(out=out, in0=in0, in1=in1, op=op), [in0, in1], [out])

    def ts(self, e, out, in0, s1, s2, op0, op1=None):
        rd = [in0]
        for s in (s1, s2):
            if s is not None and not isinstance(s, (int, float)):
                rd.append(s)
        if op1 is None:
            self.op(e, lambda v: v.tensor_scalar(out=out, in0=in0, scalar1=s1, scalar2=None, op0=op0),
                    rd, [out])
        else:
            self.op(e, lambda v: v.tensor_scalar(out=out, in0=in0, scalar1=s1, scalar2=s2, op0=op0, op1=op1),
                    rd, [out])

    def stt(self, out, in0, scalar, in1, op0, op1):
        rd = [in0, in1]
        if not isinstance(scalar, (int, float)):
            rd.append(scalar)
        self.op("dve", lambda v: v.scalar_tensor_tensor(out=out, in0=in0, scalar=scalar, in1=in1,
                                                        op0=op0, op1=op1), rd, [out])

    def scan(self, out, d0, d1, init):
        rd = [d0, d1]
        if not isinstance(init, (int, float)):
            rd.append(init)
        self.op("dve", lambda v: v.tensor_tensor_scan(out=out, data0=d0, data1=d1, initial=init,
                                                      op0=ALU.mult, op1=ALU.add), rd, [out])

    def copy(self, e, out, in_):
        if e == "act":
            self.op("act", lambda a: a.activation(out=out, in_=in_, func=AF.Copy), [in_], [out])
        else:
            self.op(e, lambda v: v.tensor_copy(out=out, in_=in_), [in_], [out])

    def memset(self, e, out, val):
        self.op(e, lambda v: v.memset(out, val), [], [out])

    def dma(self, q, out, in_, rk=(), wk=()):
        self.op(q, lambda g: g.dma_start(out=out, in_=in_), [in_] + list(rk), [out] + list(wk), dma=True)


def build_program(nlayers=NL):
    nc = bass.Bass("TRN2", target_bir_lowering=False)

    def din(name, shape, dt=F32):
        return nc.dram_tensor(name, list(shape), dt, kind="ExternalInput").ap()

    def dout(name, shape, dt=F32):
        return nc.dram_tensor(name, list(shape), dt, kind="ExternalOutput").ap()

    xin = din("xin", [8, 128, TT])
    w_in = din("w_in_r", [NL, 128, 8, IN_DIM])
    w_out = din("w_out_r", [NL, 128, 16, D])
    glu_w = din("glu_r", [NL, 128, 4, 512])
    wa_bd = din("wa_bd", [NL, 128, 4, 128])
    wx_bd = din("wx_bd", [NL, 128, 4, 128])
    pp_d = din("pp_in", [128, NL, NPP])
    pdt_d = din("pdt_in", [128, NL, 32])
    bpad_re = din("bpadT_re", [NL, 128, 16, 128])
    bpad_im = din("bpadT_im", [NL, 128, 16, 128])
    ctpad_re = din("ctpad_re", [NL, 128, 16, 128])
    ctpad_im = din("ctpad_im", [NL, 128, 16, 128])
    h0T_d = din("h0T", [NL, NSEQ, 128, 1024])
    sconv0 = din("sconv0", [NL, 128, 16, NSEQ, 3])
    lconv0 = din("lconv0", [NL, 128, 4, NSEQ, 3])
    lru0 = din("lru0", [NL, 128, 4, NSEQ])
    s5r0 = din("s5r0", [NL, 128, 16, NSEQ])
    s5i0 = din("s5i0", [NL, 128, 16, NSEQ])
    c_f32 = din("c_f32", [128, 8, 128])
    c_bf = din("c_bf", [128, 4, 128])
    c_iota = din("c_iota", [128, WT])

    yout = dout("yout", [8, 128, TT])
    ssd_p_o = dout("ssd_p_o", [NL, 128, 1024])
    ssd_s_o = dout("ssd_s_o", [NL, NSEQ, 128, 1024])
    sconv_p_o = dout("sconv_p_o", [NL, 128, 16, 3])
    sconv_s_o = dout("sconv_s_o", [NL, 128, 16, NSEQ, 3])
    lru_p_o = dout("lru_p_o", [NL, 128, 4])
    lru_s_o = dout("lru_s_o", [NL, 128, 4, NSEQ])
    lconv_p_o = dout("lconv_p_o", [NL, 128, 4, 3])
    lconv_s_o = dout("lconv_s_o", [NL, 128, 4, NSEQ, 3])
    s5r_p_o = dout("s5r_p_o", [NL, 128, 16])
    s5r_s_o = dout("s5r_s_o", [NL, 128, 16, NSEQ])
    s5i_p_o = dout("s5i_p_o", [NL, 128, 16])
    s5i_s_o = dout("s5i_s_o", [NL, 128, 16, NSEQ])
    xsc = nc.dram_tensor("xsc", [2, 8, 128, TT], F32, kind="Internal").ap()
    wbf = nc.dram_tensor("wbf", [NL, 128, 8, IN_DIM], BF16, kind="Internal").ap()
    wobf = nc.dram_tensor("wobf", [NL, 8, 128, 16, 128], BF16, kind="Internal").ap()

    with ExitStack() as es:
        k = KB(nc, es)
        k.slotw = {"F8": 512, "zs": 512, "A16": 512, "Y": 512, "xt": 512, "hn": 512, "LT": 128,
                   "hnew": 512, "h0f": 512}

        def sb(name, shape, dt):
            return es.enter_context(nc.sbuf_tensor(name, list(shape), dt))

        def ps(name, shape, dt):
            return es.enter_context(nc.psum_tensor(name, list(shape), dt))

        xt = sb("xt", [128, 8, WT], F32)
        hn = sb("hn", [128, 8, WT], BF16)
        zs = sb("zs", [128, 8, WT], BF16)
        A16 = sb("A16", [128, 16, WT], BF16)
        F8 = sb("F8", [128, 8, WT], F32)
        Y = sb("Y", [128, 16, WT], BF16)
        LT = sb("LT", [128, 16, 128], BF16)
        x_tm = sb("x_tm", [128, 1024], BF16)
        xw_tm = sb("xw_tm", [128, 1024], BF16)
        B_tm = sb("B_tm", [128, 512], BF16)
        bmring = Ring([sb("bm%d" % i, [128, 512], BF16) for i in range(1)])
        hT = sb("hT", [128, 1024], F32)
        hT_bf = sb("hT_bf", [128, 1024], BF16)
        dt_tm = sb("dt_tm", [128, 4, 16], F32)
        dtA_tm = sb("dtA_tm", [128, 4, 16], F32)
        pre = sb("pre", [128, 7, 64], F32)
        wring = Ring([sb("wbuf%d" % i, [128, 8, 512], BF16) for i in range(2)])
        woring = Ring([sb("wobuf%d" % i, [128, 16, 128], BF16) for i in range(2)])
        glu_bf = sb("glu_bf", [128, 4, 512], BF16)
        BbT_re = sb("BbT_re", [128, 16, 128], BF16)
        BbT_im = sb("BbT_im", [128, 16, 128], BF16)
        CT_re = sb("CT_re", [128, 16, 128], BF16)
        nCT_re = sb("nCT_re", [128, 16, 128], BF16)
        nCT_im = sb("nCT_im", [128, 16, 128], BF16)
        wa_bf = sb("wa_bf", [128, 4, 128], BF16)
        wx_bf = sb("wx_bf", [128, 4, 128], BF16)
        pp = sb("pp", [128, NL, NPP], F32)
        pdt = sb("pdt", [128, NL, 32], F32)
        expA = sb("expA", [128, 16], F32)
        lp = sb("lp", [128, 8, 16], F32)
        cf = sb("cf", [128, 8, 128], F32)
        cb = sb("cb", [128, 4, 128], BF16)
        iota = sb("iota", [128, WT], F32)
        iota_c = sb("iota_c", [128, WT], F32)
        carry_s = sb("carry_s", [128, 16, NSEQ, 3], F32)
        carry_l = sb("carry_l", [128, 4, NSEQ, 3], F32)
        pcar_s = sb("pcar_s", [128, 16, 3], F32)
        pcar_l = sb("pcar_l", [128, 4, 3], F32)
        pcar_h = sb("pcar_h", [128, 4], F32)
        hcar = sb("hcar", [128, 4, NSEQ], F32)
        s5car_r = sb("s5car_r", [128, 16], F32)
        s5car_i = sb("s5car_i", [128, 16], F32)
        s5p_r = sb("s5p_r", [128, 16], F32)
        s5p_i = sb("s5p_i", [128, 16], F32)
        s5o_r = sb("s5o_r", [128, 16, NSEQ], F32)
        s5o_i = sb("s5o_i", [128, 16, NSEQ], F32)
        s5h0_r = sb("s5h0_r", [128, 16, NSEQ], F32)
        s5h0_i = sb("s5h0_i", [128, 16, NSEQ], F32)
        lruh0 = sb("lruh0", [128, 4, NSEQ], F32)
        h0f = sb("h0f", [128, 1024], F32)
        h0bring = Ring([sb("h0b%d" % i, [128, 1024], BF16) for i in range(2)])
        hnew = sb("hnew", [128, 1024], F32)
        dtA_rep = h0f
        _fr = [sb("fr%d" % i, [128, WT + 4], F32) for i in range(8)]
        fring = Ring([t_[:, 0:WT] for t_ in _fr], [t_[:, :] for t_ in _fr])
        iring = Ring([sb("ir%d" % i, [128, WT], I32) for i in range(2)])
        fring_x = [h0f[:, 0:512], h0f[:, 512:1024]]
        sring2 = [hnew[:, 0:512], hnew[:, 512:1024]]
        sring = Ring([sb("sr%d" % i, [128, 128], F32) for i in range(4)])
        tring = Ring([sb("tn%d" % i, [128, 48], F32) for i in range(10)])
        dhl = sb("dhl", [128, 2, 64], BF16)
        f8ring = Ring([F8[:, i, :] for i in range(8)])
        zring = Ring([zs[:, i, :] for i in range(8)])

        pheld = ps("pheld", [128, 512], F32)
        pheld2 = ps("pheld2", [128, 512], F32)
        pheld3 = ps("pheld3", [128, 512], F32)
        pring = Ring([ps("pb%d" % i, [128, 512], F32) for i in range(4)])
        ptb = ps("ptb", [128, 1024], BF16)

        tri_b2 = cf[:, 0, :].bitcast(BF16)
        tri_p = cf[:, 1, :]
        tri_s = cf[:, 2, :]
        ones_f = cf[:, 3, :]
        ones_s = cf[:, 4, :]
        iota_s = cf[:, 5, :]
        notstart = cf[:, 6, :]
        ind = cf[:, 7, :]
        ident_b = cb[:, 0, :]
        ones_b = cb[:, 1, :]
        neg_p = cb[:, 2, :]
        neg_s = cb[:, 3, :]

        k.dma("sp", cf[:], c_f32)
        nident_b = sb("nident_b", [128, 128], BF16)
        k.dma("pool", cb[:], c_bf)
        k.dma("sp", iota[:], c_iota)
        k.dma("sp", pp[:], pp_d)
        k.dma("sp", pdt[:], pdt_d)
        k.act(nident_b[:], ident_b, AF.Copy, scale=-1.0)

        tiles = [(i * WT, WT, "p") for i in range(TP // WT)] + [(TP, TS, "s")]
        blocks = [("z", 0, 512), ("z", 512, 512), ("x", 1024, 512), ("x", 1536, 512), ("B", 2048, 512),
                  ("C", 2560, 512), ("dt", 3072, 16), ("lg", 3600, 512), ("lx", 3088, 512),
                  ("sg", 4624, 512), ("su", 4112, 512)]
        stream = [(l, ti, bi) for l in range(nlayers) for ti in range(len(tiles)) for bi in range(len(blocks))]
        wbufs = {}
        st = {"next": 0, "item": 0}

        def prefetch_w(upto):
            while st["next"] < len(stream) and st["next"] <= upto:
                l_, ti_, bi_ = stream[st["next"]]
                _, c0_, n_ = blocks[bi_]
                buf = wring.get()
                k.dma("sp", buf[:, :, 0:n_], wbf[l_][:, :, c0_:c0_ + n_], rk=["wbf%d_%d" % (l_, bi_)])
                wbufs[st["next"]] = buf
                st["next"] += 1

        def next_block():
            prefetch_w(st["item"] + 1)
            wb = wbufs.pop(st["item"])
            st["item"] += 1
            return wb

        pringA = Ring(pring.bufs + [pheld, pheld2, pheld3])

        def proj(wb, m, W):
            pm = pringA.get()
            for kt in range(8):
                k.mm(pm[:, 0:W], wb[:, kt, m * 128:(m + 1) * 128], hn[:, kt, 0:W], start=(kt == 0), stop=(kt == 7))
            return pm

        def rmsnorm_tile(src3, ncol, gcol0, l, out_fn):
            pn = pring.get()
            for kt in range(8):
                sq = zring.get()
                k.act(sq[:, 0:ncol], src3[:, kt, 0:ncol], AF.Square)
                k.mm(pn[:, 0:ncol], ones_b, sq[:, 0:ncol], start=(kt == 0), stop=(kt == 7))
            t1 = fring.get()
            k.act(t1[:, 0:ncol], pn[:, 0:ncol], AF.Ln, bias=epsc[:, 0:1], scale=1.0 / D)
            rstd = fring.get()
            k.act(rstd[:, 0:ncol], t1[:, 0:ncol], AF.Exp, scale=-0.5)
            for kt in range(8):
                k.stt(out_fn(kt), src3[:, kt, 0:ncol], pp[:, l, gcol0 + kt:gcol0 + kt + 1], rstd[:, 0:ncol],
                      ALU.mult, ALU.mult)

        def frac_sincos(u, W, sn_out, cs_out):
            ui = iring.get()
            k.copy("dve", ui[:, 0:W], u)
            r = fring.get()
            k.tt("dve", r[:, 0:W], u, ui[:, 0:W], ALU.subtract)
            k.act(sn_out, r[:, 0:W], AF.Sin, scale=TWO_PI)
            ar = fring.get()
            k.stt(ar[:, 0:W], r[:, 0:W], -1.0, r[:, 0:W], ALU.mult, ALU.max)
            k.act(cs_out, ar[:, 0:W], AF.Sin, bias=epsc[:, 1:2], scale=-TWO_PI)

        epsc = sb("epsc", [128, 4], F32)
        k.memset("dve", epsc[:, 0:1], EPS)
        k.memset("dve", epsc[:, 1:2], math.pi / 2.0)
        k.memset("dve", epsc[:, 2:3], 1.0)

        def conv_group(l, pms, W, smp, specs):
            n = len(pms)
            accs = [fring.get() for _ in range(n)]
            raws = [fring.get_full() for _ in range(n)]
            if smp:
                rawv = [r_[:, 0:NSEQ * (LS + 3)].rearrange("p (s t) -> p s t", s=NSEQ) for r_ in raws]
                pmv = [p_[:, 0:W].rearrange("p (s t) -> p s t", s=NSEQ) for p_ in pms]
                accv = [a_[:, 0:W].rearrange("p (s t) -> p s t", s=NSEQ) for a_ in accs]
                for i in range(n):
                    k.copy("dve", rawv[i][:, :, 0:3], specs[i][3])
                for i in range(n):
                    k.copy("act", rawv[i][:, :, 3:3 + LS], pmv[i])
                    k.act(accv[i], pmv[i], AF.Identity, bias=pp[:, l, specs[i][1]:specs[i][1] + 1],
                          scale=pp[:, l, specs[i][0] + 3:specs[i][0] + 4])
                for i in range(n):
                    st3 = tring.get()
                    st3v = st3[:, 0:48].rearrange("p (s t) -> p s t", s=NSEQ)
                    k.copy("dve", st3v, rawv[i][:, :, LS:LS + 3])
                    k.dma("sp", specs[i][4], st3v)
                for kk in range(3):
                    for i in range(n):
                        k.stt(accv[i], rawv[i][:, :, kk:kk + LS], pp[:, l, specs[i][0] + kk:specs[i][0] + kk + 1],
                              accv[i], ALU.mult, ALU.add)
            else:
                for i in range(n):
                    k.copy("dve", raws[i][:, 0:3], specs[i][2])
                for i in range(n):
                    k.copy("act", raws[i][:, 3:3 + W], pms[i][:, 0:W])
                    k.act(accs[i][:, 0:W], pms[i][:, 0:W], AF.Identity, bias=pp[:, l, specs[i][1]:specs[i][1] + 1],
                          scale=pp[:, l, specs[i][0] + 3:specs[i][0] + 4])
                for kk in range(3):
                    for i in range(n):
                        k.stt(accs[i][:, 0:W], raws[i][:, kk:kk + W], pp[:, l, specs[i][0] + kk:specs[i][0] + kk + 1],
                              accs[i][:, 0:W], ALU.mult, ALU.add)
                for i in range(n):
                    k.copy("dve", specs[i][2], raws[i][:, W:W + 3])
            return accs

        def ssd_stage(l, ti, W, smp):
            xc = A16
            nch = W // 128
            TRI = tri_s if smp else tri_p
            ONESM = ones_s if smp else ones_f
            NEG = neg_s if smp else neg_p
            TRIB = tri_b2[:, 128:256] if smp else tri_b2[:, 0:128]
            for ci in range(nch):
                cs = slice(ci * 128, (ci + 1) * 128)
                dtA = dtA_tm[:, ci, :]
                dtc = dt_tm[:, ci, :]
                nacum = pre[:, 1, ci * 16:(ci + 1) * 16]
                w_tm = pre[:, 3, ci * 16:(ci + 1) * 16]
                DEC = pre[:, 4, ci * 16:(ci + 1) * 16]
                k.cp("ssd A acum")
                psc = pheld
                for g in range(4):
                    k.mm(psc[:, g * 128:(g + 1) * 128], xc[:, 8 + g, cs], xc[:, 12 + g, cs])
                k.cp("ssd B scores")
                for j in range(8):
                    k.tr(ptb[:, j * 128:(j + 1) * 128], xc[:, j, cs], ident_b)
                k.cp("T1 xtr")
                k.copy("act", x_tm[:], ptb[:, :])
                k.cp("T2 xcopy")
                k.tt("dve", hnew[:].rearrange("p (h q) -> p h q", h=16), x_tm[:].rearrange("p (h q) -> p h q", h=16),
                     w_tm[:, 0:16].unsqueeze(2).to_broadcast([128, 16, 64]), ALU.mult)
                k.copy("act", xw_tm[:], hnew[:])
                k.cp("T3 xw")
                for g in range(4):
                    k.tr(ptb[:, g * 128:(g + 1) * 128], xc[:, 8 + g, cs], ident_b)
                k.cp("T4 btr")
                k.copy("act", B_tm[:], ptb[:, 0:512])
                k.cp("ssd C transposes")
                for g in range(4):
                    pab = pring.get()
                    for r in range(4):
                        h = 4 * g + r
                        cc = ci * 16 + h
                        k.mm(pab[:, r * 128:(r + 1) * 128], dhl[:, 0, cc:cc + 1].to_broadcast([128, 128]), TRIB,
                             start=True, stop=False)
                        k.mm(pab[:, r * 128:(r + 1) * 128], dhl[:, 1, cc:cc + 1].to_broadcast([128, 128]), TRIB,
                             start=False, stop=False)
                        k.mm(pab[:, r * 128:(r + 1) * 128], ident_b, NEG, start=False, stop=True)
                    for r in range(4):
                        h = 4 * g + r
                        Dh = sring.get()
                        k.act(Dh[:], pab[:, r * 128:(r + 1) * 128], AF.Exp, bias=nacum[:, h:h + 1])
                        k.stt(LT[:, h, :], Dh[:], dt_tm[:, ci, h:h + 1], psc[:, g * 128:(g + 1) * 128],
                              ALU.mult, ALU.mult)
                k.cp("ssd D LT")
                if smp:
                    EAs = [fring.get(), fring.get()]
                    k.copy("dve", dtA_rep[:].rearrange("p (h q) -> p h q", h=16),
                           dtA_tm[:, ci, :].unsqueeze(2).to_broadcast([128, 16, 64]))
                    for j in range(8):
                        pe_ = pring.get()
                        k.mm(pe_[:, 0:128], dtA_rep[:, j * 128:(j + 1) * 128], TRI)
                        k.act(EAs[j // 4][:, (j % 4) * 128:(j % 4 + 1) * 128], pe_[:, 0:128], AF.Exp)
                    pdS = pring.get()
                    for b_ in range(NSEQ):
                        k.mm(pdS[:, b_ * 16:(b_ + 1) * 16], ind[:, b_:b_ + 1].to_broadcast([128, 128]), dtA)
                    DECS = fring.get()
                    k.act(DECS[:, 0:256], pdS[:, 0:256], AF.Exp)
                    hx = Ring([h0f, hnew])
                    bufs = [hx.get()]
                    k.dma("sp", bufs[0][:], h0T_d[l, 0])
                    for b_ in range(NSEQ):
                        hb = bufs[b_]
                        if b_ + 1 < NSEQ:
                            nb_ = hx.get()
                            bufs.append(nb_)
                            k.dma("sp", nb_[:], h0T_d[l, b_ + 1])
                        h0b = h0bring.get()
                        k.copy("act", h0b[:], hb[:])
                        for j in range(8):
                            pyo = pheld2 if j < 4 else pheld3
                            jj = j % 4
                            k.mm(pyo[:, jj * 128 + b_ * 8: jj * 128 + b_ * 8 + 8], h0b[:, j * 128:(j + 1) * 128],
                                 xc[:, 12 + j // 2, b_ * 8:(b_ + 1) * 8])
                        Bm = bmring.get()
                        k.ts("dve", Bm[:], B_tm[:], ind[:, b_:b_ + 1], None, ALU.mult)
                        pS0 = pring.get()
                        pS1 = pring.get()
                        for g in range(4):
                            pS = pS0 if g < 2 else pS1
                            k.mm(pS[:, (g % 2) * 256:(g % 2 + 1) * 256], Bm[:, g * 128:(g + 1) * 128],
                                 xw_tm[:, g * 256:(g + 1) * 256])
                        hb3 = hb[:].rearrange("p (h q) -> p h q", h=16)
                        k.tt("dve", hb3, hb3, DECS[:, b_ * 16:(b_ + 1) * 16].unsqueeze(2).to_broadcast([128, 16, 64]),
                             ALU.mult)
                        k.tt("dve", hb[:, 0:512], hb[:, 0:512], pS0[:, :], ALU.add)
                        k.tt("dve", hb[:, 512:1024], hb[:, 512:1024], pS1[:, :], ALU.add)
                        k.dma("sp", ssd_s_o[l, b_], hb[:])
                    yo0 = fring.get()
                    yo1 = fring.get()
                    k.copy("act", yo0[:], pheld2[:, :])
                    k.copy("act", yo1[:], pheld3[:, :])
                if not smp:
                    k.copy("dve", dtA_rep[:].rearrange("p (h q) -> p h q", h=16),
                           dtA_tm[:, ci, :].unsqueeze(2).to_broadcast([128, 16, 64]))
                for j in range(8):
                    py = pring.get()
                    if not smp:
                        k.mm(py[:, 256:384], dtA_rep[:, j * 128:(j + 1) * 128], TRI)
                    k.mm(py[0:64, 0:128], x_tm[:, (2 * j) * 64:(2 * j + 1) * 64], LT[:, 2 * j, :])
                    k.mm(py[64:128, 0:128], x_tm[:, (2 * j + 1) * 64:(2 * j + 2) * 64], LT[:, 2 * j + 1, :])
                    tmp = sring.get()
                    if smp:
                        yo = (yo0 if j < 4 else yo1)[:, (j % 4) * 128:(j % 4 + 1) * 128]
                        k.tt("dve", tmp[:], yo, EAs[j // 4][:, (j % 4) * 128:(j % 4 + 1) * 128], ALU.mult)
                    else:
                        k.mm(py[:, 128:256], hT_bf[:, j * 128:(j + 1) * 128], xc[:, 12 + j // 2, cs])
                        EA = sring.get()
                        k.act(EA[:], py[:, 256:384], AF.Exp)
                        k.tt("dve", tmp[:], py[:, 128:256], EA[:], ALU.mult)
                    k.defer(lambda j=j, py=py, tmp=tmp, cs=cs: k.tt("dve", F8[:, j, cs], py[:, 0:128], tmp[:], ALU.add),
                            depth=1)
                k.flush()
                k.cp("ssd E y")
                if not smp:
                    for g in range(4):
                        pS = pheld2 if g < 2 else pheld3
                        k.mm(pS[:, (g % 2) * 256:(g % 2 + 1) * 256], B_tm[:, g * 128:(g + 1) * 128],
                             xw_tm[:, g * 256:(g + 1) * 256])
                    hT3 = hT[:].rearrange("p (h q) -> p h q", h=16)
                    k.tt("dve", hT3, hT3, DEC[:, 0:16].unsqueeze(2).to_broadcast([128, 16, 64]), ALU.mult)
                    k.tt("dve", hT[:, 0:512], hT[:, 0:512], pheld2[:, :], ALU.add)
                    k.tt("dve", hT[:, 512:1024], hT[:, 512:1024], pheld3[:, :], ALU.add)
                    k.copy("act", hT_bf[:], hT[:])
            k.cp("ssd F chunks done")
            pn = pring.get()
            for j in range(8):
                k.stt(F8[:, j, 0:W], xc[:, j, 0:W], pp[:, l, PP_SD + j:PP_SD + j + 1], F8[:, j, 0:W], ALU.mult, ALU.add)
            for j in range(8):
                k.tt("dve", F8[:, j, 0:W], F8[:, j, 0:W], zs[:, j, 0:W], ALU.mult)
            for j in range(8):
                sq = fring.get()
                sqb = sq[:, 0:WT // 2].bitcast(BF16)
                k.act(sqb[:, 0:W], F8[:, j, 0:W], AF.Square)
                k.mm(pn[:, 0:W], ones_b, sqb[:, 0:W], start=(j == 0), stop=(j == 7))
            t1 = fring.get()
            k.act(t1[:, 0:W], pn[:, 0:W], AF.Ln, bias=epsc[:, 0:1], scale=1.0 / D)
            rstd = fring.get()
            k.act(rstd[:, 0:W], t1[:, 0:W], AF.Exp, scale=-0.5)
            for j in range(8):
                k.stt(Y[:, j, 0:W], F8[:, j, 0:W], pp[:, l, PP_NG2 + j:PP_NG2 + j + 1], rstd[:, 0:W], ALU.mult, ALU.mult)
            if (not smp) and ti == len(tiles) - 2:
                k.dma("sp", ssd_p_o[l], hT[:])

        def lru_group(l, js, accs, W, smp):
            n = len(js)
            xrs, prs, pgs, rs, gis, as_ = [], [], [], [], [], []
            for i in range(n):
                xr_bf = zring.get()
                k.copy("act", xr_bf[:, 0:W], accs[i][:, 0:W])
                xrs.append(xr_bf)
            for i in range(n):
                pr = pring.get()
                k.mm(pr[:, 0:W], wa_bf[:, js[i], :], xrs[i][:, 0:W])
                pg = pring.get()
                k.mm(pg[:, 0:W], wx_bf[:, js[i], :], xrs[i][:, 0:W])
                prs.append(pr)
                pgs.append(pg)
            for i in range(n):
                j = js[i]
                r = fring.get()
                k.act(r[:, 0:W], prs[i][:, 0:W], AF.Sigmoid, bias=pp[:, l, PP_LBA + j:PP_LBA + j + 1])
                gi = fring.get()
                k.act(gi[:, 0:W], pgs[i][:, 0:W], AF.Sigmoid, bias=pp[:, l, PP_LBX + j:PP_LBX + j + 1])
                rs.append(r)
                gis.append(gi)
            for i in range(n):
                a = f8ring.get()
                k.act(a[:, 0:W], rs[i][:, 0:W], AF.Exp, scale=lp[:, 0, js[i]:js[i] + 1])
                as_.append(a)
            for i in range(n):
                k.tt("dve", rs[i][:, 0:W], as_[i][:, 0:W], as_[i][:, 0:W], ALU.mult)
            for i in range(n):
                k.ts("dve", rs[i][:, 0:W], rs[i][:, 0:W], 1.0, -1.0, ALU.min, ALU.mult)
            for i in range(n):
                k.act(rs[i][:, 0:W], rs[i][:, 0:W], AF.Sqrt, bias=epsc[:, 2:3])
            for i in range(n):
                k.tt("dve", gis[i][:, 0:W], gis[i][:, 0:W], rs[i][:, 0:W], ALU.mult)
            for i in range(n):
                k.tt("dve", gis[i][:, 0:W], gis[i][:, 0:W], accs[i][:, 0:W], ALU.mult)
            hss = []
            for i in range(n):
                j = js[i]
                a = as_[i]
                gi = gis[i]
                hs = f8ring.get()
                if smp:
                    am = f8ring.get()
                    k.tt("dve", am[:, 0:W], a[:, 0:W], notstart, ALU.mult)
                    t = tring.get()
                    k.tt("dve", t[:, 0:16], a[:, 0:W:LS], lruh0[:, j, :], ALU.mult)
                    k.tt("dve", gi[:, 0:W:LS], gi[:, 0:W:LS], t[:, 0:16], ALU.add)
                    k.scan(hs[:, 0:W], am[:, 0:W], gi[:, 0:W], 0.0)
                else:
                    k.scan(hs[:, 0:W], a[:, 0:W], gi[:, 0:W], pcar_h[:, j:j + 1])
                hss.append(hs)
            for i in range(n):
                j = js[i]
                if smp:
                    k.copy("dve", hcar[:, j, :], hss[i][:, LS - 1:W:LS])
                else:
                    k.copy("dve", pcar_h[:, j:j + 1], hss[i][:, W - 1:W])
                k.tt("dve", Y[:, 8 + j, 0:W], hss[i][:, 0:W], A16[:, j, 0:W], ALU.mult)

        def s5_stage(l, ti, c0, W, smp):
            last_p = (not smp) and ti == len(tiles) - 2
            if not smp:
                k.ts("dve", iota_c[:, 0:W], iota[:, 0:W], float(c0), None, ALU.add)
            tsrc = iota_s if smp else iota_c[:, 0:W]
            tabring = Ring([A16[:, 0, :], A16[:, 1, :], A16[:, 2, :], A16[:, 3, :], Y[:, 14, :], Y[:, 15, :]])
            srring = Ring([F8[:, 0, :], F8[:, 2, :]])
            siring = Ring([F8[:, 1, :], F8[:, 3, :]])
            tprod = [zs[:, i, :] for i in range(4)]
            mprod = [zs[:, 4 + i, :] for i in range(4)]
            Srb = Y[:, 12, :]
            Sib = Y[:, 13, :]
            pGr = pheld2
            pGi = pheld3

            def tables(pr_):
                thc = lp[:, 1, pr_:pr_ + 1]
                u = fring.get()
                ui = iring.get()
                k.ts("dve", ui[:, 0:W], tsrc, thc, None, ALU.mult)
                k.stt(u[:, 0:W], tsrc, thc, ui[:, 0:W], ALU.mult, ALU.subtract)
                uf = fring.get()
                sn = tabring.get()
                cs = tabring.get()
                k.act(sn[:, 0:W], u[:, 0:W], AF.Sin, scale=TWO_PI)
                k.act(uf[:, 0:W], u[:, 0:W], AF.Abs)
                k.act(cs[:, 0:W], uf[:, 0:W], AF.Sin, bias=epsc[:, 1:2], scale=-TWO_PI)
                return sn, cs

            def make_tail(pr_, q, py5, sn, cs, Sr, Si):
                def tail():
                    m1, m2, m3, m4 = mprod
                    k.tt("pool", m1[:, 0:W], cs[:, 0:W], Srb[:, 0:W], ALU.mult)
                    k.tt("pool", m2[:, 0:W], sn[:, 0:W], Sib[:, 0:W], ALU.mult)
                    k.tt("pool", m3[:, 0:W], cs[:, 0:W], Sib[:, 0:W], ALU.mult)
                    k.tt("pool", m4[:, 0:W], sn[:, 0:W], Srb[:, 0:W], ALU.mult)
                    k.mm(py5[:, 0:W], CT_re[:, pr_, :], m1[:, 0:W], start=(q == 0), stop=False)
                    k.mm(py5[:, 0:W], nCT_re[:, pr_, :], m2[:, 0:W], start=False, stop=False)
                    k.mm(py5[:, 0:W], nCT_im[:, pr_, :], m3[:, 0:W], start=False, stop=False)
                    k.mm(py5[:, 0:W], nCT_im[:, pr_, :], m4[:, 0:W], start=False, stop=(q == 3))
                    if smp or last_p:
                        if smp:
                            sel = slice(LS - 1, W, LS)
                            n_ = NSEQ
                            dr = s5o_r[:, pr_, :]
                            di = s5o_i[:, pr_, :]
                        else:
                            sel = slice(W - 1, W)
                            n_ = 1
                            dr = s5p_r[:, pr_:pr_ + 1]
                            di = s5p_i[:, pr_:pr_ + 1]
                        ta = tring.get()
                        tb2 = tring.get()
                        tc2 = tring.get()
                        td2 = tring.get()
                        k.tt("dve", ta[:, 0:n_], cs[:, sel], Sr[:, sel], ALU.mult)
                        k.tt("dve", tb2[:, 0:n_], sn[:, sel], Si[:, sel], ALU.mult)
                        k.tt("dve", tc2[:, 0:n_], cs[:, sel], Si[:, sel], ALU.mult)
                        k.tt("dve", td2[:, 0:n_], sn[:, sel], Sr[:, sel], ALU.mult)
                        k.tt("dve", dr, ta[:, 0:n_], tb2[:, 0:n_], ALU.subtract)
                        k.tt("dve", di, tc2[:, 0:n_], td2[:, 0:n_], ALU.add)
                return tail

            nxt = tables(0)
            prev_tail = None
            for kt in range(4):
                py5 = pheld
                ub = A16[:, 8 + kt, 0:W]
                for q in range(4):
                    pr_ = kt * 4 + q
                    pbr = pring.get()
                    k.mm(pbr[:, 0:W], BbT_re[:, pr_, :], ub)
                    pbi = pring.get()
                    k.mm(pbi[:, 0:W], BbT_im[:, pr_, :], ub)
                    sn, cs = nxt
                    if pr_ + 1 < 16:
                        nxt = tables(pr_ + 1)
                    t1, t2, t3, t4 = tprod
                    k.tt("dve", t1[:, 0:W], pbr[:, 0:W], cs[:, 0:W], ALU.mult)
                    k.tt("dve", t2[:, 0:W], pbi[:, 0:W], sn[:, 0:W], ALU.mult)
                    k.tt("dve", t3[:, 0:W], pbi[:, 0:W], cs[:, 0:W], ALU.mult)
                    k.tt("dve", t4[:, 0:W], pbr[:, 0:W], sn[:, 0:W], ALU.mult)
                    k.mm(pGr[:, 0:W], ident_b, t1[:, 0:W], start=True, stop=False)
                    k.mm(pGr[:, 0:W], ident_b, t2[:, 0:W], start=False, stop=True)
                    k.mm(pGi[:, 0:W], ident_b, t3[:, 0:W], start=True, stop=False)
                    k.mm(pGi[:, 0:W], nident_b[:], t4[:, 0:W], start=False, stop=True)
                    if prev_tail is not None:
                        prev_tail()
                        prev_tail = None
                    Sr = srring.get()
                    Si = siring.get()
                    if smp:
                        ar_ = lp[:, 3, pr_:pr_ + 1]
                        ai_ = lp[:, 4, pr_:pr_ + 1]
                        h0r = s5h0_r[:, pr_, :]
                        h0i = s5h0_i[:, pr_, :]
                        tb = tring.get()
                        k.ts("dve", tb[:, 0:16], h0i, ai_, None, ALU.mult)
                        injr = tring.get()
                        k.stt(injr[:, 0:16], h0r, ar_, tb[:, 0:16], ALU.mult, ALU.subtract)
                        tc = tring.get()
                        k.ts("dve", tc[:, 0:16], h0r, ai_, None, ALU.mult)
                        inji = tring.get()
                        k.stt(inji[:, 0:16], h0i, ar_, tc[:, 0:16], ALU.mult, ALU.add)
                        k.tt("dve", pGr[:, 0:W:LS], pGr[:, 0:W:LS], injr[:, 0:16], ALU.add)
                        k.tt("dve", pGi[:, 0:W:LS], pGi[:, 0:W:LS], inji[:, 0:16], ALU.add)
                        magm = sring.get()
                        k.ts("dve", magm[:], notstart, lp[:, 2, pr_:pr_ + 1], None, ALU.mult)
                        k.scan(Sr[:, 0:W], magm[:], pGr[:, 0:W], 0.0)
                        k.scan(Si[:, 0:W], magm[:], pGi[:, 0:W], 0.0)
                    else:
                        magb = lp[:, 2, pr_:pr_ + 1].to_broadcast([128, W])
                        k.scan(Sr[:, 0:W], magb, pGr[:, 0:W], s5car_r[:, pr_:pr_ + 1])
                        k.scan(Si[:, 0:W], magb, pGi[:, 0:W], s5car_i[:, pr_:pr_ + 1])
                        k.copy("dve", s5car_r[:, pr_:pr_ + 1], Sr[:, W - 1:W])
                        k.copy("dve", s5car_i[:, pr_:pr_ + 1], Si[:, W - 1:W])
                    k.copy("act", Srb[:, 0:W], Sr[:, 0:W])
                    k.copy("act", Sib[:, 0:W], Si[:, 0:W])
                    prev_tail = make_tail(pr_, q, py5, sn, cs, Sr, Si)
                    if q == 3:
                        prev_tail()
                        prev_tail = None
                k.stt(F8[:, 4 + kt, 0:W], ub, pp[:, l, PP_S5D + kt:PP_S5D + kt + 1], py5[:, 0:W], ALU.mult, ALU.add)
            x2s = [fring.get() for _ in range(4)]
            for kt in range(4):
                k.act(x2s[kt][:, 0:W], F8[:, 4 + kt, 0:W], AF.Square)
            for kt in range(4):
                k.ts("dve", x2s[kt][:, 0:W], x2s[kt][:, 0:W], 0.044715, 1.0, ALU.mult, ALU.add)
            for kt in range(4):
                k.tt("dve", x2s[kt][:, 0:W], x2s[kt][:, 0:W], F8[:, 4 + kt, 0:W], ALU.mult)
            for kt in range(4):
                k.act(x2s[kt][:, 0:W], x2s[kt][:, 0:W], AF.Sigmoid, scale=1.5957691216057308)
            for kt in range(4):
                k.tt("dve", A16[:, 12 + kt, 0:W], F8[:, 4 + kt, 0:W], x2s[kt][:, 0:W], ALU.mult)
            sgs = []
            for m in range(4):
                pg = pring.get()
                for kt in range(4):
                    k.mm(pg[:, 0:W], glu_bf[:, kt, m * 128:(m + 1) * 128], A16[:, 12 + kt, 0:W],
                         start=(kt == 0), stop=(kt == 3))
                sg = fring.get()
                k.act(sg[:, 0:W], pg[:, 0:W], AF.Sigmoid, bias=pp[:, l, PP_GLB + m:PP_GLB + m + 1])
                sgs.append(sg)
            for m in range(4):
                k.tt("dve", sgs[m][:, 0:W], sgs[m][:, 0:W], A16[:, 12 + m, 0:W], ALU.mult)
            for m in range(4):
                k.tt("dve", Y[:, 12 + m, 0:W], sgs[m][:, 0:W], A16[:, 4 + m, 0:W], ALU.mult)

        def convert_w_in(l_):
            for bi_, (_, c0_, n_) in enumerate(blocks):
                k.dma("pool", wbf[l_][:, :, c0_:c0_ + n_], w_in[l_][:, :, c0_:c0_ + n_],
                      wk=["wbf%d_%d" % (l_, bi_)])

        def convert_w_out(l_):
            for m_ in range(8):
                k.dma("pool", wobf[l_, m_], w_out[l_][:, :, m_ * 128:(m_ + 1) * 128], wk=["wobf%d_%d" % (l_, m_)])

        for l in range(nlayers):
            k.dma("pool", glu_bf[:], glu_w[l])
            k.dma("pool", wa_bf[:], wa_bd[l])
            k.dma("pool", wx_bf[:], wx_bd[l])
            k.dma("pool", CT_re[:], ctpad_re[l])
            k.dma("pool", nCT_im[:], ctpad_im[l])
            if l == 0:
                convert_w_in(0)
            k.act(nCT_re[:].rearrange("p a b -> p (a b)"), CT_re[:].rearrange("p a b -> p (a b)"), AF.Copy, scale=-1.0)
            k.act(nCT_im[:].rearrange("p a b -> p (a b)"), nCT_im[:].rearrange("p a b -> p (a b)"), AF.Copy, scale=-1.0)
            k.act(expA[:], pdt[:, l, 16:32], AF.Exp)
            t = tring.get()
            k.act(t[:, 0:4], pp[:, l, PP_LLAM:PP_LLAM + 4], AF.Exp, scale=-1.0)
            t2 = tring.get()
            k.act(t2[:, 0:4], t[:, 0:4], AF.Ln, bias=epsc[:, 2:3])
            k.ts("dve", lp[:, 0, 0:4], t2[:, 0:4], -8.0, None, ALU.mult)
            dl = tring.get()
            k.act(dl[:, 0:16], pp[:, l, PP_LDT:PP_LDT + 16], AF.Exp)
            thp = lp[:, 1, :]
            k.stt(thp, pp[:, l, PP_LIM:PP_LIM + 16], 1.0 / TWO_PI, dl[:, 0:16], ALU.mult, ALU.mult)
            lm = tring.get()
            k.tt("dve", lm[:, 0:16], pp[:, l, PP_LRE:PP_LRE + 16], dl[:, 0:16], ALU.mult)
            mag = lp[:, 2, :]
            k.act(mag, lm[:, 0:16], AF.Exp)
            sn0 = tring.get()
            cs0 = tring.get()
            frac_sincos(thp, 16, sn0[:, 0:16], cs0[:, 0:16])
            k.tt("dve", lp[:, 3, :], mag, cs0[:, 0:16], ALU.mult)
            k.tt("dve", lp[:, 4, :], mag, sn0[:, 0:16], ALU.mult)
            lre_c = pp[:, l, PP_LRE:PP_LRE + 16]
            lim_c = pp[:, l, PP_LIM:PP_LIM + 16]
            nr = tring.get()
            k.ts("dve", nr[:, 0:16], lp[:, 3, :], -1.0, None, ALU.add)
            den = tring.get()
            t_a = tring.get()
            k.tt("dve", den[:, 0:16], lre_c, lre_c, ALU.mult)
            k.tt("dve", t_a[:, 0:16], lim_c, lim_c, ALU.mult)
            k.tt("dve", den[:, 0:16], den[:, 0:16], t_a[:, 0:16], ALU.add)
            k.op("dve", lambda v, o=den[:, 0:16]: v.reciprocal(out=o, in_=o), [den], [den])
            cre = lp[:, 5, :]
            cim = lp[:, 6, :]
            t_b = tring.get()
            k.tt("dve", cre, nr[:, 0:16], lre_c, ALU.mult)
            k.tt("dve", t_b[:, 0:16], lp[:, 4, :], lim_c, ALU.mult)
            k.tt("dve", cre, cre, t_b[:, 0:16], ALU.add)
            k.tt("dve", cre, cre, den[:, 0:16], ALU.mult)
            t_c = tring.get()
            k.tt("dve", cim, lp[:, 4, :], lre_c, ALU.mult)
            k.tt("dve", t_c[:, 0:16], nr[:, 0:16], lim_c, ALU.mult)
            k.tt("dve", cim, cim, t_c[:, 0:16], ALU.subtract)
            k.tt("dve", cim, cim, den[:, 0:16], ALU.mult)
            for c4 in range(4):
                bre = fring.get()
                bim = fring.get()
                k.dma("sp", bre[:].rearrange("p (a b) -> p a b", a=4), bpad_re[l][:, c4 * 4:(c4 + 1) * 4, :])
                k.dma("sp", bim[:].rearrange("p (a b) -> p a b", a=4), bpad_im[l][:, c4 * 4:(c4 + 1) * 4, :])
                crb = cre[:, c4 * 4:(c4 + 1) * 4].unsqueeze(2).to_broadcast([128, 4, 128])
                cib = cim[:, c4 * 4:(c4 + 1) * 4].unsqueeze(2).to_broadcast([128, 4, 128])
                v3 = lambda ap_: ap_.rearrange("p (a b) -> p a b", a=4)
                m_a, m_b, m_c, m_d = [f8ring.get() for _ in range(4)]
                k.tt("dve", v3(m_a), v3(bre[:]), crb, ALU.mult)
                k.tt("dve", v3(m_b), v3(bim[:]), cib, ALU.mult)
                k.tt("dve", v3(m_c), v3(bre[:]), cib, ALU.mult)
                k.tt("dve", v3(m_d), v3(bim[:]), crb, ALU.mult)
                o_re = zring.get()
                o_im = zring.get()
                k.tt("dve", o_re, m_a, m_b, ALU.subtract)
                k.tt("dve", o_im, m_c, m_d, ALU.add)
                for i4 in range(4):
                    k.tr(ptb[:, i4 * 128:(i4 + 1) * 128], o_re[:, i4 * 128:(i4 + 1) * 128], ident_b)
                    k.tr(ptb[:, 512 + i4 * 128:512 + (i4 + 1) * 128], o_im[:, i4 * 128:(i4 + 1) * 128], ident_b)
                k.copy("act", BbT_re[:, c4 * 4:(c4 + 1) * 4, :].rearrange("p a b -> p (a b)"), ptb[:, 0:512])
                k.copy("act", BbT_im[:, c4 * 4:(c4 + 1) * 4, :].rearrange("p a b -> p (a b)"), ptb[:, 512:1024])
            k.dma("sp", carry_s[:], sconv0[l])
            k.dma("sp", carry_l[:], lconv0[l])
            k.dma("sp", lruh0[:], lru0[l])
            k.dma("sp", s5h0_r[:], s5r0[l])
            k.dma("sp", s5h0_i[:], s5i0[l])
            k.memset("dve", hT[:], 0.0)
            k.memset("dve", hT_bf[:], 0.0)
            k.memset("dve", pcar_s[:], 0.0)
            k.memset("dve", pcar_l[:], 0.0)
            k.memset("dve", pcar_h[:], 0.0)
            k.memset("dve", s5car_r[:], 0.0)
            k.memset("dve", s5car_i[:], 0.0)

            if l == 0:
                prefetch_w(1)
            k.cp("setup done l%d" % l)
            for ti, (c0, W, kind) in enumerate(tiles):
                smp = kind == "s"
                k.cp("tile start l%d t%d" % (l, ti))
                if l == 0:
                    k.dma("sp", xt[:, :, 0:W], xin.rearrange("k p t -> p k t")[:, :, c0:c0 + W])
                else:
                    k.dma("sp", xt[:, :, 0:W], xsc[(l - 1) % 2].rearrange("k p t -> p k t")[:, :, c0:c0 + W],
                          rk=["xsc%d_%d" % ((l - 1) % 2, ti)])
                rmsnorm_tile(xt, W, PP_NG, l, lambda kt: hn[:, kt, 0:W])
                k.cp("norm done")
                for zb in range(2):
                    wb = next_block()
                    for m in range(4):
                        pm = proj(wb, m, W)
                        k.act(zs[:, zb * 4 + m, 0:W], pm[:, 0:W], AF.Silu)
                for xb in range(4):
                    wb = next_block()
                    for half in range(2):
                        js = [xb * 4 + half * 2 + i for i in range(2)]
                        pms = [proj(wb, half * 2 + i, W) for i in range(2)]
                        accs = conv_group(l, pms, W, smp,
                                          [(PP_CW + 4 * j, PP_CB + j, pcar_s[:, j, :], carry_s[:, j, :, :],
                                            sconv_s_o[l][:, j, :, :]) for j in js])

                        def silus(js=js, accs=accs):
                            for j, acc in zip(js, accs):
                                k.act(A16[:, j, 0:W], acc[:, 0:W], AF.Silu)
                        k.defer(silus, depth=1)
                wb = next_block()
                k.flush()
                nchk = W // 128
                pd = pring.get()
                for ci in range(nchk):
                    for kt in range(8):
                        k.mm(pd[:, ci * 16:(ci + 1) * 16], hn[:, kt, ci * 128:(ci + 1) * 128], wb[:, kt, 0:16],
                             start=(kt == 0), stop=(kt == 7))
                dt2 = dt_tm[:, 0:nchk, :]
                dtA2 = dtA_tm[:, 0:nchk, :]
                pd3 = pd[:, 0:nchk * 16].rearrange("p (c h) -> p c h", c=nchk)
                v = pre[:, 6, 0:nchk * 16].rearrange("p (c h) -> p c h", c=nchk)
                k.tt("dve", v, pd3, pdt[:, l, 0:16].unsqueeze(1).to_broadcast([128, nchk, 16]), ALU.add)
                k.act(v, v, AF.Exp)
                k.act(dt2, v, AF.Ln, bias=epsc[:, 2:3])
                k.stt(dtA2, dt2, -1.0, expA[:].unsqueeze(1).to_broadcast([128, nchk, 16]), ALU.mult, ALU.mult)
                hi_f = pre[:, 5, 0:nchk * 16]
                k.copy("act", dhl[:, 0, 0:nchk * 16], dtA_tm[:, 0:nchk, :].rearrange("p c h -> p (c h)"))
                k.copy("act", hi_f, dhl[:, 0, 0:nchk * 16])
                k.tt("dve", hi_f, dtA_tm[:, 0:nchk, :].rearrange("p c h -> p (c h)"), hi_f, ALU.subtract)
                k.copy("act", dhl[:, 1, 0:nchk * 16], hi_f)
                TRI_ = tri_s if smp else tri_p
                ONESM_ = ones_s if smp else ones_f
                n16 = nchk * 16
                pa = pring.get()
                dtA_flat = dtA_tm[:, 0:nchk, :].rearrange("p c h -> p (c h)")
                k.mm(pa[:, 0:n16], TRI_, dtA_flat)
                k.mm(pa[:, 64:64 + n16], ONESM_, dtA_flat)
                k.copy("act", pre[:, 0, 0:n16], pa[:, 0:n16])
                k.ts("dve", pre[:, 1, 0:n16], pa[:, 0:n16], -1.0, None, ALU.mult)
                k.tt("dve", pre[:, 2, 0:n16], pa[:, 64:64 + n16], pre[:, 0, 0:n16], ALU.subtract)
                k.act(pre[:, 2, 0:n16], pre[:, 2, 0:n16], AF.Exp)
                k.tt("dve", pre[:, 3, 0:n16], pre[:, 2, 0:n16], dt_tm[:, 0:nchk, :].rearrange("p c h -> p (c h)"),
                     ALU.mult)
                k.act(pre[:, 4, 0:n16], pa[:, 64:64 + n16], AF.Exp)
                if l == 0 and ti == 0:
                    convert_w_out(0)
                k.cp("stage A done")
                ssd_stage(l, ti, W, smp)
                k.cp("ssd done")
                wb = next_block()
                for m in range(4):
                    pm = proj(wb, m, W)
                    k.act(A16[:, m, 0:W], pm[:, 0:W], AF.Silu)
                wb = next_block()
                for half in range(2):
                    js = [half * 2, half * 2 + 1]
                    pms = [proj(wb, j, W) for j in js]
                    accs = conv_group(l, pms, W, smp, [(PP_LCW + 4 * j, PP_LCB + j, pcar_l[:, j, :],
                                                         carry_l[:, j, :, :], lconv_s_o[l][:, j, :, :]) for j in js])
                    lru_group(l, js, accs, W, smp)
                k.cp("lru done")
                k.flush()
                wb = next_block()
                for m in range(4):
                    pm = proj(wb, m, W)
                    k.act(A16[:, 4 + m, 0:W], pm[:, 0:W], AF.Silu)
                wb = next_block()
                for m in range(4):
                    pm = proj(wb, m, W)
                    k.copy("act", A16[:, 8 + m, 0:W], pm[:, 0:W])
                s5_stage(l, ti, c0, W, smp)
                if l + 1 < nlayers and ti == 0:
                    convert_w_in(l + 1)
                if l + 1 < nlayers and ti == 1:
                    convert_w_out(l + 1)
                k.cp("s5 done")
                wo = woring.get()
                k.dma("sp", wo[:], wobf[l, 0], rk=["wobf%d_0" % l])
                for m in range(8):
                    if m + 1 < 8:
                        wo_n = woring.get()
                        k.dma("sp", wo_n[:], wobf[l, m + 1], rk=["wobf%d_%d" % (l, m + 1)])
                    po = pring.get()
                    for kt in range(16):
                        k.mm(po[:, 0:W], wo[:, kt, :], Y[:, kt, 0:W], start=(kt == 0), stop=(kt == 15))
                    k.tt("dve", xt[:, m, 0:W], po[:, 0:W], xt[:, m, 0:W], ALU.add)
                    if m + 1 < 8:
                        wo = wo_n
                if l < nlayers - 1:
                    k.dma("sp", xsc[l % 2].rearrange("k p t -> p k t")[:, :, c0:c0 + W], xt[:, :, 0:W],
                          wk=["xsc%d_%d" % (l % 2, ti)])
                else:
                    rmsnorm_tile(xt, W, PP_FG, l, lambda kt: F8[:, kt, 0:W])
                    k.dma("sp", yout.rearrange("k p t -> p k t")[:, :, c0:c0 + W], F8[:, :, 0:W])
            k.dma("sp", sconv_p_o[l], pcar_s[:])
            k.dma("sp", lconv_p_o[l], pcar_l[:])
            k.dma("sp", lru_p_o[l], pcar_h[:])
            k.dma("sp", lru_s_o[l], hcar[:])
            k.dma("sp", s5r_p_o[l], s5p_r[:])
            k.dma("sp", s5i_p_o[l], s5p_i[:])
            k.dma("sp", s5r_s_o[l], s5o_r[:])
            k.dma("sp", s5i_s_o[l], s5o_i[:])
        k.finish()
        k.emit()
    return nc


def _consts():
    i = np.arange(128)
    ident = np.eye(128, dtype=np.float32)
    tri_p = (i[:, None] <= i[None, :]).astype(np.float32)
    same = (i[:, None] // LS == i[None, :] // LS)
    tri_s = (tri_p > 0) & same
    cfa = np.zeros((128, 8, 128), np.float32)
    import ml_dtypes
    trib = np.concatenate([tri_p, tri_s.astype(np.float32)], axis=1).astype(ml_dtypes.bfloat16)
    cfa[:, 0] = np.ascontiguousarray(trib).view(np.float32)
    cfa[:, 1] = tri_p
    cfa[:, 2] = tri_s
    cfa[:, 3] = 1.0
    cfa[:, 4] = same
    cfa[:, 5] = np.broadcast_to((i % LS)[None, :], (128, 128))
    cfa[:, 6] = np.broadcast_to((i % LS != 0)[None, :], (128, 128))
    cfa[:, 7, 0:NSEQ] = (i[:, None] // LS == np.arange(NSEQ)[None, :])
    cba = np.zeros((128, 4, 128), np.float32)
    cba[:, 0] = ident
    cba[:, 1] = 1.0
    cba[:, 2] = np.where(tri_p > 0, 0.0, -30000.0)
    cba[:, 3] = np.where(tri_s, 0.0, -30000.0)
    iota = np.broadcast_to(np.arange(WT, dtype=np.float32)[None, :], (128, WT)).copy()
    return cfa, cba, iota


def _fm(v, nt):
    return np.moveaxis(v.reshape(v.shape[:-1] + (nt, 128)), -1, -2)


def _prep_shared(inp):
    f = lambda a: np.ascontiguousarray(a, dtype=np.float32)
    sh = {}
    sh["w_in_r"] = f(inp["w_in"].reshape(NL, 8, 128, IN_DIM).transpose(0, 2, 1, 3))
    sh["w_out_r"] = f(inp["w_out"].reshape(NL, 16, 128, D).transpose(0, 2, 1, 3))
    sh["glu_r"] = f(inp["s5_glu_w"].reshape(NL, 4, 128, 512).transpose(0, 2, 1, 3))
    for nm, src in (("wa_bd", inp["lru_wa"]), ("wx_bd", inp["lru_wx"])):
        bd = np.zeros((NL, 128, 4, 128), np.float32)
        for m in range(4):
            for k2 in range(2):
                bd[:, k2 * 64:(k2 + 1) * 64, m, k2 * 64:(k2 + 1) * 64] = src[:, 2 * m + k2]
        sh[nm] = bd
    pp = np.zeros((128, NL, NPP), np.float32)
    for l in range(NL):
        pp[:, l, PP_NG:PP_NG + 8] = _fm(inp["norm_g"][l], 8)
        pp[:, l, PP_NG2:PP_NG2 + 8] = _fm(inp["ssd_norm_g"][l], 8)
        pp[:, l, PP_SD:PP_SD + 8] = _fm(np.repeat(inp["ssd_d"][l], 64), 8)
        pp[:, l, PP_CW:PP_CW + 64] = inp["ssd_conv_w"][l].reshape(4, 16, 128).transpose(2, 1, 0).reshape(128, 64)
        pp[:, l, PP_CB:PP_CB + 16] = _fm(inp["ssd_conv_b"][l], 16)
        pp[:, l, PP_LCW:PP_LCW + 16] = inp["lru_conv_w"][l].reshape(4, 4, 128).transpose(2, 1, 0).reshape(128, 16)
        pp[:, l, PP_LCB:PP_LCB + 4] = _fm(inp["lru_conv_b"][l], 4)
        pp[:, l, PP_LBA:PP_LBA + 4] = _fm(inp["lru_ba"][l], 4)
        pp[:, l, PP_LBX:PP_LBX + 4] = _fm(inp["lru_bx"][l], 4)
        pp[:, l, PP_LLAM:PP_LLAM + 4] = _fm(inp["lru_lambda"][l], 4)
        pp[:, l, PP_S5D:PP_S5D + 4] = _fm(inp["s5_d"][l], 4)
        pp[:, l, PP_GLB:PP_GLB + 4] = _fm(inp["s5_glu_b"][l], 4)
        pp[:, l, PP_LRE:PP_LRE + 16] = inp["s5_lambda_re"][l].reshape(16, 128).T
        pp[:, l, PP_LIM:PP_LIM + 16] = inp["s5_lambda_im"][l].reshape(16, 128).T
        pp[:, l, PP_LDT:PP_LDT + 16] = np.repeat(inp["s5_log_dt"][l], 64).reshape(16, 128).T
        pp[:, l, PP_FG:PP_FG + 8] = _fm(inp["final_norm_g"], 8)
    sh["pp_in"] = pp
    pdt = np.zeros((128, NL, 32), np.float32)
    pdt[:, :, 0:16] = inp["ssd_dt_bias"][None]
    pdt[:, :, 16:32] = inp["ssd_a_log"][None]
    sh["pdt_in"] = pdt
    bre = np.zeros((NL, 128, 16, 128), np.float32)
    bim = np.zeros((NL, 128, 16, 128), np.float32)
    cre = np.zeros((NL, 128, 16, 128), np.float32)
    cim = np.zeros((NL, 128, 16, 128), np.float32)
    for g in range(32):
        pr, g2, gl = g // 2, g % 2, g % 8
        bre[:, g2 * 64:(g2 + 1) * 64, pr, gl * 16:(gl + 1) * 16] = inp["s5_b_re"][:, g]
        bim[:, g2 * 64:(g2 + 1) * 64, pr, gl * 16:(gl + 1) * 16] = inp["s5_b_im"][:, g]
        cre[:, g2 * 64:(g2 + 1) * 64, pr, gl * 16:(gl + 1) * 16] = inp["s5_c_re"][:, g].transpose(0, 2, 1)
        cim[:, g2 * 64:(g2 + 1) * 64, pr, gl * 16:(gl + 1) * 16] = inp["s5_c_im"][:, g].transpose(0, 2, 1)
    sh["bpadT_re"], sh["bpadT_im"], sh["ctpad_re"], sh["ctpad_im"] = bre, bim, cre, cim
    cfa, cba, iota = _consts()
    sh["c_f32"], sh["c_bf"], sh["c_iota"] = cfa, cba, iota
    return sh


def _prep_core(inp, c):
    f = lambda a: np.ascontiguousarray(a, dtype=np.float32)
    sl = slice(NSEQ * c, NSEQ * (c + 1))
    m = {}
    x_tok = np.concatenate([inp["x_prompt"][c], inp["x_sample"][sl].reshape(TS, D)], axis=0)
    m["xin"] = f(x_tok.T.reshape(8, 128, TT))
    m["h0T"] = f(inp["state_ssd"][:, sl].transpose(0, 1, 4, 2, 3).reshape(NL, NSEQ, 128, 1024))
    m["sconv0"] = f(inp["state_ssd_conv"][:, sl].reshape(NL, NSEQ, 3, 16, 128).transpose(0, 4, 3, 1, 2))
    m["lconv0"] = f(inp["state_lru_conv"][:, sl].reshape(NL, NSEQ, 3, 4, 128).transpose(0, 4, 3, 1, 2))
    m["lru0"] = f(inp["state_lru"][:, sl].reshape(NL, NSEQ, 4, 128).transpose(0, 3, 2, 1))
    m["s5r0"] = f(inp["state_s5_re"][:, sl].reshape(NL, NSEQ, 16, 128).transpose(0, 3, 2, 1))
    m["s5i0"] = f(inp["state_s5_im"][:, sl].reshape(NL, NSEQ, 16, 128).transpose(0, 3, 2, 1))
    return m


_PROG = {}


def kernel(**inputs):
    inp = {k_: np.asarray(v) for k_, v in inputs.items()}
    if "nc" not in _PROG:
        _PROG["nc"] = build_program(NL)
    nc = _PROG["nc"]
    shared = _prep_shared(inp)
    in_maps = []
    for c in range(NCORES):
        m = dict(shared)
        m.update(_prep_core(inp, c))
        in_maps.append(m)
    res = run_bass_kernel_spmd(nc, in_maps, core_ids=list(range(NCORES)))
    R = res.results
    B = NCORES
    y_prompt = np.zeros((B, TP, D), np.float32)
    y_sample = np.zeros((B * NSEQ, LS, D), np.float32)
    ssd_p = np.zeros((NL, B, 16, 64, 128), np.float32)
    ssd_s = np.zeros((NL, B * NSEQ, 16, 64, 128), np.float32)
    ssd_conv_p = np.zeros((NL, B, 3, 2048), np.float32)
    ssd_conv_s = np.zeros((NL, B * NSEQ, 3, 2048), np.float32)
    lru_p = np.zeros((NL, B, 512), np.float32)
    lru_s = np.zeros((NL, B * NSEQ, 512), np.float32)
    lru_conv_p = np.zeros((NL, B, 3, 512), np.float32)
    lru_conv_s = np.zeros((NL, B * NSEQ, 3, 512), np.float32)
    s5_re_p = np.zeros((NL, B, 32, 64), np.float32)
    s5_re_s = np.zeros((NL, B * NSEQ, 32, 64), np.float32)
    s5_im_p = np.zeros((NL, B, 32, 64), np.float32)
    s5_im_s = np.zeros((NL, B * NSEQ, 32, 64), np.float32)
    for c in range(B):
        r = R[c]
        sl = slice(NSEQ * c, NSEQ * (c + 1))
        y = np.asarray(r["yout"]).reshape(D, TT).T
        y_prompt[c] = y[0:TP]
        y_sample[sl] = y[TP:].reshape(NSEQ, LS, D)
        ssd_p[:, c] = np.asarray(r["ssd_p_o"]).reshape(NL, 128, 16, 64).transpose(0, 2, 3, 1)
        ssd_s[:, sl] = np.asarray(r["ssd_s_o"]).reshape(NL, NSEQ, 128, 16, 64).transpose(0, 1, 3, 4, 2)
        ssd_conv_p[:, c] = np.asarray(r["sconv_p_o"]).transpose(0, 3, 2, 1).reshape(NL, 3, 2048)
        ssd_conv_s[:, sl] = np.asarray(r["sconv_s_o"]).transpose(0, 3, 4, 2, 1).reshape(NL, NSEQ, 3, 2048)
        lru_p[:, c] = np.asarray(r["lru_p_o"]).transpose(0, 2, 1).reshape(NL, 512)
        lru_s[:, sl] = np.asarray(r["lru_s_o"]).transpose(0, 3, 2, 1).reshape(NL, NSEQ, 512)
        lru_conv_p[:, c] = np.asarray(r["lconv_p_o"]).transpose(0, 3, 2, 1).reshape(NL, 3, 512)
        lru_conv_s[:, sl] = np.asarray(r["lconv_s_o"]).transpose(0, 3, 4, 2, 1).reshape(NL, NSEQ, 3, 512)
        s5_re_p[:, c] = np.asarray(r["s5r_p_o"]).transpose(0, 2, 1).reshape(NL, 32, 64)
        s5_im_p[:, c] = np.asarray(r["s5i_p_o"]).transpose(0, 2, 1).reshape(NL, 32, 64)
        s5_re_s[:, sl] = np.asarray(r["s5r_s_o"]).transpose(0, 3, 2, 1).reshape(NL, NSEQ, 32, 64)
        s5_im_s[:, sl] = np.asarray(r["s5i_s_o"]).transpose(0, 3, 2, 1).reshape(NL, NSEQ, 32, 64)
    return (y_prompt, y_sample, ssd_p, ssd_s, ssd_conv_p, ssd_conv_s, lru_p, lru_s,
            lru_conv_p, lru_conv_s, s5_re_p, s5_re_s, s5_im_p, s5_im_s)
```

```python
import math
import os
from contextlib import ExitStack

import numpy as np
import concourse.bass as bass
import concourse.mybir as mybir
from concourse.bass_utils import run_bass_kernel_spmd

F32 = mybir.dt.float32
BF16 = mybir.dt.bfloat16
I32 = mybir.dt.int32
ALU = mybir.AluOpType
AF = mybir.ActivationFunctionType

NCORES = 8
D = 1024
NL = 4
TP = 2048
NSEQ = 16
LS = 8
TS = NSEQ * LS
TT = TP + TS
WT = 512
IN_DIM = 5136
EPS = 1e-6
TWO_PI = 2.0 * math.pi

PP_NG = 0
PP_NG2 = 8
PP_SD = 16
PP_CW = 24
PP_CB = 88
PP_LCW = 104
PP_LCB = 120
PP_LBA = 124
PP_LBX = 128
PP_LLAM = 132
PP_S5D = 136
PP_GLB = 140
PP_LRE = 144
PP_LIM = 160
PP_LDT = 176
PP_FG = 192
NPP = 200

EPOCH = 12000


class Ring:
    def __init__(self, bufs, full=None):
        self.bufs = bufs
        self.full = full
        self.i = 0

    def get(self):
        b = self.bufs[self.i % len(self.bufs)]
        self.i += 1
        return b

    def get_full(self):
        b = self.full[self.i % len(self.full)]
        self.i += 1
        return b


class KB:
    ENG = ("pe", "act", "dve", "pool", "sp")

    def __init__(self, nc, es):
        self.nc = nc
        self.es = es
        self.prog = {e: [] for e in self.ENG}
        self.cnt = {e: 0 for e in self.ENG}
        self.sem = {}
        self.nsem = 0
        for e in ("pe", "act", "dve", "pool"):
            self.sem[e] = self._newsem("c_" + e)
        self.waited = {e: {} for e in self.ENG}
        self.dead = False
        self.pending = []
        self.ncp = 0
        self.stop = int(os.environ.get("KSTOP", "-1"))
        self.lastw = {}
        self.readers = {}
        self.dsem = {}
        self.drr = {}
        for q, n in (("sp", 16), ("pool", 24), ("act", 2)):
            self.dsem[q] = [[self._newsem("d_%s%d" % (q, i)), 0] for i in range(n)]
            self.drr[q] = 0

    def _newsem(self, name):
        self.nsem += 1
        return self.es.enter_context(self.nc.semaphore("%s_%d" % (name, self.nsem)))

    slotw = {}

    def _keys(self, r):
        if isinstance(r, str):
            return [r]
        name = r.name
        w = self.slotw.get(name)
        if w is None:
            return [name]
        ap = r.ap
        off = r.offset % ap[0][0]
        hi = off + sum((c - 1) * s for s, c in ap[1:])
        return ["%s:%d" % (name, i) for i in range(off // w, hi // w + 1)]

    def defer(self, fn, depth=1):
        self.pending.append(fn)
        while len(self.pending) > depth:
            self.pending.pop(0)()

    def flush(self):
        while self.pending:
            self.pending.pop(0)()

    def cp(self, name=""):
        self.ncp += 1
        if self.stop >= 0 and self.ncp > self.stop and not self.dead:
            self.dead = True
            print("KSTOP: program truncated before checkpoint", self.ncp, name, flush=True)

    def op(self, e, fn, reads=(), writes=(), dma=False):
        if self.dead:
            return None
        waits = {}

        def need(tok, raw):
            if tok is None:
                return
            sem, val, src, isdma = tok
            if src == e and not isdma and e == "pe":
                return
            if self.waited[e].get(sem.name, 0) >= val:
                return
            if sem.name not in waits or waits[sem.name][1] < val:
                waits[sem.name] = (sem, val)

        rk = [x for r in reads for x in self._keys(r)]
        wk = [x for w in writes for x in self._keys(w)]
        for r in rk:
            need(self.lastw.get(r), True)
        for w in wk:
            need(self.lastw.get(w), False)
            for t in self.readers.get(w, {}).values():
                need(t, False)
        if dma:
            slot = self.dsem[e][self.drr[e] % len(self.dsem[e])]
            self.drr[e] += 1
            if slot[1] > 0:
                need((slot[0], slot[1], e, True), True)
            slot[1] += 16
            tok = (slot[0], slot[1], e, True)
            inc = (slot[0], 16)
        else:
            if self.cnt[e] >= EPOCH:
                self.sem[e] = self._newsem("c_" + e)
                self.cnt[e] = 0
            self.cnt[e] += 1
            tok = (self.sem[e], self.cnt[e], e, False)
            inc = (self.sem[e], 1)
        for s, v in waits.values():
            self.waited[e][s.name] = v
        self.prog[e].append((list(waits.values()), fn, inc))
        for r in rk:
            self.readers.setdefault(r, {})[tok[0].name] = tok
        for w in wk:
            self.lastw[w] = tok
            self.readers[w] = {}
        return tok

    def finish(self):
        fin = []
        for q in self.dsem:
            for sem, v in self.dsem[q]:
                if v > 0:
                    fin.append((sem, v))
        self.final_waits = fin

    def emit(self):
        nc = self.nc
        handles = {"pe": "tensor", "act": "scalar", "dve": "vector", "pool": "gpsimd", "sp": "sync"}
        with nc.Block() as block:
            for e in self.ENG:
                prog = self.prog[e]
                extra = self.final_waits if e == "sp" else []

                def body(eng, prog=prog, extra=extra):
                    for waits, fn, inc in prog:
                        for s, v in waits:
                            eng.wait_ge(s, v)
                        ins = fn(eng)
                        ins.then_inc(inc[0], inc[1])
                    for s, v in extra:
                        eng.wait_ge(s, v)

                getattr(block, handles[e])(body)

    def mm(self, out, lhsT, rhs, start=True, stop=True):
        self.op("pe", lambda t: t.matmul(out, lhsT=lhsT, rhs=rhs, start=start, stop=stop),
                [lhsT, rhs], [out])

    def tr(self, out, in_, ident):
        self.op("pe", lambda t: t.transpose(out, in_, ident), [in_, ident], [out])

    def act(self, out, in_, func, bias=None, scale=None):
        rd = [in_]
        kw = {}
        if bias is not None:
            kw["bias"] = bias
            if not isinstance(bias, (int, float)):
                rd.append(bias)
        if scale is not None:
            kw["scale"] = scale
            if not isinstance(scale, (int, float)):
                rd.append(scale)
        self.op("act", lambda a: a.activation(out=out, in_=in_, func=func, **kw), rd, [out])

    def tt(self, e, out, in0, in1, op):
        self.op(e, lambda v: v.tensor_tensor(out=out, in0=in0, in1=in1, op=op), [in0, in1], [out])

    def ts(self, e, out, in0, s1, s2, op0, op1=None):
        rd = [in0]
        for s in (s1, s2):
            if s is not None and not isinstance(s, (int, float)):
                rd.append(s)
        if op1 is None:
            self.op(e, lambda v: v.tensor_scalar(out=out, in0=in0, scalar1=s1, scalar2=None, op0=op0),
                    rd, [out])
        else:
            self.op(e, lambda v: v.tensor_scalar(out=out, in0=in0, scalar1=s1, scalar2=s2, op0=op0, op1=op1),
                    rd, [out])

    def stt(self, out, in0, scalar, in1, op0, op1):
        rd = [in0, in1]
        if not isinstance(scalar, (int, float)):
            rd.append(scalar)
        self.op("dve", lambda v: v.scalar_tensor_tensor(out=out, in0=in0, scalar=scalar, in1=in1,
                                                        op0=op0, op1=op1), rd, [out])

    def scan(self, out, d0, d1, init):
        rd = [d0, d1]
        if not isinstance(init, (int, float)):
            rd.append(init)
        self.op("dve", lambda v: v.tensor_tensor_scan(out=out, data0=d0, data1=d1, initial=init,
                                                      op0=ALU.mult, op1=ALU.add), rd, [out])

    def copy(self, e, out, in_):
        if e == "act":
            self.op("act", lambda a: a.activation(out=out, in_=in_, func=AF.Copy), [in_], [out])
        else:
            self.op(e, lambda v: v.tensor_copy(out=out, in_=in_), [in_], [out])

    def memset(self, e, out, val):
        self.op(e, lambda v: v.memset(out, val), [], [out])

    def dma(self, q, out, in_, rk=(), wk=()):
        self.op(q, lambda g: g.dma_start(out=out, in_=in_), [in_] + list(rk), [out] + list(wk), dma=True)


def build_program(nlayers=NL):
    nc = bass.Bass("TRN2", target_bir_lowering=False)

    def din(name, shape, dt=F32):
        return nc.dram_tensor(name, list(shape), dt, kind="ExternalInput").ap()

    def dout(name, shape, dt=F32):
        return nc.dram_tensor(name, list(shape), dt, kind="ExternalOutput").ap()

    xin = din("xin", [8, 128, TT])
    w_in = din("w_in_r", [NL, 128, 8, IN_DIM])
    w_out = din("w_out_r", [NL, 128, 16, D])
    glu_w = din("glu_r", [NL, 128, 4, 512])
    wa_bd = din("wa_bd", [NL, 128, 4, 128])
    wx_bd = din("wx_bd", [NL, 128, 4, 128])
    pp_d = din("pp_in", [128, NL, NPP])
    pdt_d = din("pdt_in", [128, NL, 32])
    bpad_re = din("bpadT_re", [NL, 128, 16, 128])
    bpad_im = din("bpadT_im", [NL, 128, 16, 128])
    ctpad_re = din("ctpad_re", [NL, 128, 16, 128])
    ctpad_im = din("ctpad_im", [NL, 128, 16, 128])
    h0T_d = din("h0T", [NL, NSEQ, 128, 1024])
    sconv0 = din("sconv0", [NL, 128, 16, NSEQ, 3])
    lconv0 = din("lconv0", [NL, 128, 4, NSEQ, 3])
    lru0 = din("lru0", [NL, 128, 4, NSEQ])
    s5r0 = din("s5r0", [NL, 128, 16, NSEQ])
    s5i0 = din("s5i0", [NL, 128, 16, NSEQ])
    c_f32 = din("c_f32", [128, 8, 128])
    c_bf = din("c_bf", [128, 4, 128])
    c_iota = din("c_iota", [128, WT])

    yout = dout("yout", [8, 128, TT])
    ssd_p_o = dout("ssd_p_o", [NL, 128, 1024])
    ssd_s_o = dout("ssd_s_o", [NL, NSEQ, 128, 1024])
    sconv_p_o = dout("sconv_p_o", [NL, 128, 16, 3])
    sconv_s_o = dout("sconv_s_o", [NL, 128, 16, NSEQ, 3])
    lru_p_o = dout("lru_p_o", [NL, 128, 4])
    lru_s_o = dout("lru_s_o", [NL, 128, 4, NSEQ])
    lconv_p_o = dout("lconv_p_o", [NL, 128, 4, 3])
    lconv_s_o = dout("lconv_s_o", [NL, 128, 4, NSEQ, 3])
    s5r_p_o = dout("s5r_p_o", [NL, 128, 16])
    s5r_s_o = dout("s5r_s_o", [NL, 128, 16, NSEQ])
    s5i_p_o = dout("s5i_p_o", [NL, 128, 16])
    s5i_s_o = dout("s5i_s_o", [NL, 128, 16, NSEQ])
    xsc = nc.dram_tensor("xsc", [2, 8, 128, TT], F32, kind="Internal").ap()
    wbf = nc.dram_tensor("wbf", [NL, 128, 8, IN_DIM], BF16, kind="Internal").ap()
    wobf = nc.dram_tensor("wobf", [NL, 8, 128, 16, 128], BF16, kind="Internal").ap()

    with ExitStack() as es:
        k = KB(nc, es)
        k.slotw = {"F8": 512, "zs": 512, "A16": 512, "Y": 512, "xt": 512, "hn": 512, "LT": 128,
                   "hnew": 512, "h0f": 512}

        def sb(name, shape, dt):
            return es.enter_context(nc.sbuf_tensor(name, list(shape), dt))

        def ps(name, shape, dt):
            return es.enter_context(nc.psum_tensor(name, list(shape), dt))

        xt = sb("xt", [128, 8, WT], F32)
        hn = sb("hn", [128, 8, WT], BF16)
        zs = sb("zs", [128, 8, WT], BF16)
        A16 = sb("A16", [128, 16, WT], BF16)
        F8 = sb("F8", [128, 8, WT], F32)
        Y = sb("Y", [128, 16, WT], BF16)
        LT = sb("LT", [128, 16, 128], BF16)
        x_tm = sb("x_tm", [128, 1024], BF16)
        xw_tm = sb("xw_tm", [128, 1024], BF16)
        B_tm = sb("B_tm", [128, 512], BF16)
        bmring = Ring([sb("bm%d" % i, [128, 512], BF16) for i in range(1)])
        hT = sb("hT", [128, 1024], F32)
        hT_bf = sb("hT_bf", [128, 1024], BF16)
        dt_tm = sb("dt_tm", [128, 4, 16], F32)
        dtA_tm = sb("dtA_tm", [128, 4, 16], F32)
        pre = sb("pre", [128, 7, 64], F32)
        wring = Ring([sb("wbuf%d" % i, [128, 8, 512], BF16) for i in range(2)])
        woring = Ring([sb("wobuf%d" % i, [128, 16, 128], BF16) for i in range(2)])
        glu_bf = sb("glu_bf", [128, 4, 512], BF16)
        BbT_re = sb("BbT_re", [128, 16, 128], BF16)
        BbT_im = sb("BbT_im", [128, 16, 128], BF16)
        CT_re = sb("CT_re", [128, 16, 128], BF16)
        nCT_re = sb("nCT_re", [128, 16, 128], BF16)
        nCT_im = sb("nCT_im", [128, 16, 128], BF16)
        wa_bf = sb("wa_bf", [128, 4, 128], BF16)
        wx_bf = sb("wx_bf", [128, 4, 128], BF16)
        pp = sb("pp", [128, NL, NPP], F32)
        pdt = sb("pdt", [128, NL, 32], F32)
        expA = sb("expA", [128, 16], F32)
        lp = sb("lp", [128, 8, 16], F32)
        cf = sb("cf", [128, 8, 128], F32)
        cb = sb("cb", [128, 4, 128], BF16)
        iota = sb("iota", [128, WT], F32)
        iota_c = sb("iota_c", [128, WT], F32)
        carry_s = sb("carry_s", [128, 16, NSEQ, 3], F32)
        carry_l = sb("carry_l", [128, 4, NSEQ, 3], F32)
        pcar_s = sb("pcar_s", [128, 16, 3], F32)
        pcar_l = sb("pcar_l", [128, 4, 3], F32)
        pcar_h = sb("pcar_h", [128, 4], F32)
        hcar = sb("hcar", [128, 4, NSEQ], F32)
        s5car_r = sb("s5car_r", [128, 16], F32)
        s5car_i = sb("s5car_i", [128, 16], F32)
        s5p_r = sb("s5p_r", [128, 16], F32)
        s5p_i = sb("s5p_i", [128, 16], F32)
        s5o_r = sb("s5o_r", [128, 16, NSEQ], F32)
        s5o_i = sb("s5o_i", [128, 16, NSEQ], F32)
        s5h0_r = sb("s5h0_r", [128, 16, NSEQ], F32)
        s5h0_i = sb("s5h0_i", [128, 16, NSEQ], F32)
        lruh0 = sb("lruh0", [128, 4, NSEQ], F32)
        h0f = sb("h0f", [128, 1024], F32)
        h0bring = Ring([sb("h0b%d" % i, [128, 1024], BF16) for i in range(2)])
        hnew = sb("hnew", [128, 1024], F32)
        dtA_rep = h0f
        _fr = [sb("fr%d" % i, [128, WT + 4], F32) for i in range(8)]
        fring = Ring([t_[:, 0:WT] for t_ in _fr], [t_[:, :] for t_ in _fr])
        iring = Ring([sb("ir%d" % i, [128, WT], I32) for i in range(2)])
        fring_x = [h0f[:, 0:512], h0f[:, 512:1024]]
        sring2 = [hnew[:, 0:512], hnew[:, 512:1024]]
        sring = Ring([sb("sr%d" % i, [128, 128], F32) for i in range(4)])
        tring = Ring([sb("tn%d" % i, [128, 48], F32) for i in range(10)])
        dhl = sb("dhl", [128, 2, 64], BF16)
        f8ring = Ring([F8[:, i, :] for i in range(8)])
        zring = Ring([zs[:, i, :] for i in range(8)])

        pheld = ps("pheld", [128, 512], F32)
        pheld2 = ps("pheld2", [128, 512], F32)
        pheld3 = ps("pheld3", [128, 512], F32)
        pring = Ring([ps("pb%d" % i, [128, 512], F32) for i in range(4)])
        ptb = ps("ptb", [128, 1024], BF16)

        tri_b2 = cf[:, 0, :].bitcast(BF16)
        tri_p = cf[:, 1, :]
        tri_s = cf[:, 2, :]
        ones_f = cf[:, 3, :]
        ones_s = cf[:, 4, :]
        iota_s = cf[:, 5, :]
        notstart = cf[:, 6, :]
        ind = cf[:, 7, :]
        ident_b = cb[:, 0, :]
        ones_b = cb[:, 1, :]
        neg_p = cb[:, 2, :]
        neg_s = cb[:, 3, :]

        k.dma("sp", cf[:], c_f32)
        nident_b = sb("nident_b", [128, 128], BF16)
        k.dma("pool", cb[:], c_bf)
        k.dma("sp", iota[:], c_iota)
        k.dma("sp", pp[:], pp_d)
        k.dma("sp", pdt[:], pdt_d)
        k.act(nident_b[:], ident_b, AF.Copy, scale=-1.0)

        tiles = [(i * WT, WT, "p") for i in range(TP // WT)] + [(TP, TS, "s")]
        blocks = [("z", 0, 512), ("z", 512, 512), ("x", 1024, 512), ("x", 1536, 512), ("B", 2048, 512),
                  ("C", 2560, 512), ("dt", 3072, 16), ("lg", 3600, 512), ("lx", 3088, 512),
                  ("sg", 4624, 512), ("su", 4112, 512)]
        stream = [(l, ti, bi) for l in range(nlayers) for ti in range(len(tiles)) for bi in range(len(blocks))]
        wbufs = {}
        st = {"next": 0, "item": 0}

        def prefetch_w(upto):
            while st["next"] < len(stream) and st["next"] <= upto:
                l_, ti_, bi_ = stream[st["next"]]
                _, c0_, n_ = blocks[bi_]
                buf = wring.get()
                k.dma("sp", buf[:, :, 0:n_], wbf[l_][:, :, c0_:c0_ + n_], rk=["wbf%d_%d" % (l_, bi_)])
                wbufs[st["next"]] = buf
                st["next"] += 1

        def next_block():
            prefetch_w(st["item"] + 1)
            wb = wbufs.pop(st["item"])
            st["item"] += 1
            return wb

        pringA = Ring(pring.bufs + [pheld, pheld2, pheld3])

        def proj(wb, m, W):
            pm = pringA.get()
            for kt in range(8):
                k.mm(pm[:, 0:W], wb[:, kt, m * 128:(m + 1) * 128], hn[:, kt, 0:W], start=(kt == 0), stop=(kt == 7))
            return pm

        def rmsnorm_tile(src3, ncol, gcol0, l, out_fn):
            pn = pring.get()
            for kt in range(8):
                sq = zring.get()
                k.act(sq[:, 0:ncol], src3[:, kt, 0:ncol], AF.Square)
                k.mm(pn[:, 0:ncol], ones_b, sq[:, 0:ncol], start=(kt == 0), stop=(kt == 7))
            t1 = fring.get()
            k.act(t1[:, 0:ncol], pn[:, 0:ncol], AF.Ln, bias=epsc[:, 0:1], scale=1.0 / D)
            rstd = fring.get()
            k.act(rstd[:, 0:ncol], t1[:, 0:ncol], AF.Exp, scale=-0.5)
            for kt in range(8):
                k.stt(out_fn(kt), src3[:, kt, 0:ncol], pp[:, l, gcol0 + kt:gcol0 + kt + 1], rstd[:, 0:ncol],
                      ALU.mult, ALU.mult)

        def frac_sincos(u, W, sn_out, cs_out):
            ui = iring.get()
            k.copy("dve", ui[:, 0:W], u)
            r = fring.get()
            k.tt("dve", r[:, 0:W], u, ui[:, 0:W], ALU.subtract)
            k.act(sn_out, r[:, 0:W], AF.Sin, scale=TWO_PI)
            ar = fring.get()
            k.stt(ar[:, 0:W], r[:, 0:W], -1.0, r[:, 0:W], ALU.mult, ALU.max)
            k.act(cs_out, ar[:, 0:W], AF.Sin, bias=epsc[:, 1:2], scale=-TWO_PI)

        epsc = sb("epsc", [128, 4], F32)
        k.memset("dve", epsc[:, 0:1], EPS)
        k.memset("dve", epsc[:, 1:2], math.pi / 2.0)
        k.memset("dve", epsc[:, 2:3], 1.0)

        def conv_group(l, pms, W, smp, specs):
            n = len(pms)
            accs = [fring.get() for _ in range(n)]
            raws = [fring.get_full() for _ in range(n)]
            if smp:
                rawv = [r_[:, 0:NSEQ * (LS + 3)].rearrange("p (s t) -> p s t", s=NSEQ) for r_ in raws]
                pmv = [p_[:, 0:W].rearrange("p (s t) -> p s t", s=NSEQ) for p_ in pms]
                accv = [a_[:, 0:W].rearrange("p (s t) -> p s t", s=NSEQ) for a_ in accs]
                for i in range(n):
                    k.copy("dve", rawv[i][:, :, 0:3], specs[i][3])
                for i in range(n):
                    k.copy("act", rawv[i][:, :, 3:3 + LS], pmv[i])
                    k.act(accv[i], pmv[i], AF.Identity, bias=pp[:, l, specs[i][1]:specs[i][1] + 1],
                          scale=pp[:, l, specs[i][0] + 3:specs[i][0] + 4])
                for i in range(n):
                    st3 = tring.get()
                    st3v = st3[:, 0:48].rearrange("p (s t) -> p s t", s=NSEQ)
                    k.copy("dve", st3v, rawv[i][:, :, LS:LS + 3])
                    k.dma("sp", specs[i][4], st3v)
                for kk in range(3):
                    for i in range(n):
                        k.stt(accv[i], rawv[i][:, :, kk:kk + LS], pp[:, l, specs[i][0] + kk:specs[i][0] + kk + 1],
                              accv[i], ALU.mult, ALU.add)
            else:
                for i in range(n):
                    k.copy("dve", raws[i][:, 0:3], specs[i][2])
                for i in range(n):
                    k.copy("act", raws[i][:, 3:3 + W], pms[i][:, 0:W])
                    k.act(accs[i][:, 0:W], pms[i][:, 0:W], AF.Identity, bias=pp[:, l, specs[i][1]:specs[i][1] + 1],
                          scale=pp[:, l, specs[i][0] + 3:specs[i][0] + 4])
                for kk in range(3):
                    for i in range(n):
                        k.stt(accs[i][:, 0:W], raws[i][:, kk:kk + W], pp[:, l, specs[i][0] + kk:specs[i][0] + kk + 1],
                              accs[i][:, 0:W], ALU.mult, ALU.add)
                for i in range(n):
                    k.copy("dve", specs[i][2], raws[i][:, W:W + 3])
            return accs

        def ssd_stage(l, ti, W, smp):
            xc = A16
            nch = W // 128
            TRI = tri_s if smp else tri_p
            ONESM = ones_s if smp else ones_f
            NEG = neg_s if smp else neg_p
            TRIB = tri_b2[:, 128:256] if smp else tri_b2[:, 0:128]
            for ci in range(nch):
                cs = slice(ci * 128, (ci + 1) * 128)
                dtA = dtA_tm[:, ci, :]
                dtc = dt_tm[:, ci, :]
                nacum = pre[:, 1, ci * 16:(ci + 1) * 16]
                w_tm = pre[:, 3, ci * 16:(ci + 1) * 16]
                DEC = pre[:, 4, ci * 16:(ci + 1) * 16]
                k.cp("ssd A acum")
                psc = pheld
                for g in range(4):
                    k.mm(psc[:, g * 128:(g + 1) * 128], xc[:, 8 + g, cs], xc[:, 12 + g, cs])
                k.cp("ssd B scores")
                for j in range(8):
                    k.tr(ptb[:, j * 128:(j + 1) * 128], xc[:, j, cs], ident_b)
                k.cp("T1 xtr")
                k.copy("act", x_tm[:], ptb[:, :])
                k.cp("T2 xcopy")
                k.tt("dve", hnew[:].rearrange("p (h q) -> p h q", h=16), x_tm[:].rearrange("p (h q) -> p h q", h=16),
                     w_tm[:, 0:16].unsqueeze(2).to_broadcast([128, 16, 64]), ALU.mult)
                k.copy("act", xw_tm[:], hnew[:])
                k.cp("T3 xw")
                for g in range(4):
                    k.tr(ptb[:, g * 128:(g + 1) * 128], xc[:, 8 + g, cs], ident_b)
                k.cp("T4 btr")
                k.copy("act", B_tm[:], ptb[:, 0:512])
                k.cp("ssd C transposes")
                for g in range(4):
                    pab = pring.get()
                    for r in range(4):
                        h = 4 * g + r
                        cc = ci * 16 + h
                        k.mm(pab[:, r * 128:(r + 1) * 128], dhl[:, 0, cc:cc + 1].to_broadcast([128, 128]), TRIB,
                             start=True, stop=False)
                        k.mm(pab[:, r * 128:(r + 1) * 128], dhl[:, 1, cc:cc + 1].to_broadcast([128, 128]), TRIB,
                             start=False, stop=False)
                        k.mm(pab[:, r * 128:(r + 1) * 128], ident_b, NEG, start=False, stop=True)
                    for r in range(4):
                        h = 4 * g + r
                        Dh = sring.get()
                        k.act(Dh[:], pab[:, r * 128:(r + 1) * 128], AF.Exp, bias=nacum[:, h:h + 1])
                        k.stt(LT[:, h, :], Dh[:], dt_tm[:, ci, h:h + 1], psc[:, g * 128:(g + 1) * 128],
                              ALU.mult, ALU.mult)
                k.cp("ssd D LT")
                if smp:
                    EAs = [fring.get(), fring.get()]
                    k.copy("dve", dtA_rep[:].rearrange("p (h q) -> p h q", h=16),
                           dtA_tm[:, ci, :].unsqueeze(2).to_broadcast([128, 16, 64]))
                    for j in range(8):
                        pe_ = pring.get()
                        k.mm(pe_[:, 0:128], dtA_rep[:, j * 128:(j + 1) * 128], TRI)
                        k.act(EAs[j // 4][:, (j % 4) * 128:(j % 4 + 1) * 128], pe_[:, 0:128], AF.Exp)
                    pdS = pring.get()
                    for b_ in range(NSEQ):
                        k.mm(pdS[:, b_ * 16:(b_ + 1) * 16], ind[:, b_:b_ + 1].to_broadcast([128, 128]), dtA)
                    DECS = fring.get()
                    k.act(DECS[:, 0:256], pdS[:, 0:256], AF.Exp)
                    hx = Ring([h0f, hnew])
                    bufs = [hx.get()]
                    k.dma("sp", bufs[0][:], h0T_d[l, 0])
                    for b_ in range(NSEQ):
                        hb = bufs[b_]
                        if b_ + 1 < NSEQ:
                            nb_ = hx.get()
                            bufs.append(nb_)
                            k.dma("sp", nb_[:], h0T_d[l, b_ + 1])
                        h0b = h0bring.get()
                        k.copy("act", h0b[:], hb[:])
                        for j in range(8):
                            pyo = pheld2 if j < 4 else pheld3
                            jj = j % 4
                            k.mm(pyo[:, jj * 128 + b_ * 8: jj * 128 + b_ * 8 + 8], h0b[:, j * 128:(j + 1) * 128],
                                 xc[:, 12 + j // 2, b_ * 8:(b_ + 1) * 8])
                        Bm = bmring.get()
                        k.ts("dve", Bm[:], B_tm[:], ind[:, b_:b_ + 1], None, ALU.mult)
                        pS0 = pring.get()
                        pS1 = pring.get()
                        for g in range(4):
                            pS = pS0 if g < 2 else pS1
                            k.mm(pS[:, (g % 2) * 256:(g % 2 + 1) * 256], Bm[:, g * 128:(g + 1) * 128],
                                 xw_tm[:, g * 256:(g + 1) * 256])
                        hb3 = hb[:].rearrange("p (h q) -> p h q", h=16)
                        k.tt("dve", hb3, hb3, DECS[:, b_ * 16:(b_ + 1) * 16].unsqueeze(2).to_broadcast([128, 16, 64]),
                             ALU.mult)
                        k.tt("dve", hb[:, 0:512], hb[:, 0:512], pS0[:, :], ALU.add)
                        k.tt("dve", hb[:, 512:1024], hb[:, 512:1024], pS1[:, :], ALU.add)
                        k.dma("sp", ssd_s_o[l, b_], hb[:])
                    yo0 = fring.get()
                    yo1 = fring.get()
                    k.copy("act", yo0[:], pheld2[:, :])
                    k.copy("act", yo1[:], pheld3[:, :])
                if not smp:
                    k.copy("dve", dtA_rep[:].rearrange("p (h q) -> p h q", h=16),
                           dtA_tm[:, ci, :].unsqueeze(2).to_broadcast([128, 16, 64]))
                for j in range(8):
                    py = pring.get()
                    if not smp:
                        k.mm(py[:, 256:384], dtA_rep[:, j * 128:(j + 1) * 128], TRI)
                    k.mm(py[0:64, 0:128], x_tm[:, (2 * j) * 64:(2 * j + 1) * 64], LT[:, 2 * j, :])
                    k.mm(py[64:128, 0:128], x_tm[:, (2 * j + 1) * 64:(2 * j + 2) * 64], LT[:, 2 * j + 1, :])
                    tmp = sring.get()
                    if smp:
                        yo = (yo0 if j < 4 else yo1)[:, (j % 4) * 128:(j % 4 + 1) * 128]
                        k.tt("dve", tmp[:], yo, EAs[j // 4][:, (j % 4) * 128:(j % 4 + 1) * 128], ALU.mult)
                    else:
                        k.mm(py[:, 128:256], hT_bf[:, j * 128:(j + 1) * 128], xc[:, 12 + j // 2, cs])
                        EA = sring.get()
                        k.act(EA[:], py[:, 256:384], AF.Exp)
                        k.tt("dve", tmp[:], py[:, 128:256], EA[:], ALU.mult)
                    k.defer(lambda j=j, py=py, tmp=tmp, cs=cs: k.tt("dve", F8[:, j, cs], py[:, 0:128], tmp[:], ALU.add),
                            depth=1)
                k.flush()
                k.cp("ssd E y")
                if not smp:
                    for g in range(4):
                        pS = pheld2 if g < 2 else pheld3
                        k.mm(pS[:, (g % 2) * 256:(g % 2 + 1) * 256], B_tm[:, g * 128:(g + 1) * 128],
                             xw_tm[:, g * 256:(g + 1) * 256])
                    hT3 = hT[:].rearrange("p (h q) -> p h q", h=16)
                    k.tt("dve", hT3, hT3, DEC[:, 0:16].unsqueeze(2).to_broadcast([128, 16, 64]), ALU.mult)
                    k.tt("dve", hT[:, 0:512], hT[:, 0:512], pheld2[:, :], ALU.add)
                    k.tt("dve", hT[:, 512:1024], hT[:, 512:1024], pheld3[:, :], ALU.add)
                    k.copy("act", hT_bf[:], hT[:])
            k.cp("ssd F chunks done")
            pn = pring.get()
            for j in range(8):
                k.stt(F8[:, j, 0:W], xc[:, j, 0:W], pp[:, l, PP_SD + j:PP_SD + j + 1], F8[:, j, 0:W], ALU.mult, ALU.add)
            for j in range(8):
                k.tt("dve", F8[:, j, 0:W], F8[:, j, 0:W], zs[:, j, 0:W], ALU.mult)
            for j in range(8):
                sq = fring.get()
                sqb = sq[:, 0:WT // 2].bitcast(BF16)
                k.act(sqb[:, 0:W], F8[:, j, 0:W], AF.Square)
                k.mm(pn[:, 0:W], ones_b, sqb[:, 0:W], start=(j == 0), stop=(j == 7))
            t1 = fring.get()
            k.act(t1[:, 0:W], pn[:, 0:W], AF.Ln, bias=epsc[:, 0:1], scale=1.0 / D)
            rstd = fring.get()
            k.act(rstd[:, 0:W], t1[:, 0:W], AF.Exp, scale=-0.5)
            for j in range(8):
                k.stt(Y[:, j, 0:W], F8[:, j, 0:W], pp[:, l, PP_NG2 + j:PP_NG2 + j + 1], rstd[:, 0:W], ALU.mult, ALU.mult)
            if (not smp) and ti == len(tiles) - 2:
                k.dma("sp", ssd_p_o[l], hT[:])

        def lru_group(l, js, accs, W, smp):
            n = len(js)
            xrs, prs, pgs, rs, gis, as_ = [], [], [], [], [], []
            for i in range(n):
                xr_bf = zring.get()
                k.copy("act", xr_bf[:, 0:W], accs[i][:, 0:W])
                xrs.append(xr_bf)
            for i in range(n):
                pr = pring.get()
                k.mm(pr[:, 0:W], wa_bf[:, js[i], :], xrs[i][:, 0:W])
                pg = pring.get()
                k.mm(pg[:, 0:W], wx_bf[:, js[i], :], xrs[i][:, 0:W])
                prs.append(pr)
                pgs.append(pg)
            for i in range(n):
                j = js[i]
                r = fring.get()
                k.act(r[:, 0:W], prs[i][:, 0:W], AF.Sigmoid, bias=pp[:, l, PP_LBA + j:PP_LBA + j + 1])
                gi = fring.get()
                k.act(gi[:, 0:W], pgs[i][:, 0:W], AF.Sigmoid, bias=pp[:, l, PP_LBX + j:PP_LBX + j + 1])
                rs.append(r)
                gis.append(gi)
            for i in range(n):
                a = f8ring.get()
                k.act(a[:, 0:W], rs[i][:, 0:W], AF.Exp, scale=lp[:, 0, js[i]:js[i] + 1])
                as_.append(a)
            for i in range(n):
                k.tt("dve", rs[i][:, 0:W], as_[i][:, 0:W], as_[i][:, 0:W], ALU.mult)
            for i in range(n):
                k.ts("dve", rs[i][:, 0:W], rs[i][:, 0:W], 1.0, -1.0, ALU.min, ALU.mult)
            for i in range(n):
                k.act(rs[i][:, 0:W], rs[i][:, 0:W], AF.Sqrt, bias=epsc[:, 2:3])
            for i in range(n):
                k.tt("dve", gis[i][:, 0:W], gis[i][:, 0:W], rs[i][:, 0:W], ALU.mult)
            for i in range(n):
                k.tt("dve", gis[i][:, 0:W], gis[i][:, 0:W], accs[i][:, 0:W], ALU.mult)
            hss = []
            for i in range(n):
                j = js[i]
                a = as_[i]
                gi = gis[i]
                hs = f8ring.get()
                if smp:
                    am = f8ring.get()
                    k.tt("dve", am[:, 0:W], a[:, 0:W], notstart, ALU.mult)
                    t = tring.get()
                    k.tt("dve", t[:, 0:16], a[:, 0:W:LS], lruh0[:, j, :], ALU.mult)
                    k.tt("dve", gi[:, 0:W:LS], gi[:, 0:W:LS], t[:, 0:16], ALU.add)
                    k.scan(hs[:, 0:W], am[:, 0:W], gi[:, 0:W], 0.0)
                else:
                    k.scan(hs[:, 0:W], a[:, 0:W], gi[:, 0:W], pcar_h[:, j:j + 1])
                hss.append(hs)
            for i in range(n):
                j = js[i]
                if smp:
                    k.copy("dve", hcar[:, j, :], hss[i][:, LS - 1:W:LS])
                else:
                    k.copy("dve", pcar_h[:, j:j + 1], hss[i][:, W - 1:W])
                k.tt("dve", Y[:, 8 + j, 0:W], hss[i][:, 0:W], A16[:, j, 0:W], ALU.mult)

        def s5_stage(l, ti, c0, W, smp):
            last_p = (not smp) and ti == len(tiles) - 2
            if not smp:
                k.ts("dve", iota_c[:, 0:W], iota[:, 0:W], float(c0), None, ALU.add)
            tsrc = iota_s if smp else iota_c[:, 0:W]
            tabring = Ring([A16[:, 0, :], A16[:, 1, :], A16[:, 2, :], A16[:, 3, :], Y[:, 14, :], Y[:, 15, :]])
            srring = Ring([F8[:, 0, :], F8[:, 2, :]])
            siring = Ring([F8[:, 1, :], F8[:, 3, :]])
            tprod = [zs[:, i, :] for i in range(4)]
            mprod = [zs[:, 4 + i, :] for i in range(4)]
            Srb = Y[:, 12, :]
            Sib = Y[:, 13, :]
            pGr = pheld2
            pGi = pheld3

            def tables(pr_):
                thc = lp[:, 1, pr_:pr_ + 1]
                u = fring.get()
                ui = iring.get()
                k.ts("dve", ui[:, 0:W], tsrc, thc, None, ALU.mult)
                k.stt(u[:, 0:W], tsrc, thc, ui[:, 0:W], ALU.mult, ALU.subtract)
                uf = fring.get()
                sn = tabring.get()
                cs = tabring.get()
                k.act(sn[:, 0:W], u[:, 0:W], AF.Sin, scale=TWO_PI)
                k.act(uf[:, 0:W], u[:, 0:W], AF.Abs)
                k.act(cs[:, 0:W], uf[:, 0:W], AF.Sin, bias=epsc[:, 1:2], scale=-TWO_PI)
                return sn, cs

            def make_tail(pr_, q, py5, sn, cs, Sr, Si):
                def tail():
                    m1, m2, m3, m4 = mprod
                    k.tt("pool", m1[:, 0:W], cs[:, 0:W], Srb[:, 0:W], ALU.mult)
                    k.tt("pool", m2[:, 0:W], sn[:, 0:W], Sib[:, 0:W], ALU.mult)
                    k.tt("pool", m3[:, 0:W], cs[:, 0:W], Sib[:, 0:W], ALU.mult)
                    k.tt("pool", m4[:, 0:W], sn[:, 0:W], Srb[:, 0:W], ALU.mult)
                    k.mm(py5[:, 0:W], CT_re[:, pr_, :], m1[:, 0:W], start=(q == 0), stop=False)
                    k.mm(py5[:, 0:W], nCT_re[:, pr_, :], m2[:, 0:W], start=False, stop=False)
                    k.mm(py5[:, 0:W], nCT_im[:, pr_, :], m3[:, 0:W], start=False, stop=False)
                    k.mm(py5[:, 0:W], nCT_im[:, pr_, :], m4[:, 0:W], start=False, stop=(q == 3))
                    if smp or last_p:
                        if smp:
                            sel = slice(LS - 1, W, LS)
                            n_ = NSEQ
                            dr = s5o_r[:, pr_, :]
                            di = s5o_i[:, pr_, :]
                        else:
                            sel = slice(W - 1, W)
                            n_ = 1
                            dr = s5p_r[:, pr_:pr_ + 1]
                            di = s5p_i[:, pr_:pr_ + 1]
                        ta = tring.get()
                        tb2 = tring.get()
                        tc2 = tring.get()
                        td2 = tring.get()
                        k.tt("dve", ta[:, 0:n_], cs[:, sel], Sr[:, sel], ALU.mult)
                        k.tt("dve", tb2[:, 0:n_], sn[:, sel], Si[:, sel], ALU.mult)
                        k.tt("dve", tc2[:, 0:n_], cs[:, sel], Si[:, sel], ALU.mult)
                        k.tt("dve", td2[:, 0:n_], sn[:, sel], Sr[:, sel], ALU.mult)
                        k.tt("dve", dr, ta[:, 0:n_], tb2[:, 0:n_], ALU.subtract)
                        k.tt("dve", di, tc2[:, 0:n_], td2[:, 0:n_], ALU.add)
                return tail

            nxt = tables(0)
            prev_tail = None
            for kt in range(4):
                py5 = pheld
                ub = A16[:, 8 + kt, 0:W]
                for q in range(4):
                    pr_ = kt * 4 + q
                    pbr = pring.get()
                    k.mm(pbr[:, 0:W], BbT_re[:, pr_, :], ub)
                    pbi = pring.get()
                    k.mm(pbi[:, 0:W], BbT_im[:, pr_, :], ub)
                    sn, cs = nxt
                    if pr_ + 1 < 16:
                        nxt = tables(pr_ + 1)
                    t1, t2, t3, t4 = tprod
                    k.tt("dve", t1[:, 0:W], pbr[:, 0:W], cs[:, 0:W], ALU.mult)
                    k.tt("dve", t2[:, 0:W], pbi[:, 0:W], sn[:, 0:W], ALU.mult)
                    k.tt("dve", t3[:, 0:W], pbi[:, 0:W], cs[:, 0:W], ALU.mult)
                    k.tt("dve", t4[:, 0:W], pbr[:, 0:W], sn[:, 0:W], ALU.mult)
                    k.mm(pGr[:, 0:W], ident_b, t1[:, 0:W], start=True, stop=False)
                    k.mm(pGr[:, 0:W], ident_b, t2[:, 0:W], start=False, stop=True)
                    k.mm(pGi[:, 0:W], ident_b, t3[:, 0:W], start=True, stop=False)
                    k.mm(pGi[:, 0:W], nident_b[:], t4[:, 0:W], start=False, stop=True)
                    if prev_tail is not None:
                        prev_tail()
                        prev_tail = None
                    Sr = srring.get()
                    Si = siring.get()
                    if smp:
                        ar_ = lp[:, 3, pr_:pr_ + 1]
                        ai_ = lp[:, 4, pr_:pr_ + 1]
                        h0r = s5h0_r[:, pr_, :]
                        h0i = s5h0_i[:, pr_, :]
                        tb = tring.get()
                        k.ts("dve", tb[:, 0:16], h0i, ai_, None, ALU.mult)
                        injr = tring.get()
                        k.stt(injr[:, 0:16], h0r, ar_, tb[:, 0:16], ALU.mult, ALU.subtract)
                        tc = tring.get()
                        k.ts("dve", tc[:, 0:16], h0r, ai_, None, ALU.mult)
                        inji = tring.get()
                        k.stt(inji[:, 0:16], h0i, ar_, tc[:, 0:16], ALU.mult, ALU.add)
                        k.tt("dve", pGr[:, 0:W:LS], pGr[:, 0:W:LS], injr[:, 0:16], ALU.add)
                        k.tt("dve", pGi[:, 0:W:LS], pGi[:, 0:W:LS], inji[:, 0:16], ALU.add)
                        magm = sring.get()
                        k.ts("dve", magm[:], notstart, lp[:, 2, pr_:pr_ + 1], None, ALU.mult)
                        k.scan(Sr[:, 0:W], magm[:], pGr[:, 0:W], 0.0)
                        k.scan(Si[:, 0:W], magm[:], pGi[:, 0:W], 0.0)
                    else:
                        magb = lp[:, 2, pr_:pr_ + 1].to_broadcast([128, W])
                        k.scan(Sr[:, 0:W], magb, pGr[:, 0:W], s5car_r[:, pr_:pr_ + 1])
                        k.scan(Si[:, 0:W], magb, pGi[:, 0:W], s5car_i[:, pr_:pr_ + 1])
                        k.copy("dve", s5car_r[:, pr_:pr_ + 1], Sr[:, W - 1:W])
                        k.copy("dve", s5car_i[:, pr_:pr_ + 1], Si[:, W - 1:W])
                    k.copy("act", Srb[:, 0:W], Sr[:, 0:W])
                    k.copy("act", Sib[:, 0:W], Si[:, 0:W])
                    prev_tail = make_tail(pr_, q, py5, sn, cs, Sr, Si)
                    if q == 3:
                        prev_tail()
                        prev_tail = None
                k.stt(F8[:, 4 + kt, 0:W], ub, pp[:, l, PP_S5D + kt:PP_S5D + kt + 1], py5[:, 0:W], ALU.mult, ALU.add)
            x2s = [fring.get() for _ in range(4)]
            for kt in range(4):
                k.act(x2s[kt][:, 0:W], F8[:, 4 + kt, 0:W], AF.Square)
            for kt in range(4):
                k.ts("dve", x2s[kt][:, 0:W], x2s[kt][:, 0:W], 0.044715, 1.0, ALU.mult, ALU.add)
            for kt in range(4):
                k.tt("dve", x2s[kt][:, 0:W], x2s[kt][:, 0:W], F8[:, 4 + kt, 0:W], ALU.mult)
            for kt in range(4):
                k.act(x2s[kt][:, 0:W], x2s[kt][:, 0:W], AF.Sigmoid, scale=1.5957691216057308)
            for kt in range(4):
                k.tt("dve", A16[:, 12 + kt, 0:W], F8[:, 4 + kt, 0:W], x2s[kt][:, 0:W], ALU.mult)
            sgs = []
            for m in range(4):
                pg = pring.get()
                for kt in range(4):
                    k.mm(pg[:, 0:W], glu_bf[:, kt, m * 128:(m + 1) * 128], A16[:, 12 + kt, 0:W],
                         start=(kt == 0), stop=(kt == 3))
                sg = fring.get()
                k.act(sg[:, 0:W], pg[:, 0:W], AF.Sigmoid, bias=pp[:, l, PP_GLB + m:PP_GLB + m + 1])
                sgs.append(sg)
            for m in range(4):
                k.tt("dve", sgs[m][:, 0:W], sgs[m][:, 0:W], A16[:, 12 + m, 0:W], ALU.mult)
            for m in range(4):
                k.tt("dve", Y[:, 12 + m, 0:W], sgs[m][:, 0:W], A16[:, 4 + m, 0:W], ALU.mult)

        def convert_w_in(l_):
            for bi_, (_, c0_, n_) in enumerate(blocks):
                k.dma("pool", wbf[l_][:, :, c0_:c0_ + n_], w_in[l_][:, :, c0_:c0_ + n_],
                      wk=["wbf%d_%d" % (l_, bi_)])

        def convert_w_out(l_):
            for m_ in range(8):
                k.dma("pool", wobf[l_, m_], w_out[l_][:, :, m_ * 128:(m_ + 1) * 128], wk=["wobf%d_%d" % (l_, m_)])

        for l in range(nlayers):
            k.dma("pool", glu_bf[:], glu_w[l])
            k.dma("pool", wa_bf[:], wa_bd[l])
            k.dma("pool", wx_bf[:], wx_bd[l])
            k.dma("pool", CT_re[:], ctpad_re[l])
            k.dma("pool", nCT_im[:], ctpad_im[l])
            if l == 0:
                convert_w_in(0)
            k.act(nCT_re[:].rearrange("p a b -> p (a b)"), CT_re[:].rearrange("p a b -> p (a b)"), AF.Copy, scale=-1.0)
            k.act(nCT_im[:].rearrange("p a b -> p (a b)"), nCT_im[:].rearrange("p a b -> p (a b)"), AF.Copy, scale=-1.0)
            k.act(expA[:], pdt[:, l, 16:32], AF.Exp)
            t = tring.get()
            k.act(t[:, 0:4], pp[:, l, PP_LLAM:PP_LLAM + 4], AF.Exp, scale=-1.0)
            t2 = tring.get()
            k.act(t2[:, 0:4], t[:, 0:4], AF.Ln, bias=epsc[:, 2:3])
            k.ts("dve", lp[:, 0, 0:4], t2[:, 0:4], -8.0, None, ALU.mult)
            dl = tring.get()
            k.act(dl[:, 0:16], pp[:, l, PP_LDT:PP_LDT + 16], AF.Exp)
            thp = lp[:, 1, :]
            k.stt(thp, pp[:, l, PP_LIM:PP_LIM + 16], 1.0 / TWO_PI, dl[:, 0:16], ALU.mult, ALU.mult)
            lm = tring.get()
            k.tt("dve", lm[:, 0:16], pp[:, l, PP_LRE:PP_LRE + 16], dl[:, 0:16], ALU.mult)
            mag = lp[:, 2, :]
            k.act(mag, lm[:, 0:16], AF.Exp)
            sn0 = tring.get()
            cs0 = tring.get()
            frac_sincos(thp, 16, sn0[:, 0:16], cs0[:, 0:16])
            k.tt("dve", lp[:, 3, :], mag, cs0[:, 0:16], ALU.mult)
            k.tt("dve", lp[:, 4, :], mag, sn0[:, 0:16], ALU.mult)
            lre_c = pp[:, l, PP_LRE:PP_LRE + 16]
            lim_c = pp[:, l, PP_LIM:PP_LIM + 16]
            nr = tring.get()
            k.ts("dve", nr[:, 0:16], lp[:, 3, :], -1.0, None, ALU.add)
            den = tring.get()
            t_a = tring.get()
            k.tt("dve", den[:, 0:16], lre_c, lre_c, ALU.mult)
            k.tt("dve", t_a[:, 0:16], lim_c, lim_c, ALU.mult)
            k.tt("dve", den[:, 0:16], den[:, 0:16], t_a[:, 0:16], ALU.add)
            k.op("dve", lambda v, o=den[:, 0:16]: v.reciprocal(out=o, in_=o), [den], [den])
            cre = lp[:, 5, :]
            cim = lp[:, 6, :]
            t_b = tring.get()
            k.tt("dve", cre, nr[:, 0:16], lre_c, ALU.mult)
            k.tt("dve", t_b[:, 0:16], lp[:, 4, :], lim_c, ALU.mult)
            k.tt("dve", cre, cre, t_b[:, 0:16], ALU.add)
            k.tt("dve", cre, cre, den[:, 0:16], ALU.mult)
            t_c = tring.get()
            k.tt("dve", cim, lp[:, 4, :], lre_c, ALU.mult)
            k.tt("dve", t_c[:, 0:16], nr[:, 0:16], lim_c, ALU.mult)
            k.tt("dve", cim, cim, t_c[:, 0:16], ALU.subtract)
            k.tt("dve", cim, cim, den[:, 0:16], ALU.mult)
            for c4 in range(4):
                bre = fring.get()
                bim = fring.get()
                k.dma("sp", bre[:].rearrange("p (a b) -> p a b", a=4), bpad_re[l][:, c4 * 4:(c4 + 1) * 4, :])
                k.dma("sp", bim[:].rearrange("p (a b) -> p a b", a=4), bpad_im[l][:, c4 * 4:(c4 + 1) * 4, :])
                crb = cre[:, c4 * 4:(c4 + 1) * 4].unsqueeze(2).to_broadcast([128, 4, 128])
                cib = cim[:, c4 * 4:(c4 + 1) * 4].unsqueeze(2).to_broadcast([128, 4, 128])
                v3 = lambda ap_: ap_.rearrange("p (a b) -> p a b", a=4)
                m_a, m_b, m_c, m_d = [f8ring.get() for _ in range(4)]
                k.tt("dve", v3(m_a), v3(bre[:]), crb, ALU.mult)
                k.tt("dve", v3(m_b), v3(bim[:]), cib, ALU.mult)
                k.tt("dve", v3(m_c), v3(bre[:]), cib, ALU.mult)
                k.tt("dve", v3(m_d), v3(bim[:]), crb, ALU.mult)
                o_re = zring.get()
                o_im = zring.get()
                k.tt("dve", o_re, m_a, m_b, ALU.subtract)
                k.tt("dve", o_im, m_c, m_d, ALU.add)
                for i4 in range(4):
                    k.tr(ptb[:, i4 * 128:(i4 + 1) * 128], o_re[:, i4 * 128:(i4 + 1) * 128], ident_b)
                    k.tr(ptb[:, 512 + i4 * 128:512 + (i4 + 1) * 128], o_im[:, i4 * 128:(i4 + 1) * 128], ident_b)
                k.copy("act", BbT_re[:, c4 * 4:(c4 + 1) * 4, :].rearrange("p a b -> p (a b)"), ptb[:, 0:512])
                k.copy("act", BbT_im[:, c4 * 4:(c4 + 1) * 4, :].rearrange("p a b -> p (a b)"), ptb[:, 512:1024])
            k.dma("sp", carry_s[:], sconv0[l])
            k.dma("sp", carry_l[:], lconv0[l])
            k.dma("sp", lruh0[:], lru0[l])
            k.dma("sp", s5h0_r[:], s5r0[l])
            k.dma("sp", s5h0_i[:], s5i0[l])
            k.memset("dve", hT[:], 0.0)
            k.memset("dve", hT_bf[:], 0.0)
            k.memset("dve", pcar_s[:], 0.0)
            k.memset("dve", pcar_l[:], 0.0)
            k.memset("dve", pcar_h[:], 0.0)
            k.memset("dve", s5car_r[:], 0.0)
            k.memset("dve", s5car_i[:], 0.0)

            if l == 0:
                prefetch_w(1)
            k.cp("setup done l%d" % l)
            for ti, (c0, W, kind) in enumerate(tiles):
                smp = kind == "s"
                k.cp("tile start l%d t%d" % (l, ti))
                for kt in range(8):
                    if l == 0:
                        k.dma("sp", xt[:, kt, 0:W], xin[kt][:, c0:c0 + W])
                    else:
                        k.dma("sp", xt[:, kt, 0:W], xsc[(l - 1) % 2][kt][:, c0:c0 + W],
                              rk=["xsc%d_%d_%d" % ((l - 1) % 2, ti, kt)])
                rmsnorm_tile(xt, W, PP_NG, l, lambda kt: hn[:, kt, 0:W])
                k.cp("norm done")
                for zb in range(2):
                    wb = next_block()
                    for m in range(4):
                        pm = proj(wb, m, W)
                        k.act(zs[:, zb * 4 + m, 0:W], pm[:, 0:W], AF.Silu)
                for xb in range(4):
                    wb = next_block()
                    for half in range(2):
                        js = [xb * 4 + half * 2 + i for i in range(2)]
                        pms = [proj(wb, half * 2 + i, W) for i in range(2)]
                        accs = conv_group(l, pms, W, smp,
                                          [(PP_CW + 4 * j, PP_CB + j, pcar_s[:, j, :], carry_s[:, j, :, :],
                                            sconv_s_o[l][:, j, :, :]) for j in js])

                        def silus(js=js, accs=accs):
                            for j, acc in zip(js, accs):
                                k.act(A16[:, j, 0:W], acc[:, 0:W], AF.Silu)
                        k.defer(silus, depth=1)
                wb = next_block()
                k.flush()
                nchk = W // 128
                pd = pring.get()
                for ci in range(nchk):
                    for kt in range(8):
                        k.mm(pd[:, ci * 16:(ci + 1) * 16], hn[:, kt, ci * 128:(ci + 1) * 128], wb[:, kt, 0:16],
                             start=(kt == 0), stop=(kt == 7))
                dt2 = dt_tm[:, 0:nchk, :]
                dtA2 = dtA_tm[:, 0:nchk, :]
                pd3 = pd[:, 0:nchk * 16].rearrange("p (c h) -> p c h", c=nchk)
                v = pre[:, 6, 0:nchk * 16].rearrange("p (c h) -> p c h", c=nchk)
                k.tt("dve", v, pd3, pdt[:, l, 0:16].unsqueeze(1).to_broadcast([128, nchk, 16]), ALU.add)
                k.act(v, v, AF.Exp)
                k.act(dt2, v, AF.Ln, bias=epsc[:, 2:3])
                k.stt(dtA2, dt2, -1.0, expA[:].unsqueeze(1).to_broadcast([128, nchk, 16]), ALU.mult, ALU.mult)
                hi_f = pre[:, 5, 0:nchk * 16]
                k.copy("act", dhl[:, 0, 0:nchk * 16], dtA_tm[:, 0:nchk, :].rearrange("p c h -> p (c h)"))
                k.copy("act", hi_f, dhl[:, 0, 0:nchk * 16])
                k.tt("dve", hi_f, dtA_tm[:, 0:nchk, :].rearrange("p c h -> p (c h)"), hi_f, ALU.subtract)
                k.copy("act", dhl[:, 1, 0:nchk * 16], hi_f)
                TRI_ = tri_s if smp else tri_p
                ONESM_ = ones_s if smp else ones_f
                n16 = nchk * 16
                pa = pring.get()
                dtA_flat = dtA_tm[:, 0:nchk, :].rearrange("p c h -> p (c h)")
                k.mm(pa[:, 0:n16], TRI_, dtA_flat)
                k.mm(pa[:, 64:64 + n16], ONESM_, dtA_flat)
                k.copy("act", pre[:, 0, 0:n16], pa[:, 0:n16])
                k.ts("dve", pre[:, 1, 0:n16], pa[:, 0:n16], -1.0, None, ALU.mult)
                k.tt("dve", pre[:, 2, 0:n16], pa[:, 64:64 + n16], pre[:, 0, 0:n16], ALU.subtract)
                k.act(pre[:, 2, 0:n16], pre[:, 2, 0:n16], AF.Exp)
                k.tt("dve", pre[:, 3, 0:n16], pre[:, 2, 0:n16], dt_tm[:, 0:nchk, :].rearrange("p c h -> p (c h)"),
                     ALU.mult)
                k.act(pre[:, 4, 0:n16], pa[:, 64:64 + n16], AF.Exp)
                if l == 0 and ti == 0:
                    convert_w_out(0)
                k.cp("stage A done")
                ssd_stage(l, ti, W, smp)
                k.cp("ssd done")
                wb = next_block()
                for m in range(4):
                    pm = proj(wb, m, W)
                    k.act(A16[:, m, 0:W], pm[:, 0:W], AF.Silu)
                wb = next_block()
                for half in range(2):
                    js = [half * 2, half * 2 + 1]
                    pms = [proj(wb, j, W) for j in js]
                    accs = conv_group(l, pms, W, smp, [(PP_LCW + 4 * j, PP_LCB + j, pcar_l[:, j, :],
                                                         carry_l[:, j, :, :], lconv_s_o[l][:, j, :, :]) for j in js])
                    lru_group(l, js, accs, W, smp)
                k.cp("lru done")
                k.flush()
                wb = next_block()
                for m in range(4):
                    pm = proj(wb, m, W)
                    k.act(A16[:, 4 + m, 0:W], pm[:, 0:W], AF.Silu)
                wb = next_block()
                for m in range(4):
                    pm = proj(wb, m, W)
                    k.copy("act", A16[:, 8 + m, 0:W], pm[:, 0:W])
                s5_stage(l, ti, c0, W, smp)
                if l + 1 < nlayers and ti == 0:
                    convert_w_in(l + 1)
                if l + 1 < nlayers and ti == 1:
                    convert_w_out(l + 1)
                k.cp("s5 done")
                wo = woring.get()
                k.dma("sp", wo[:], wobf[l, 0], rk=["wobf%d_0" % l])
                for m in range(8):
                    if m + 1 < 8:
                        wo_n = woring.get()
                        k.dma("sp", wo_n[:], wobf[l, m + 1], rk=["wobf%d_%d" % (l, m + 1)])
                    po = pring.get()
                    for kt in range(16):
                        k.mm(po[:, 0:W], wo[:, kt, :], Y[:, kt, 0:W], start=(kt == 0), stop=(kt == 15))
                    k.tt("dve", xt[:, m, 0:W], po[:, 0:W], xt[:, m, 0:W], ALU.add)
                    if l < nlayers - 1:
                        k.dma("sp", xsc[l % 2][m][:, c0:c0 + W], xt[:, m, 0:W], wk=["xsc%d_%d_%d" % (l % 2, ti, m)])
                    if m + 1 < 8:
                        wo = wo_n
                if l < nlayers - 1:
                    pass
                else:
                    rmsnorm_tile(xt, W, PP_FG, l, lambda kt: F8[:, kt, 0:W])
                    k.dma("sp", yout.rearrange("k p t -> p k t")[:, :, c0:c0 + W], F8[:, :, 0:W])
            k.dma("sp", sconv_p_o[l], pcar_s[:])
            k.dma("sp", lconv_p_o[l], pcar_l[:])
            k.dma("sp", lru_p_o[l], pcar_h[:])
            k.dma("sp", lru_s_o[l], hcar[:])
            k.dma("sp", s5r_p_o[l], s5p_r[:])
            k.dma("sp", s5i_p_o[l], s5p_i[:])
            k.dma("sp", s5r_s_o[l], s5o_r[:])
            k.dma("sp", s5i_s_o[l], s5o_i[:])
        k.finish()
        k.emit()
    return nc


def _consts():
    i = np.arange(128)
    ident = np.eye(128, dtype=np.float32)
    tri_p = (i[:, None] <= i[None, :]).astype(np.float32)
    same = (i[:, None] // LS == i[None, :] // LS)
    tri_s = (tri_p > 0) & same
    cfa = np.zeros((128, 8, 128), np.float32)
    import ml_dtypes
    trib = np.concatenate([tri_p, tri_s.astype(np.float32)], axis=1).astype(ml_dtypes.bfloat16)
    cfa[:, 0] = np.ascontiguousarray(trib).view(np.float32)
    cfa[:, 1] = tri_p
    cfa[:, 2] = tri_s
    cfa[:, 3] = 1.0
    cfa[:, 4] = same
    cfa[:, 5] = np.broadcast_to((i % LS)[None, :], (128, 128))
    cfa[:, 6] = np.broadcast_to((i % LS != 0)[None, :], (128, 128))
    cfa[:, 7, 0:NSEQ] = (i[:, None] // LS == np.arange(NSEQ)[None, :])
    cba = np.zeros((128, 4, 128), np.float32)
    cba[:, 0] = ident
    cba[:, 1] = 1.0
    cba[:, 2] = np.where(tri_p > 0, 0.0, -30000.0)
    cba[:, 3] = np.where(tri_s, 0.0, -30000.0)
    iota = np.broadcast_to(np.arange(WT, dtype=np.float32)[None, :], (128, WT)).copy()
    return cfa, cba, iota


def _fm(v, nt):
    return np.moveaxis(v.reshape(v.shape[:-1] + (nt, 128)), -1, -2)


def _prep_shared(inp):
    f = lambda a: np.ascontiguousarray(a, dtype=np.float32)
    sh = {}
    sh["w_in_r"] = f(inp["w_in"].reshape(NL, 8, 128, IN_DIM).transpose(0, 2, 1, 3))
    sh["w_out_r"] = f(inp["w_out"].reshape(NL, 16, 128, D).transpose(0, 2, 1, 3))
    sh["glu_r"] = f(inp["s5_glu_w"].reshape(NL, 4, 128, 512).transpose(0, 2, 1, 3))
    for nm, src in (("wa_bd", inp["lru_wa"]), ("wx_bd", inp["lru_wx"])):
        bd = np.zeros((NL, 128, 4, 128), np.float32)
        for m in range(4):
            for k2 in range(2):
                bd[:, k2 * 64:(k2 + 1) * 64, m, k2 * 64:(k2 + 1) * 64] = src[:, 2 * m + k2]
        sh[nm] = bd
    pp = np.zeros((128, NL, NPP), np.float32)
    for l in range(NL):
        pp[:, l, PP_NG:PP_NG + 8] = _fm(inp["norm_g"][l], 8)
        pp[:, l, PP_NG2:PP_NG2 + 8] = _fm(inp["ssd_norm_g"][l], 8)
        pp[:, l, PP_SD:PP_SD + 8] = _fm(np.repeat(inp["ssd_d"][l], 64), 8)
        pp[:, l, PP_CW:PP_CW + 64] = inp["ssd_conv_w"][l].reshape(4, 16, 128).transpose(2, 1, 0).reshape(128, 64)
        pp[:, l, PP_CB:PP_CB + 16] = _fm(inp["ssd_conv_b"][l], 16)
        pp[:, l, PP_LCW:PP_LCW + 16] = inp["lru_conv_w"][l].reshape(4, 4, 128).transpose(2, 1, 0).reshape(128, 16)
        pp[:, l, PP_LCB:PP_LCB + 4] = _fm(inp["lru_conv_b"][l], 4)
        pp[:, l, PP_LBA:PP_LBA + 4] = _fm(inp["lru_ba"][l], 4)
        pp[:, l, PP_LBX:PP_LBX + 4] = _fm(inp["lru_bx"][l], 4)
        pp[:, l, PP_LLAM:PP_LLAM + 4] = _fm(inp["lru_lambda"][l], 4)
        pp[:, l, PP_S5D:PP_S5D + 4] = _fm(inp["s5_d"][l], 4)
        pp[:, l, PP_GLB:PP_GLB + 4] = _fm(inp["s5_glu_b"][l], 4)
        pp[:, l, PP_LRE:PP_LRE + 16] = inp["s5_lambda_re"][l].reshape(16, 128).T
        pp[:, l, PP_LIM:PP_LIM + 16] = inp["s5_lambda_im"][l].reshape(16, 128).T
        pp[:, l, PP_LDT:PP_LDT + 16] = np.repeat(inp["s5_log_dt"][l], 64).reshape(16, 128).T
        pp[:, l, PP_FG:PP_FG + 8] = _fm(inp["final_norm_g"], 8)
    sh["pp_in"] = pp
    pdt = np.zeros((128, NL, 32), np.float32)
    pdt[:, :, 0:16] = inp["ssd_dt_bias"][None]
    pdt[:, :, 16:32] = inp["ssd_a_log"][None]
    sh["pdt_in"] = pdt
    bre = np.zeros((NL, 128, 16, 128), np.float32)
    bim = np.zeros((NL, 128, 16, 128), np.float32)
    cre = np.zeros((NL, 128, 16, 128), np.float32)
    cim = np.zeros((NL, 128, 16, 128), np.float32)
    for g in range(32):
        pr, g2, gl = g // 2, g % 2, g % 8
        bre[:, g2 * 64:(g2 + 1) * 64, pr, gl * 16:(gl + 1) * 16] = inp["s5_b_re"][:, g]
        bim[:, g2 * 64:(g2 + 1) * 64, pr, gl * 16:(gl + 1) * 16] = inp["s5_b_im"][:, g]
        cre[:, g2 * 64:(g2 + 1) * 64, pr, gl * 16:(gl + 1) * 16] = inp["s5_c_re"][:, g].transpose(0, 2, 1)
        cim[:, g2 * 64:(g2 + 1) * 64, pr, gl * 16:(gl + 1) * 16] = inp["s5_c_im"][:, g].transpose(0, 2, 1)
    sh["bpadT_re"], sh["bpadT_im"], sh["ctpad_re"], sh["ctpad_im"] = bre, bim, cre, cim
    cfa, cba, iota = _consts()
    sh["c_f32"], sh["c_bf"], sh["c_iota"] = cfa, cba, iota
    return sh


def _prep_core(inp, c):
    f = lambda a: np.ascontiguousarray(a, dtype=np.float32)
    sl = slice(NSEQ * c, NSEQ * (c + 1))
    m = {}
    x_tok = np.concatenate([inp["x_prompt"][c], inp["x_sample"][sl].reshape(TS, D)], axis=0)
    m["xin"] = f(x_tok.T.reshape(8, 128, TT))
    m["h0T"] = f(inp["state_ssd"][:, sl].transpose(0, 1, 4, 2, 3).reshape(NL, NSEQ, 128, 1024))
    m["sconv0"] = f(inp["state_ssd_conv"][:, sl].reshape(NL, NSEQ, 3, 16, 128).transpose(0, 4, 3, 1, 2))
    m["lconv0"] = f(inp["state_lru_conv"][:, sl].reshape(NL, NSEQ, 3, 4, 128).transpose(0, 4, 3, 1, 2))
    m["lru0"] = f(inp["state_lru"][:, sl].reshape(NL, NSEQ, 4, 128).transpose(0, 3, 2, 1))
    m["s5r0"] = f(inp["state_s5_re"][:, sl].reshape(NL, NSEQ, 16, 128).transpose(0, 3, 2, 1))
    m["s5i0"] = f(inp["state_s5_im"][:, sl].reshape(NL, NSEQ, 16, 128).transpose(0, 3, 2, 1))
    return m


_PROG = {}


def kernel(**inputs):
    inp = {k_: np.asarray(v) for k_, v in inputs.items()}
    if "nc" not in _PROG:
        _PROG["nc"] = build_program(NL)
    nc = _PROG["nc"]
    shared = _prep_shared(inp)
    in_maps = []
    for c in range(NCORES):
        m = dict(shared)
        m.update(_prep_core(inp, c))
        in_maps.append(m)
    res = run_bass_kernel_spmd(nc, in_maps, core_ids=list(range(NCORES)))
    R = res.results
    B = NCORES
    y_prompt = np.zeros((B, TP, D), np.float32)
    y_sample = np.zeros((B * NSEQ, LS, D), np.float32)
    ssd_p = np.zeros((NL, B, 16, 64, 128), np.float32)
    ssd_s = np.zeros((NL, B * NSEQ, 16, 64, 128), np.float32)
    ssd_conv_p = np.zeros((NL, B, 3, 2048), np.float32)
    ssd_conv_s = np.zeros((NL, B * NSEQ, 3, 2048), np.float32)
    lru_p = np.zeros((NL, B, 512), np.float32)
    lru_s = np.zeros((NL, B * NSEQ, 512), np.float32)
    lru_conv_p = np.zeros((NL, B, 3, 512), np.float32)
    lru_conv_s = np.zeros((NL, B * NSEQ, 3, 512), np.float32)
    s5_re_p = np.zeros((NL, B, 32, 64), np.float32)
    s5_re_s = np.zeros((NL, B * NSEQ, 32, 64), np.float32)
    s5_im_p = np.zeros((NL, B, 32, 64), np.float32)
    s5_im_s = np.zeros((NL, B * NSEQ, 32, 64), np.float32)
    for c in range(B):
        r = R[c]
        sl = slice(NSEQ * c, NSEQ * (c + 1))
        y = np.asarray(r["yout"]).reshape(D, TT).T
        y_prompt[c] = y[0:TP]
        y_sample[sl] = y[TP:].reshape(NSEQ, LS, D)
        ssd_p[:, c] = np.asarray(r["ssd_p_o"]).reshape(NL, 128, 16, 64).transpose(0, 2, 3, 1)
        ssd_s[:, sl] = np.asarray(r["ssd_s_o"]).reshape(NL, NSEQ, 128, 16, 64).transpose(0, 1, 3, 4, 2)
        ssd_conv_p[:, c] = np.asarray(r["sconv_p_o"]).transpose(0, 3, 2, 1).reshape(NL, 3, 2048)
        ssd_conv_s[:, sl] = np.asarray(r["sconv_s_o"]).transpose(0, 3, 4, 2, 1).reshape(NL, NSEQ, 3, 2048)
        lru_p[:, c] = np.asarray(r["lru_p_o"]).transpose(0, 2, 1).reshape(NL, 512)
        lru_s[:, sl] = np.asarray(r["lru_s_o"]).transpose(0, 3, 2, 1).reshape(NL, NSEQ, 512)
        lru_conv_p[:, c] = np.asarray(r["lconv_p_o"]).transpose(0, 3, 2, 1).reshape(NL, 3, 512)
        lru_conv_s[:, sl] = np.asarray(r["lconv_s_o"]).transpose(0, 3, 4, 2, 1).reshape(NL, NSEQ, 3, 512)
        s5_re_p[:, c] = np.asarray(r["s5r_p_o"]).transpose(0, 2, 1).reshape(NL, 32, 64)
        s5_im_p[:, c] = np.asarray(r["s5i_p_o"]).transpose(0, 2, 1).reshape(NL, 32, 64)
        s5_re_s[:, sl] = np.asarray(r["s5r_s_o"]).transpose(0, 3, 2, 1).reshape(NL, NSEQ, 32, 64)
        s5_im_s[:, sl] = np.asarray(r["s5i_s_o"]).transpose(0, 3, 2, 1).reshape(NL, NSEQ, 32, 64)
    return (y_prompt, y_sample, ssd_p, ssd_s, ssd_conv_p, ssd_conv_s, lru_p, lru_s,
            lru_conv_p, lru_conv_s, s5_re_p, s5_re_s, s5_im_p, s5_im_s)
```

```python
import math
import os
from contextlib import ExitStack

import numpy as np
import concourse.bass as bass
import concourse.mybir as mybir
from concourse.bass_utils import run_bass_kernel_spmd

F32 = mybir.dt.float32
BF16 = mybir.dt.bfloat16
I32 = mybir.dt.int32
ALU = mybir.AluOpType
AF = mybir.ActivationFunctionType

NCORES = 8
D = 1024
NL = 4
TP = 2048
NSEQ = 16
LS = 8
TS = NSEQ * LS
TT = TP + TS
WT = 512
IN_DIM = 5136
EPS = 1e-6
TWO_PI = 2.0 * math.pi

PP_NG = 0
PP_NG2 = 8
PP_SD = 16
PP_CW = 24
PP_CB = 88
PP_LCW = 104
PP_LCB = 120
PP_LBA = 124
PP_LBX = 128
PP_LLAM = 132
PP_S5D = 136
PP_GLB = 140
PP_LRE = 144
PP_LIM = 160
PP_LDT = 176
PP_FG = 192
NPP = 200

EPOCH = 12000


class Ring:
    def __init__(self, bufs, full=None):
        self.bufs = bufs
        self.full = full
        self.i = 0

    def get(self):
        b = self.bufs[self.i % len(self.bufs)]
        self.i += 1
        return b

    def get_full(self):
        b = self.full[self.i % len(self.full)]
        self.i += 1
        return b


class KB:
    ENG = ("pe", "act", "dve", "pool", "sp")

    def __init__(self, nc, es):
        self.nc = nc
        self.es = es
        self.prog = {e: [] for e in self.ENG}
        self.cnt = {e: 0 for e in self.ENG}
        self.sem = {}
        self.nsem = 0
        for e in ("pe", "act", "dve", "pool"):
            self.sem[e] = self._newsem("c_" + e)
        self.waited = {e: {} for e in self.ENG}
        self.dead = False
        self.pending = []
        self.ncp = 0
        self.stop = int(os.environ.get("KSTOP", "-1"))
        self.lastw = {}
        self.readers = {}
        self.dsem = {}
        self.drr = {}
        for q, n in (("sp", 16), ("pool", 24), ("act", 2)):
            self.dsem[q] = [[self._newsem("d_%s%d" % (q, i)), 0] for i in range(n)]
            self.drr[q] = 0

    def _newsem(self, name):
        self.nsem += 1
        return self.es.enter_context(self.nc.semaphore("%s_%d" % (name, self.nsem)))

    slotw = {}

    def _keys(self, r):
        if isinstance(r, str):
            return [r]
        name = r.name
        w = self.slotw.get(name)
        if w is None:
            return [name]
        ap = r.ap
        off = r.offset % ap[0][0]
        hi = off + sum((c - 1) * s for s, c in ap[1:])
        return ["%s:%d" % (name, i) for i in range(off // w, hi // w + 1)]

    def defer(self, fn, depth=1):
        self.pending.append(fn)
        while len(self.pending) > depth:
            self.pending.pop(0)()

    def flush(self):
        while self.pending:
            self.pending.pop(0)()

    def cp(self, name=""):
        self.ncp += 1
        if self.stop >= 0 and self.ncp > self.stop and not self.dead:
            self.dead = True
            print("KSTOP: program truncated before checkpoint", self.ncp, name, flush=True)

    def op(self, e, fn, reads=(), writes=(), dma=False):
        if self.dead:
            return None
        waits = {}

        def need(tok, raw):
            if tok is None:
                return
            sem, val, src, isdma = tok
            if src == e and not isdma and e == "pe":
                return
            if self.waited[e].get(sem.name, 0) >= val:
                return
            if sem.name not in waits or waits[sem.name][1] < val:
                waits[sem.name] = (sem, val)

        rk = [x for r in reads for x in self._keys(r)]
        wk = [x for w in writes for x in self._keys(w)]
        for r in rk:
            need(self.lastw.get(r), True)
        for w in wk:
            need(self.lastw.get(w), False)
            for t in self.readers.get(w, {}).values():
                need(t, False)
        if dma:
            slot = self.dsem[e][self.drr[e] % len(self.dsem[e])]
            self.drr[e] += 1
            if slot[1] > 0:
                need((slot[0], slot[1], e, True), True)
            slot[1] += 16
            tok = (slot[0], slot[1], e, True)
            inc = (slot[0], 16)
        else:
            if self.cnt[e] >= EPOCH:
                self.sem[e] = self._newsem("c_" + e)
                self.cnt[e] = 0
            self.cnt[e] += 1
            tok = (self.sem[e], self.cnt[e], e, False)
            inc = (self.sem[e], 1)
        for s, v in waits.values():
            self.waited[e][s.name] = v
        self.prog[e].append((list(waits.values()), fn, inc))
        for r in rk:
            self.readers.setdefault(r, {})[tok[0].name] = tok
        for w in wk:
            self.lastw[w] = tok
            self.readers[w] = {}
        return tok

    def finish(self):
        fin = []
        for q in self.dsem:
            for sem, v in self.dsem[q]:
                if v > 0:
                    fin.append((sem, v))
        self.final_waits = fin

    def emit(self):
        nc = self.nc
        handles = {"pe": "tensor", "act": "scalar", "dve": "vector", "pool": "gpsimd", "sp": "sync"}
        with nc.Block() as block:
            for e in self.ENG:
                prog = self.prog[e]
                extra = self.final_waits if e == "sp" else []

                def body(eng, prog=prog, extra=extra):
                    for waits, fn, inc in prog:
                        for s, v in waits:
                            eng.wait_ge(s, v)
                        ins = fn(eng)
                        ins.then_inc(inc[0], inc[1])
                    for s, v in extra:
                        eng.wait_ge(s, v)

                getattr(block, handles[e])(body)

    def mm(self, out, lhsT, rhs, start=True, stop=True):
        self.op("pe", lambda t: t.matmul(out, lhsT=lhsT, rhs=rhs, start=start, stop=stop),
                [lhsT, rhs], [out])

    def tr(self, out, in_, ident):
        self.op("pe", lambda t: t.transpose(out, in_, ident), [in_, ident], [out])

    def act(self, out, in_, func, bias=None, scale=None):
        rd = [in_]
        kw = {}
        if bias is not None:
            kw["bias"] = bias
            if not isinstance(bias, (int, float)):
                rd.append(bias)
        if scale is not None:
            kw["scale"] = scale
            if not isinstance(scale, (int, float)):
                rd.append(scale)
        self.op("act", lambda a: a.activation(out=out, in_=in_, func=func, **kw), rd, [out])

    def tt(self, e, out, in0, in1, op):
        self.op(e, lambda v: v.tensor_tensor(out=out, in0=in0, in1=in1, op=op), [in0, in1], [out])

    def ts(self, e, out, in0, s1, s2, op0, op1=None):
        rd = [in0]
        for s in (s1, s2):
            if s is not None and not isinstance(s, (int, float)):
                rd.append(s)
        if op1 is None:
            self.op(e, lambda v: v.tensor_scalar(out=out, in0=in0, scalar1=s1, scalar2=None, op0=op0),
                    rd, [out])
        else:
            self.op(e, lambda v: v.tensor_scalar(out=out, in0=in0, scalar1=s1, scalar2=s2, op0=op0, op1=op1),
                    rd, [out])

    def stt(self, out, in0, scalar, in1, op0, op1):
        rd = [in0, in1]
        if not isinstance(scalar, (int, float)):
            rd.append(scalar)
        self.op("dve", lambda v: v.scalar_tensor_tensor(out=out, in0=in0, scalar=scalar, in1=in1,
                                                        op0=op0, op1=op1), rd, [out])

    def scan(self, out, d0, d1, init):
        rd = [d0, d1]
        if not isinstance(init, (int, float)):
            rd.append(init)
        self.op("dve", lambda v: v.tensor_tensor_scan(out=out, data0=d0, data1=d1, initial=init,
                                                      op0=ALU.mult, op1=ALU.add), rd, [out])

    def copy(self, e, out, in_):
        if e == "act":
            self.op("act", lambda a: a.activation(out=out, in_=in_, func=AF.Copy), [in_], [out])
        else:
            self.op(e, lambda v: v.tensor_copy(out=out, in_=in_), [in_], [out])

    def memset(self, e, out, val):
        self.op(e, lambda v: v.memset(out, val), [], [out])

    def dma(self, q, out, in_, rk=(), wk=()):
        self.op(q, lambda g: g.dma_start(out=out, in_=in_), [in_] + list(rk), [out] + list(wk), dma=True)


def build_program(nlayers=NL):
    nc = bass.Bass("TRN2", target_bir_lowering=False)

    def din(name, shape, dt=F32):
        return nc.dram_tensor(name, list(shape), dt, kind="ExternalInput").ap()

    def dout(name, shape, dt=F32):
        return nc.dram_tensor(name, list(shape), dt, kind="ExternalOutput").ap()

    xin = din("xin", [8, 128, TT])
    w_in = din("w_in_r", [NL, 128, 8, IN_DIM])
    w_out = din("w_out_r", [NL, 128, 16, D])
    glu_w = din("glu_r", [NL, 128, 4, 512])
    wa_bd = din("wa_bd", [NL, 128, 4, 128])
    wx_bd = din("wx_bd", [NL, 128, 4, 128])
    pp_d = din("pp_in", [128, NL, NPP])
    pdt_d = din("pdt_in", [128, NL, 32])
    bpad_re = din("bpadT_re", [NL, 128, 16, 128])
    bpad_im = din("bpadT_im", [NL, 128, 16, 128])
    ctpad_re = din("ctpad_re", [NL, 128, 16, 128])
    ctpad_im = din("ctpad_im", [NL, 128, 16, 128])
    h0T_d = din("h0T", [NL, NSEQ, 128, 1024])
    sconv0 = din("sconv0", [NL, 128, 16, NSEQ, 3])
    lconv0 = din("lconv0", [NL, 128, 4, NSEQ, 3])
    lru0 = din("lru0", [NL, 128, 4, NSEQ])
    s5r0 = din("s5r0", [NL, 128, 16, NSEQ])
    s5i0 = din("s5i0", [NL, 128, 16, NSEQ])
    c_f32 = din("c_f32", [128, 8, 128])
    c_bf = din("c_bf", [128, 4, 128])
    c_iota = din("c_iota", [128, WT])

    yout = dout("yout", [8, 128, TT])
    ssd_p_o = dout("ssd_p_o", [NL, 128, 1024])
    ssd_s_o = dout("ssd_s_o", [NL, NSEQ, 128, 1024])
    sconv_p_o = dout("sconv_p_o", [NL, 128, 16, 3])
    sconv_s_o = dout("sconv_s_o", [NL, 128, 16, NSEQ, 3])
    lru_p_o = dout("lru_p_o", [NL, 128, 4])
    lru_s_o = dout("lru_s_o", [NL, 128, 4, NSEQ])
    lconv_p_o = dout("lconv_p_o", [NL, 128, 4, 3])
    lconv_s_o = dout("lconv_s_o", [NL, 128, 4, NSEQ, 3])
    s5r_p_o = dout("s5r_p_o", [NL, 128, 16])
    s5r_s_o = dout("s5r_s_o", [NL, 128, 16, NSEQ])
    s5i_p_o = dout("s5i_p_o", [NL, 128, 16])
    s5i_s_o = dout("s5i_s_o", [NL, 128, 16, NSEQ])
    xsc = nc.dram_tensor("xsc", [2, 8, 128, TT], F32, kind="Internal").ap()
    wbf = nc.dram_tensor("wbf", [NL, 128, 8, IN_DIM], BF16, kind="Internal").ap()
    wobf = nc.dram_tensor("wobf", [NL, 8, 128, 16, 128], BF16, kind="Internal").ap()

    with ExitStack() as es:
        k = KB(nc, es)
        k.slotw = {"F8": 512, "zs": 512, "A16": 512, "Y": 512, "xt": 512, "hn": 512, "LT": 128,
                   "hnew": 512, "h0f": 512}

        def sb(name, shape, dt):
            return es.enter_context(nc.sbuf_tensor(name, list(shape), dt))

        def ps(name, shape, dt):
            return es.enter_context(nc.psum_tensor(name, list(shape), dt))

        xt = sb("xt", [128, 8, WT], F32)
        hn = sb("hn", [128, 8, WT], BF16)
        zs = sb("zs", [128, 8, WT], BF16)
        A16 = sb("A16", [128, 16, WT], BF16)
        F8 = sb("F8", [128, 8, WT], F32)
        Y = sb("Y", [128, 16, WT], BF16)
        LT = sb("LT", [128, 16, 128], BF16)
        x_tm = sb("x_tm", [128, 1024], BF16)
        xw_tm = sb("xw_tm", [128, 1024], BF16)
        B_tm = sb("B_tm", [128, 512], BF16)
        bmring = Ring([sb("bm%d" % i, [128, 512], BF16) for i in range(1)])
        hT = sb("hT", [128, 1024], F32)
        hT_bf = sb("hT_bf", [128, 1024], BF16)
        dt_tm = sb("dt_tm", [128, 4, 16], F32)
        dtA_tm = sb("dtA_tm", [128, 4, 16], F32)
        pre = sb("pre", [128, 7, 64], F32)
        wring = Ring([sb("wbuf%d" % i, [128, 8, 512], BF16) for i in range(2)])
        woring = Ring([sb("wobuf%d" % i, [128, 16, 128], BF16) for i in range(2)])
        glu_bf = sb("glu_bf", [128, 4, 512], BF16)
        BbT_re = sb("BbT_re", [128, 16, 128], BF16)
        BbT_im = sb("BbT_im", [128, 16, 128], BF16)
        CT_re = sb("CT_re", [128, 16, 128], BF16)
        nCT_re = sb("nCT_re", [128, 16, 128], BF16)
        nCT_im = sb("nCT_im", [128, 16, 128], BF16)
        wa_bf = sb("wa_bf", [128, 4, 128], BF16)
        wx_bf = sb("wx_bf", [128, 4, 128], BF16)
        pp = sb("pp", [128, NL, NPP], F32)
        pdt = sb("pdt", [128, NL, 32], F32)
        expA = sb("expA", [128, 16], F32)
        lp = sb("lp", [128, 8, 16], F32)
        cf = sb("cf", [128, 8, 128], F32)
        cb = sb("cb", [128, 4, 128], BF16)
        iota = sb("iota", [128, WT], F32)
        iota_c = sb("iota_c", [128, WT], F32)
        carry_s = sb("carry_s", [128, 16, NSEQ, 3], F32)
        carry_l = sb("carry_l", [128, 4, NSEQ, 3], F32)
        pcar_s = sb("pcar_s", [128, 16, 3], F32)
        pcar_l = sb("pcar_l", [128, 4, 3], F32)
        pcar_h = sb("pcar_h", [128, 4], F32)
        hcar = sb("hcar", [128, 4, NSEQ], F32)
        s5car_r = sb("s5car_r", [128, 16], F32)
        s5car_i = sb("s5car_i", [128, 16], F32)
        s5p_r = sb("s5p_r", [128, 16], F32)
        s5p_i = sb("s5p_i", [128, 16], F32)
        s5o_r = sb("s5o_r", [128, 16, NSEQ], F32)
        s5o_i = sb("s5o_i", [128, 16, NSEQ], F32)
        s5h0_r = sb("s5h0_r", [128, 16, NSEQ], F32)
        s5h0_i = sb("s5h0_i", [128, 16, NSEQ], F32)
        lruh0 = sb("lruh0", [128, 4, NSEQ], F32)
        h0f = sb("h0f", [128, 1024], F32)
        h0bring = Ring([sb("h0b%d" % i, [128, 1024], BF16) for i in range(2)])
        hnew = sb("hnew", [128, 1024], F32)
        dtA_rep = h0f
        _fr = [sb("fr%d" % i, [128, WT + 4], F32) for i in range(8)]
        fring = Ring([t_[:, 0:WT] for t_ in _fr], [t_[:, :] for t_ in _fr])
        iring = Ring([sb("ir%d" % i, [128, WT], I32) for i in range(2)])
        fring_x = [h0f[:, 0:512], h0f[:, 512:1024]]
        sring2 = [hnew[:, 0:512], hnew[:, 512:1024]]
        sring = Ring([sb("sr%d" % i, [128, 128], F32) for i in range(4)])
        tring = Ring([sb("tn%d" % i, [128, 48], F32) for i in range(10)])
        dhl = sb("dhl", [128, 2, 64], BF16)
        f8ring = Ring([F8[:, i, :] for i in range(8)])
        zring = Ring([zs[:, i, :] for i in range(8)])

        pheld = ps("pheld", [128, 512], F32)
        pheld2 = ps("pheld2", [128, 512], F32)
        pheld3 = ps("pheld3", [128, 512], F32)
        pring = Ring([ps("pb%d" % i, [128, 512], F32) for i in range(4)])
        ptb = ps("ptb", [128, 1024], BF16)

        tri_b2 = cf[:, 0, :].bitcast(BF16)
        tri_p = cf[:, 1, :]
        tri_s = cf[:, 2, :]
        ones_f = cf[:, 3, :]
        ones_s = cf[:, 4, :]
        iota_s = cf[:, 5, :]
        notstart = cf[:, 6, :]
        ind = cf[:, 7, :]
        ident_b = cb[:, 0, :]
        ones_b = cb[:, 1, :]
        neg_p = cb[:, 2, :]
        neg_s = cb[:, 3, :]

        k.dma("sp", cf[:], c_f32)
        nident_b = sb("nident_b", [128, 128], BF16)
        k.dma("pool", cb[:], c_bf)
        k.dma("sp", iota[:], c_iota)
        k.dma("sp", pp[:], pp_d)
        k.dma("sp", pdt[:], pdt_d)
        k.act(nident_b[:], ident_b, AF.Copy, scale=-1.0)

        tiles = [(i * WT, WT, "p") for i in range(TP // WT)] + [(TP, TS, "s")]
        blocks = [("z", 0, 512), ("z", 512, 512), ("x", 1024, 512), ("x", 1536, 512), ("B", 2048, 512),
                  ("C", 2560, 512), ("dt", 3072, 16), ("lg", 3600, 512), ("lx", 3088, 512),
                  ("sg", 4624, 512), ("su", 4112, 512)]
        stream = [(l, ti, bi) for l in range(nlayers) for ti in range(len(tiles)) for bi in range(len(blocks))]
        wbufs = {}
        st = {"next": 0, "item": 0}

        def prefetch_w(upto):
            while st["next"] < len(stream) and st["next"] <= upto:
                l_, ti_, bi_ = stream[st["next"]]
                _, c0_, n_ = blocks[bi_]
                buf = wring.get()
                k.dma("sp", buf[:, :, 0:n_], wbf[l_][:, :, c0_:c0_ + n_], rk=["wbf%d_%d" % (l_, bi_)])
                wbufs[st["next"]] = buf
                st["next"] += 1

        def next_block():
            prefetch_w(st["item"] + 1)
            wb = wbufs.pop(st["item"])
            st["item"] += 1
            return wb

        pringA = Ring(pring.bufs + [pheld, pheld2, pheld3])

        def proj(wb, m, W):
            pm = pringA.get()
            for kt in range(8):
                k.mm(pm[:, 0:W], wb[:, kt, m * 128:(m + 1) * 128], hn[:, kt, 0:W], start=(kt == 0), stop=(kt == 7))
            return pm

        def rmsnorm_tile(src3, ncol, gcol0, l, out_fn):
            pn = pring.get()
            for kt in range(8):
                sq = zring.get()
                k.act(sq[:, 0:ncol], src3[:, kt, 0:ncol], AF.Square)
                k.mm(pn[:, 0:ncol], ones_b, sq[:, 0:ncol], start=(kt == 0), stop=(kt == 7))
            t1 = fring.get()
            k.act(t1[:, 0:ncol], pn[:, 0:ncol], AF.Ln, bias=epsc[:, 0:1], scale=1.0 / D)
            rstd = fring.get()
            k.act(rstd[:, 0:ncol], t1[:, 0:ncol], AF.Exp, scale=-0.5)
            for kt in range(8):
                k.stt(out_fn(kt), src3[:, kt, 0:ncol], pp[:, l, gcol0 + kt:gcol0 + kt + 1], rstd[:, 0:ncol],
                      ALU.mult, ALU.mult)

        def frac_sincos(u, W, sn_out, cs_out):
            ui = iring.get()
            k.copy("dve", ui[:, 0:W], u)
            r = fring.get()
            k.tt("dve", r[:, 0:W], u, ui[:, 0:W], ALU.subtract)
            k.act(sn_out, r[:, 0:W], AF.Sin, scale=TWO_PI)
            ar = fring.get()
            k.stt(ar[:, 0:W], r[:, 0:W], -1.0, r[:, 0:W], ALU.mult, ALU.max)
            k.act(cs_out, ar[:, 0:W], AF.Sin, bias=epsc[:, 1:2], scale=-TWO_PI)

        epsc = sb("epsc", [128, 4], F32)
        k.memset("dve", epsc[:, 0:1], EPS)
        k.memset("dve", epsc[:, 1:2], math.pi / 2.0)
        k.memset("dve", epsc[:, 2:3], 1.0)

        def load_x(l_, ti_):
            c0_, W_, _ = tiles[ti_]
            if l_ == 0:
                k.dma("sp", F8[:, :, 0:W_], xin.rearrange("k p t -> p k t")[:, :, c0_:c0_ + W_])
            else:
                k.dma("sp", F8[:, :, 0:W_], xsc[(l_ - 1) % 2].rearrange("k p t -> p k t")[:, :, c0_:c0_ + W_],
                      rk=["xsc%d_%d" % ((l_ - 1) % 2, ti_)])

        def conv_group(l, pms, W, smp, specs):
            n = len(pms)
            accs = [fring.get() for _ in range(n)]
            raws = [fring.get_full() for _ in range(n)]
            if smp:
                rawv = [r_[:, 0:NSEQ * (LS + 3)].rearrange("p (s t) -> p s t", s=NSEQ) for r_ in raws]
                pmv = [p_[:, 0:W].rearrange("p (s t) -> p s t", s=NSEQ) for p_ in pms]
                accv = [a_[:, 0:W].rearrange("p (s t) -> p s t", s=NSEQ) for a_ in accs]
                for i in range(n):
                    k.copy("dve", rawv[i][:, :, 0:3], specs[i][3])
                for i in range(n):
                    k.copy("act", rawv[i][:, :, 3:3 + LS], pmv[i])
                    k.act(accv[i], pmv[i], AF.Identity, bias=pp[:, l, specs[i][1]:specs[i][1] + 1],
                          scale=pp[:, l, specs[i][0] + 3:specs[i][0] + 4])
                for i in range(n):
                    st3 = tring.get()
                    st3v = st3[:, 0:48].rearrange("p (s t) -> p s t", s=NSEQ)
                    k.copy("dve", st3v, rawv[i][:, :, LS:LS + 3])
                    k.dma("sp", specs[i][4], st3v)
                for kk in range(3):
                    for i in range(n):
                        k.stt(accv[i], rawv[i][:, :, kk:kk + LS], pp[:, l, specs[i][0] + kk:specs[i][0] + kk + 1],
                              accv[i], ALU.mult, ALU.add)
            else:
                for i in range(n):
                    k.copy("dve", raws[i][:, 0:3], specs[i][2])
                for i in range(n):
                    k.copy("act", raws[i][:, 3:3 + W], pms[i][:, 0:W])
                    k.act(accs[i][:, 0:W], pms[i][:, 0:W], AF.Identity, bias=pp[:, l, specs[i][1]:specs[i][1] + 1],
                          scale=pp[:, l, specs[i][0] + 3:specs[i][0] + 4])
                for kk in range(3):
                    for i in range(n):
                        k.stt(accs[i][:, 0:W], raws[i][:, kk:kk + W], pp[:, l, specs[i][0] + kk:specs[i][0] + kk + 1],
                              accs[i][:, 0:W], ALU.mult, ALU.add)
                for i in range(n):
                    k.copy("dve", specs[i][2], raws[i][:, W:W + 3])
            return accs

        def ssd_stage(l, ti, W, smp):
            xc = A16
            nch = W // 128
            TRI = tri_s if smp else tri_p
            ONESM = ones_s if smp else ones_f
            NEG = neg_s if smp else neg_p
            TRIB = tri_b2[:, 128:256] if smp else tri_b2[:, 0:128]
            for ci in range(nch):
                cs = slice(ci * 128, (ci + 1) * 128)
                dtA = dtA_tm[:, ci, :]
                dtc = dt_tm[:, ci, :]
                nacum = pre[:, 1, ci * 16:(ci + 1) * 16]
                w_tm = pre[:, 3, ci * 16:(ci + 1) * 16]
                DEC = pre[:, 4, ci * 16:(ci + 1) * 16]
                k.cp("ssd A acum")
                psc = pheld
                for g in range(4):
                    k.mm(psc[:, g * 128:(g + 1) * 128], xc[:, 8 + g, cs], xc[:, 12 + g, cs])
                k.cp("ssd B scores")
                for j in range(8):
                    k.tr(ptb[:, j * 128:(j + 1) * 128], xc[:, j, cs], ident_b)
                k.cp("T1 xtr")
                k.copy("act", x_tm[:], ptb[:, :])
                k.cp("T2 xcopy")
                k.tt("dve", hnew[:].rearrange("p (h q) -> p h q", h=16), x_tm[:].rearrange("p (h q) -> p h q", h=16),
                     w_tm[:, 0:16].unsqueeze(2).to_broadcast([128, 16, 64]), ALU.mult)
                k.copy("act", xw_tm[:], hnew[:])
                k.cp("T3 xw")
                for g in range(4):
                    k.tr(ptb[:, g * 128:(g + 1) * 128], xc[:, 8 + g, cs], ident_b)
                k.cp("T4 btr")
                k.copy("act", B_tm[:], ptb[:, 0:512])
                k.cp("ssd C transposes")
                for g in range(4):
                    pab = pring.get()
                    for r in range(4):
                        h = 4 * g + r
                        cc = ci * 16 + h
                        k.mm(pab[:, r * 128:(r + 1) * 128], dhl[:, 0, cc:cc + 1].to_broadcast([128, 128]), TRIB,
                             start=True, stop=False)
                        k.mm(pab[:, r * 128:(r + 1) * 128], dhl[:, 1, cc:cc + 1].to_broadcast([128, 128]), TRIB,
                             start=False, stop=False)
                        k.mm(pab[:, r * 128:(r + 1) * 128], ident_b, NEG, start=False, stop=True)
                    for r in range(4):
                        h = 4 * g + r
                        Dh = sring.get()
                        k.act(Dh[:], pab[:, r * 128:(r + 1) * 128], AF.Exp, bias=nacum[:, h:h + 1])
                        k.stt(LT[:, h, :], Dh[:], dt_tm[:, ci, h:h + 1], psc[:, g * 128:(g + 1) * 128],
                              ALU.mult, ALU.mult)
                k.cp("ssd D LT")
                if smp:
                    EAs = [fring.get(), fring.get()]
                    k.copy("dve", dtA_rep[:].rearrange("p (h q) -> p h q", h=16),
                           dtA_tm[:, ci, :].unsqueeze(2).to_broadcast([128, 16, 64]))
                    for j in range(8):
                        pe_ = pring.get()
                        k.mm(pe_[:, 0:128], dtA_rep[:, j * 128:(j + 1) * 128], TRI)
                        k.act(EAs[j // 4][:, (j % 4) * 128:(j % 4 + 1) * 128], pe_[:, 0:128], AF.Exp)
                    pdS = pring.get()
                    for b_ in range(NSEQ):
                        k.mm(pdS[:, b_ * 16:(b_ + 1) * 16], ind[:, b_:b_ + 1].to_broadcast([128, 128]), dtA)
                    DECS = fring.get()
                    k.act(DECS[:, 0:256], pdS[:, 0:256], AF.Exp)
                    hx = Ring([h0f, hnew])
                    bufs = [hx.get()]
                    k.dma("sp", bufs[0][:], h0T_d[l, 0])
                    for b_ in range(NSEQ):
                        hb = bufs[b_]
                        if b_ + 1 < NSEQ:
                            nb_ = hx.get()
                            bufs.append(nb_)
                            k.dma("sp", nb_[:], h0T_d[l, b_ + 1])
                        h0b = h0bring.get()
                        k.copy("act", h0b[:], hb[:])
                        for j in range(8):
                            pyo = pheld2 if j < 4 else pheld3
                            jj = j % 4
                            k.mm(pyo[:, jj * 128 + b_ * 8: jj * 128 + b_ * 8 + 8], h0b[:, j * 128:(j + 1) * 128],
                                 xc[:, 12 + j // 2, b_ * 8:(b_ + 1) * 8])
                        Bm = bmring.get()
                        k.ts("dve", Bm[:], B_tm[:], ind[:, b_:b_ + 1], None, ALU.mult)
                        pS0 = pring.get()
                        pS1 = pring.get()
                        for g in range(4):
                            pS = pS0 if g < 2 else pS1
                            k.mm(pS[:, (g % 2) * 256:(g % 2 + 1) * 256], Bm[:, g * 128:(g + 1) * 128],
                                 xw_tm[:, g * 256:(g + 1) * 256])
                        hb3 = hb[:].rearrange("p (h q) -> p h q", h=16)
                        k.tt("dve", hb3, hb3, DECS[:, b_ * 16:(b_ + 1) * 16].unsqueeze(2).to_broadcast([128, 16, 64]),
                             ALU.mult)
                        k.tt("dve", hb[:, 0:512], hb[:, 0:512], pS0[:, :], ALU.add)
                        k.tt("dve", hb[:, 512:1024], hb[:, 512:1024], pS1[:, :], ALU.add)
                        k.dma("sp", ssd_s_o[l, b_], hb[:])
                    yo0 = fring.get()
                    yo1 = fring.get()
                    k.copy("act", yo0[:], pheld2[:, :])
                    k.copy("act", yo1[:], pheld3[:, :])
                if not smp:
                    k.copy("dve", dtA_rep[:].rearrange("p (h q) -> p h q", h=16),
                           dtA_tm[:, ci, :].unsqueeze(2).to_broadcast([128, 16, 64]))
                for j in range(8):
                    py = pring.get()
                    if not smp:
                        k.mm(py[:, 256:384], dtA_rep[:, j * 128:(j + 1) * 128], TRI)
                    k.mm(py[0:64, 0:128], x_tm[:, (2 * j) * 64:(2 * j + 1) * 64], LT[:, 2 * j, :])
                    k.mm(py[64:128, 0:128], x_tm[:, (2 * j + 1) * 64:(2 * j + 2) * 64], LT[:, 2 * j + 1, :])
                    tmp = sring.get()
                    if smp:
                        yo = (yo0 if j < 4 else yo1)[:, (j % 4) * 128:(j % 4 + 1) * 128]
                        k.tt("dve", tmp[:], yo, EAs[j // 4][:, (j % 4) * 128:(j % 4 + 1) * 128], ALU.mult)
                    else:
                        k.mm(py[:, 128:256], hT_bf[:, j * 128:(j + 1) * 128], xc[:, 12 + j // 2, cs])
                        EA = sring.get()
                        k.act(EA[:], py[:, 256:384], AF.Exp)
                        k.tt("dve", tmp[:], py[:, 128:256], EA[:], ALU.mult)
                    k.defer(lambda j=j, py=py, tmp=tmp, cs=cs: k.tt("dve", F8[:, j, cs], py[:, 0:128], tmp[:], ALU.add),
                            depth=1)
                k.flush()
                k.cp("ssd E y")
                if not smp:
                    for g in range(4):
                        pS = pheld2 if g < 2 else pheld3
                        k.mm(pS[:, (g % 2) * 256:(g % 2 + 1) * 256], B_tm[:, g * 128:(g + 1) * 128],
                             xw_tm[:, g * 256:(g + 1) * 256])
                    hT3 = hT[:].rearrange("p (h q) -> p h q", h=16)
                    k.tt("dve", hT3, hT3, DEC[:, 0:16].unsqueeze(2).to_broadcast([128, 16, 64]), ALU.mult)
                    k.tt("dve", hT[:, 0:512], hT[:, 0:512], pheld2[:, :], ALU.add)
                    k.tt("dve", hT[:, 512:1024], hT[:, 512:1024], pheld3[:, :], ALU.add)
                    k.copy("act", hT_bf[:], hT[:])
            k.cp("ssd F chunks done")
            pn = pring.get()
            for j in range(8):
                k.stt(F8[:, j, 0:W], xc[:, j, 0:W], pp[:, l, PP_SD + j:PP_SD + j + 1], F8[:, j, 0:W], ALU.mult, ALU.add)
            for j in range(8):
                k.tt("dve", F8[:, j, 0:W], F8[:, j, 0:W], zs[:, j, 0:W], ALU.mult)
            for j in range(8):
                sq = fring.get()
                sqb = sq[:, 0:WT // 2].bitcast(BF16)
                k.act(sqb[:, 0:W], F8[:, j, 0:W], AF.Square)
                k.mm(pn[:, 0:W], ones_b, sqb[:, 0:W], start=(j == 0), stop=(j == 7))
            t1 = fring.get()
            k.act(t1[:, 0:W], pn[:, 0:W], AF.Ln, bias=epsc[:, 0:1], scale=1.0 / D)
            rstd = fring.get()
            k.act(rstd[:, 0:W], t1[:, 0:W], AF.Exp, scale=-0.5)
            for j in range(8):
                k.stt(Y[:, j, 0:W], F8[:, j, 0:W], pp[:, l, PP_NG2 + j:PP_NG2 + j + 1], rstd[:, 0:W], ALU.mult, ALU.mult)
            if (not smp) and ti == len(tiles) - 2:
                k.dma("sp", ssd_p_o[l], hT[:])

        def lru_group(l, js, accs, W, smp):
            n = len(js)
            xrs, prs, pgs, rs, gis, as_ = [], [], [], [], [], []
            for i in range(n):
                xr_bf = zring.get()
                k.copy("act", xr_bf[:, 0:W], accs[i][:, 0:W])
                xrs.append(xr_bf)
            for i in range(n):
                pr = pring.get()
                k.mm(pr[:, 0:W], wa_bf[:, js[i], :], xrs[i][:, 0:W])
                pg = pring.get()
                k.mm(pg[:, 0:W], wx_bf[:, js[i], :], xrs[i][:, 0:W])
                prs.append(pr)
                pgs.append(pg)
            for i in range(n):
                j = js[i]
                r = fring.get()
                k.act(r[:, 0:W], prs[i][:, 0:W], AF.Sigmoid, bias=pp[:, l, PP_LBA + j:PP_LBA + j + 1])
                gi = fring.get()
                k.act(gi[:, 0:W], pgs[i][:, 0:W], AF.Sigmoid, bias=pp[:, l, PP_LBX + j:PP_LBX + j + 1])
                rs.append(r)
                gis.append(gi)
            for i in range(n):
                a = f8ring.get()
                k.act(a[:, 0:W], rs[i][:, 0:W], AF.Exp, scale=lp[:, 0, js[i]:js[i] + 1])
                as_.append(a)
            for i in range(n):
                k.tt("dve", rs[i][:, 0:W], as_[i][:, 0:W], as_[i][:, 0:W], ALU.mult)
            for i in range(n):
                k.ts("dve", rs[i][:, 0:W], rs[i][:, 0:W], 1.0, -1.0, ALU.min, ALU.mult)
            for i in range(n):
                k.act(rs[i][:, 0:W], rs[i][:, 0:W], AF.Sqrt, bias=epsc[:, 2:3])
            for i in range(n):
                k.tt("dve", gis[i][:, 0:W], gis[i][:, 0:W], rs[i][:, 0:W], ALU.mult)
            for i in range(n):
                k.tt("dve", gis[i][:, 0:W], gis[i][:, 0:W], accs[i][:, 0:W], ALU.mult)
            hss = []
            for i in range(n):
                j = js[i]
                a = as_[i]
                gi = gis[i]
                hs = f8ring.get()
                if smp:
                    am = f8ring.get()
                    k.tt("dve", am[:, 0:W], a[:, 0:W], notstart, ALU.mult)
                    t = tring.get()
                    k.tt("dve", t[:, 0:16], a[:, 0:W:LS], lruh0[:, j, :], ALU.mult)
                    k.tt("dve", gi[:, 0:W:LS], gi[:, 0:W:LS], t[:, 0:16], ALU.add)
                    k.scan(hs[:, 0:W], am[:, 0:W], gi[:, 0:W], 0.0)
                else:
                    k.scan(hs[:, 0:W], a[:, 0:W], gi[:, 0:W], pcar_h[:, j:j + 1])
                hss.append(hs)
            for i in range(n):
                j = js[i]
                if smp:
                    k.copy("dve", hcar[:, j, :], hss[i][:, LS - 1:W:LS])
                else:
                    k.copy("dve", pcar_h[:, j:j + 1], hss[i][:, W - 1:W])
                k.tt("dve", Y[:, 8 + j, 0:W], hss[i][:, 0:W], A16[:, j, 0:W], ALU.mult)

        def s5_stage(l, ti, c0, W, smp):
            last_p = (not smp) and ti == len(tiles) - 2
            if not smp:
                k.ts("dve", iota_c[:, 0:W], iota[:, 0:W], float(c0), None, ALU.add)
            tsrc = iota_s if smp else iota_c[:, 0:W]
            tabring = Ring([A16[:, 0, :], A16[:, 1, :], A16[:, 2, :], A16[:, 3, :], Y[:, 14, :], Y[:, 15, :]])
            srring = Ring([F8[:, 0, :], F8[:, 2, :]])
            siring = Ring([F8[:, 1, :], F8[:, 3, :]])
            tprod = [zs[:, i, :] for i in range(4)]
            mprod = [zs[:, 4 + i, :] for i in range(4)]
            Srb = Y[:, 12, :]
            Sib = Y[:, 13, :]
            pGr = pheld2
            pGi = pheld3

            def tables(pr_):
                thc = lp[:, 1, pr_:pr_ + 1]
                u = fring.get()
                ui = iring.get()
                k.ts("dve", ui[:, 0:W], tsrc, thc, None, ALU.mult)
                k.stt(u[:, 0:W], tsrc, thc, ui[:, 0:W], ALU.mult, ALU.subtract)
                uf = fring.get()
                sn = tabring.get()
                cs = tabring.get()
                k.act(sn[:, 0:W], u[:, 0:W], AF.Sin, scale=TWO_PI)
                k.act(uf[:, 0:W], u[:, 0:W], AF.Abs)
                k.act(cs[:, 0:W], uf[:, 0:W], AF.Sin, bias=epsc[:, 1:2], scale=-TWO_PI)
                return sn, cs

            def make_tail(pr_, q, py5, sn, cs, Sr, Si):
                def tail():
                    m1, m2, m3, m4 = mprod
                    k.tt("pool", m1[:, 0:W], cs[:, 0:W], Srb[:, 0:W], ALU.mult)
                    k.tt("pool", m2[:, 0:W], sn[:, 0:W], Sib[:, 0:W], ALU.mult)
                    k.tt("pool", m3[:, 0:W], cs[:, 0:W], Sib[:, 0:W], ALU.mult)
                    k.tt("pool", m4[:, 0:W], sn[:, 0:W], Srb[:, 0:W], ALU.mult)
                    k.mm(py5[:, 0:W], CT_re[:, pr_, :], m1[:, 0:W], start=(q == 0), stop=False)
                    k.mm(py5[:, 0:W], nCT_re[:, pr_, :], m2[:, 0:W], start=False, stop=False)
                    k.mm(py5[:, 0:W], nCT_im[:, pr_, :], m3[:, 0:W], start=False, stop=False)
                    k.mm(py5[:, 0:W], nCT_im[:, pr_, :], m4[:, 0:W], start=False, stop=(q == 3))
                    if smp or last_p:
                        if smp:
                            sel = slice(LS - 1, W, LS)
                            n_ = NSEQ
                            dr = s5o_r[:, pr_, :]
                            di = s5o_i[:, pr_, :]
                        else:
                            sel = slice(W - 1, W)
                            n_ = 1
                            dr = s5p_r[:, pr_:pr_ + 1]
                            di = s5p_i[:, pr_:pr_ + 1]
                        ta = tring.get()
                        tb2 = tring.get()
                        tc2 = tring.get()
                        td2 = tring.get()
                        k.tt("dve", ta[:, 0:n_], cs[:, sel], Sr[:, sel], ALU.mult)
                        k.tt("dve", tb2[:, 0:n_], sn[:, sel], Si[:, sel], ALU.mult)
                        k.tt("dve", tc2[:, 0:n_], cs[:, sel], Si[:, sel], ALU.mult)
                        k.tt("dve", td2[:, 0:n_], sn[:, sel], Sr[:, sel], ALU.mult)
                        k.tt("dve", dr, ta[:, 0:n_], tb2[:, 0:n_], ALU.subtract)
                        k.tt("dve", di, tc2[:, 0:n_], td2[:, 0:n_], ALU.add)
                return tail

            nxt = tables(0)
            prev_tail = None
            for kt in range(4):
                py5 = pheld
                ub = A16[:, 8 + kt, 0:W]
                for q in range(4):
                    pr_ = kt * 4 + q
                    pbr = pring.get()
                    k.mm(pbr[:, 0:W], BbT_re[:, pr_, :], ub)
                    pbi = pring.get()
                    k.mm(pbi[:, 0:W], BbT_im[:, pr_, :], ub)
                    sn, cs = nxt
                    if pr_ + 1 < 16:
                        nxt = tables(pr_ + 1)
                    t1, t2, t3, t4 = tprod
                    k.tt("dve", t1[:, 0:W], pbr[:, 0:W], cs[:, 0:W], ALU.mult)
                    k.tt("dve", t2[:, 0:W], pbi[:, 0:W], sn[:, 0:W], ALU.mult)
                    k.tt("dve", t3[:, 0:W], pbi[:, 0:W], cs[:, 0:W], ALU.mult)
                    k.tt("dve", t4[:, 0:W], pbr[:, 0:W], sn[:, 0:W], ALU.mult)
                    k.mm(pGr[:, 0:W], ident_b, t1[:, 0:W], start=True, stop=False)
                    k.mm(pGr[:, 0:W], ident_b, t2[:, 0:W], start=False, stop=True)
                    k.mm(pGi[:, 0:W], ident_b, t3[:, 0:W], start=True, stop=False)
                    k.mm(pGi[:, 0:W], nident_b[:], t4[:, 0:W], start=False, stop=True)
                    if prev_tail is not None:
                        prev_tail()
                        prev_tail = None
                    Sr = srring.get()
                    Si = siring.get()
                    if smp:
                        ar_ = lp[:, 3, pr_:pr_ + 1]
                        ai_ = lp[:, 4, pr_:pr_ + 1]
                        h0r = s5h0_r[:, pr_, :]
                        h0i = s5h0_i[:, pr_, :]
                        tb = tring.get()
                        k.ts("dve", tb[:, 0:16], h0i, ai_, None, ALU.mult)
                        injr = tring.get()
                        k.stt(injr[:, 0:16], h0r, ar_, tb[:, 0:16], ALU.mult, ALU.subtract)
                        tc = tring.get()
                        k.ts("dve", tc[:, 0:16], h0r, ai_, None, ALU.mult)
                        inji = tring.get()
                        k.stt(inji[:, 0:16], h0i, ar_, tc[:, 0:16], ALU.mult, ALU.add)
                        k.tt("dve", pGr[:, 0:W:LS], pGr[:, 0:W:LS], injr[:, 0:16], ALU.add)
                        k.tt("dve", pGi[:, 0:W:LS], pGi[:, 0:W:LS], inji[:, 0:16], ALU.add)
                        magm = sring.get()
                        k.ts("dve", magm[:], notstart, lp[:, 2, pr_:pr_ + 1], None, ALU.mult)
                        k.scan(Sr[:, 0:W], magm[:], pGr[:, 0:W], 0.0)
                        k.scan(Si[:, 0:W], magm[:], pGi[:, 0:W], 0.0)
                    else:
                        magb = lp[:, 2, pr_:pr_ + 1].to_broadcast([128, W])
                        k.scan(Sr[:, 0:W], magb, pGr[:, 0:W], s5car_r[:, pr_:pr_ + 1])
                        k.scan(Si[:, 0:W], magb, pGi[:, 0:W], s5car_i[:, pr_:pr_ + 1])
                        k.copy("dve", s5car_r[:, pr_:pr_ + 1], Sr[:, W - 1:W])
                        k.copy("dve", s5car_i[:, pr_:pr_ + 1], Si[:, W - 1:W])
                    k.copy("act", Srb[:, 0:W], Sr[:, 0:W])
                    k.copy("act", Sib[:, 0:W], Si[:, 0:W])
                    prev_tail = make_tail(pr_, q, py5, sn, cs, Sr, Si)
                    if q == 3:
                        prev_tail()
                        prev_tail = None
                k.stt(F8[:, 4 + kt, 0:W], ub, pp[:, l, PP_S5D + kt:PP_S5D + kt + 1], py5[:, 0:W], ALU.mult, ALU.add)
            x2s = [fring.get() for _ in range(4)]
            for kt in range(4):
                k.act(x2s[kt][:, 0:W], F8[:, 4 + kt, 0:W], AF.Square)
            for kt in range(4):
                k.ts("dve", x2s[kt][:, 0:W], x2s[kt][:, 0:W], 0.044715, 1.0, ALU.mult, ALU.add)
            for kt in range(4):
                k.tt("dve", x2s[kt][:, 0:W], x2s[kt][:, 0:W], F8[:, 4 + kt, 0:W], ALU.mult)
            for kt in range(4):
                k.act(x2s[kt][:, 0:W], x2s[kt][:, 0:W], AF.Sigmoid, scale=1.5957691216057308)
            for kt in range(4):
                k.tt("dve", A16[:, 12 + kt, 0:W], F8[:, 4 + kt, 0:W], x2s[kt][:, 0:W], ALU.mult)
            sgs = []
            for m in range(4):
                pg = pring.get()
                for kt in range(4):
                    k.mm(pg[:, 0:W], glu_bf[:, kt, m * 128:(m + 1) * 128], A16[:, 12 + kt, 0:W],
                         start=(kt == 0), stop=(kt == 3))
                sg = fring.get()
                k.act(sg[:, 0:W], pg[:, 0:W], AF.Sigmoid, bias=pp[:, l, PP_GLB + m:PP_GLB + m + 1])
                sgs.append(sg)
            for m in range(4):
                k.tt("dve", sgs[m][:, 0:W], sgs[m][:, 0:W], A16[:, 12 + m, 0:W], ALU.mult)
            for m in range(4):
                k.tt("dve", Y[:, 12 + m, 0:W], sgs[m][:, 0:W], A16[:, 4 + m, 0:W], ALU.mult)

        def convert_w_in(l_):
            for bi_, (_, c0_, n_) in enumerate(blocks):
                k.dma("pool", wbf[l_][:, :, c0_:c0_ + n_], w_in[l_][:, :, c0_:c0_ + n_],
                      wk=["wbf%d_%d" % (l_, bi_)])

        def convert_w_out(l_):
            for m_ in range(8):
                k.dma("pool", wobf[l_, m_], w_out[l_][:, :, m_ * 128:(m_ + 1) * 128], wk=["wobf%d_%d" % (l_, m_)])

        for l in range(nlayers):
            k.dma("pool", glu_bf[:], glu_w[l])
            k.dma("pool", wa_bf[:], wa_bd[l])
            k.dma("pool", wx_bf[:], wx_bd[l])
            k.dma("pool", CT_re[:], ctpad_re[l])
            k.dma("pool", nCT_im[:], ctpad_im[l])
            if l == 0:
                convert_w_in(0)
            k.act(nCT_re[:].rearrange("p a b -> p (a b)"), CT_re[:].rearrange("p a b -> p (a b)"), AF.Copy, scale=-1.0)
            k.act(nCT_im[:].rearrange("p a b -> p (a b)"), nCT_im[:].rearrange("p a b -> p (a b)"), AF.Copy, scale=-1.0)
            k.act(expA[:], pdt[:, l, 16:32], AF.Exp)
            t = tring.get()
            k.act(t[:, 0:4], pp[:, l, PP_LLAM:PP_LLAM + 4], AF.Exp, scale=-1.0)
            t2 = tring.get()
            k.act(t2[:, 0:4], t[:, 0:4], AF.Ln, bias=epsc[:, 2:3])
            k.ts("dve", lp[:, 0, 0:4], t2[:, 0:4], -8.0, None, ALU.mult)
            dl = tring.get()
            k.act(dl[:, 0:16], pp[:, l, PP_LDT:PP_LDT + 16], AF.Exp)
            thp = lp[:, 1, :]
            k.stt(thp, pp[:, l, PP_LIM:PP_LIM + 16], 1.0 / TWO_PI, dl[:, 0:16], ALU.mult, ALU.mult)
            lm = tring.get()
            k.tt("dve", lm[:, 0:16], pp[:, l, PP_LRE:PP_LRE + 16], dl[:, 0:16], ALU.mult)
            mag = lp[:, 2, :]
            k.act(mag, lm[:, 0:16], AF.Exp)
            sn0 = tring.get()
            cs0 = tring.get()
            frac_sincos(thp, 16, sn0[:, 0:16], cs0[:, 0:16])
            k.tt("dve", lp[:, 3, :], mag, cs0[:, 0:16], ALU.mult)
            k.tt("dve", lp[:, 4, :], mag, sn0[:, 0:16], ALU.mult)
            lre_c = pp[:, l, PP_LRE:PP_LRE + 16]
            lim_c = pp[:, l, PP_LIM:PP_LIM + 16]
            nr = tring.get()
            k.ts("dve", nr[:, 0:16], lp[:, 3, :], -1.0, None, ALU.add)
            den = tring.get()
            t_a = tring.get()
            k.tt("dve", den[:, 0:16], lre_c, lre_c, ALU.mult)
            k.tt("dve", t_a[:, 0:16], lim_c, lim_c, ALU.mult)
            k.tt("dve", den[:, 0:16], den[:, 0:16], t_a[:, 0:16], ALU.add)
            k.op("dve", lambda v, o=den[:, 0:16]: v.reciprocal(out=o, in_=o), [den], [den])
            cre = lp[:, 5, :]
            cim = lp[:, 6, :]
            t_b = tring.get()
            k.tt("dve", cre, nr[:, 0:16], lre_c, ALU.mult)
            k.tt("dve", t_b[:, 0:16], lp[:, 4, :], lim_c, ALU.mult)
            k.tt("dve", cre, cre, t_b[:, 0:16], ALU.add)
            k.tt("dve", cre, cre, den[:, 0:16], ALU.mult)
            t_c = tring.get()
            k.tt("dve", cim, lp[:, 4, :], lre_c, ALU.mult)
            k.tt("dve", t_c[:, 0:16], nr[:, 0:16], lim_c, ALU.mult)
            k.tt("dve", cim, cim, t_c[:, 0:16], ALU.subtract)
            k.tt("dve", cim, cim, den[:, 0:16], ALU.mult)
            for c4 in range(4):
                bre = fring.get()
                bim = fring.get()
                k.dma("sp", bre[:].rearrange("p (a b) -> p a b", a=4), bpad_re[l][:, c4 * 4:(c4 + 1) * 4, :])
                k.dma("sp", bim[:].rearrange("p (a b) -> p a b", a=4), bpad_im[l][:, c4 * 4:(c4 + 1) * 4, :])
                crb = cre[:, c4 * 4:(c4 + 1) * 4].unsqueeze(2).to_broadcast([128, 4, 128])
                cib = cim[:, c4 * 4:(c4 + 1) * 4].unsqueeze(2).to_broadcast([128, 4, 128])
                v3 = lambda ap_: ap_.rearrange("p (a b) -> p a b", a=4)
                m_a, m_b, m_c, m_d = [f8ring.get() for _ in range(4)]
                k.tt("dve", v3(m_a), v3(bre[:]), crb, ALU.mult)
                k.tt("dve", v3(m_b), v3(bim[:]), cib, ALU.mult)
                k.tt("dve", v3(m_c), v3(bre[:]), cib, ALU.mult)
                k.tt("dve", v3(m_d), v3(bim[:]), crb, ALU.mult)
                o_re = zring.get()
                o_im = zring.get()
                k.tt("dve", o_re, m_a, m_b, ALU.subtract)
                k.tt("dve", o_im, m_c, m_d, ALU.add)
                for i4 in range(4):
                    k.tr(ptb[:, i4 * 128:(i4 + 1) * 128], o_re[:, i4 * 128:(i4 + 1) * 128], ident_b)
                    k.tr(ptb[:, 512 + i4 * 128:512 + (i4 + 1) * 128], o_im[:, i4 * 128:(i4 + 1) * 128], ident_b)
                k.copy("act", BbT_re[:, c4 * 4:(c4 + 1) * 4, :].rearrange("p a b -> p (a b)"), ptb[:, 0:512])
                k.copy("act", BbT_im[:, c4 * 4:(c4 + 1) * 4, :].rearrange("p a b -> p (a b)"), ptb[:, 512:1024])
            k.dma("sp", carry_s[:], sconv0[l])
            k.dma("sp", carry_l[:], lconv0[l])
            k.dma("sp", lruh0[:], lru0[l])
            k.dma("sp", s5h0_r[:], s5r0[l])
            k.dma("sp", s5h0_i[:], s5i0[l])
            k.memset("dve", hT[:], 0.0)
            k.memset("dve", hT_bf[:], 0.0)
            k.memset("dve", pcar_s[:], 0.0)
            k.memset("dve", pcar_l[:], 0.0)
            k.memset("dve", pcar_h[:], 0.0)
            k.memset("dve", s5car_r[:], 0.0)
            k.memset("dve", s5car_i[:], 0.0)

            if l == 0:
                prefetch_w(1)
            k.cp("setup done l%d" % l)
            for ti, (c0, W, kind) in enumerate(tiles):
                smp = kind == "s"
                k.cp("tile start l%d t%d" % (l, ti))
                if ti == 0:
                    load_x(l, ti)
                rmsnorm_tile(F8, W, PP_NG, l, lambda kt: hn[:, kt, 0:W])
                for kt in range(8):
                    k.copy("pool", xt[:, kt, 0:W], F8[:, kt, 0:W])
                k.cp("norm done")
                for zb in range(2):
                    wb = next_block()
                    for m in range(4):
                        pm = proj(wb, m, W)
                        k.act(zs[:, zb * 4 + m, 0:W], pm[:, 0:W], AF.Silu)
                for xb in range(4):
                    wb = next_block()
                    for half in range(2):
                        js = [xb * 4 + half * 2 + i for i in range(2)]
                        pms = [proj(wb, half * 2 + i, W) for i in range(2)]
                        accs = conv_group(l, pms, W, smp,
                                          [(PP_CW + 4 * j, PP_CB + j, pcar_s[:, j, :], carry_s[:, j, :, :],
                                            sconv_s_o[l][:, j, :, :]) for j in js])

                        def silus(js=js, accs=accs):
                            for j, acc in zip(js, accs):
                                k.act(A16[:, j, 0:W], acc[:, 0:W], AF.Silu)
                        k.defer(silus, depth=1)
                wb = next_block()
                k.flush()
                nchk = W // 128
                pd = pring.get()
                for ci in range(nchk):
                    for kt in range(8):
                        k.mm(pd[:, ci * 16:(ci + 1) * 16], hn[:, kt, ci * 128:(ci + 1) * 128], wb[:, kt, 0:16],
                             start=(kt == 0), stop=(kt == 7))
                dt2 = dt_tm[:, 0:nchk, :]
                dtA2 = dtA_tm[:, 0:nchk, :]
                pd3 = pd[:, 0:nchk * 16].rearrange("p (c h) -> p c h", c=nchk)
                v = pre[:, 6, 0:nchk * 16].rearrange("p (c h) -> p c h", c=nchk)
                k.tt("dve", v, pd3, pdt[:, l, 0:16].unsqueeze(1).to_broadcast([128, nchk, 16]), ALU.add)
                k.act(v, v, AF.Exp)
                k.act(dt2, v, AF.Ln, bias=epsc[:, 2:3])
                k.stt(dtA2, dt2, -1.0, expA[:].unsqueeze(1).to_broadcast([128, nchk, 16]), ALU.mult, ALU.mult)
                hi_f = pre[:, 5, 0:nchk * 16]
                k.copy("act", dhl[:, 0, 0:nchk * 16], dtA_tm[:, 0:nchk, :].rearrange("p c h -> p (c h)"))
                k.copy("act", hi_f, dhl[:, 0, 0:nchk * 16])
                k.tt("dve", hi_f, dtA_tm[:, 0:nchk, :].rearrange("p c h -> p (c h)"), hi_f, ALU.subtract)
                k.copy("act", dhl[:, 1, 0:nchk * 16], hi_f)
                TRI_ = tri_s if smp else tri_p
                ONESM_ = ones_s if smp else ones_f
                n16 = nchk * 16
                pa = pring.get()
                dtA_flat = dtA_tm[:, 0:nchk, :].rearrange("p c h -> p (c h)")
                k.mm(pa[:, 0:n16], TRI_, dtA_flat)
                k.mm(pa[:, 64:64 + n16], ONESM_, dtA_flat)
                k.copy("act", pre[:, 0, 0:n16], pa[:, 0:n16])
                k.ts("dve", pre[:, 1, 0:n16], pa[:, 0:n16], -1.0, None, ALU.mult)
                k.tt("dve", pre[:, 2, 0:n16], pa[:, 64:64 + n16], pre[:, 0, 0:n16], ALU.subtract)
                k.act(pre[:, 2, 0:n16], pre[:, 2, 0:n16], AF.Exp)
                k.tt("dve", pre[:, 3, 0:n16], pre[:, 2, 0:n16], dt_tm[:, 0:nchk, :].rearrange("p c h -> p (c h)"),
                     ALU.mult)
                k.act(pre[:, 4, 0:n16], pa[:, 64:64 + n16], AF.Exp)
                if l == 0 and ti == 0:
                    convert_w_out(0)
                k.cp("stage A done")
                ssd_stage(l, ti, W, smp)
                k.cp("ssd done")
                wb = next_block()
                for m in range(4):
                    pm = proj(wb, m, W)
                    k.act(A16[:, m, 0:W], pm[:, 0:W], AF.Silu)
                wb = next_block()
                for half in range(2):
                    js = [half * 2, half * 2 + 1]
                    pms = [proj(wb, j, W) for j in js]
                    accs = conv_group(l, pms, W, smp, [(PP_LCW + 4 * j, PP_LCB + j, pcar_l[:, j, :],
                                                         carry_l[:, j, :, :], lconv_s_o[l][:, j, :, :]) for j in js])
                    lru_group(l, js, accs, W, smp)
                k.cp("lru done")
                k.flush()
                wb = next_block()
                for m in range(4):
                    pm = proj(wb, m, W)
                    k.act(A16[:, 4 + m, 0:W], pm[:, 0:W], AF.Silu)
                wb = next_block()
                for m in range(4):
                    pm = proj(wb, m, W)
                    k.copy("act", A16[:, 8 + m, 0:W], pm[:, 0:W])
                s5_stage(l, ti, c0, W, smp)
                if l + 1 < nlayers and ti == 0:
                    convert_w_in(l + 1)
                if l + 1 < nlayers and ti == 1:
                    convert_w_out(l + 1)
                k.cp("s5 done")
                wo = woring.get()
                k.dma("sp", wo[:], wobf[l, 0], rk=["wobf%d_0" % l])
                if ti + 1 < len(tiles):
                    load_x(l, ti + 1)
                for m in range(8):
                    if m + 1 < 8:
                        wo_n = woring.get()
                        k.dma("sp", wo_n[:], wobf[l, m + 1], rk=["wobf%d_%d" % (l, m + 1)])
                    po = pring.get()
                    for kt in range(16):
                        k.mm(po[:, 0:W], wo[:, kt, :], Y[:, kt, 0:W], start=(kt == 0), stop=(kt == 15))
                    k.tt("dve", xt[:, m, 0:W], po[:, 0:W], xt[:, m, 0:W], ALU.add)
                    if m + 1 < 8:
                        wo = wo_n
                if l < nlayers - 1:
                    k.dma("sp", xsc[l % 2].rearrange("k p t -> p k t")[:, :, c0:c0 + W], xt[:, :, 0:W],
                          wk=["xsc%d_%d" % (l % 2, ti)])
                else:
                    rmsnorm_tile(xt, W, PP_FG, l, lambda kt: xt[:, kt, 0:W])
                    k.dma("sp", yout.rearrange("k p t -> p k t")[:, :, c0:c0 + W], xt[:, :, 0:W])
            k.dma("sp", sconv_p_o[l], pcar_s[:])
            k.dma("sp", lconv_p_o[l], pcar_l[:])
            k.dma("sp", lru_p_o[l], pcar_h[:])
            k.dma("sp", lru_s_o[l], hcar[:])
            k.dma("sp", s5r_p_o[l], s5p_r[:])
            k.dma("sp", s5i_p_o[l], s5p_i[:])
            k.dma("sp", s5r_s_o[l], s5o_r[:])
            k.dma("sp", s5i_s_o[l], s5o_i[:])
        k.finish()
        k.emit()
    return nc


def _consts():
    i = np.arange(128)
    ident = np.eye(128, dtype=np.float32)
    tri_p = (i[:, None] <= i[None, :]).astype(np.float32)
    same = (i[:, None] // LS == i[None, :] // LS)
    tri_s = (tri_p > 0) & same
    cfa = np.zeros((128, 8, 128), np.float32)
    import ml_dtypes
    trib = np.concatenate([tri_p, tri_s.astype(np.float32)], axis=1).astype(ml_dtypes.bfloat16)
    cfa[:, 0] = np.ascontiguousarray(trib).view(np.float32)
    cfa[:, 1] = tri_p
    cfa[:, 2] = tri_s
    cfa[:, 3] = 1.0
    cfa[:, 4] = same
    cfa[:, 5] = np.broadcast_to((i % LS)[None, :], (128, 128))
    cfa[:, 6] = np.broadcast_to((i % LS != 0)[None, :], (128, 128))
    cfa[:, 7, 0:NSEQ] = (i[:, None] // LS == np.arange(NSEQ)[None, :])
    cba = np.zeros((128, 4, 128), np.float32)
    cba[:, 0] = ident
    cba[:, 1] = 1.0
    cba[:, 2] = np.where(tri_p > 0, 0.0, -30000.0)
    cba[:, 3] = np.where(tri_s, 0.0, -30000.0)
    iota = np.broadcast_to(np.arange(WT, dtype=np.float32)[None, :], (128, WT)).copy()
    return cfa, cba, iota


def _fm(v, nt):
    return np.moveaxis(v.reshape(v.shape[:-1] + (nt, 128)), -1, -2)


def _prep_shared(inp):
    f = lambda a: np.ascontiguousarray(a, dtype=np.float32)
    sh = {}
    sh["w_in_r"] = f(inp["w_in"].reshape(NL, 8, 128, IN_DIM).transpose(0, 2, 1, 3))
    sh["w_out_r"] = f(inp["w_out"].reshape(NL, 16, 128, D).transpose(0, 2, 1, 3))
    sh["glu_r"] = f(inp["s5_glu_w"].reshape(NL, 4, 128, 512).transpose(0, 2, 1, 3))
    for nm, src in (("wa_bd", inp["lru_wa"]), ("wx_bd", inp["lru_wx"])):
        bd = np.zeros((NL, 128, 4, 128), np.float32)
        for m in range(4):
            for k2 in range(2):
                bd[:, k2 * 64:(k2 + 1) * 64, m, k2 * 64:(k2 + 1) * 64] = src[:, 2 * m + k2]
        sh[nm] = bd
    pp = np.zeros((128, NL, NPP), np.float32)
    for l in range(NL):
        pp[:, l, PP_NG:PP_NG + 8] = _fm(inp["norm_g"][l], 8)
        pp[:, l, PP_NG2:PP_NG2 + 8] = _fm(inp["ssd_norm_g"][l], 8)
        pp[:, l, PP_SD:PP_SD + 8] = _fm(np.repeat(inp["ssd_d"][l], 64), 8)
        pp[:, l, PP_CW:PP_CW + 64] = inp["ssd_conv_w"][l].reshape(4, 16, 128).transpose(2, 1, 0).reshape(128, 64)
        pp[:, l, PP_CB:PP_CB + 16] = _fm(inp["ssd_conv_b"][l], 16)
        pp[:, l, PP_LCW:PP_LCW + 16] = inp["lru_conv_w"][l].reshape(4, 4, 128).transpose(2, 1, 0).reshape(128, 16)
        pp[:, l, PP_LCB:PP_LCB + 4] = _fm(inp["lru_conv_b"][l], 4)
        pp[:, l, PP_LBA:PP_LBA + 4] = _fm(inp["lru_ba"][l], 4)
        pp[:, l, PP_LBX:PP_LBX + 4] = _fm(inp["lru_bx"][l], 4)
        pp[:, l, PP_LLAM:PP_LLAM + 4] = _fm(inp["lru_lambda"][l], 4)
        pp[:, l, PP_S5D:PP_S5D + 4] = _fm(inp["s5_d"][l], 4)
        pp[:, l, PP_GLB:PP_GLB + 4] = _fm(inp["s5_glu_b"][l], 4)
        pp[:, l, PP_LRE:PP_LRE + 16] = inp["s5_lambda_re"][l].reshape(16, 128).T
        pp[:, l, PP_LIM:PP_LIM + 16] = inp["s5_lambda_im"][l].reshape(16, 128).T
        pp[:, l, PP_LDT:PP_LDT + 16] = np.repeat(inp["s5_log_dt"][l], 64).reshape(16, 128).T
        pp[:, l, PP_FG:PP_FG + 8] = _fm(inp["final_norm_g"], 8)
    sh["pp_in"] = pp
    pdt = np.zeros((128, NL, 32), np.float32)
    pdt[:, :, 0:16] = inp["ssd_dt_bias"][None]
    pdt[:, :, 16:32] = inp["ssd_a_log"][None]
    sh["pdt_in"] = pdt
    bre = np.zeros((NL, 128, 16, 128), np.float32)
    bim = np.zeros((NL, 128, 16, 128), np.float32)
    cre = np.zeros((NL, 128, 16, 128), np.float32)
    cim = np.zeros((NL, 128, 16, 128), np.float32)
    for g in range(32):
        pr, g2, gl = g // 2, g % 2, g % 8
        bre[:, g2 * 64:(g2 + 1) * 64, pr, gl * 16:(gl + 1) * 16] = inp["s5_b_re"][:, g]
        bim[:, g2 * 64:(g2 + 1) * 64, pr, gl * 16:(gl + 1) * 16] = inp["s5_b_im"][:, g]
        cre[:, g2 * 64:(g2 + 1) * 64, pr, gl * 16:(gl + 1) * 16] = inp["s5_c_re"][:, g].transpose(0, 2, 1)
        cim[:, g2 * 64:(g2 + 1) * 64, pr, gl * 16:(gl + 1) * 16] = inp["s5_c_im"][:, g].transpose(0, 2, 1)
    sh["bpadT_re"], sh["bpadT_im"], sh["ctpad_re"], sh["ctpad_im"] = bre, bim, cre, cim
    cfa, cba, iota = _consts()
    sh["c_f32"], sh["c_bf"], sh["c_iota"] = cfa, cba, iota
    return sh


def _prep_core(inp, c):
    f = lambda a: np.ascontiguousarray(a, dtype=np.float32)
    sl = slice(NSEQ * c, NSEQ * (c + 1))
    m = {}
    x_tok = np.concatenate([inp["x_prompt"][c], inp["x_sample"][sl].reshape(TS, D)], axis=0)
    m["xin"] = f(x_tok.T.reshape(8, 128, TT))
    m["h0T"] = f(inp["state_ssd"][:, sl].transpose(0, 1, 4, 2, 3).reshape(NL, NSEQ, 128, 1024))
    m["sconv0"] = f(inp["state_ssd_conv"][:, sl].reshape(NL, NSEQ, 3, 16, 128).transpose(0, 4, 3, 1, 2))
    m["lconv0"] = f(inp["state_lru_conv"][:, sl].reshape(NL, NSEQ, 3, 4, 128).transpose(0, 4, 3, 1, 2))
    m["lru0"] = f(inp["state_lru"][:, sl].reshape(NL, NSEQ, 4, 128).transpose(0, 3, 2, 1))
    m["s5r0"] = f(inp["state_s5_re"][:, sl].reshape(NL, NSEQ, 16, 128).transpose(0, 3, 2, 1))
    m["s5i0"] = f(inp["state_s5_im"][:, sl].reshape(NL, NSEQ, 16, 128).transpose(0, 3, 2, 1))
    return m


_PROG = {}


def kernel(**inputs):
    inp = {k_: np.asarray(v) for k_, v in inputs.items()}
    if "nc" not in _PROG:
        _PROG["nc"] = build_program(NL)
    nc = _PROG["nc"]
    shared = _prep_shared(inp)
    in_maps = []
    for c in range(NCORES):
        m = dict(shared)
        m.update(_prep_core(inp, c))
        in_maps.append(m)
    res = run_bass_kernel_spmd(nc, in_maps, core_ids=list(range(NCORES)))
    R = res.results
    B = NCORES
    y_prompt = np.zeros((B, TP, D), np.float32)
    y_sample = np.zeros((B * NSEQ, LS, D), np.float32)
    ssd_p = np.zeros((NL, B, 16, 64, 128), np.float32)
    ssd_s = np.zeros((NL, B * NSEQ, 16, 64, 128), np.float32)
    ssd_conv_p = np.zeros((NL, B, 3, 2048), np.float32)
    ssd_conv_s = np.zeros((NL, B * NSEQ, 3, 2048), np.float32)
    lru_p = np.zeros((NL, B, 512), np.float32)
    lru_s = np.zeros((NL, B * NSEQ, 512), np.float32)
    lru_conv_p = np.zeros((NL, B, 3, 512), np.float32)
    lru_conv_s = np.zeros((NL, B * NSEQ, 3, 512), np.float32)
    s5_re_p = np.zeros((NL, B, 32, 64), np.float32)
    s5_re_s = np.zeros((NL, B * NSEQ, 32, 64), np.float32)
    s5_im_p = np.zeros((NL, B, 32, 64), np.float32)
    s5_im_s = np.zeros((NL, B * NSEQ, 32, 64), np.float32)
    for c in range(B):
        r = R[c]
        sl = slice(NSEQ * c, NSEQ * (c + 1))
        y = np.asarray(r["yout"]).reshape(D, TT).T
        y_prompt[c] = y[0:TP]
        y_sample[sl] = y[TP:].reshape(NSEQ, LS, D)
        ssd_p[:, c] = np.asarray(r["ssd_p_o"]).reshape(NL, 128, 16, 64).transpose(0, 2, 3, 1)
        ssd_s[:, sl] = np.asarray(r["ssd_s_o"]).reshape(NL, NSEQ, 128, 16, 64).transpose(0, 1, 3, 4, 2)
        ssd_conv_p[:, c] = np.asarray(r["sconv_p_o"]).transpose(0, 3, 2, 1).reshape(NL, 3, 2048)
        ssd_conv_s[:, sl] = np.asarray(r["sconv_s_o"]).transpose(0, 3, 4, 2, 1).reshape(NL, NSEQ, 3, 2048)
        lru_p[:, c] = np.asarray(r["lru_p_o"]).transpose(0, 2, 1).reshape(NL, 512)
        lru_s[:, sl] = np.asarray(r["lru_s_o"]).transpose(0, 3, 2, 1).reshape(NL, NSEQ, 512)
        lru_conv_p[:, c] = np.asarray(r["lconv_p_o"]).transpose(0, 3, 2, 1).reshape(NL, 3, 512)
        lru_conv_s[:, sl] = np.asarray(r["lconv_s_o"]).transpose(0, 3, 4, 2, 1).reshape(NL, NSEQ, 3, 512)
        s5_re_p[:, c] = np.asarray(r["s5r_p_o"]).transpose(0, 2, 1).reshape(NL, 32, 64)
        s5_im_p[:, c] = np.asarray(r["s5i_p_o"]).transpose(0, 2, 1).reshape(NL, 32, 64)
        s5_re_s[:, sl] = np.asarray(r["s5r_s_o"]).transpose(0, 3, 2, 1).reshape(NL, NSEQ, 32, 64)
        s5_im_s[:, sl] = np.asarray(r["s5i_s_o"]).transpose(0, 3, 2, 1).reshape(NL, NSEQ, 32, 64)
    return (y_prompt, y_sample, ssd_p, ssd_s, ssd_conv_p, ssd_conv_s, lru_p, lru_s,
            lru_conv_p, lru_conv_s, s5_re_p, s5_re_s, s5_im_p, s5_im_s)
```

```python
import math
import os
from contextlib import ExitStack

import numpy as np
import concourse.bass as bass
import concourse.mybir as mybir
from concourse.bass_utils import run_bass_kernel_spmd

F32 = mybir.dt.float32
BF16 = mybir.dt.bfloat16
I32 = mybir.dt.int32
ALU = mybir.AluOpType
AF = mybir.ActivationFunctionType

NCORES = 8
D = 1024
NL = 4
TP = 2048
NSEQ = 16
LS = 8
TS = NSEQ * LS
TT = TP + TS
WT = 512
IN_DIM = 5136
EPS = 1e-6
TWO_PI = 2.0 * math.pi

PP_NG = 0
PP_NG2 = 8
PP_SD = 16
PP_CW = 24
PP_CB = 88
PP_LCW = 104
PP_LCB = 120
PP_LBA = 124
PP_LBX = 128
PP_LLAM = 132
PP_S5D = 136
PP_GLB = 140
PP_LRE = 144
PP_LIM = 160
PP_LDT = 176
PP_FG = 192
NPP = 200

EPOCH = 12000


class Ring:
    def __init__(self, bufs, full=None):
        self.bufs = bufs
        self.full = full
        self.i = 0

    def get(self):
        b = self.bufs[self.i % len(self.bufs)]
        self.i += 1
        return b

    def get_full(self):
        b = self.full[self.i % len(self.full)]
        self.i += 1
        return b


class KB:
    ENG = ("pe", "act", "dve", "pool", "sp")

    def __init__(self, nc, es):
        self.nc = nc
        self.es = es
        self.prog = {e: [] for e in self.ENG}
        self.cnt = {e: 0 for e in self.ENG}
        self.sem = {}
        self.nsem = 0
        for e in ("pe", "act", "dve", "pool"):
            self.sem[e] = self._newsem("c_" + e)
        self.waited = {e: {} for e in self.ENG}
        self.dead = False
        self.pending = []
        self.ncp = 0
        self.stop = int(os.environ.get("KSTOP", "-1"))
        self.lastw = {}
        self.readers = {}
        self.dsem = {}
        self.drr = {}
        for q, n in (("sp", 16), ("pool", 24), ("act", 2)):
            self.dsem[q] = [[self._newsem("d_%s%d" % (q, i)), 0] for i in range(n)]
            self.drr[q] = 0

    def _newsem(self, name):
        self.nsem += 1
        return self.es.enter_context(self.nc.semaphore("%s_%d" % (name, self.nsem)))

    slotw = {}

    def _keys(self, r):
        if isinstance(r, str):
            return [r]
        name = r.name
        w = self.slotw.get(name)
        if w is None:
            return [name]
        ap = r.ap
        off = r.offset % ap[0][0]
        hi = off + sum((c - 1) * s for s, c in ap[1:])
        return ["%s:%d" % (name, i) for i in range(off // w, hi // w + 1)]

    def defer(self, fn, depth=1):
        self.pending.append(fn)
        while len(self.pending) > depth:
            self.pending.pop(0)()

    def flush(self):
        while self.pending:
            self.pending.pop(0)()

    def cp(self, name=""):
        self.ncp += 1
        if self.stop >= 0 and self.ncp > self.stop and not self.dead:
            self.dead = True
            print("KSTOP: program truncated before checkpoint", self.ncp, name, flush=True)

    def op(self, e, fn, reads=(), writes=(), dma=False):
        if self.dead:
            return None
        waits = {}

        def need(tok, raw):
            if tok is None:
                return
            sem, val, src, isdma = tok
            if src == e and not isdma and e == "pe":
                return
            if self.waited[e].get(sem.name, 0) >= val:
                return
            if sem.name not in waits or waits[sem.name][1] < val:
                waits[sem.name] = (sem, val)

        rk = [x for r in reads for x in self._keys(r)]
        wk = [x for w in writes for x in self._keys(w)]
        for r in rk:
            need(self.lastw.get(r), True)
        for w in wk:
            need(self.lastw.get(w), False)
            for t in self.readers.get(w, {}).values():
                need(t, False)
        if dma:
            slot = self.dsem[e][self.drr[e] % len(self.dsem[e])]
            self.drr[e] += 1
            if slot[1] > 0:
                need((slot[0], slot[1], e, True), True)
            slot[1] += 16
            tok = (slot[0], slot[1], e, True)
            inc = (slot[0], 16)
        else:
            if self.cnt[e] >= EPOCH:
                self.sem[e] = self._newsem("c_" + e)
                self.cnt[e] = 0
            self.cnt[e] += 1
            tok = (self.sem[e], self.cnt[e], e, False)
            inc = (self.sem[e], 1)
        for s, v in waits.values():
            self.waited[e][s.name] = v
        self.prog[e].append((list(waits.values()), fn, inc))
        for r in rk:
            self.readers.setdefault(r, {})[tok[0].name] = tok
        for w in wk:
            self.lastw[w] = tok
            self.readers[w] = {}
        return tok

    def finish(self):
        fin = []
        for q in self.dsem:
            for sem, v in self.dsem[q]:
                if v > 0:
                    fin.append((sem, v))
        self.final_waits = fin

    def emit(self):
        nc = self.nc
        handles = {"pe": "tensor", "act": "scalar", "dve": "vector", "pool": "gpsimd", "sp": "sync"}
        with nc.Block() as block:
            for e in self.ENG:
                prog = self.prog[e]
                extra = self.final_waits if e == "sp" else []

                def body(eng, prog=prog, extra=extra):
                    for waits, fn, inc in prog:
                        for s, v in waits:
                            eng.wait_ge(s, v)
                        ins = fn(eng)
                        ins.then_inc(inc[0], inc[1])
                    for s, v in extra:
                        eng.wait_ge(s, v)

                getattr(block, handles[e])(body)

    def mm(self, out, lhsT, rhs, start=True, stop=True):
        self.op("pe", lambda t: t.matmul(out, lhsT=lhsT, rhs=rhs, start=start, stop=stop),
                [lhsT, rhs], [out])

    def tr(self, out, in_, ident):
        self.op("pe", lambda t: t.transpose(out, in_, ident), [in_, ident], [out])

    def act(self, out, in_, func, bias=None, scale=None):
        rd = [in_]
        kw = {}
        if bias is not None:
            kw["bias"] = bias
            if not isinstance(bias, (int, float)):
                rd.append(bias)
        if scale is not None:
            kw["scale"] = scale
            if not isinstance(scale, (int, float)):
                rd.append(scale)
        self.op("act", lambda a: a.activation(out=out, in_=in_, func=func, **kw), rd, [out])

    def tt(self, e, out, in0, in1, op):
        self.op(e, lambda v: v.tensor_tensor(out=out, in0=in0, in1=in1, op=op), [in0, in1], [out])

    def ts(self, e, out, in0, s1, s2, op0, op1=None):
        rd = [in0]
        for s in (s1, s2):
            if s is not None and not isinstance(s, (int, float)):
                rd.append(s)
        if op1 is None:
            self.op(e, lambda v: v.tensor_scalar(out=out, in0=in0, scalar1=s1, scalar2=None, op0=op0),
                    rd, [out])
        else:
            self.op(e, lambda v: v.tensor_scalar(out=out, in0=in0, scalar1=s1, scalar2=s2, op0=op0, op1=op1),
                    rd, [out])

    def stt(self, out, in0, scalar, in1, op0, op1):
        rd = [in0, in1]
        if not isinstance(scalar, (int, float)):
            rd.append(scalar)
        self.op("dve", lambda v: v.scalar_tensor_tensor(out=out, in0=in0, scalar=scalar, in1=in1,
                                                        op0=op0, op1=op1), rd, [out])

    def scan(self, out, d0, d1, init):
        rd = [d0, d1]
        if not isinstance(init, (int, float)):
            rd.append(init)
        self.op("dve", lambda v: v.tensor_tensor_scan(out=out, data0=d0, data1=d1, initial=init,
                                                      op0=ALU.mult, op1=ALU.add), rd, [out])

    def copy(self, e, out, in_):
        if e == "act":
            self.op("act", lambda a: a.activation(out=out, in_=in_, func=AF.Copy), [in_], [out])
        else:
            self.op(e, lambda v: v.tensor_copy(out=out, in_=in_), [in_], [out])

    def memset(self, e, out, val):
        self.op(e, lambda v: v.memset(out, val), [], [out])

    def dma(self, q, out, in_, rk=(), wk=()):
        self.op(q, lambda g: g.dma_start(out=out, in_=in_), [in_] + list(rk), [out] + list(wk), dma=True)


def build_program(nlayers=NL):
    nc = bass.Bass("TRN2", target_bir_lowering=False)

    def din(name, shape, dt=F32):
        return nc.dram_tensor(name, list(shape), dt, kind="ExternalInput").ap()

    def dout(name, shape, dt=F32):
        return nc.dram_tensor(name, list(shape), dt, kind="ExternalOutput").ap()

    xin = din("xin", [8, 128, TT])
    w_in = din("w_in_r", [NL, 128, 8, IN_DIM])
    w_out = din("w_out_r", [NL, 128, 16, D])
    glu_w = din("glu_r", [NL, 128, 4, 512])
    wa_bd = din("wa_bd", [NL, 128, 4, 128])
    wx_bd = din("wx_bd", [NL, 128, 4, 128])
    pp_d = din("pp_in", [128, NL, NPP])
    pdt_d = din("pdt_in", [128, NL, 32])
    bpad_re = din("bpadT_re", [NL, 128, 16, 128])
    bpad_im = din("bpadT_im", [NL, 128, 16, 128])
    ctpad_re = din("ctpad_re", [NL, 128, 16, 128])
    ctpad_im = din("ctpad_im", [NL, 128, 16, 128])
    h0T_d = din("h0T", [NL, NSEQ, 128, 1024])
    sconv0 = din("sconv0", [NL, 128, 16, NSEQ, 3])
    lconv0 = din("lconv0", [NL, 128, 4, NSEQ, 3])
    lru0 = din("lru0", [NL, 128, 4, NSEQ])
    s5r0 = din("s5r0", [NL, 128, 16, NSEQ])
    s5i0 = din("s5i0", [NL, 128, 16, NSEQ])
    c_f32 = din("c_f32", [128, 8, 128])
    c_bf = din("c_bf", [128, 4, 128])
    c_iota = din("c_iota", [128, WT])

    yout = dout("yout", [8, 128, TT])
    ssd_p_o = dout("ssd_p_o", [NL, 128, 1024])
    ssd_s_o = dout("ssd_s_o", [NL, NSEQ, 128, 1024])
    sconv_p_o = dout("sconv_p_o", [NL, 128, 16, 3])
    sconv_s_o = dout("sconv_s_o", [NL, 128, 16, NSEQ, 3])
    lru_p_o = dout("lru_p_o", [NL, 128, 4])
    lru_s_o = dout("lru_s_o", [NL, 128, 4, NSEQ])
    lconv_p_o = dout("lconv_p_o", [NL, 128, 4, 3])
    lconv_s_o = dout("lconv_s_o", [NL, 128, 4, NSEQ, 3])
    s5r_p_o = dout("s5r_p_o", [NL, 128, 16])
    s5r_s_o = dout("s5r_s_o", [NL, 128, 16, NSEQ])
    s5i_p_o = dout("s5i_p_o", [NL, 128, 16])
    s5i_s_o = dout("s5i_s_o", [NL, 128, 16, NSEQ])
    xsc = nc.dram_tensor("xsc", [2, 8, 128, TT], F32, kind="Internal").ap()
    wbf = nc.dram_tensor("wbf", [NL, 128, 8, IN_DIM], BF16, kind="Internal").ap()
    wobf = nc.dram_tensor("wobf", [NL, 8, 128, 16, 128], BF16, kind="Internal").ap()

    with ExitStack() as es:
        k = KB(nc, es)
        k.slotw = {"F8": 512, "zs": 512, "A16": 512, "Y": 512, "xt": 512, "hn": 512, "LT": 128,
                   "hnew": 512, "h0f": 512}

        def sb(name, shape, dt):
            return es.enter_context(nc.sbuf_tensor(name, list(shape), dt))

        def ps(name, shape, dt):
            return es.enter_context(nc.psum_tensor(name, list(shape), dt))

        xt = sb("xt", [128, 8, WT], F32)
        hn = sb("hn", [128, 8, WT], BF16)
        zs = sb("zs", [128, 8, WT], BF16)
        A16 = sb("A16", [128, 16, WT], BF16)
        F8 = sb("F8", [128, 8, WT], F32)
        Y = sb("Y", [128, 16, WT], BF16)
        LT = sb("LT", [128, 16, 128], BF16)
        x_tm = sb("x_tm", [128, 1024], BF16)
        xw_tm = sb("xw_tm", [128, 1024], BF16)
        B_tm = sb("B_tm", [128, 512], BF16)
        bmring = Ring([sb("bm%d" % i, [128, 512], BF16) for i in range(1)])
        hT = sb("hT", [128, 1024], F32)
        hT_bf = sb("hT_bf", [128, 1024], BF16)
        dt_tm = sb("dt_tm", [128, 4, 16], F32)
        dtA_tm = sb("dtA_tm", [128, 4, 16], F32)
        pre = sb("pre", [128, 7, 64], F32)
        wring = Ring([sb("wbuf%d" % i, [128, 8, 512], BF16) for i in range(2)])
        woring = Ring([sb("wobuf%d" % i, [128, 16, 128], BF16) for i in range(2)])
        glu_bf = sb("glu_bf", [128, 4, 512], BF16)
        BbT_re = sb("BbT_re", [128, 16, 128], BF16)
        BbT_im = sb("BbT_im", [128, 16, 128], BF16)
        CT_re = sb("CT_re", [128, 16, 128], BF16)
        nCT_re = sb("nCT_re", [128, 16, 128], BF16)
        nCT_im = sb("nCT_im", [128, 16, 128], BF16)
        wa_bf = sb("wa_bf", [128, 4, 128], BF16)
        wx_bf = sb("wx_bf", [128, 4, 128], BF16)
        pp = sb("pp", [128, NL, NPP], F32)
        pdt = sb("pdt", [128, NL, 32], F32)
        expA = sb("expA", [128, 16], F32)
        lp = sb("lp", [128, 8, 16], F32)
        cf = sb("cf", [128, 8, 128], F32)
        cb = sb("cb", [128, 4, 128], BF16)
        iota = sb("iota", [128, WT], F32)
        iota_c = sb("iota_c", [128, WT], F32)
        carry_s = sb("carry_s", [128, 16, NSEQ, 3], F32)
        carry_l = sb("carry_l", [128, 4, NSEQ, 3], F32)
        pcar_s = sb("pcar_s", [128, 16, 3], F32)
        pcar_l = sb("pcar_l", [128, 4, 3], F32)
        pcar_h = sb("pcar_h", [128, 4], F32)
        hcar = sb("hcar", [128, 4, NSEQ], F32)
        s5car_r = sb("s5car_r", [128, 16], F32)
        s5car_i = sb("s5car_i", [128, 16], F32)
        s5p_r = sb("s5p_r", [128, 16], F32)
        s5p_i = sb("s5p_i", [128, 16], F32)
        s5o_r = sb("s5o_r", [128, 16, NSEQ], F32)
        s5o_i = sb("s5o_i", [128, 16, NSEQ], F32)
        s5h0_r = sb("s5h0_r", [128, 16, NSEQ], F32)
        s5h0_i = sb("s5h0_i", [128, 16, NSEQ], F32)
        lruh0 = sb("lruh0", [128, 4, NSEQ], F32)
        h0f = sb("h0f", [128, 1024], F32)
        h0bring = Ring([sb("h0b%d" % i, [128, 1024], BF16) for i in range(2)])
        hnew = sb("hnew", [128, 1024], F32)
        dtA_rep = h0f
        _fr = [sb("fr%d" % i, [128, WT + 4], F32) for i in range(8)]
        fring = Ring([t_[:, 0:WT] for t_ in _fr], [t_[:, :] for t_ in _fr])
        iring = Ring([sb("ir%d" % i, [128, WT], I32) for i in range(2)])
        fring_x = [h0f[:, 0:512], h0f[:, 512:1024]]
        sring2 = [hnew[:, 0:512], hnew[:, 512:1024]]
        sring = Ring([sb("sr%d" % i, [128, 128], F32) for i in range(4)])
        tring = Ring([sb("tn%d" % i, [128, 48], F32) for i in range(10)])
        dhl = sb("dhl", [128, 2, 64], BF16)
        f8ring = Ring([F8[:, i, :] for i in range(8)])
        zring = Ring([zs[:, i, :] for i in range(8)])

        pheld = ps("pheld", [128, 512], F32)
        pheld2 = ps("pheld2", [128, 512], F32)
        pheld3 = ps("pheld3", [128, 512], F32)
        pring = Ring([ps("pb%d" % i, [128, 512], F32) for i in range(4)])
        ptb = ps("ptb", [128, 1024], BF16)

        tri_b2 = cf[:, 0, :].bitcast(BF16)
        tri_p = cf[:, 1, :]
        tri_s = cf[:, 2, :]
        ones_f = cf[:, 3, :]
        ones_s = cf[:, 4, :]
        iota_s = cf[:, 5, :]
        notstart = cf[:, 6, :]
        ind = cf[:, 7, :]
        ident_b = cb[:, 0, :]
        ones_b = cb[:, 1, :]
        neg_p = cb[:, 2, :]
        neg_s = cb[:, 3, :]

        k.dma("sp", cf[:], c_f32)
        nident_b = sb("nident_b", [128, 128], BF16)
        k.dma("pool", cb[:], c_bf)
        k.dma("sp", iota[:], c_iota)
        k.dma("sp", pp[:], pp_d)
        k.dma("sp", pdt[:], pdt_d)
        k.act(nident_b[:], ident_b, AF.Copy, scale=-1.0)

        tiles = [(i * WT, WT, "p") for i in range(TP // WT)] + [(TP, TS, "s")]
        blocks = [("z", 0, 512), ("z", 512, 512), ("x", 1024, 512), ("x", 1536, 512), ("B", 2048, 512),
                  ("C", 2560, 512), ("dt", 3072, 16), ("lg", 3600, 512), ("lx", 3088, 512),
                  ("sg", 4624, 512), ("su", 4112, 512)]
        stream = [(l, ti, bi) for l in range(nlayers) for ti in range(len(tiles)) for bi in range(len(blocks))]
        wbufs = {}
        st = {"next": 0, "item": 0}

        def prefetch_w(upto):
            while st["next"] < len(stream) and st["next"] <= upto:
                l_, ti_, bi_ = stream[st["next"]]
                _, c0_, n_ = blocks[bi_]
                buf = wring.get()
                k.dma("sp", buf[:, :, 0:n_], wbf[l_][:, :, c0_:c0_ + n_], rk=["wbf%d_%d" % (l_, bi_)])
                wbufs[st["next"]] = buf
                st["next"] += 1

        def next_block():
            prefetch_w(st["item"] + 1)
            wb = wbufs.pop(st["item"])
            st["item"] += 1
            return wb

        pringA = Ring(pring.bufs + [pheld, pheld2, pheld3])

        def proj(wb, m, W):
            pm = pringA.get()
            for kt in range(8):
                k.mm(pm[:, 0:W], wb[:, kt, m * 128:(m + 1) * 128], hn[:, kt, 0:W], start=(kt == 0), stop=(kt == 7))
            return pm

        def rmsnorm_tile(src3, ncol, gcol0, l, out_fn):
            pn = pring.get()
            for kt in range(8):
                sq = zring.get()
                k.act(sq[:, 0:ncol], src3[:, kt, 0:ncol], AF.Square)
                k.mm(pn[:, 0:ncol], ones_b, sq[:, 0:ncol], start=(kt == 0), stop=(kt == 7))
            t1 = fring.get()
            k.act(t1[:, 0:ncol], pn[:, 0:ncol], AF.Ln, bias=epsc[:, 0:1], scale=1.0 / D)
            rstd = fring.get()
            k.act(rstd[:, 0:ncol], t1[:, 0:ncol], AF.Exp, scale=-0.5)
            for kt in range(8):
                k.stt(out_fn(kt), src3[:, kt, 0:ncol], pp[:, l, gcol0 + kt:gcol0 + kt + 1], rstd[:, 0:ncol],
                      ALU.mult, ALU.mult)

        def frac_sincos(u, W, sn_out, cs_out):
            ui = iring.get()
            k.copy("dve", ui[:, 0:W], u)
            r = fring.get()
            k.tt("dve", r[:, 0:W], u, ui[:, 0:W], ALU.subtract)
            k.act(sn_out, r[:, 0:W], AF.Sin, scale=TWO_PI)
            ar = fring.get()
            k.stt(ar[:, 0:W], r[:, 0:W], -1.0, r[:, 0:W], ALU.mult, ALU.max)
            k.act(cs_out, ar[:, 0:W], AF.Sin, bias=epsc[:, 1:2], scale=-TWO_PI)

        epsc = sb("epsc", [128, 4], F32)
        k.memset("dve", epsc[:, 0:1], EPS)
        k.memset("dve", epsc[:, 1:2], math.pi / 2.0)
        k.memset("dve", epsc[:, 2:3], 1.0)

        def load_x(l_, ti_):
            c0_, W_, _ = tiles[ti_]
            if l_ == 0:
                k.dma("sp", F8[:, :, 0:W_], xin.rearrange("k p t -> p k t")[:, :, c0_:c0_ + W_])
            else:
                k.dma("sp", F8[:, :, 0:W_], xsc[(l_ - 1) % 2].rearrange("k p t -> p k t")[:, :, c0_:c0_ + W_],
                      rk=["xsc%d_%d" % ((l_ - 1) % 2, ti_)])

        def conv_group(l, pms, W, smp, specs):
            n = len(pms)
            accs = [fring.get() for _ in range(n)]
            raws = [fring.get_full() for _ in range(n)]
            if smp:
                rawv = [r_[:, 0:NSEQ * (LS + 3)].rearrange("p (s t) -> p s t", s=NSEQ) for r_ in raws]
                pmv = [p_[:, 0:W].rearrange("p (s t) -> p s t", s=NSEQ) for p_ in pms]
                accv = [a_[:, 0:W].rearrange("p (s t) -> p s t", s=NSEQ) for a_ in accs]
                for i in range(n):
                    k.copy("dve", rawv[i][:, :, 0:3], specs[i][3])
                for i in range(n):
                    k.copy("act", rawv[i][:, :, 3:3 + LS], pmv[i])
                    k.act(accv[i], pmv[i], AF.Identity, bias=pp[:, l, specs[i][1]:specs[i][1] + 1],
                          scale=pp[:, l, specs[i][0] + 3:specs[i][0] + 4])
                for i in range(n):
                    st3 = tring.get()
                    st3v = st3[:, 0:48].rearrange("p (s t) -> p s t", s=NSEQ)
                    k.copy("dve", st3v, rawv[i][:, :, LS:LS + 3])
                    k.dma("sp", specs[i][4], st3v)
                for kk in range(3):
                    for i in range(n):
                        k.stt(accv[i], rawv[i][:, :, kk:kk + LS], pp[:, l, specs[i][0] + kk:specs[i][0] + kk + 1],
                              accv[i], ALU.mult, ALU.add)
            else:
                for i in range(n):
                    k.copy("dve", raws[i][:, 0:3], specs[i][2])
                for i in range(n):
                    k.copy("act", raws[i][:, 3:3 + W], pms[i][:, 0:W])
                    k.act(accs[i][:, 0:W], pms[i][:, 0:W], AF.Identity, bias=pp[:, l, specs[i][1]:specs[i][1] + 1],
                          scale=pp[:, l, specs[i][0] + 3:specs[i][0] + 4])
                for kk in range(3):
                    for i in range(n):
                        k.stt(accs[i][:, 0:W], raws[i][:, kk:kk + W], pp[:, l, specs[i][0] + kk:specs[i][0] + kk + 1],
                              accs[i][:, 0:W], ALU.mult, ALU.add)
                for i in range(n):
                    k.copy("dve", specs[i][2], raws[i][:, W:W + 3])
            return accs

        def ssd_stage(l, ti, W, smp):
            xc = A16
            nch = W // 128
            TRI = tri_s if smp else tri_p
            ONESM = ones_s if smp else ones_f
            NEG = neg_s if smp else neg_p
            TRIB = tri_b2[:, 128:256] if smp else tri_b2[:, 0:128]
            for ci in range(nch):
                cs = slice(ci * 128, (ci + 1) * 128)
                dtA = dtA_tm[:, ci, :]
                dtc = dt_tm[:, ci, :]
                nacum = pre[:, 1, ci * 16:(ci + 1) * 16]
                w_tm = pre[:, 3, ci * 16:(ci + 1) * 16]
                DEC = pre[:, 4, ci * 16:(ci + 1) * 16]
                k.cp("ssd A acum")
                psc = pheld
                for g in range(4):
                    k.mm(psc[:, g * 128:(g + 1) * 128], xc[:, 8 + g, cs], xc[:, 12 + g, cs])
                k.cp("ssd B scores")
                for j in range(8):
                    k.tr(ptb[:, j * 128:(j + 1) * 128], xc[:, j, cs], ident_b)
                k.cp("T1 xtr")
                k.copy("act", x_tm[:], ptb[:, :])
                k.cp("T2 xcopy")
                k.tt("dve", hnew[:].rearrange("p (h q) -> p h q", h=16), x_tm[:].rearrange("p (h q) -> p h q", h=16),
                     w_tm[:, 0:16].unsqueeze(2).to_broadcast([128, 16, 64]), ALU.mult)
                k.copy("act", xw_tm[:], hnew[:])
                k.cp("T3 xw")
                for g in range(4):
                    k.tr(ptb[:, g * 128:(g + 1) * 128], xc[:, 8 + g, cs], ident_b)
                k.cp("T4 btr")
                k.copy("act", B_tm[:], ptb[:, 0:512])
                k.cp("ssd C transposes")
                for g in range(4):
                    pab = pring.get()
                    for r in range(4):
                        h = 4 * g + r
                        cc = ci * 16 + h
                        k.mm(pab[:, r * 128:(r + 1) * 128], dhl[:, 0, cc:cc + 1].to_broadcast([128, 128]), TRIB,
                             start=True, stop=False)
                        k.mm(pab[:, r * 128:(r + 1) * 128], dhl[:, 1, cc:cc + 1].to_broadcast([128, 128]), TRIB,
                             start=False, stop=False)
                        k.mm(pab[:, r * 128:(r + 1) * 128], ident_b, NEG, start=False, stop=True)
                    for r in range(4):
                        h = 4 * g + r
                        Dh = sring.get()
                        k.act(Dh[:], pab[:, r * 128:(r + 1) * 128], AF.Exp, bias=nacum[:, h:h + 1])
                        k.stt(LT[:, h, :], Dh[:], dt_tm[:, ci, h:h + 1], psc[:, g * 128:(g + 1) * 128],
                              ALU.mult, ALU.mult)
                k.cp("ssd D LT")
                if smp:
                    EAs = [fring.get(), fring.get()]
                    k.copy("dve", dtA_rep[:].rearrange("p (h q) -> p h q", h=16),
                           dtA_tm[:, ci, :].unsqueeze(2).to_broadcast([128, 16, 64]))
                    for j in range(8):
                        pe_ = pring.get()
                        k.mm(pe_[:, 0:128], dtA_rep[:, j * 128:(j + 1) * 128], TRI)
                        k.act(EAs[j // 4][:, (j % 4) * 128:(j % 4 + 1) * 128], pe_[:, 0:128], AF.Exp)
                    pdS = pring.get()
                    for b_ in range(NSEQ):
                        k.mm(pdS[:, b_ * 16:(b_ + 1) * 16], ind[:, b_:b_ + 1].to_broadcast([128, 128]), dtA)
                    DECS = fring.get()
                    k.act(DECS[:, 0:256], pdS[:, 0:256], AF.Exp)
                    hx = Ring([h0f, hnew])
                    bufs = [hx.get()]
                    k.dma("sp", bufs[0][:], h0T_d[l, 0])
                    for b_ in range(NSEQ):
                        hb = bufs[b_]
                        if b_ + 1 < NSEQ:
                            nb_ = hx.get()
                            bufs.append(nb_)
                            k.dma("sp", nb_[:], h0T_d[l, b_ + 1])
                        h0b = h0bring.get()
                        k.copy("act", h0b[:], hb[:])
                        for j in range(8):
                            pyo = pheld2 if j < 4 else pheld3
                            jj = j % 4
                            k.mm(pyo[:, jj * 128 + b_ * 8: jj * 128 + b_ * 8 + 8], h0b[:, j * 128:(j + 1) * 128],
                                 xc[:, 12 + j // 2, b_ * 8:(b_ + 1) * 8])
                        Bm = bmring.get()
                        k.ts("dve", Bm[:], B_tm[:], ind[:, b_:b_ + 1], None, ALU.mult)
                        pS0 = pring.get()
                        pS1 = pring.get()
                        for g in range(4):
                            pS = pS0 if g < 2 else pS1
                            k.mm(pS[:, (g % 2) * 256:(g % 2 + 1) * 256], Bm[:, g * 128:(g + 1) * 128],
                                 xw_tm[:, g * 256:(g + 1) * 256])
                        hb3 = hb[:].rearrange("p (h q) -> p h q", h=16)
                        k.tt("dve", hb3, hb3, DECS[:, b_ * 16:(b_ + 1) * 16].unsqueeze(2).to_broadcast([128, 16, 64]),
                             ALU.mult)
                        k.tt("dve", hb[:, 0:512], hb[:, 0:512], pS0[:, :], ALU.add)
                        k.tt("dve", hb[:, 512:1024], hb[:, 512:1024], pS1[:, :], ALU.add)
                        k.dma("sp", ssd_s_o[l, b_], hb[:])
                    yo0 = fring.get()
                    yo1 = fring.get()
                    k.copy("act", yo0[:], pheld2[:, :])
                    k.copy("act", yo1[:], pheld3[:, :])
                if not smp:
                    k.copy("dve", dtA_rep[:].rearrange("p (h q) -> p h q", h=16),
                           dtA_tm[:, ci, :].unsqueeze(2).to_broadcast([128, 16, 64]))
                for j in range(8):
                    py = pring.get()
                    if not smp:
                        k.mm(py[:, 256:384], dtA_rep[:, j * 128:(j + 1) * 128], TRI)
                    k.mm(py[0:64, 0:128], x_tm[:, (2 * j) * 64:(2 * j + 1) * 64], LT[:, 2 * j, :])
                    k.mm(py[64:128, 0:128], x_tm[:, (2 * j + 1) * 64:(2 * j + 2) * 64], LT[:, 2 * j + 1, :])
                    tmp = sring.get()
                    if smp:
                        yo = (yo0 if j < 4 else yo1)[:, (j % 4) * 128:(j % 4 + 1) * 128]
                        k.tt("dve", tmp[:], yo, EAs[j // 4][:, (j % 4) * 128:(j % 4 + 1) * 128], ALU.mult)
                    else:
                        k.mm(py[:, 128:256], hT_bf[:, j * 128:(j + 1) * 128], xc[:, 12 + j // 2, cs])
                        EA = sring.get()
                        k.act(EA[:], py[:, 256:384], AF.Exp)
                        k.tt("dve", tmp[:], py[:, 128:256], EA[:], ALU.mult)
                    k.defer(lambda j=j, py=py, tmp=tmp, cs=cs: k.tt("dve", F8[:, j, cs], py[:, 0:128], tmp[:], ALU.add),
                            depth=1)
                k.flush()
                k.cp("ssd E y")
                if not smp:
                    for g in range(4):
                        pS = pheld2 if g < 2 else pheld3
                        k.mm(pS[:, (g % 2) * 256:(g % 2 + 1) * 256], B_tm[:, g * 128:(g + 1) * 128],
                             xw_tm[:, g * 256:(g + 1) * 256])
                    hT3 = hT[:].rearrange("p (h q) -> p h q", h=16)
                    k.tt("dve", hT3, hT3, DEC[:, 0:16].unsqueeze(2).to_broadcast([128, 16, 64]), ALU.mult)
                    k.tt("dve", hT[:, 0:512], hT[:, 0:512], pheld2[:, :], ALU.add)
                    k.tt("dve", hT[:, 512:1024], hT[:, 512:1024], pheld3[:, :], ALU.add)
                    k.copy("act", hT_bf[:], hT[:])
            k.cp("ssd F chunks done")
            pn = pring.get()
            for j in range(8):
                k.stt(F8[:, j, 0:W], xc[:, j, 0:W], pp[:, l, PP_SD + j:PP_SD + j + 1], F8[:, j, 0:W], ALU.mult, ALU.add)
            for j in range(8):
                k.tt("dve", F8[:, j, 0:W], F8[:, j, 0:W], zs[:, j, 0:W], ALU.mult)
            for j in range(8):
                sq = fring.get()
                sqb = sq[:, 0:WT // 2].bitcast(BF16)
                k.act(sqb[:, 0:W], F8[:, j, 0:W], AF.Square)
                k.mm(pn[:, 0:W], ones_b, sqb[:, 0:W], start=(j == 0), stop=(j == 7))
            t1 = fring.get()
            k.act(t1[:, 0:W], pn[:, 0:W], AF.Ln, bias=epsc[:, 0:1], scale=1.0 / D)
            rstd = fring.get()
            k.act(rstd[:, 0:W], t1[:, 0:W], AF.Exp, scale=-0.5)
            for j in range(8):
                k.stt(Y[:, j, 0:W], F8[:, j, 0:W], pp[:, l, PP_NG2 + j:PP_NG2 + j + 1], rstd[:, 0:W], ALU.mult, ALU.mult)
            if (not smp) and ti == len(tiles) - 2:
                k.dma("sp", ssd_p_o[l], hT[:])

        def lru_group(l, js, accs, W, smp):
            n = len(js)
            xrs, prs, pgs, rs, gis, as_ = [], [], [], [], [], []
            for i in range(n):
                xr_bf = zring.get()
                k.copy("act", xr_bf[:, 0:W], accs[i][:, 0:W])
                xrs.append(xr_bf)
            for i in range(n):
                pr = pring.get()
                k.mm(pr[:, 0:W], wa_bf[:, js[i], :], xrs[i][:, 0:W])
                pg = pring.get()
                k.mm(pg[:, 0:W], wx_bf[:, js[i], :], xrs[i][:, 0:W])
                prs.append(pr)
                pgs.append(pg)
            for i in range(n):
                j = js[i]
                r = fring.get()
                k.act(r[:, 0:W], prs[i][:, 0:W], AF.Sigmoid, bias=pp[:, l, PP_LBA + j:PP_LBA + j + 1])
                gi = fring.get()
                k.act(gi[:, 0:W], pgs[i][:, 0:W], AF.Sigmoid, bias=pp[:, l, PP_LBX + j:PP_LBX + j + 1])
                rs.append(r)
                gis.append(gi)
            for i in range(n):
                a = f8ring.get()
                k.act(a[:, 0:W], rs[i][:, 0:W], AF.Exp, scale=lp[:, 0, js[i]:js[i] + 1])
                as_.append(a)
            for i in range(n):
                k.tt("dve", rs[i][:, 0:W], as_[i][:, 0:W], as_[i][:, 0:W], ALU.mult)
            for i in range(n):
                k.ts("dve", rs[i][:, 0:W], rs[i][:, 0:W], 1.0, -1.0, ALU.min, ALU.mult)
            for i in range(n):
                k.act(rs[i][:, 0:W], rs[i][:, 0:W], AF.Sqrt, bias=epsc[:, 2:3])
            for i in range(n):
                k.tt("dve", gis[i][:, 0:W], gis[i][:, 0:W], rs[i][:, 0:W], ALU.mult)
            for i in range(n):
                k.tt("dve", gis[i][:, 0:W], gis[i][:, 0:W], accs[i][:, 0:W], ALU.mult)
            hss = []
            for i in range(n):
                j = js[i]
                a = as_[i]
                gi = gis[i]
                hs = f8ring.get()
                if smp:
                    am = f8ring.get()
                    k.tt("dve", am[:, 0:W], a[:, 0:W], notstart, ALU.mult)
                    t = tring.get()
                    k.tt("dve", t[:, 0:16], a[:, 0:W:LS], lruh0[:, j, :], ALU.mult)
                    k.tt("dve", gi[:, 0:W:LS], gi[:, 0:W:LS], t[:, 0:16], ALU.add)
                    k.scan(hs[:, 0:W], am[:, 0:W], gi[:, 0:W], 0.0)
                else:
                    k.scan(hs[:, 0:W], a[:, 0:W], gi[:, 0:W], pcar_h[:, j:j + 1])
                hss.append(hs)
            for i in range(n):
                j = js[i]
                if smp:
                    k.copy("dve", hcar[:, j, :], hss[i][:, LS - 1:W:LS])
                else:
                    k.copy("dve", pcar_h[:, j:j + 1], hss[i][:, W - 1:W])
                k.tt("dve", Y[:, 8 + j, 0:W], hss[i][:, 0:W], A16[:, j, 0:W], ALU.mult)

        def s5_stage(l, ti, c0, W, smp):
            last_p = (not smp) and ti == len(tiles) - 2
            if not smp:
                k.ts("dve", iota_c[:, 0:W], iota[:, 0:W], float(c0), None, ALU.add)
            tsrc = iota_s if smp else iota_c[:, 0:W]
            tabring = Ring([A16[:, 0, :], A16[:, 1, :], A16[:, 2, :], A16[:, 3, :], Y[:, 14, :], Y[:, 15, :]])
            srring = Ring([F8[:, 0, :], F8[:, 2, :]])
            siring = Ring([F8[:, 1, :], F8[:, 3, :]])
            tprod = [zs[:, i, :] for i in range(4)]
            mprod = [zs[:, 4 + i, :] for i in range(4)]
            Srb = Y[:, 12, :]
            Sib = Y[:, 13, :]
            pGr = pheld2
            pGi = pheld3

            def tables(pr_):
                thc = lp[:, 1, pr_:pr_ + 1]
                u = fring.get()
                ui = iring.get()
                k.ts("dve", ui[:, 0:W], tsrc, thc, None, ALU.mult)
                k.stt(u[:, 0:W], tsrc, thc, ui[:, 0:W], ALU.mult, ALU.subtract)
                uf = fring.get()
                sn = tabring.get()
                cs = tabring.get()
                k.act(sn[:, 0:W], u[:, 0:W], AF.Sin, scale=TWO_PI)
                k.act(uf[:, 0:W], u[:, 0:W], AF.Abs)
                k.act(cs[:, 0:W], uf[:, 0:W], AF.Sin, bias=epsc[:, 1:2], scale=-TWO_PI)
                return sn, cs

            def make_tail(pr_, q, py5, sn, cs, Sr, Si):
                def tail():
                    m1, m2, m3, m4 = mprod
                    k.tt("pool", m1[:, 0:W], cs[:, 0:W], Srb[:, 0:W], ALU.mult)
                    k.tt("pool", m2[:, 0:W], sn[:, 0:W], Sib[:, 0:W], ALU.mult)
                    k.tt("pool", m3[:, 0:W], cs[:, 0:W], Sib[:, 0:W], ALU.mult)
                    k.tt("pool", m4[:, 0:W], sn[:, 0:W], Srb[:, 0:W], ALU.mult)
                    k.mm(py5[:, 0:W], CT_re[:, pr_, :], m1[:, 0:W], start=(q == 0), stop=False)
                    k.mm(py5[:, 0:W], nCT_re[:, pr_, :], m2[:, 0:W], start=False, stop=False)
                    k.mm(py5[:, 0:W], nCT_im[:, pr_, :], m3[:, 0:W], start=False, stop=False)
                    k.mm(py5[:, 0:W], nCT_im[:, pr_, :], m4[:, 0:W], start=False, stop=(q == 3))
                    if smp or last_p:
                        if smp:
                            sel = slice(LS - 1, W, LS)
                            n_ = NSEQ
                            dr = s5o_r[:, pr_, :]
                            di = s5o_i[:, pr_, :]
                        else:
                            sel = slice(W - 1, W)
                            n_ = 1
                            dr = s5p_r[:, pr_:pr_ + 1]
                            di = s5p_i[:, pr_:pr_ + 1]
                        ta = tring.get()
                        tb2 = tring.get()
                        tc2 = tring.get()
                        td2 = tring.get()
                        k.tt("dve", ta[:, 0:n_], cs[:, sel], Sr[:, sel], ALU.mult)
                        k.tt("dve", tb2[:, 0:n_], sn[:, sel], Si[:, sel], ALU.mult)
                        k.tt("dve", tc2[:, 0:n_], cs[:, sel], Si[:, sel], ALU.mult)
                        k.tt("dve", td2[:, 0:n_], sn[:, sel], Sr[:, sel], ALU.mult)
                        k.tt("dve", dr, ta[:, 0:n_], tb2[:, 0:n_], ALU.subtract)
                        k.tt("dve", di, tc2[:, 0:n_], td2[:, 0:n_], ALU.add)
                return tail

            nxt = tables(0)
            prev_tail = None
            for kt in range(4):
                py5 = pheld
                ub = A16[:, 8 + kt, 0:W]
                for q in range(4):
                    pr_ = kt * 4 + q
                    pbr = pring.get()
                    k.mm(pbr[:, 0:W], BbT_re[:, pr_, :], ub)
                    pbi = pring.get()
                    k.mm(pbi[:, 0:W], BbT_im[:, pr_, :], ub)
                    sn, cs = nxt
                    if pr_ + 1 < 16:
                        nxt = tables(pr_ + 1)
                    t1, t2, t3, t4 = tprod
                    k.tt("dve", t1[:, 0:W], pbr[:, 0:W], cs[:, 0:W], ALU.mult)
                    k.tt("dve", t2[:, 0:W], pbi[:, 0:W], sn[:, 0:W], ALU.mult)
                    k.tt("dve", t3[:, 0:W], pbi[:, 0:W], cs[:, 0:W], ALU.mult)
                    k.tt("dve", t4[:, 0:W], pbr[:, 0:W], sn[:, 0:W], ALU.mult)
                    k.mm(pGr[:, 0:W], ident_b, t1[:, 0:W], start=True, stop=False)
                    k.mm(pGr[:, 0:W], ident_b, t2[:, 0:W], start=False, stop=True)
                    k.mm(pGi[:, 0:W], ident_b, t3[:, 0:W], start=True, stop=False)
                    k.mm(pGi[:, 0:W], nident_b[:], t4[:, 0:W], start=False, stop=True)
                    if prev_tail is not None:
                        prev_tail()
                        prev_tail = None
                    Sr = srring.get()
                    Si = siring.get()
                    if smp:
                        ar_ = lp[:, 3, pr_:pr_ + 1]
                        ai_ = lp[:, 4, pr_:pr_ + 1]
                        h0r = s5h0_r[:, pr_, :]
                        h0i = s5h0_i[:, pr_, :]
                        tb = tring.get()
                        k.ts("dve", tb[:, 0:16], h0i, ai_, None, ALU.mult)
                        injr = tring.get()
                        k.stt(injr[:, 0:16], h0r, ar_, tb[:, 0:16], ALU.mult, ALU.subtract)
                        tc = tring.get()
                        k.ts("dve", tc[:, 0:16], h0r, ai_, None, ALU.mult)
                        inji = tring.get()
                        k.stt(inji[:, 0:16], h0i, ar_, tc[:, 0:16], ALU.mult, ALU.add)
                        k.tt("dve", pGr[:, 0:W:LS], pGr[:, 0:W:LS], injr[:, 0:16], ALU.add)
                        k.tt("dve", pGi[:, 0:W:LS], pGi[:, 0:W:LS], inji[:, 0:16], ALU.add)
                        magm = sring.get()
                        k.ts("dve", magm[:], notstart, lp[:, 2, pr_:pr_ + 1], None, ALU.mult)
                        k.scan(Sr[:, 0:W], magm[:], pGr[:, 0:W], 0.0)
                        k.scan(Si[:, 0:W], magm[:], pGi[:, 0:W], 0.0)
                    else:
                        magb = lp[:, 2, pr_:pr_ + 1].to_broadcast([128, W])
                        k.scan(Sr[:, 0:W], magb, pGr[:, 0:W], s5car_r[:, pr_:pr_ + 1])
                        k.scan(Si[:, 0:W], magb, pGi[:, 0:W], s5car_i[:, pr_:pr_ + 1])
                        k.copy("dve", s5car_r[:, pr_:pr_ + 1], Sr[:, W - 1:W])
                        k.copy("dve", s5car_i[:, pr_:pr_ + 1], Si[:, W - 1:W])
                    k.copy("act", Srb[:, 0:W], Sr[:, 0:W])
                    k.copy("act", Sib[:, 0:W], Si[:, 0:W])
                    prev_tail = make_tail(pr_, q, py5, sn, cs, Sr, Si)
                    if q == 3:
                        prev_tail()
                        prev_tail = None
                k.stt(F8[:, 4 + kt, 0:W], ub, pp[:, l, PP_S5D + kt:PP_S5D + kt + 1], py5[:, 0:W], ALU.mult, ALU.add)
            x2s = [fring.get() for _ in range(4)]
            for kt in range(4):
                k.act(x2s[kt][:, 0:W], F8[:, 4 + kt, 0:W], AF.Square)
            for kt in range(4):
                k.ts("dve", x2s[kt][:, 0:W], x2s[kt][:, 0:W], 0.044715, 1.0, ALU.mult, ALU.add)
            for kt in range(4):
                k.tt("dve", x2s[kt][:, 0:W], x2s[kt][:, 0:W], F8[:, 4 + kt, 0:W], ALU.mult)
            for kt in range(4):
                k.act(x2s[kt][:, 0:W], x2s[kt][:, 0:W], AF.Sigmoid, scale=1.5957691216057308)
            for kt in range(4):
                k.tt("dve", A16[:, 12 + kt, 0:W], F8[:, 4 + kt, 0:W], x2s[kt][:, 0:W], ALU.mult)
            sgs = []
            for m in range(4):
                pg = pring.get()
                for kt in range(4):
                    k.mm(pg[:, 0:W], glu_bf[:, kt, m * 128:(m + 1) * 128], A16[:, 12 + kt, 0:W],
                         start=(kt == 0), stop=(kt == 3))
                sg = fring.get()
                k.act(sg[:, 0:W], pg[:, 0:W], AF.Sigmoid, bias=pp[:, l, PP_GLB + m:PP_GLB + m + 1])
                sgs.append(sg)
            for m in range(4):
                k.tt("dve", sgs[m][:, 0:W], sgs[m][:, 0:W], A16[:, 12 + m, 0:W], ALU.mult)
            for m in range(4):
                k.tt("dve", Y[:, 12 + m, 0:W], sgs[m][:, 0:W], A16[:, 4 + m, 0:W], ALU.mult)

        def convert_w_in(l_):
            for bi_, (_, c0_, n_) in enumerate(blocks):
                k.dma("pool", wbf[l_][:, :, c0_:c0_ + n_], w_in[l_][:, :, c0_:c0_ + n_],
                      wk=["wbf%d_%d" % (l_, bi_)])

        def convert_w_out(l_):
            for m_ in range(8):
                k.dma("pool", wobf[l_, m_], w_out[l_][:, :, m_ * 128:(m_ + 1) * 128], wk=["wobf%d_%d" % (l_, m_)])

        for l in range(nlayers):
            k.dma("pool", glu_bf[:], glu_w[l])
            k.dma("pool", wa_bf[:], wa_bd[l])
            k.dma("pool", wx_bf[:], wx_bd[l])
            k.dma("pool", CT_re[:], ctpad_re[l])
            k.dma("pool", nCT_im[:], ctpad_im[l])
            if l == 0:
                convert_w_in(0)
            k.act(nCT_re[:].rearrange("p a b -> p (a b)"), CT_re[:].rearrange("p a b -> p (a b)"), AF.Copy, scale=-1.0)
            k.act(nCT_im[:].rearrange("p a b -> p (a b)"), nCT_im[:].rearrange("p a b -> p (a b)"), AF.Copy, scale=-1.0)
            k.act(expA[:], pdt[:, l, 16:32], AF.Exp)
            t = tring.get()
            k.act(t[:, 0:4], pp[:, l, PP_LLAM:PP_LLAM + 4], AF.Exp, scale=-1.0)
            t2 = tring.get()
            k.act(t2[:, 0:4], t[:, 0:4], AF.Ln, bias=epsc[:, 2:3])
            k.ts("dve", lp[:, 0, 0:4], t2[:, 0:4], -8.0, None, ALU.mult)
            dl = tring.get()
            k.act(dl[:, 0:16], pp[:, l, PP_LDT:PP_LDT + 16], AF.Exp)
            thp = lp[:, 1, :]
            k.stt(thp, pp[:, l, PP_LIM:PP_LIM + 16], 1.0 / TWO_PI, dl[:, 0:16], ALU.mult, ALU.mult)
            lm = tring.get()
            k.tt("dve", lm[:, 0:16], pp[:, l, PP_LRE:PP_LRE + 16], dl[:, 0:16], ALU.mult)
            mag = lp[:, 2, :]
            k.act(mag, lm[:, 0:16], AF.Exp)
            sn0 = tring.get()
            cs0 = tring.get()
            frac_sincos(thp, 16, sn0[:, 0:16], cs0[:, 0:16])
            k.tt("dve", lp[:, 3, :], mag, cs0[:, 0:16], ALU.mult)
            k.tt("dve", lp[:, 4, :], mag, sn0[:, 0:16], ALU.mult)
            lre_c = pp[:, l, PP_LRE:PP_LRE + 16]
            lim_c = pp[:, l, PP_LIM:PP_LIM + 16]
            nr = tring.get()
            k.ts("dve", nr[:, 0:16], lp[:, 3, :], -1.0, None, ALU.add)
            den = tring.get()
            t_a = tring.get()
            k.tt("dve", den[:, 0:16], lre_c, lre_c, ALU.mult)
            k.tt("dve", t_a[:, 0:16], lim_c, lim_c, ALU.mult)
            k.tt("dve", den[:, 0:16], den[:, 0:16], t_a[:, 0:16], ALU.add)
            k.op("dve", lambda v, o=den[:, 0:16]: v.reciprocal(out=o, in_=o), [den], [den])
            cre = lp[:, 5, :]
            cim = lp[:, 6, :]
            t_b = tring.get()
            k.tt("dve", cre, nr[:, 0:16], lre_c, ALU.mult)
            k.tt("dve", t_b[:, 0:16], lp[:, 4, :], lim_c, ALU.mult)
            k.tt("dve", cre, cre, t_b[:, 0:16], ALU.add)
            k.tt("dve", cre, cre, den[:, 0:16], ALU.mult)
            t_c = tring.get()
            k.tt("dve", cim, lp[:, 4, :], lre_c, ALU.mult)
            k.tt("dve", t_c[:, 0:16], nr[:, 0:16], lim_c, ALU.mult)
            k.tt("dve", cim, cim, t_c[:, 0:16], ALU.subtract)
            k.tt("dve", cim, cim, den[:, 0:16], ALU.mult)
            for c4 in range(4):
                bre = fring.get()
                bim = fring.get()
                k.dma("sp", bre[:].rearrange("p (a b) -> p a b", a=4), bpad_re[l][:, c4 * 4:(c4 + 1) * 4, :])
                k.dma("sp", bim[:].rearrange("p (a b) -> p a b", a=4), bpad_im[l][:, c4 * 4:(c4 + 1) * 4, :])
                crb = cre[:, c4 * 4:(c4 + 1) * 4].unsqueeze(2).to_broadcast([128, 4, 128])
                cib = cim[:, c4 * 4:(c4 + 1) * 4].unsqueeze(2).to_broadcast([128, 4, 128])
                v3 = lambda ap_: ap_.rearrange("p (a b) -> p a b", a=4)
                m_a, m_b, m_c, m_d = [f8ring.get() for _ in range(4)]
                k.tt("dve", v3(m_a), v3(bre[:]), crb, ALU.mult)
                k.tt("dve", v3(m_b), v3(bim[:]), cib, ALU.mult)
                k.tt("dve", v3(m_c), v3(bre[:]), cib, ALU.mult)
                k.tt("dve", v3(m_d), v3(bim[:]), crb, ALU.mult)
                o_re = zring.get()
                o_im = zring.get()
                k.tt("dve", o_re, m_a, m_b, ALU.subtract)
                k.tt("dve", o_im, m_c, m_d, ALU.add)
                for i4 in range(4):
                    k.tr(ptb[:, i4 * 128:(i4 + 1) * 128], o_re[:, i4 * 128:(i4 + 1) * 128], ident_b)
                    k.tr(ptb[:, 512 + i4 * 128:512 + (i4 + 1) * 128], o_im[:, i4 * 128:(i4 + 1) * 128], ident_b)
                k.copy("act", BbT_re[:, c4 * 4:(c4 + 1) * 4, :].rearrange("p a b -> p (a b)"), ptb[:, 0:512])
                k.copy("act", BbT_im[:, c4 * 4:(c4 + 1) * 4, :].rearrange("p a b -> p (a b)"), ptb[:, 512:1024])
            k.dma("sp", carry_s[:], sconv0[l])
            k.dma("sp", carry_l[:], lconv0[l])
            k.dma("sp", lruh0[:], lru0[l])
            k.dma("sp", s5h0_r[:], s5r0[l])
            k.dma("sp", s5h0_i[:], s5i0[l])
            k.memset("dve", hT[:], 0.0)
            k.memset("dve", hT_bf[:], 0.0)
            k.memset("dve", pcar_s[:], 0.0)
            k.memset("dve", pcar_l[:], 0.0)
            k.memset("dve", pcar_h[:], 0.0)
            k.memset("dve", s5car_r[:], 0.0)
            k.memset("dve", s5car_i[:], 0.0)

            if l == 0:
                prefetch_w(1)
            k.cp("setup done l%d" % l)
            for ti, (c0, W, kind) in enumerate(tiles):
                smp = kind == "s"
                k.cp("tile start l%d t%d" % (l, ti))
                if ti == 0:
                    load_x(l, ti)
                    rmsnorm_tile(F8, W, PP_NG, l, lambda kt: hn[:, kt, 0:W])
                for kt in range(8):
                    k.copy("pool", xt[:, kt, 0:W], F8[:, kt, 0:W])
                k.cp("norm done")
                for zb in range(2):
                    wb = next_block()
                    for m in range(4):
                        pm = proj(wb, m, W)
                        k.act(zs[:, zb * 4 + m, 0:W], pm[:, 0:W], AF.Silu)
                for xb in range(4):
                    wb = next_block()
                    for half in range(2):
                        js = [xb * 4 + half * 2 + i for i in range(2)]
                        pms = [proj(wb, half * 2 + i, W) for i in range(2)]
                        accs = conv_group(l, pms, W, smp,
                                          [(PP_CW + 4 * j, PP_CB + j, pcar_s[:, j, :], carry_s[:, j, :, :],
                                            sconv_s_o[l][:, j, :, :]) for j in js])

                        def silus(js=js, accs=accs):
                            for j, acc in zip(js, accs):
                                k.act(A16[:, j, 0:W], acc[:, 0:W], AF.Silu)
                        k.defer(silus, depth=1)
                wb = next_block()
                k.flush()
                nchk = W // 128
                pd = pring.get()
                for ci in range(nchk):
                    for kt in range(8):
                        k.mm(pd[:, ci * 16:(ci + 1) * 16], hn[:, kt, ci * 128:(ci + 1) * 128], wb[:, kt, 0:16],
                             start=(kt == 0), stop=(kt == 7))
                dt2 = dt_tm[:, 0:nchk, :]
                dtA2 = dtA_tm[:, 0:nchk, :]
                pd3 = pd[:, 0:nchk * 16].rearrange("p (c h) -> p c h", c=nchk)
                v = pre[:, 6, 0:nchk * 16].rearrange("p (c h) -> p c h", c=nchk)
                k.tt("dve", v, pd3, pdt[:, l, 0:16].unsqueeze(1).to_broadcast([128, nchk, 16]), ALU.add)
                k.act(v, v, AF.Exp)
                k.act(dt2, v, AF.Ln, bias=epsc[:, 2:3])
                k.stt(dtA2, dt2, -1.0, expA[:].unsqueeze(1).to_broadcast([128, nchk, 16]), ALU.mult, ALU.mult)
                hi_f = pre[:, 5, 0:nchk * 16]
                k.copy("act", dhl[:, 0, 0:nchk * 16], dtA_tm[:, 0:nchk, :].rearrange("p c h -> p (c h)"))
                k.copy("act", hi_f, dhl[:, 0, 0:nchk * 16])
                k.tt("dve", hi_f, dtA_tm[:, 0:nchk, :].rearrange("p c h -> p (c h)"), hi_f, ALU.subtract)
                k.copy("act", dhl[:, 1, 0:nchk * 16], hi_f)
                TRI_ = tri_s if smp else tri_p
                ONESM_ = ones_s if smp else ones_f
                n16 = nchk * 16
                pa = pring.get()
                dtA_flat = dtA_tm[:, 0:nchk, :].rearrange("p c h -> p (c h)")
                k.mm(pa[:, 0:n16], TRI_, dtA_flat)
                k.mm(pa[:, 64:64 + n16], ONESM_, dtA_flat)
                k.copy("act", pre[:, 0, 0:n16], pa[:, 0:n16])
                k.ts("dve", pre[:, 1, 0:n16], pa[:, 0:n16], -1.0, None, ALU.mult)
                k.tt("dve", pre[:, 2, 0:n16], pa[:, 64:64 + n16], pre[:, 0, 0:n16], ALU.subtract)
                k.act(pre[:, 2, 0:n16], pre[:, 2, 0:n16], AF.Exp)
                k.tt("dve", pre[:, 3, 0:n16], pre[:, 2, 0:n16], dt_tm[:, 0:nchk, :].rearrange("p c h -> p (c h)"),
                     ALU.mult)
                k.act(pre[:, 4, 0:n16], pa[:, 64:64 + n16], AF.Exp)
                if l == 0 and ti == 0:
                    convert_w_out(0)
                k.cp("stage A done")
                ssd_stage(l, ti, W, smp)
                k.cp("ssd done")
                wb = next_block()
                for m in range(4):
                    pm = proj(wb, m, W)
                    k.act(A16[:, m, 0:W], pm[:, 0:W], AF.Silu)
                wb = next_block()
                for half in range(2):
                    js = [half * 2, half * 2 + 1]
                    pms = [proj(wb, j, W) for j in js]
                    accs = conv_group(l, pms, W, smp, [(PP_LCW + 4 * j, PP_LCB + j, pcar_l[:, j, :],
                                                         carry_l[:, j, :, :], lconv_s_o[l][:, j, :, :]) for j in js])
                    lru_group(l, js, accs, W, smp)
                k.cp("lru done")
                k.flush()
                wb = next_block()
                for m in range(4):
                    pm = proj(wb, m, W)
                    k.act(A16[:, 4 + m, 0:W], pm[:, 0:W], AF.Silu)
                wb = next_block()
                for m in range(4):
                    pm = proj(wb, m, W)
                    k.copy("act", A16[:, 8 + m, 0:W], pm[:, 0:W])
                s5_stage(l, ti, c0, W, smp)
                if l + 1 < nlayers and ti == 0:
                    convert_w_in(l + 1)
                if l + 1 < nlayers and ti == 1:
                    convert_w_out(l + 1)
                k.cp("s5 done")
                wo = woring.get()
                k.dma("sp", wo[:], wobf[l, 0], rk=["wobf%d_0" % l])
                if ti + 1 < len(tiles):
                    load_x(l, ti + 1)
                for m in range(8):
                    if m + 1 < 8:
                        wo_n = woring.get()
                        k.dma("sp", wo_n[:], wobf[l, m + 1], rk=["wobf%d_%d" % (l, m + 1)])
                    po = pring.get()
                    for kt in range(16):
                        k.mm(po[:, 0:W], wo[:, kt, :], Y[:, kt, 0:W], start=(kt == 0), stop=(kt == 15))
                    k.tt("dve", xt[:, m, 0:W], po[:, 0:W], xt[:, m, 0:W], ALU.add)
                    if m == 3 and ti + 1 < len(tiles):
                        Wn = tiles[ti + 1][1]
                        rmsnorm_tile(F8, Wn, PP_NG, l, lambda kt, Wn=Wn: hn[:, kt, 0:Wn])
                    if m + 1 < 8:
                        wo = wo_n
                if l < nlayers - 1:
                    k.dma("sp", xsc[l % 2].rearrange("k p t -> p k t")[:, :, c0:c0 + W], xt[:, :, 0:W],
                          wk=["xsc%d_%d" % (l % 2, ti)])
                else:
                    rmsnorm_tile(xt, W, PP_FG, l, lambda kt: xt[:, kt, 0:W])
                    k.dma("sp", yout.rearrange("k p t -> p k t")[:, :, c0:c0 + W], xt[:, :, 0:W])
            k.dma("sp", sconv_p_o[l], pcar_s[:])
            k.dma("sp", lconv_p_o[l], pcar_l[:])
            k.dma("sp", lru_p_o[l], pcar_h[:])
            k.dma("sp", lru_s_o[l], hcar[:])
            k.dma("sp", s5r_p_o[l], s5p_r[:])
            k.dma("sp", s5i_p_o[l], s5p_i[:])
            k.dma("sp", s5r_s_o[l], s5o_r[:])
            k.dma("sp", s5i_s_o[l], s5o_i[:])
        k.finish()
        k.emit()
    return nc


def _consts():
    i = np.arange(128)
    ident = np.eye(128, dtype=np.float32)
    tri_p = (i[:, None] <= i[None, :]).astype(np.float32)
    same = (i[:, None] // LS == i[None, :] // LS)
    tri_s = (tri_p > 0) & same
    cfa = np.zeros((128, 8, 128), np.float32)
    import ml_dtypes
    trib = np.concatenate([tri_p, tri_s.astype(np.float32)], axis=1).astype(ml_dtypes.bfloat16)
    cfa[:, 0] = np.ascontiguousarray(trib).view(np.float32)
    cfa[:, 1] = tri_p
    cfa[:, 2] = tri_s
    cfa[:, 3] = 1.0
    cfa[:, 4] = same
    cfa[:, 5] = np.broadcast_to((i % LS)[None, :], (128, 128))
    cfa[:, 6] = np.broadcast_to((i % LS != 0)[None, :], (128, 128))
    cfa[:, 7, 0:NSEQ] = (i[:, None] // LS == np.arange(NSEQ)[None, :])
    cba = np.zeros((128, 4, 128), np.float32)
    cba[:, 0] = ident
    cba[:, 1] = 1.0
    cba[:, 2] = np.where(tri_p > 0, 0.0, -30000.0)
    cba[:, 3] = np.where(tri_s, 0.0, -30000.0)
    iota = np.broadcast_to(np.arange(WT, dtype=np.float32)[None, :], (128, WT)).copy()
    return cfa, cba, iota


def _fm(v, nt):
    return np.moveaxis(v.reshape(v.shape[:-1] + (nt, 128)), -1, -2)


def _prep_shared(inp):
    f = lambda a: np.ascontiguousarray(a, dtype=np.float32)
    sh = {}
    sh["w_in_r"] = f(inp["w_in"].reshape(NL, 8, 128, IN_DIM).transpose(0, 2, 1, 3))
    sh["w_out_r"] = f(inp["w_out"].reshape(NL, 16, 128, D).transpose(0, 2, 1, 3))
    sh["glu_r"] = f(inp["s5_glu_w"].reshape(NL, 4, 128, 512).transpose(0, 2, 1, 3))
    for nm, src in (("wa_bd", inp["lru_wa"]), ("wx_bd", inp["lru_wx"])):
        bd = np.zeros((NL, 128, 4, 128), np.float32)
        for m in range(4):
            for k2 in range(2):
                bd[:, k2 * 64:(k2 + 1) * 64, m, k2 * 64:(k2 + 1) * 64] = src[:, 2 * m + k2]
        sh[nm] = bd
    pp = np.zeros((128, NL, NPP), np.float32)
    for l in range(NL):
        pp[:, l, PP_NG:PP_NG + 8] = _fm(inp["norm_g"][l], 8)
        pp[:, l, PP_NG2:PP_NG2 + 8] = _fm(inp["ssd_norm_g"][l], 8)
        pp[:, l, PP_SD:PP_SD + 8] = _fm(np.repeat(inp["ssd_d"][l], 64), 8)
        pp[:, l, PP_CW:PP_CW + 64] = inp["ssd_conv_w"][l].reshape(4, 16, 128).transpose(2, 1, 0).reshape(128, 64)
        pp[:, l, PP_CB:PP_CB + 16] = _fm(inp["ssd_conv_b"][l], 16)
        pp[:, l, PP_LCW:PP_LCW + 16] = inp["lru_conv_w"][l].reshape(4, 4, 128).transpose(2, 1, 0).reshape(128, 16)
        pp[:, l, PP_LCB:PP_LCB + 4] = _fm(inp["lru_conv_b"][l], 4)
        pp[:, l, PP_LBA:PP_LBA + 4] = _fm(inp["lru_ba"][l], 4)
        pp[:, l, PP_LBX:PP_LBX + 4] = _fm(inp["lru_bx"][l], 4)
        pp[:, l, PP_LLAM:PP_LLAM + 4] = _fm(inp["lru_lambda"][l], 4)
        pp[:, l, PP_S5D:PP_S5D + 4] = _fm(inp["s5_d"][l], 4)
        pp[:, l, PP_GLB:PP_GLB + 4] = _fm(inp["s5_glu_b"][l], 4)
        pp[:, l, PP_LRE:PP_LRE + 16] = inp["s5_lambda_re"][l].reshape(16, 128).T
        pp[:, l, PP_LIM:PP_LIM + 16] = inp["s5_lambda_im"][l].reshape(16, 128).T
        pp[:, l, PP_LDT:PP_LDT + 16] = np.repeat(inp["s5_log_dt"][l], 64).reshape(16, 128).T
        pp[:, l, PP_FG:PP_FG + 8] = _fm(inp["final_norm_g"], 8)
    sh["pp_in"] = pp
    pdt = np.zeros((128, NL, 32), np.float32)
    pdt[:, :, 0:16] = inp["ssd_dt_bias"][None]
    pdt[:, :, 16:32] = inp["ssd_a_log"][None]
    sh["pdt_in"] = pdt
    bre = np.zeros((NL, 128, 16, 128), np.float32)
    bim = np.zeros((NL, 128, 16, 128), np.float32)
    cre = np.zeros((NL, 128, 16, 128), np.float32)
    cim = np.zeros((NL, 128, 16, 128), np.float32)
    for g in range(32):
        pr, g2, gl = g // 2, g % 2, g % 8
        bre[:, g2 * 64:(g2 + 1) * 64, pr, gl * 16:(gl + 1) * 16] = inp["s5_b_re"][:, g]
        bim[:, g2 * 64:(g2 + 1) * 64, pr, gl * 16:(gl + 1) * 16] = inp["s5_b_im"][:, g]
        cre[:, g2 * 64:(g2 + 1) * 64, pr, gl * 16:(gl + 1) * 16] = inp["s5_c_re"][:, g].transpose(0, 2, 1)
        cim[:, g2 * 64:(g2 + 1) * 64, pr, gl * 16:(gl + 1) * 16] = inp["s5_c_im"][:, g].transpose(0, 2, 1)
    sh["bpadT_re"], sh["bpadT_im"], sh["ctpad_re"], sh["ctpad_im"] = bre, bim, cre, cim
    cfa, cba, iota = _consts()
    sh["c_f32"], sh["c_bf"], sh["c_iota"] = cfa, cba, iota
    return sh


def _prep_core(inp, c):
    f = lambda a: np.ascontiguousarray(a, dtype=np.float32)
    sl = slice(NSEQ * c, NSEQ * (c + 1))
    m = {}
    x_tok = np.concatenate([inp["x_prompt"][c], inp["x_sample"][sl].reshape(TS, D)], axis=0)
    m["xin"] = f(x_tok.T.reshape(8, 128, TT))
    m["h0T"] = f(inp["state_ssd"][:, sl].transpose(0, 1, 4, 2, 3).reshape(NL, NSEQ, 128, 1024))
    m["sconv0"] = f(inp["state_ssd_conv"][:, sl].reshape(NL, NSEQ, 3, 16, 128).transpose(0, 4, 3, 1, 2))
    m["lconv0"] = f(inp["state_lru_conv"][:, sl].reshape(NL, NSEQ, 3, 4, 128).transpose(0, 4, 3, 1, 2))
    m["lru0"] = f(inp["state_lru"][:, sl].reshape(NL, NSEQ, 4, 128).transpose(0, 3, 2, 1))
    m["s5r0"] = f(inp["state_s5_re"][:, sl].reshape(NL, NSEQ, 16, 128).transpose(0, 3, 2, 1))
    m["s5i0"] = f(inp["state_s5_im"][:, sl].reshape(NL, NSEQ, 16, 128).transpose(0, 3, 2, 1))
    return m


_PROG = {}


def kernel(**inputs):
    inp = {k_: np.asarray(v) for k_, v in inputs.items()}
    if "nc" not in _PROG:
        _PROG["nc"] = build_program(NL)
    nc = _PROG["nc"]
    shared = _prep_shared(inp)
    in_maps = []
    for c in range(NCORES):
        m = dict(shared)
        m.update(_prep_core(inp, c))
        in_maps.append(m)
    res = run_bass_kernel_spmd(nc, in_maps, core_ids=list(range(NCORES)))
    R = res.results
    B = NCORES
    y_prompt = np.zeros((B, TP, D), np.float32)
    y_sample = np.zeros((B * NSEQ, LS, D), np.float32)
    ssd_p = np.zeros((NL, B, 16, 64, 128), np.float32)
    ssd_s = np.zeros((NL, B * NSEQ, 16, 64, 128), np.float32)
    ssd_conv_p = np.zeros((NL, B, 3, 2048), np.float32)
    ssd_conv_s = np.zeros((NL, B * NSEQ, 3, 2048), np.float32)
    lru_p = np.zeros((NL, B, 512), np.float32)
    lru_s = np.zeros((NL, B * NSEQ, 512), np.float32)
    lru_conv_p = np.zeros((NL, B, 3, 512), np.float32)
    lru_conv_s = np.zeros((NL, B * NSEQ, 3, 512), np.float32)
    s5_re_p = np.zeros((NL, B, 32, 64), np.float32)
    s5_re_s = np.zeros((NL, B * NSEQ, 32, 64), np.float32)
    s5_im_p = np.zeros((NL, B, 32, 64), np.float32)
    s5_im_s = np.zeros((NL, B * NSEQ, 32, 64), np.float32)
    for c in range(B):
        r = R[c]
        sl = slice(NSEQ * c, NSEQ * (c + 1))
        y = np.asarray(r["yout"]).reshape(D, TT).T
        y_prompt[c] = y[0:TP]
        y_sample[sl] = y[TP:].reshape(NSEQ, LS, D)
        ssd_p[:, c] = np.asarray(r["ssd_p_o"]).reshape(NL, 128, 16, 64).transpose(0, 2, 3, 1)
        ssd_s[:, sl] = np.asarray(r["ssd_s_o"]).reshape(NL, NSEQ, 128, 16, 64).transpose(0, 1, 3, 4, 2)
        ssd_conv_p[:, c] = np.asarray(r["sconv_p_o"]).transpose(0, 3, 2, 1).reshape(NL, 3, 2048)
        ssd_conv_s[:, sl] = np.asarray(r["sconv_s_o"]).transpose(0, 3, 4, 2, 1).reshape(NL, NSEQ, 3, 2048)
        lru_p[:, c] = np.asarray(r["lru_p_o"]).transpose(0, 2, 1).reshape(NL, 512)
        lru_s[:, sl] = np.asarray(r["lru_s_o"]).transpose(0, 3, 2, 1).reshape(NL, NSEQ, 512)
        lru_conv_p[:, c] = np.asarray(r["lconv_p_o"]).transpose(0, 3, 2, 1).reshape(NL, 3, 512)
        lru_conv_s[:, sl] = np.asarray(r["lconv_s_o"]).transpose(0, 3, 4, 2, 1).reshape(NL, NSEQ, 3, 512)
        s5_re_p[:, c] = np.asarray(r["s5r_p_o"]).transpose(0, 2, 1).reshape(NL, 32, 64)
        s5_im_p[:, c] = np.asarray(r["s5i_p_o"]).transpose(0, 2, 1).reshape(NL, 32, 64)
        s5_re_s[:, sl] = np.asarray(r["s5r_s_o"]).transpose(0, 3, 2, 1).reshape(NL, NSEQ, 32, 64)
        s5_im_s[:, sl] = np.asarray(r["s5i_s_o"]).transpose(0, 3, 2, 1).reshape(NL, NSEQ, 32, 64)
    return (y_prompt, y_sample, ssd_p, ssd_s, ssd_conv_p, ssd_conv_s, lru_p, lru_s,
            lru_conv_p, lru_conv_s, s5_re_p, s5_re_s, s5_im_p, s5_im_s)
```

```python
import math
import os
from contextlib import ExitStack

import numpy as np
import concourse.bass as bass
import concourse.mybir as mybir
from concourse.bass_utils import run_bass_kernel_spmd

F32 = mybir.dt.float32
BF16 = mybir.dt.bfloat16
I32 = mybir.dt.int32
ALU = mybir.AluOpType
AF = mybir.ActivationFunctionType

NCORES = 8
D = 1024
NL = 4
TP = 2048
NSEQ = 16
LS = 8
TS = NSEQ * LS
TT = TP + TS
WT = 512
IN_DIM = 5136
EPS = 1e-6
TWO_PI = 2.0 * math.pi

PP_NG = 0
PP_NG2 = 8
PP_SD = 16
PP_CW = 24
PP_CB = 88
PP_LCW = 104
PP_LCB = 120
PP_LBA = 124
PP_LBX = 128
PP_LLAM = 132
PP_S5D = 136
PP_GLB = 140
PP_LRE = 144
PP_LIM = 160
PP_LDT = 176
PP_FG = 192
NPP = 200

EPOCH = 12000


class Ring:
    def __init__(self, bufs, full=None):
        self.bufs = bufs
        self.full = full
        self.i = 0

    def get(self):
        b = self.bufs[self.i % len(self.bufs)]
        self.i += 1
        return b

    def get_full(self):
        b = self.full[self.i % len(self.full)]
        self.i += 1
        return b


class KB:
    ENG = ("pe", "act", "dve", "pool", "sp")

    def __init__(self, nc, es):
        self.nc = nc
        self.es = es
        self.prog = {e: [] for e in self.ENG}
        self.cnt = {e: 0 for e in self.ENG}
        self.sem = {}
        self.nsem = 0
        for e in ("pe", "act", "dve", "pool"):
            self.sem[e] = self._newsem("c_" + e)
        self.waited = {e: {} for e in self.ENG}
        self.dead = False
        self.pending = []
        self.ncp = 0
        self.stop = int(os.environ.get("KSTOP", "-1"))
        self.lastw = {}
        self.readers = {}
        self.dsem = {}
        self.drr = {}
        for q, n in (("sp", 16), ("pool", 24), ("act", 2)):
            self.dsem[q] = [[self._newsem("d_%s%d" % (q, i)), 0] for i in range(n)]
            self.drr[q] = 0

    def _newsem(self, name):
        self.nsem += 1
        return self.es.enter_context(self.nc.semaphore("%s_%d" % (name, self.nsem)))

    slotw = {}

    def _keys(self, r):
        if isinstance(r, str):
            return [r]
        name = r.name
        w = self.slotw.get(name)
        if w is None:
            return [name]
        ap = r.ap
        off = r.offset % ap[0][0]
        hi = off + sum((c - 1) * s for s, c in ap[1:])
        return ["%s:%d" % (name, i) for i in range(off // w, hi // w + 1)]

    def defer(self, fn, depth=1):
        self.pending.append(fn)
        while len(self.pending) > depth:
            self.pending.pop(0)()

    def flush(self):
        while self.pending:
            self.pending.pop(0)()

    def cp(self, name=""):
        self.ncp += 1
        if self.stop >= 0 and self.ncp > self.stop and not self.dead:
            self.dead = True
            print("KSTOP: program truncated before checkpoint", self.ncp, name, flush=True)

    def op(self, e, fn, reads=(), writes=(), dma=False):
        if self.dead:
            return None
        waits = {}

        def need(tok, raw):
            if tok is None:
                return
            sem, val, src, isdma = tok
            if src == e and not isdma and e == "pe":
                return
            if self.waited[e].get(sem.name, 0) >= val:
                return
            if sem.name not in waits or waits[sem.name][1] < val:
                waits[sem.name] = (sem, val)

        rk = [x for r in reads for x in self._keys(r)]
        wk = [x for w in writes for x in self._keys(w)]
        for r in rk:
            need(self.lastw.get(r), True)
        for w in wk:
            need(self.lastw.get(w), False)
            for t in self.readers.get(w, {}).values():
                need(t, False)
        if dma:
            slot = self.dsem[e][self.drr[e] % len(self.dsem[e])]
            self.drr[e] += 1
            if slot[1] > 0:
                need((slot[0], slot[1], e, True), True)
            slot[1] += 16
            tok = (slot[0], slot[1], e, True)
            inc = (slot[0], 16)
        else:
            if self.cnt[e] >= EPOCH:
                self.sem[e] = self._newsem("c_" + e)
                self.cnt[e] = 0
            self.cnt[e] += 1
            tok = (self.sem[e], self.cnt[e], e, False)
            inc = (self.sem[e], 1)
        for s, v in waits.values():
            self.waited[e][s.name] = v
        self.prog[e].append((list(waits.values()), fn, inc))
        for r in rk:
            self.readers.setdefault(r, {})[tok[0].name] = tok
        for w in wk:
            self.lastw[w] = tok
            self.readers[w] = {}
        return tok

    def finish(self):
        fin = []
        for q in self.dsem:
            for sem, v in self.dsem[q]:
                if v > 0:
                    fin.append((sem, v))
        self.final_waits = fin

    def emit(self):
        nc = self.nc
        handles = {"pe": "tensor", "act": "scalar", "dve": "vector", "pool": "gpsimd", "sp": "sync"}
        with nc.Block() as block:
            for e in self.ENG:
                prog = self.prog[e]
                extra = self.final_waits if e == "sp" else []

                def body(eng, prog=prog, extra=extra):
                    for waits, fn, inc in prog:
                        for s, v in waits:
                            eng.wait_ge(s, v)
                        ins = fn(eng)
                        ins.then_inc(inc[0], inc[1])
                    for s, v in extra:
                        eng.wait_ge(s, v)

                getattr(block, handles[e])(body)

    def mm(self, out, lhsT, rhs, start=True, stop=True):
        self.op("pe", lambda t: t.matmul(out, lhsT=lhsT, rhs=rhs, start=start, stop=stop),
                [lhsT, rhs], [out])

    def tr(self, out, in_, ident):
        self.op("pe", lambda t: t.transpose(out, in_, ident), [in_, ident], [out])

    def act(self, out, in_, func, bias=None, scale=None):
        rd = [in_]
        kw = {}
        if bias is not None:
            kw["bias"] = bias
            if not isinstance(bias, (int, float)):
                rd.append(bias)
        if scale is not None:
            kw["scale"] = scale
            if not isinstance(scale, (int, float)):
                rd.append(scale)
        self.op("act", lambda a: a.activation(out=out, in_=in_, func=func, **kw), rd, [out])

    def tt(self, e, out, in0, in1, op):
        self.op(e, lambda v: v.tensor_tensor(out=out, in0=in0, in1=in1, op=op), [in0, in1], [out])

    def ts(self, e, out, in0, s1, s2, op0, op1=None):
        rd = [in0]
        for s in (s1, s2):
            if s is not None and not isinstance(s, (int, float)):
                rd.append(s)
        if op1 is None:
            self.op(e, lambda v: v.tensor_scalar(out=out, in0=in0, scalar1=s1, scalar2=None, op0=op0),
                    rd, [out])
        else:
            self.op(e, lambda v: v.tensor_scalar(out=out, in0=in0, scalar1=s1, scalar2=s2, op0=op0, op1=op1),
                    rd, [out])

    def stt(self, out, in0, scalar, in1, op0, op1):
        rd = [in0, in1]
        if not isinstance(scalar, (int, float)):
            rd.append(scalar)
        self.op("dve", lambda v: v.scalar_tensor_tensor(out=out, in0=in0, scalar=scalar, in1=in1,
                                                        op0=op0, op1=op1), rd, [out])

    def scan(self, out, d0, d1, init):
        rd = [d0, d1]
        if not isinstance(init, (int, float)):
            rd.append(init)
        self.op("dve", lambda v: v.tensor_tensor_scan(out=out, data0=d0, data1=d1, initial=init,
                                                      op0=ALU.mult, op1=ALU.add), rd, [out])

    def copy(self, e, out, in_):
        if e == "act":
            self.op("act", lambda a: a.activation(out=out, in_=in_, func=AF.Copy), [in_], [out])
        else:
            self.op(e, lambda v: v.tensor_copy(out=out, in_=in_), [in_], [out])

    def memset(self, e, out, val):
        self.op(e, lambda v: v.memset(out, val), [], [out])

    def dma(self, q, out, in_, rk=(), wk=()):
        self.op(q, lambda g: g.dma_start(out=out, in_=in_), [in_] + list(rk), [out] + list(wk), dma=True)


def build_program(nlayers=NL):
    nc = bass.Bass("TRN2", target_bir_lowering=False)

    def din(name, shape, dt=F32):
        return nc.dram_tensor(name, list(shape), dt, kind="ExternalInput").ap()

    def dout(name, shape, dt=F32):
        return nc.dram_tensor(name, list(shape), dt, kind="ExternalOutput").ap()

    xin = din("xin", [8, 128, TT])
    w_in = din("w_in_r", [NL, 128, 8, IN_DIM])
    w_out = din("w_out_r", [NL, 128, 16, D])
    glu_w = din("glu_r", [NL, 128, 4, 512])
    wa_bd = din("wa_bd", [NL, 128, 4, 128])
    wx_bd = din("wx_bd", [NL, 128, 4, 128])
    pp_d = din("pp_in", [128, NL, NPP])
    pdt_d = din("pdt_in", [128, NL, 32])
    bpad_re = din("bpadT_re", [NL, 128, 16, 128])
    bpad_im = din("bpadT_im", [NL, 128, 16, 128])
    ctpad_re = din("ctpad_re", [NL, 128, 16, 128])
    ctpad_im = din("ctpad_im", [NL, 128, 16, 128])
    h0T_d = din("h0T", [NL, NSEQ, 128, 1024])
    sconv0 = din("sconv0", [NL, 128, 16, NSEQ, 3])
    lconv0 = din("lconv0", [NL, 128, 4, NSEQ, 3])
    lru0 = din("lru0", [NL, 128, 4, NSEQ])
    s5r0 = din("s5r0", [NL, 128, 16, NSEQ])
    s5i0 = din("s5i0", [NL, 128, 16, NSEQ])
    c_f32 = din("c_f32", [128, 8, 128])
    c_bf = din("c_bf", [128, 4, 128])
    c_iota = din("c_iota", [128, WT])

    yout = dout("yout", [8, 128, TT])
    ssd_p_o = dout("ssd_p_o", [NL, 128, 1024])
    ssd_s_o = dout("ssd_s_o", [NL, NSEQ, 128, 1024])
    sconv_p_o = dout("sconv_p_o", [NL, 128, 16, 3])
    sconv_s_o = dout("sconv_s_o", [NL, 128, 16, NSEQ, 3])
    lru_p_o = dout("lru_p_o", [NL, 128, 4])
    lru_s_o = dout("lru_s_o", [NL, 128, 4, NSEQ])
    lconv_p_o = dout("lconv_p_o", [NL, 128, 4, 3])
    lconv_s_o = dout("lconv_s_o", [NL, 128, 4, NSEQ, 3])
    s5r_p_o = dout("s5r_p_o", [NL, 128, 16])
    s5r_s_o = dout("s5r_s_o", [NL, 128, 16, NSEQ])
    s5i_p_o = dout("s5i_p_o", [NL, 128, 16])
    s5i_s_o = dout("s5i_s_o", [NL, 128, 16, NSEQ])
    xsc = nc.dram_tensor("xsc", [2, 8, 128, TT], F32, kind="Internal").ap()
    wbf = nc.dram_tensor("wbf", [NL, 128, 8, IN_DIM], BF16, kind="Internal").ap()
    wobf = nc.dram_tensor("wobf", [NL, 8, 128, 16, 128], BF16, kind="Internal").ap()

    with ExitStack() as es:
        k = KB(nc, es)
        k.slotw = {"F8": 512, "zs": 512, "A16": 512, "Y": 512, "xt": 512, "hn": 512, "LT": 128,
                   "hnew": 512, "h0f": 512}

        def sb(name, shape, dt):
            return es.enter_context(nc.sbuf_tensor(name, list(shape), dt))

        def ps(name, shape, dt):
            return es.enter_context(nc.psum_tensor(name, list(shape), dt))

        xt = sb("xt", [128, 8, WT], F32)
        hn = sb("hn", [128, 8, WT], BF16)
        zs = sb("zs", [128, 8, WT], BF16)
        A16 = sb("A16", [128, 16, WT], BF16)
        F8 = sb("F8", [128, 8, WT], F32)
        Y = sb("Y", [128, 16, WT], BF16)
        LT = sb("LT", [128, 16, 128], BF16)
        x_tm = sb("x_tm", [128, 1024], BF16)
        xw_tm = sb("xw_tm", [128, 1024], BF16)
        B_tm = sb("B_tm", [128, 512], BF16)
        bmring = Ring([sb("bm%d" % i, [128, 512], BF16) for i in range(1)])
        hT = sb("hT", [128, 1024], F32)
        hT_bf = sb("hT_bf", [128, 1024], BF16)
        dt_tm = sb("dt_tm", [128, 4, 16], F32)
        dtA_tm = sb("dtA_tm", [128, 4, 16], F32)
        pre = sb("pre", [128, 7, 64], F32)
        wring = Ring([sb("wbuf%d" % i, [128, 8, 512], BF16) for i in range(2)])
        woring = Ring([sb("wobuf%d" % i, [128, 16, 128], BF16) for i in range(2)])
        glu_bf = sb("glu_bf", [128, 4, 512], BF16)
        BbT_re = sb("BbT_re", [128, 16, 128], BF16)
        BbT_im = sb("BbT_im", [128, 16, 128], BF16)
        CT_re = sb("CT_re", [128, 16, 128], BF16)
        nCT_re = sb("nCT_re", [128, 16, 128], BF16)
        nCT_im = sb("nCT_im", [128, 16, 128], BF16)
        wa_bf = sb("wa_bf", [128, 4, 128], BF16)
        wx_bf = sb("wx_bf", [128, 4, 128], BF16)
        pp = sb("pp", [128, NL, NPP], F32)
        pdt = sb("pdt", [128, NL, 32], F32)
        expA = sb("expA", [128, 16], F32)
        lp = sb("lp", [128, 8, 16], F32)
        cf = sb("cf", [128, 8, 128], F32)
        cb = sb("cb", [128, 4, 128], BF16)
        iota = sb("iota", [128, WT], F32)
        iota_c = sb("iota_c", [128, WT], F32)
        carry_s = sb("carry_s", [128, 16, NSEQ, 3], F32)
        carry_l = sb("carry_l", [128, 4, NSEQ, 3], F32)
        pcar_s = sb("pcar_s", [128, 16, 3], F32)
        pcar_l = sb("pcar_l", [128, 4, 3], F32)
        pcar_h = sb("pcar_h", [128, 4], F32)
        hcar = sb("hcar", [128, 4, NSEQ], F32)
        s5car_r = sb("s5car_r", [128, 16], F32)
        s5car_i = sb("s5car_i", [128, 16], F32)
        s5p_r = sb("s5p_r", [128, 16], F32)
        s5p_i = sb("s5p_i", [128, 16], F32)
        s5o_r = sb("s5o_r", [128, 16, NSEQ], F32)
        s5o_i = sb("s5o_i", [128, 16, NSEQ], F32)
        s5h0_r = sb("s5h0_r", [128, 16, NSEQ], F32)
        s5h0_i = sb("s5h0_i", [128, 16, NSEQ], F32)
        lruh0 = sb("lruh0", [128, 4, NSEQ], F32)
        h0f = sb("h0f", [128, 1024], F32)
        h0bring = Ring([sb("h0b%d" % i, [128, 1024], BF16) for i in range(2)])
        hnew = sb("hnew", [128, 1024], F32)
        dtA_rep = h0f
        _fr = [sb("fr%d" % i, [128, WT + 4], F32) for i in range(8)]
        fring = Ring([t_[:, 0:WT] for t_ in _fr], [t_[:, :] for t_ in _fr])
        iring = Ring([sb("ir%d" % i, [128, WT], I32) for i in range(2)])
        fring_x = [h0f[:, 0:512], h0f[:, 512:1024]]
        sring2 = [hnew[:, 0:512], hnew[:, 512:1024]]
        sring = Ring([sb("sr%d" % i, [128, 128], F32) for i in range(4)])
        tring = Ring([sb("tn%d" % i, [128, 48], F32) for i in range(10)])
        dhl = sb("dhl", [128, 2, 64], BF16)
        f8ring = Ring([F8[:, i, :] for i in range(8)])
        zring = Ring([zs[:, i, :] for i in range(8)])

        pheld = ps("pheld", [128, 512], F32)
        pheld2 = ps("pheld2", [128, 512], F32)
        pheld3 = ps("pheld3", [128, 512], F32)
        pring = Ring([ps("pb%d" % i, [128, 512], F32) for i in range(4)])
        ptb = ps("ptb", [128, 1024], BF16)

        tri_b2 = cf[:, 0, :].bitcast(BF16)
        tri_p = cf[:, 1, :]
        tri_s = cf[:, 2, :]
        ones_f = cf[:, 3, :]
        ones_s = cf[:, 4, :]
        iota_s = cf[:, 5, :]
        notstart = cf[:, 6, :]
        ind = cf[:, 7, :]
        ident_b = cb[:, 0, :]
        ones_b = cb[:, 1, :]
        neg_p = cb[:, 2, :]
        neg_s = cb[:, 3, :]

        k.dma("sp", cf[:], c_f32)
        nident_b = sb("nident_b", [128, 128], BF16)
        k.dma("pool", cb[:], c_bf)
        k.dma("sp", iota[:], c_iota)
        k.dma("sp", pp[:], pp_d)
        k.dma("sp", pdt[:], pdt_d)
        k.act(nident_b[:], ident_b, AF.Copy, scale=-1.0)

        tiles = [(i * WT, WT, "p") for i in range(TP // WT)] + [(TP, TS, "s")]
        blocks = [("z", 0, 512), ("z", 512, 512), ("x", 1024, 512), ("x", 1536, 512), ("B", 2048, 512),
                  ("C", 2560, 512), ("dt", 3072, 16), ("lg", 3600, 512), ("lx", 3088, 512),
                  ("sg", 4624, 512), ("su", 4112, 512)]
        stream = [(l, ti, bi) for l in range(nlayers) for ti in range(len(tiles)) for bi in range(len(blocks))]
        wbufs = {}
        st = {"next": 0, "item": 0}

        def prefetch_w(upto):
            while st["next"] < len(stream) and st["next"] <= upto:
                l_, ti_, bi_ = stream[st["next"]]
                _, c0_, n_ = blocks[bi_]
                buf = wring.get()
                k.dma("sp", buf[:, :, 0:n_], wbf[l_][:, :, c0_:c0_ + n_], rk=["wbf%d_%d" % (l_, bi_)])
                wbufs[st["next"]] = buf
                st["next"] += 1

        def next_block():
            prefetch_w(st["item"] + 1)
            wb = wbufs.pop(st["item"])
            st["item"] += 1
            return wb

        pringA = Ring(pring.bufs + [pheld, pheld2, pheld3])

        def proj(wb, m, W):
            pm = pringA.get()
            for kt in range(8):
                k.mm(pm[:, 0:W], wb[:, kt, m * 128:(m + 1) * 128], hn[:, kt, 0:W], start=(kt == 0), stop=(kt == 7))
            return pm

        def rmsnorm_tile(src3, ncol, gcol0, l, out_fn):
            pn = pring.get()
            for kt in range(8):
                sq = zring.get()
                k.act(sq[:, 0:ncol], src3[:, kt, 0:ncol], AF.Square)
                k.mm(pn[:, 0:ncol], ones_b, sq[:, 0:ncol], start=(kt == 0), stop=(kt == 7))
            t1 = fring.get()
            k.act(t1[:, 0:ncol], pn[:, 0:ncol], AF.Ln, bias=epsc[:, 0:1], scale=1.0 / D)
            rstd = fring.get()
            k.act(rstd[:, 0:ncol], t1[:, 0:ncol], AF.Exp, scale=-0.5)
            for kt in range(8):
                k.stt(out_fn(kt), src3[:, kt, 0:ncol], pp[:, l, gcol0 + kt:gcol0 + kt + 1], rstd[:, 0:ncol],
                      ALU.mult, ALU.mult)

        def frac_sincos(u, W, sn_out, cs_out):
            ui = iring.get()
            k.copy("dve", ui[:, 0:W], u)
            r = fring.get()
            k.tt("dve", r[:, 0:W], u, ui[:, 0:W], ALU.subtract)
            k.act(sn_out, r[:, 0:W], AF.Sin, scale=TWO_PI)
            ar = fring.get()
            k.stt(ar[:, 0:W], r[:, 0:W], -1.0, r[:, 0:W], ALU.mult, ALU.max)
            k.act(cs_out, ar[:, 0:W], AF.Sin, bias=epsc[:, 1:2], scale=-TWO_PI)

        epsc = sb("epsc", [128, 4], F32)
        k.memset("dve", epsc[:, 0:1], EPS)
        k.memset("dve", epsc[:, 1:2], math.pi / 2.0)
        k.memset("dve", epsc[:, 2:3], 1.0)

        def load_x(l_, ti_):
            c0_, W_, _ = tiles[ti_]
            if l_ == 0:
                k.dma("sp", F8[:, :, 0:W_], xin.rearrange("k p t -> p k t")[:, :, c0_:c0_ + W_])
            else:
                k.dma("sp", F8[:, :, 0:W_], xsc[(l_ - 1) % 2].rearrange("k p t -> p k t")[:, :, c0_:c0_ + W_],
                      rk=["xsc%d_%d" % ((l_ - 1) % 2, ti_)])

        def conv_group(l, pms, W, smp, specs):
            n = len(pms)
            accs = [fring.get() for _ in range(n)]
            raws = [fring.get_full() for _ in range(n)]
            if smp:
                rawv = [r_[:, 0:NSEQ * (LS + 3)].rearrange("p (s t) -> p s t", s=NSEQ) for r_ in raws]
                pmv = [p_[:, 0:W].rearrange("p (s t) -> p s t", s=NSEQ) for p_ in pms]
                accv = [a_[:, 0:W].rearrange("p (s t) -> p s t", s=NSEQ) for a_ in accs]
                for i in range(n):
                    k.copy("dve", rawv[i][:, :, 0:3], specs[i][3])
                for i in range(n):
                    k.copy("act", rawv[i][:, :, 3:3 + LS], pmv[i])
                    k.act(accv[i], pmv[i], AF.Identity, bias=pp[:, l, specs[i][1]:specs[i][1] + 1],
                          scale=pp[:, l, specs[i][0] + 3:specs[i][0] + 4])
                for i in range(n):
                    st3 = tring.get()
                    st3v = st3[:, 0:48].rearrange("p (s t) -> p s t", s=NSEQ)
                    k.copy("dve", st3v, rawv[i][:, :, LS:LS + 3])
                    k.dma("sp", specs[i][4], st3v)
                for kk in range(3):
                    for i in range(n):
                        k.stt(accv[i], rawv[i][:, :, kk:kk + LS], pp[:, l, specs[i][0] + kk:specs[i][0] + kk + 1],
                              accv[i], ALU.mult, ALU.add)
            else:
                for i in range(n):
                    k.copy("pool", raws[i][:, 0:3], specs[i][2])
                for i in range(n):
                    k.copy("act", raws[i][:, 3:3 + W], pms[i][:, 0:W])
                    k.act(accs[i][:, 0:W], pms[i][:, 0:W], AF.Identity, bias=pp[:, l, specs[i][1]:specs[i][1] + 1],
                          scale=pp[:, l, specs[i][0] + 3:specs[i][0] + 4])
                for kk in range(3):
                    for i in range(n):
                        k.stt(accs[i][:, 0:W], raws[i][:, kk:kk + W], pp[:, l, specs[i][0] + kk:specs[i][0] + kk + 1],
                              accs[i][:, 0:W], ALU.mult, ALU.add)
                for i in range(n):
                    k.copy("pool", specs[i][2], raws[i][:, W:W + 3])
            return accs

        def ssd_stage(l, ti, W, smp):
            xc = A16
            nch = W // 128
            TRI = tri_s if smp else tri_p
            ONESM = ones_s if smp else ones_f
            NEG = neg_s if smp else neg_p
            TRIB = tri_b2[:, 128:256] if smp else tri_b2[:, 0:128]
            for ci in range(nch):
                cs = slice(ci * 128, (ci + 1) * 128)
                dtA = dtA_tm[:, ci, :]
                dtc = dt_tm[:, ci, :]
                nacum = pre[:, 1, ci * 16:(ci + 1) * 16]
                w_tm = pre[:, 3, ci * 16:(ci + 1) * 16]
                DEC = pre[:, 4, ci * 16:(ci + 1) * 16]
                k.cp("ssd A acum")
                psc = pheld
                for g in range(4):
                    k.mm(psc[:, g * 128:(g + 1) * 128], xc[:, 8 + g, cs], xc[:, 12 + g, cs])
                k.cp("ssd B scores")
                for j in range(8):
                    k.tr(ptb[:, j * 128:(j + 1) * 128], xc[:, j, cs], ident_b)
                k.cp("T1 xtr")
                k.copy("act", x_tm[:], ptb[:, :])
                k.cp("T2 xcopy")
                k.tt("dve", hnew[:].rearrange("p (h q) -> p h q", h=16), x_tm[:].rearrange("p (h q) -> p h q", h=16),
                     w_tm[:, 0:16].unsqueeze(2).to_broadcast([128, 16, 64]), ALU.mult)
                k.copy("act", xw_tm[:], hnew[:])
                k.cp("T3 xw")
                for g in range(4):
                    k.tr(ptb[:, g * 128:(g + 1) * 128], xc[:, 8 + g, cs], ident_b)
                k.cp("T4 btr")
                k.copy("act", B_tm[:], ptb[:, 0:512])
                k.cp("ssd C transposes")
                for g in range(4):
                    pab = pring.get()
                    for r in range(4):
                        h = 4 * g + r
                        cc = ci * 16 + h
                        k.mm(pab[:, r * 128:(r + 1) * 128], dhl[:, 0, cc:cc + 1].to_broadcast([128, 128]), TRIB,
                             start=True, stop=False)
                        k.mm(pab[:, r * 128:(r + 1) * 128], dhl[:, 1, cc:cc + 1].to_broadcast([128, 128]), TRIB,
                             start=False, stop=False)
                        k.mm(pab[:, r * 128:(r + 1) * 128], ident_b, NEG, start=False, stop=True)
                    for r in range(4):
                        h = 4 * g + r
                        Dh = sring.get()
                        k.act(Dh[:], pab[:, r * 128:(r + 1) * 128], AF.Exp, bias=nacum[:, h:h + 1])
                        k.stt(LT[:, h, :], Dh[:], dt_tm[:, ci, h:h + 1], psc[:, g * 128:(g + 1) * 128],
                              ALU.mult, ALU.mult)
                k.cp("ssd D LT")
                if smp:
                    EAs = [fring.get(), fring.get()]
                    k.copy("dve", dtA_rep[:].rearrange("p (h q) -> p h q", h=16),
                           dtA_tm[:, ci, :].unsqueeze(2).to_broadcast([128, 16, 64]))
                    for j in range(8):
                        pe_ = pring.get()
                        k.mm(pe_[:, 0:128], dtA_rep[:, j * 128:(j + 1) * 128], TRI)
                        k.act(EAs[j // 4][:, (j % 4) * 128:(j % 4 + 1) * 128], pe_[:, 0:128], AF.Exp)
                    pdS = pring.get()
                    for b_ in range(NSEQ):
                        k.mm(pdS[:, b_ * 16:(b_ + 1) * 16], ind[:, b_:b_ + 1].to_broadcast([128, 128]), dtA)
                    DECS = fring.get()
                    k.act(DECS[:, 0:256], pdS[:, 0:256], AF.Exp)
                    hx = Ring([h0f, hnew])
                    bufs = [hx.get()]
                    k.dma("sp", bufs[0][:], h0T_d[l, 0])
                    for b_ in range(NSEQ):
                        hb = bufs[b_]
                        if b_ + 1 < NSEQ:
                            nb_ = hx.get()
                            bufs.append(nb_)
                            k.dma("sp", nb_[:], h0T_d[l, b_ + 1])
                        h0b = h0bring.get()
                        k.copy("act", h0b[:], hb[:])
                        for j in range(8):
                            pyo = pheld2 if j < 4 else pheld3
                            jj = j % 4
                            k.mm(pyo[:, jj * 128 + b_ * 8: jj * 128 + b_ * 8 + 8], h0b[:, j * 128:(j + 1) * 128],
                                 xc[:, 12 + j // 2, b_ * 8:(b_ + 1) * 8])
                        Bm = bmring.get()
                        k.ts("dve", Bm[:], B_tm[:], ind[:, b_:b_ + 1], None, ALU.mult)
                        pS0 = pring.get()
                        pS1 = pring.get()
                        for g in range(4):
                            pS = pS0 if g < 2 else pS1
                            k.mm(pS[:, (g % 2) * 256:(g % 2 + 1) * 256], Bm[:, g * 128:(g + 1) * 128],
                                 xw_tm[:, g * 256:(g + 1) * 256])
                        hb3 = hb[:].rearrange("p (h q) -> p h q", h=16)
                        k.tt("dve", hb3, hb3, DECS[:, b_ * 16:(b_ + 1) * 16].unsqueeze(2).to_broadcast([128, 16, 64]),
                             ALU.mult)
                        k.tt("dve", hb[:, 0:512], hb[:, 0:512], pS0[:, :], ALU.add)
                        k.tt("dve", hb[:, 512:1024], hb[:, 512:1024], pS1[:, :], ALU.add)
                        k.dma("sp", ssd_s_o[l, b_], hb[:])
                    yo0 = fring.get()
                    yo1 = fring.get()
                    k.copy("act", yo0[:], pheld2[:, :])
                    k.copy("act", yo1[:], pheld3[:, :])
                if not smp:
                    k.copy("dve", dtA_rep[:].rearrange("p (h q) -> p h q", h=16),
                           dtA_tm[:, ci, :].unsqueeze(2).to_broadcast([128, 16, 64]))
                for j in range(8):
                    py = pring.get()
                    if not smp:
                        k.mm(py[:, 256:384], dtA_rep[:, j * 128:(j + 1) * 128], TRI)
                    k.mm(py[0:64, 0:128], x_tm[:, (2 * j) * 64:(2 * j + 1) * 64], LT[:, 2 * j, :])
                    k.mm(py[64:128, 0:128], x_tm[:, (2 * j + 1) * 64:(2 * j + 2) * 64], LT[:, 2 * j + 1, :])
                    tmp = sring.get()
                    if smp:
                        yo = (yo0 if j < 4 else yo1)[:, (j % 4) * 128:(j % 4 + 1) * 128]
                        k.tt("dve", tmp[:], yo, EAs[j // 4][:, (j % 4) * 128:(j % 4 + 1) * 128], ALU.mult)
                    else:
                        k.mm(py[:, 128:256], hT_bf[:, j * 128:(j + 1) * 128], xc[:, 12 + j // 2, cs])
                        EA = sring.get()
                        k.act(EA[:], py[:, 256:384], AF.Exp)
                        k.tt("dve", tmp[:], py[:, 128:256], EA[:], ALU.mult)
                    k.defer(lambda j=j, py=py, tmp=tmp, cs=cs: k.tt("dve", F8[:, j, cs], py[:, 0:128], tmp[:], ALU.add),
                            depth=1)
                k.flush()
                k.cp("ssd E y")
                if not smp:
                    for g in range(4):
                        pS = pheld2 if g < 2 else pheld3
                        k.mm(pS[:, (g % 2) * 256:(g % 2 + 1) * 256], B_tm[:, g * 128:(g + 1) * 128],
                             xw_tm[:, g * 256:(g + 1) * 256])
                    hT3 = hT[:].rearrange("p (h q) -> p h q", h=16)
                    k.tt("dve", hT3, hT3, DEC[:, 0:16].unsqueeze(2).to_broadcast([128, 16, 64]), ALU.mult)
                    k.tt("dve", hT[:, 0:512], hT[:, 0:512], pheld2[:, :], ALU.add)
                    k.tt("dve", hT[:, 512:1024], hT[:, 512:1024], pheld3[:, :], ALU.add)
                    k.copy("act", hT_bf[:], hT[:])
            k.cp("ssd F chunks done")
            pn = pring.get()
            for j in range(8):
                k.stt(F8[:, j, 0:W], xc[:, j, 0:W], pp[:, l, PP_SD + j:PP_SD + j + 1], F8[:, j, 0:W], ALU.mult, ALU.add)
            for j in range(8):
                k.tt("dve", F8[:, j, 0:W], F8[:, j, 0:W], zs[:, j, 0:W], ALU.mult)
            for j in range(8):
                sq = fring.get()
                sqb = sq[:, 0:WT // 2].bitcast(BF16)
                k.act(sqb[:, 0:W], F8[:, j, 0:W], AF.Square)
                k.mm(pn[:, 0:W], ones_b, sqb[:, 0:W], start=(j == 0), stop=(j == 7))
            t1 = fring.get()
            k.act(t1[:, 0:W], pn[:, 0:W], AF.Ln, bias=epsc[:, 0:1], scale=1.0 / D)
            rstd = fring.get()
            k.act(rstd[:, 0:W], t1[:, 0:W], AF.Exp, scale=-0.5)
            for j in range(8):
                k.stt(Y[:, j, 0:W], F8[:, j, 0:W], pp[:, l, PP_NG2 + j:PP_NG2 + j + 1], rstd[:, 0:W], ALU.mult, ALU.mult)
            if (not smp) and ti == len(tiles) - 2:
                k.dma("sp", ssd_p_o[l], hT[:])

        def lru_group(l, js, accs, W, smp):
            n = len(js)
            xrs, prs, pgs, rs, gis, as_ = [], [], [], [], [], []
            for i in range(n):
                xr_bf = zring.get()
                k.copy("act", xr_bf[:, 0:W], accs[i][:, 0:W])
                xrs.append(xr_bf)
            for i in range(n):
                pr = pring.get()
                k.mm(pr[:, 0:W], wa_bf[:, js[i], :], xrs[i][:, 0:W])
                pg = pring.get()
                k.mm(pg[:, 0:W], wx_bf[:, js[i], :], xrs[i][:, 0:W])
                prs.append(pr)
                pgs.append(pg)
            for i in range(n):
                j = js[i]
                r = fring.get()
                k.act(r[:, 0:W], prs[i][:, 0:W], AF.Sigmoid, bias=pp[:, l, PP_LBA + j:PP_LBA + j + 1])
                gi = fring.get()
                k.act(gi[:, 0:W], pgs[i][:, 0:W], AF.Sigmoid, bias=pp[:, l, PP_LBX + j:PP_LBX + j + 1])
                rs.append(r)
                gis.append(gi)
            for i in range(n):
                a = f8ring.get()
                k.act(a[:, 0:W], rs[i][:, 0:W], AF.Exp, scale=lp[:, 0, js[i]:js[i] + 1])
                as_.append(a)
            for i in range(n):
                k.tt("dve", rs[i][:, 0:W], as_[i][:, 0:W], as_[i][:, 0:W], ALU.mult)
            for i in range(n):
                k.ts("dve", rs[i][:, 0:W], rs[i][:, 0:W], 1.0, -1.0, ALU.min, ALU.mult)
            for i in range(n):
                k.act(rs[i][:, 0:W], rs[i][:, 0:W], AF.Sqrt, bias=epsc[:, 2:3])
            for i in range(n):
                k.tt("dve", gis[i][:, 0:W], gis[i][:, 0:W], rs[i][:, 0:W], ALU.mult)
            for i in range(n):
                k.tt("dve", gis[i][:, 0:W], gis[i][:, 0:W], accs[i][:, 0:W], ALU.mult)
            hss = []
            for i in range(n):
                j = js[i]
                a = as_[i]
                gi = gis[i]
                hs = f8ring.get()
                if smp:
                    am = f8ring.get()
                    k.tt("dve", am[:, 0:W], a[:, 0:W], notstart, ALU.mult)
                    t = tring.get()
                    k.tt("dve", t[:, 0:16], a[:, 0:W:LS], lruh0[:, j, :], ALU.mult)
                    k.tt("dve", gi[:, 0:W:LS], gi[:, 0:W:LS], t[:, 0:16], ALU.add)
                    k.scan(hs[:, 0:W], am[:, 0:W], gi[:, 0:W], 0.0)
                else:
                    k.scan(hs[:, 0:W], a[:, 0:W], gi[:, 0:W], pcar_h[:, j:j + 1])
                hss.append(hs)
            for i in range(n):
                j = js[i]
                if smp:
                    k.copy("dve", hcar[:, j, :], hss[i][:, LS - 1:W:LS])
                else:
                    k.copy("dve", pcar_h[:, j:j + 1], hss[i][:, W - 1:W])
                k.tt("dve", Y[:, 8 + j, 0:W], hss[i][:, 0:W], A16[:, j, 0:W], ALU.mult)

        def s5_stage(l, ti, c0, W, smp):
            last_p = (not smp) and ti == len(tiles) - 2
            if not smp:
                k.ts("dve", iota_c[:, 0:W], iota[:, 0:W], float(c0), None, ALU.add)
            tsrc = iota_s if smp else iota_c[:, 0:W]
            tabring = Ring([A16[:, 0, :], A16[:, 1, :], A16[:, 2, :], A16[:, 3, :], Y[:, 14, :], Y[:, 15, :]])
            srring = Ring([F8[:, 0, :], F8[:, 2, :]])
            siring = Ring([F8[:, 1, :], F8[:, 3, :]])
            tprod = [zs[:, i, :] for i in range(4)]
            mprod = [zs[:, 4 + i, :] for i in range(4)]
            Srb = Y[:, 12, :]
            Sib = Y[:, 13, :]
            pGr = pheld2
            pGi = pheld3

            def tables(pr_):
                thc = lp[:, 1, pr_:pr_ + 1]
                u = fring.get()
                ui = iring.get()
                k.ts("dve", ui[:, 0:W], tsrc, thc, None, ALU.mult)
                k.stt(u[:, 0:W], tsrc, thc, ui[:, 0:W], ALU.mult, ALU.subtract)
                uf = fring.get()
                sn = tabring.get()
                cs = tabring.get()
                k.act(sn[:, 0:W], u[:, 0:W], AF.Sin, scale=TWO_PI)
                k.act(uf[:, 0:W], u[:, 0:W], AF.Abs)
                k.act(cs[:, 0:W], uf[:, 0:W], AF.Sin, bias=epsc[:, 1:2], scale=-TWO_PI)
                return sn, cs

            def make_tail(pr_, q, py5, sn, cs, Sr, Si):
                def tail():
                    m1, m2, m3, m4 = mprod
                    k.tt("pool", m1[:, 0:W], cs[:, 0:W], Srb[:, 0:W], ALU.mult)
                    k.tt("pool", m2[:, 0:W], sn[:, 0:W], Sib[:, 0:W], ALU.mult)
                    k.tt("pool", m3[:, 0:W], cs[:, 0:W], Sib[:, 0:W], ALU.mult)
                    k.tt("pool", m4[:, 0:W], sn[:, 0:W], Srb[:, 0:W], ALU.mult)
                    k.mm(py5[:, 0:W], CT_re[:, pr_, :], m1[:, 0:W], start=(q == 0), stop=False)
                    k.mm(py5[:, 0:W], nCT_re[:, pr_, :], m2[:, 0:W], start=False, stop=False)
                    k.mm(py5[:, 0:W], nCT_im[:, pr_, :], m3[:, 0:W], start=False, stop=False)
                    k.mm(py5[:, 0:W], nCT_im[:, pr_, :], m4[:, 0:W], start=False, stop=(q == 3))
                    if smp or last_p:
                        if smp:
                            sel = slice(LS - 1, W, LS)
                            n_ = NSEQ
                            dr = s5o_r[:, pr_, :]
                            di = s5o_i[:, pr_, :]
                        else:
                            sel = slice(W - 1, W)
                            n_ = 1
                            dr = s5p_r[:, pr_:pr_ + 1]
                            di = s5p_i[:, pr_:pr_ + 1]
                        ta = tring.get()
                        tb2 = tring.get()
                        tc2 = tring.get()
                        td2 = tring.get()
                        k.tt("dve", ta[:, 0:n_], cs[:, sel], Sr[:, sel], ALU.mult)
                        k.tt("dve", tb2[:, 0:n_], sn[:, sel], Si[:, sel], ALU.mult)
                        k.tt("dve", tc2[:, 0:n_], cs[:, sel], Si[:, sel], ALU.mult)
                        k.tt("dve", td2[:, 0:n_], sn[:, sel], Sr[:, sel], ALU.mult)
                        k.tt("dve", dr, ta[:, 0:n_], tb2[:, 0:n_], ALU.subtract)
                        k.tt("dve", di, tc2[:, 0:n_], td2[:, 0:n_], ALU.add)
                return tail

            nxt = tables(0)
            prev_tail = None
            for kt in range(4):
                py5 = pheld
                ub = A16[:, 8 + kt, 0:W]
                for q in range(4):
                    pr_ = kt * 4 + q
                    pbr = pring.get()
                    k.mm(pbr[:, 0:W], BbT_re[:, pr_, :], ub)
                    pbi = pring.get()
                    k.mm(pbi[:, 0:W], BbT_im[:, pr_, :], ub)
                    sn, cs = nxt
                    if pr_ + 1 < 16:
                        nxt = tables(pr_ + 1)
                    t1, t2, t3, t4 = tprod
                    k.tt("dve", t1[:, 0:W], pbr[:, 0:W], cs[:, 0:W], ALU.mult)
                    k.tt("dve", t2[:, 0:W], pbi[:, 0:W], sn[:, 0:W], ALU.mult)
                    k.tt("dve", t3[:, 0:W], pbi[:, 0:W], cs[:, 0:W], ALU.mult)
                    k.tt("dve", t4[:, 0:W], pbr[:, 0:W], sn[:, 0:W], ALU.mult)
                    k.mm(pGr[:, 0:W], ident_b, t1[:, 0:W], start=True, stop=False)
                    k.mm(pGr[:, 0:W], ident_b, t2[:, 0:W], start=False, stop=True)
                    k.mm(pGi[:, 0:W], ident_b, t3[:, 0:W], start=True, stop=False)
                    k.mm(pGi[:, 0:W], nident_b[:], t4[:, 0:W], start=False, stop=True)
                    if prev_tail is not None:
                        prev_tail()
                        prev_tail = None
                    Sr = srring.get()
                    Si = siring.get()
                    if smp:
                        ar_ = lp[:, 3, pr_:pr_ + 1]
                        ai_ = lp[:, 4, pr_:pr_ + 1]
                        h0r = s5h0_r[:, pr_, :]
                        h0i = s5h0_i[:, pr_, :]
                        tb = tring.get()
                        k.ts("dve", tb[:, 0:16], h0i, ai_, None, ALU.mult)
                        injr = tring.get()
                        k.stt(injr[:, 0:16], h0r, ar_, tb[:, 0:16], ALU.mult, ALU.subtract)
                        tc = tring.get()
                        k.ts("dve", tc[:, 0:16], h0r, ai_, None, ALU.mult)
                        inji = tring.get()
                        k.stt(inji[:, 0:16], h0i, ar_, tc[:, 0:16], ALU.mult, ALU.add)
                        k.tt("dve", pGr[:, 0:W:LS], pGr[:, 0:W:LS], injr[:, 0:16], ALU.add)
                        k.tt("dve", pGi[:, 0:W:LS], pGi[:, 0:W:LS], inji[:, 0:16], ALU.add)
                        magm = sring.get()
                        k.ts("dve", magm[:], notstart, lp[:, 2, pr_:pr_ + 1], None, ALU.mult)
                        k.scan(Sr[:, 0:W], magm[:], pGr[:, 0:W], 0.0)
                        k.scan(Si[:, 0:W], magm[:], pGi[:, 0:W], 0.0)
                    else:
                        magb = lp[:, 2, pr_:pr_ + 1].to_broadcast([128, W])
                        k.scan(Sr[:, 0:W], magb, pGr[:, 0:W], s5car_r[:, pr_:pr_ + 1])
                        k.scan(Si[:, 0:W], magb, pGi[:, 0:W], s5car_i[:, pr_:pr_ + 1])
                        k.copy("act", s5car_r[:, pr_:pr_ + 1], Sr[:, W - 1:W])
                        k.copy("act", s5car_i[:, pr_:pr_ + 1], Si[:, W - 1:W])
                    k.copy("act", Srb[:, 0:W], Sr[:, 0:W])
                    k.copy("act", Sib[:, 0:W], Si[:, 0:W])
                    prev_tail = make_tail(pr_, q, py5, sn, cs, Sr, Si)
                    if q == 3:
                        prev_tail()
                        prev_tail = None
                k.stt(F8[:, 4 + kt, 0:W], ub, pp[:, l, PP_S5D + kt:PP_S5D + kt + 1], py5[:, 0:W], ALU.mult, ALU.add)
            x2s = [fring.get() for _ in range(4)]
            for kt in range(4):
                k.act(x2s[kt][:, 0:W], F8[:, 4 + kt, 0:W], AF.Square)
            for kt in range(4):
                k.ts("dve", x2s[kt][:, 0:W], x2s[kt][:, 0:W], 0.044715, 1.0, ALU.mult, ALU.add)
            for kt in range(4):
                k.tt("dve", x2s[kt][:, 0:W], x2s[kt][:, 0:W], F8[:, 4 + kt, 0:W], ALU.mult)
            for kt in range(4):
                k.act(x2s[kt][:, 0:W], x2s[kt][:, 0:W], AF.Sigmoid, scale=1.5957691216057308)
            for kt in range(4):
                k.tt("dve", A16[:, 12 + kt, 0:W], F8[:, 4 + kt, 0:W], x2s[kt][:, 0:W], ALU.mult)
            sgs = []
            for m in range(4):
                pg = pring.get()
                for kt in range(4):
                    k.mm(pg[:, 0:W], glu_bf[:, kt, m * 128:(m + 1) * 128], A16[:, 12 + kt, 0:W],
                         start=(kt == 0), stop=(kt == 3))
                sg = fring.get()
                k.act(sg[:, 0:W], pg[:, 0:W], AF.Sigmoid, bias=pp[:, l, PP_GLB + m:PP_GLB + m + 1])
                sgs.append(sg)
            for m in range(4):
                k.tt("dve", sgs[m][:, 0:W], sgs[m][:, 0:W], A16[:, 12 + m, 0:W], ALU.mult)
            for m in range(4):
                k.tt("dve", Y[:, 12 + m, 0:W], sgs[m][:, 0:W], A16[:, 4 + m, 0:W], ALU.mult)

        def convert_w_in(l_):
            for bi_, (_, c0_, n_) in enumerate(blocks):
                k.dma("pool", wbf[l_][:, :, c0_:c0_ + n_], w_in[l_][:, :, c0_:c0_ + n_],
                      wk=["wbf%d_%d" % (l_, bi_)])

        def convert_w_out(l_):
            for m_ in range(8):
                k.dma("pool", wobf[l_, m_], w_out[l_][:, :, m_ * 128:(m_ + 1) * 128], wk=["wobf%d_%d" % (l_, m_)])

        for l in range(nlayers):
            k.dma("pool", glu_bf[:], glu_w[l])
            k.dma("pool", wa_bf[:], wa_bd[l])
            k.dma("pool", wx_bf[:], wx_bd[l])
            k.dma("pool", CT_re[:], ctpad_re[l])
            k.dma("pool", nCT_im[:], ctpad_im[l])
            if l == 0:
                convert_w_in(0)
            k.act(nCT_re[:].rearrange("p a b -> p (a b)"), CT_re[:].rearrange("p a b -> p (a b)"), AF.Copy, scale=-1.0)
            k.act(nCT_im[:].rearrange("p a b -> p (a b)"), nCT_im[:].rearrange("p a b -> p (a b)"), AF.Copy, scale=-1.0)
            k.act(expA[:], pdt[:, l, 16:32], AF.Exp)
            t = tring.get()
            k.act(t[:, 0:4], pp[:, l, PP_LLAM:PP_LLAM + 4], AF.Exp, scale=-1.0)
            t2 = tring.get()
            k.act(t2[:, 0:4], t[:, 0:4], AF.Ln, bias=epsc[:, 2:3])
            k.ts("dve", lp[:, 0, 0:4], t2[:, 0:4], -8.0, None, ALU.mult)
            dl = tring.get()
            k.act(dl[:, 0:16], pp[:, l, PP_LDT:PP_LDT + 16], AF.Exp)
            thp = lp[:, 1, :]
            k.stt(thp, pp[:, l, PP_LIM:PP_LIM + 16], 1.0 / TWO_PI, dl[:, 0:16], ALU.mult, ALU.mult)
            lm = tring.get()
            k.tt("dve", lm[:, 0:16], pp[:, l, PP_LRE:PP_LRE + 16], dl[:, 0:16], ALU.mult)
            mag = lp[:, 2, :]
            k.act(mag, lm[:, 0:16], AF.Exp)
            sn0 = tring.get()
            cs0 = tring.get()
            frac_sincos(thp, 16, sn0[:, 0:16], cs0[:, 0:16])
            k.tt("dve", lp[:, 3, :], mag, cs0[:, 0:16], ALU.mult)
            k.tt("dve", lp[:, 4, :], mag, sn0[:, 0:16], ALU.mult)
            lre_c = pp[:, l, PP_LRE:PP_LRE + 16]
            lim_c = pp[:, l, PP_LIM:PP_LIM + 16]
            nr = tring.get()
            k.ts("dve", nr[:, 0:16], lp[:, 3, :], -1.0, None, ALU.add)
            den = tring.get()
            t_a = tring.get()
            k.tt("dve", den[:, 0:16], lre_c, lre_c, ALU.mult)
            k.tt("dve", t_a[:, 0:16], lim_c, lim_c, ALU.mult)
            k.tt("dve", den[:, 0:16], den[:, 0:16], t_a[:, 0:16], ALU.add)
            k.op("dve", lambda v, o=den[:, 0:16]: v.reciprocal(out=o, in_=o), [den], [den])
            cre = lp[:, 5, :]
            cim = lp[:, 6, :]
            t_b = tring.get()
            k.tt("dve", cre, nr[:, 0:16], lre_c, ALU.mult)
            k.tt("dve", t_b[:, 0:16], lp[:, 4, :], lim_c, ALU.mult)
            k.tt("dve", cre, cre, t_b[:, 0:16], ALU.add)
            k.tt("dve", cre, cre, den[:, 0:16], ALU.mult)
            t_c = tring.get()
            k.tt("dve", cim, lp[:, 4, :], lre_c, ALU.mult)
            k.tt("dve", t_c[:, 0:16], nr[:, 0:16], lim_c, ALU.mult)
            k.tt("dve", cim, cim, t_c[:, 0:16], ALU.subtract)
            k.tt("dve", cim, cim, den[:, 0:16], ALU.mult)
            for c4 in range(4):
                bre = fring.get()
                bim = fring.get()
                k.dma("sp", bre[:].rearrange("p (a b) -> p a b", a=4), bpad_re[l][:, c4 * 4:(c4 + 1) * 4, :])
                k.dma("sp", bim[:].rearrange("p (a b) -> p a b", a=4), bpad_im[l][:, c4 * 4:(c4 + 1) * 4, :])
                crb = cre[:, c4 * 4:(c4 + 1) * 4].unsqueeze(2).to_broadcast([128, 4, 128])
                cib = cim[:, c4 * 4:(c4 + 1) * 4].unsqueeze(2).to_broadcast([128, 4, 128])
                v3 = lambda ap_: ap_.rearrange("p (a b) -> p a b", a=4)
                m_a, m_b, m_c, m_d = [f8ring.get() for _ in range(4)]
                k.tt("dve", v3(m_a), v3(bre[:]), crb, ALU.mult)
                k.tt("dve", v3(m_b), v3(bim[:]), cib, ALU.mult)
                k.tt("dve", v3(m_c), v3(bre[:]), cib, ALU.mult)
                k.tt("dve", v3(m_d), v3(bim[:]), crb, ALU.mult)
                o_re = zring.get()
                o_im = zring.get()
                k.tt("dve", o_re, m_a, m_b, ALU.subtract)
                k.tt("dve", o_im, m_c, m_d, ALU.add)
                for i4 in range(4):
                    k.tr(ptb[:, i4 * 128:(i4 + 1) * 128], o_re[:, i4 * 128:(i4 + 1) * 128], ident_b)
                    k.tr(ptb[:, 512 + i4 * 128:512 + (i4 + 1) * 128], o_im[:, i4 * 128:(i4 + 1) * 128], ident_b)
                k.copy("act", BbT_re[:, c4 * 4:(c4 + 1) * 4, :].rearrange("p a b -> p (a b)"), ptb[:, 0:512])
                k.copy("act", BbT_im[:, c4 * 4:(c4 + 1) * 4, :].rearrange("p a b -> p (a b)"), ptb[:, 512:1024])
            k.dma("sp", carry_s[:], sconv0[l])
            k.dma("sp", carry_l[:], lconv0[l])
            k.dma("sp", lruh0[:], lru0[l])
            k.dma("sp", s5h0_r[:], s5r0[l])
            k.dma("sp", s5h0_i[:], s5i0[l])
            k.memset("dve", hT[:], 0.0)
            k.memset("dve", hT_bf[:], 0.0)
            k.memset("dve", pcar_s[:], 0.0)
            k.memset("dve", pcar_l[:], 0.0)
            k.memset("dve", pcar_h[:], 0.0)
            k.memset("dve", s5car_r[:], 0.0)
            k.memset("dve", s5car_i[:], 0.0)

            if l == 0:
                prefetch_w(1)
            k.cp("setup done l%d" % l)
            for ti, (c0, W, kind) in enumerate(tiles):
                smp = kind == "s"
                k.cp("tile start l%d t%d" % (l, ti))
                if ti == 0:
                    load_x(l, ti)
                rmsnorm_tile(F8, W, PP_NG, l, lambda kt: hn[:, kt, 0:W])
                for kt in range(8):
                    k.copy("pool", xt[:, kt, 0:W], F8[:, kt, 0:W])
                k.cp("norm done")
                for zb in range(2):
                    wb = next_block()
                    for m in range(4):
                        pm = proj(wb, m, W)
                        k.act(zs[:, zb * 4 + m, 0:W], pm[:, 0:W], AF.Silu)
                for xb in range(4):
                    wb = next_block()
                    for half in range(2):
                        js = [xb * 4 + half * 2 + i for i in range(2)]
                        pms = [proj(wb, half * 2 + i, W) for i in range(2)]
                        accs = conv_group(l, pms, W, smp,
                                          [(PP_CW + 4 * j, PP_CB + j, pcar_s[:, j, :], carry_s[:, j, :, :],
                                            sconv_s_o[l][:, j, :, :]) for j in js])

                        def silus(js=js, accs=accs):
                            for j, acc in zip(js, accs):
                                k.act(A16[:, j, 0:W], acc[:, 0:W], AF.Silu)
                        k.defer(silus, depth=1)
                wb = next_block()
                k.flush()
                nchk = W // 128
                pd = pring.get()
                for ci in range(nchk):
                    for kt in range(8):
                        k.mm(pd[:, ci * 16:(ci + 1) * 16], hn[:, kt, ci * 128:(ci + 1) * 128], wb[:, kt, 0:16],
                             start=(kt == 0), stop=(kt == 7))
                dt2 = dt_tm[:, 0:nchk, :]
                dtA2 = dtA_tm[:, 0:nchk, :]
                pd3 = pd[:, 0:nchk * 16].rearrange("p (c h) -> p c h", c=nchk)
                v = pre[:, 6, 0:nchk * 16].rearrange("p (c h) -> p c h", c=nchk)
                k.tt("dve", v, pd3, pdt[:, l, 0:16].unsqueeze(1).to_broadcast([128, nchk, 16]), ALU.add)
                k.act(v, v, AF.Exp)
                k.act(dt2, v, AF.Ln, bias=epsc[:, 2:3])
                k.stt(dtA2, dt2, -1.0, expA[:].unsqueeze(1).to_broadcast([128, nchk, 16]), ALU.mult, ALU.mult)
                hi_f = pre[:, 5, 0:nchk * 16]
                k.copy("act", dhl[:, 0, 0:nchk * 16], dtA_tm[:, 0:nchk, :].rearrange("p c h -> p (c h)"))
                k.copy("act", hi_f, dhl[:, 0, 0:nchk * 16])
                k.tt("dve", hi_f, dtA_tm[:, 0:nchk, :].rearrange("p c h -> p (c h)"), hi_f, ALU.subtract)
                k.copy("act", dhl[:, 1, 0:nchk * 16], hi_f)
                TRI_ = tri_s if smp else tri_p
                ONESM_ = ones_s if smp else ones_f
                n16 = nchk * 16
                pa = pring.get()
                dtA_flat = dtA_tm[:, 0:nchk, :].rearrange("p c h -> p (c h)")
                k.mm(pa[:, 0:n16], TRI_, dtA_flat)
                k.mm(pa[:, 64:64 + n16], ONESM_, dtA_flat)
                k.copy("act", pre[:, 0, 0:n16], pa[:, 0:n16])
                k.ts("dve", pre[:, 1, 0:n16], pa[:, 0:n16], -1.0, None, ALU.mult)
                k.tt("dve", pre[:, 2, 0:n16], pa[:, 64:64 + n16], pre[:, 0, 0:n16], ALU.subtract)
                k.act(pre[:, 2, 0:n16], pre[:, 2, 0:n16], AF.Exp)
                k.tt("dve", pre[:, 3, 0:n16], pre[:, 2, 0:n16], dt_tm[:, 0:nchk, :].rearrange("p c h -> p (c h)"),
                     ALU.mult)
                k.act(pre[:, 4, 0:n16], pa[:, 64:64 + n16], AF.Exp)
                if l == 0 and ti == 0:
                    convert_w_out(0)
                k.cp("stage A done")
                ssd_stage(l, ti, W, smp)
                k.cp("ssd done")
                wb = next_block()
                for m in range(4):
                    pm = proj(wb, m, W)
                    k.act(A16[:, m, 0:W], pm[:, 0:W], AF.Silu)
                wb = next_block()
                for half in range(2):
                    js = [half * 2, half * 2 + 1]
                    pms = [proj(wb, j, W) for j in js]
                    accs = conv_group(l, pms, W, smp, [(PP_LCW + 4 * j, PP_LCB + j, pcar_l[:, j, :],
                                                         carry_l[:, j, :, :], lconv_s_o[l][:, j, :, :]) for j in js])
                    lru_group(l, js, accs, W, smp)
                k.cp("lru done")
                k.flush()
                wb = next_block()
                for m in range(4):
                    pm = proj(wb, m, W)
                    k.act(A16[:, 4 + m, 0:W], pm[:, 0:W], AF.Silu)
                wb = next_block()
                for m in range(4):
                    pm = proj(wb, m, W)
                    k.copy("act", A16[:, 8 + m, 0:W], pm[:, 0:W])
                s5_stage(l, ti, c0, W, smp)
                if l + 1 < nlayers and ti == 0:
                    convert_w_in(l + 1)
                if l + 1 < nlayers and ti == 1:
                    convert_w_out(l + 1)
                k.cp("s5 done")
                wo = woring.get()
                k.dma("sp", wo[:], wobf[l, 0], rk=["wobf%d_0" % l])
                if ti + 1 < len(tiles):
                    load_x(l, ti + 1)
                for m in range(8):
                    if m + 1 < 8:
                        wo_n = woring.get()
                        k.dma("sp", wo_n[:], wobf[l, m + 1], rk=["wobf%d_%d" % (l, m + 1)])
                    po = pring.get()
                    for kt in range(16):
                        k.mm(po[:, 0:W], wo[:, kt, :], Y[:, kt, 0:W], start=(kt == 0), stop=(kt == 15))
                    k.tt("dve", xt[:, m, 0:W], po[:, 0:W], xt[:, m, 0:W], ALU.add)
                    if m + 1 < 8:
                        wo = wo_n
                if l < nlayers - 1:
                    k.dma("sp", xsc[l % 2].rearrange("k p t -> p k t")[:, :, c0:c0 + W], xt[:, :, 0:W],
                          wk=["xsc%d_%d" % (l % 2, ti)])
                else:
                    rmsnorm_tile(xt, W, PP_FG, l, lambda kt: xt[:, kt, 0:W])
                    k.dma("sp", yout.rearrange("k p t -> p k t")[:, :, c0:c0 + W], xt[:, :, 0:W])
            k.dma("sp", sconv_p_o[l], pcar_s[:])
            k.dma("sp", lconv_p_o[l], pcar_l[:])
            k.dma("sp", lru_p_o[l], pcar_h[:])
            k.dma("sp", lru_s_o[l], hcar[:])
            k.dma("sp", s5r_p_o[l], s5p_r[:])
            k.dma("sp", s5i_p_o[l], s5p_i[:])
            k.dma("sp", s5r_s_o[l], s5o_r[:])
            k.dma("sp", s5i_s_o[l], s5o_i[:])
        k.finish()
        k.emit()
    return nc


def _consts():
    i = np.arange(128)
    ident = np.eye(128, dtype=np.float32)
    tri_p = (i[:, None] <= i[None, :]).astype(np.float32)
    same = (i[:, None] // LS == i[None, :] // LS)
    tri_s = (tri_p > 0) & same
    cfa = np.zeros((128, 8, 128), np.float32)
    import ml_dtypes
    trib = np.concatenate([tri_p, tri_s.astype(np.float32)], axis=1).astype(ml_dtypes.bfloat16)
    cfa[:, 0] = np.ascontiguousarray(trib).view(np.float32)
    cfa[:, 1] = tri_p
    cfa[:, 2] = tri_s
    cfa[:, 3] = 1.0
    cfa[:, 4] = same
    cfa[:, 5] = np.broadcast_to((i % LS)[None, :], (128, 128))
    cfa[:, 6] = np.broadcast_to((i % LS != 0)[None, :], (128, 128))
    cfa[:, 7, 0:NSEQ] = (i[:, None] // LS == np.arange(NSEQ)[None, :])
    cba = np.zeros((128, 4, 128), np.float32)
    cba[:, 0] = ident
    cba[:, 1] = 1.0
    cba[:, 2] = np.where(tri_p > 0, 0.0, -30000.0)
    cba[:, 3] = np.where(tri_s, 0.0, -30000.0)
    iota = np.broadcast_to(np.arange(WT, dtype=np.float32)[None, :], (128, WT)).copy()
    return cfa, cba, iota


def _fm(v, nt):
    return np.moveaxis(v.reshape(v.shape[:-1] + (nt, 128)), -1, -2)


def _prep_shared(inp):
    f = lambda a: np.ascontiguousarray(a, dtype=np.float32)
    sh = {}
    sh["w_in_r"] = f(inp["w_in"].reshape(NL, 8, 128, IN_DIM).transpose(0, 2, 1, 3))
    sh["w_out_r"] = f(inp["w_out"].reshape(NL, 16, 128, D).transpose(0, 2, 1, 3))
    sh["glu_r"] = f(inp["s5_glu_w"].reshape(NL, 4, 128, 512).transpose(0, 2, 1, 3))
    for nm, src in (("wa_bd", inp["lru_wa"]), ("wx_bd", inp["lru_wx"])):
        bd = np.zeros((NL, 128, 4, 128), np.float32)
        for m in range(4):
            for k2 in range(2):
                bd[:, k2 * 64:(k2 + 1) * 64, m, k2 * 64:(k2 + 1) * 64] = src[:, 2 * m + k2]
        sh[nm] = bd
    pp = np.zeros((128, NL, NPP), np.float32)
    for l in range(NL):
        pp[:, l, PP_NG:PP_NG + 8] = _fm(inp["norm_g"][l], 8)
        pp[:, l, PP_NG2:PP_NG2 + 8] = _fm(inp["ssd_norm_g"][l], 8)
        pp[:, l, PP_SD:PP_SD + 8] = _fm(np.repeat(inp["ssd_d"][l], 64), 8)
        pp[:, l, PP_CW:PP_CW + 64] = inp["ssd_conv_w"][l].reshape(4, 16, 128).transpose(2, 1, 0).reshape(128, 64)
        pp[:, l, PP_CB:PP_CB + 16] = _fm(inp["ssd_conv_b"][l], 16)
        pp[:, l, PP_LCW:PP_LCW + 16] = inp["lru_conv_w"][l].reshape(4, 4, 128).transpose(2, 1, 0).reshape(128, 16)
        pp[:, l, PP_LCB:PP_LCB + 4] = _fm(inp["lru_conv_b"][l], 4)
        pp[:, l, PP_LBA:PP_LBA + 4] = _fm(inp["lru_ba"][l], 4)
        pp[:, l, PP_LBX:PP_LBX + 4] = _fm(inp["lru_bx"][l], 4)
        pp[:, l, PP_LLAM:PP_LLAM + 4] = _fm(inp["lru_lambda"][l], 4)
        pp[:, l, PP_S5D:PP_S5D + 4] = _fm(inp["s5_d"][l], 4)
        pp[:, l, PP_GLB:PP_GLB + 4] = _fm(inp["s5_glu_b"][l], 4)
        pp[:, l, PP_LRE:PP_LRE + 16] = inp["s5_lambda_re"][l].reshape(16, 128).T
        pp[:, l, PP_LIM:PP_LIM + 16] = inp["s5_lambda_im"][l].reshape(16, 128).T
        pp[:, l, PP_LDT:PP_LDT + 16] = np.repeat(inp["s5_log_dt"][l], 64).reshape(16, 128).T
        pp[:, l, PP_FG:PP_FG + 8] = _fm(inp["final_norm_g"], 8)
    sh["pp_in"] = pp
    pdt = np.zeros((128, NL, 32), np.float32)
    pdt[:, :, 0:16] = inp["ssd_dt_bias"][None]
    pdt[:, :, 16:32] = inp["ssd_a_log"][None]
    sh["pdt_in"] = pdt
    bre = np.zeros((NL, 128, 16, 128), np.float32)
    bim = np.zeros((NL, 128, 16, 128), np.float32)
    cre = np.zeros((NL, 128, 16, 128), np.float32)
    cim = np.zeros((NL, 128, 16, 128), np.float32)
    for g in range(32):
        pr, g2, gl = g // 2, g % 2, g % 8
        bre[:, g2 * 64:(g2 + 1) * 64, pr, gl * 16:(gl + 1) * 16] = inp["s5_b_re"][:, g]
        bim[:, g2 * 64:(g2 + 1) * 64, pr, gl * 16:(gl + 1) * 16] = inp["s5_b_im"][:, g]
        cre[:, g2 * 64:(g2 + 1) * 64, pr, gl * 16:(gl + 1) * 16] = inp["s5_c_re"][:, g].transpose(0, 2, 1)
        cim[:, g2 * 64:(g2 + 1) * 64, pr, gl * 16:(gl + 1) * 16] = inp["s5_c_im"][:, g].transpose(0, 2, 1)
    sh["bpadT_re"], sh["bpadT_im"], sh["ctpad_re"], sh["ctpad_im"] = bre, bim, cre, cim
    cfa, cba, iota = _consts()
    sh["c_f32"], sh["c_bf"], sh["c_iota"] = cfa, cba, iota
    return sh


def _prep_core(inp, c):
    f = lambda a: np.ascontiguousarray(a, dtype=np.float32)
    sl = slice(NSEQ * c, NSEQ * (c + 1))
    m = {}
    x_tok = np.concatenate([inp["x_prompt"][c], inp["x_sample"][sl].reshape(TS, D)], axis=0)
    m["xin"] = f(x_tok.T.reshape(8, 128, TT))
    m["h0T"] = f(inp["state_ssd"][:, sl].transpose(0, 1, 4, 2, 3).reshape(NL, NSEQ, 128, 1024))
    m["sconv0"] = f(inp["state_ssd_conv"][:, sl].reshape(NL, NSEQ, 3, 16, 128).transpose(0, 4, 3, 1, 2))
    m["lconv0"] = f(inp["state_lru_conv"][:, sl].reshape(NL, NSEQ, 3, 4, 128).transpose(0, 4, 3, 1, 2))
    m["lru0"] = f(inp["state_lru"][:, sl].reshape(NL, NSEQ, 4, 128).transpose(0, 3, 2, 1))
    m["s5r0"] = f(inp["state_s5_re"][:, sl].reshape(NL, NSEQ, 16, 128).transpose(0, 3, 2, 1))
    m["s5i0"] = f(inp["state_s5_im"][:, sl].reshape(NL, NSEQ, 16, 128).transpose(0, 3, 2, 1))
    return m


_PROG = {}


def kernel(**inputs):
    inp = {k_: np.asarray(v) for k_, v in inputs.items()}
    if "nc" not in _PROG:
        _PROG["nc"] = build_program(NL)
    nc = _PROG["nc"]
    shared = _prep_shared(inp)
    in_maps = []
    for c in range(NCORES):
        m = dict(shared)
        m.update(_prep_core(inp, c))
        in_maps.append(m)
    res = run_bass_kernel_spmd(nc, in_maps, core_ids=list(range(NCORES)))
    R = res.results
    B = NCORES
    y_prompt = np.zeros((B, TP, D), np.float32)
    y_sample = np.zeros((B * NSEQ, LS, D), np.float32)
    ssd_p = np.zeros((NL, B, 16, 64, 128), np.float32)
    ssd_s = np.zeros((NL, B * NSEQ, 16, 64, 128), np.float32)
    ssd_conv_p = np.zeros((NL, B, 3, 2048), np.float32)
    ssd_conv_s = np.zeros((NL, B * NSEQ, 3, 2048), np.float32)
    lru_p = np.zeros((NL, B, 512), np.float32)
    lru_s = np.zeros((NL, B * NSEQ, 512), np.float32)
    lru_conv_p = np.zeros((NL, B, 3, 512), np.float32)
    lru_conv_s = np.zeros((NL, B * NSEQ, 3, 512), np.float32)
    s5_re_p = np.zeros((NL, B, 32, 64), np.float32)
    s5_re_s = np.zeros((NL, B * NSEQ, 32, 64), np.float32)
    s5_im_p = np.zeros((NL, B, 32, 64), np.float32)
    s5_im_s = np.zeros((NL, B * NSEQ, 32, 64), np.float32)
    for c in range(B):
        r = R[c]
        sl = slice(NSEQ * c, NSEQ * (c + 1))
        y = np.asarray(r["yout"]).reshape(D, TT).T
        y_prompt[c] = y[0:TP]
        y_sample[sl] = y[TP:].reshape(NSEQ, LS, D)
        ssd_p[:, c] = np.asarray(r["ssd_p_o"]).reshape(NL, 128, 16, 64).transpose(0, 2, 3, 1)
        ssd_s[:, sl] = np.asarray(r["ssd_s_o"]).reshape(NL, NSEQ, 128, 16, 64).transpose(0, 1, 3, 4, 2)
        ssd_conv_p[:, c] = np.asarray(r["sconv_p_o"]).transpose(0, 3, 2, 1).reshape(NL, 3, 2048)
        ssd_conv_s[:, sl] = np.asarray(r["sconv_s_o"]).transpose(0, 3, 4, 2, 1).reshape(NL, NSEQ, 3, 2048)
        lru_p[:, c] = np.asarray(r["lru_p_o"]).transpose(0, 2, 1).reshape(NL, 512)
        lru_s[:, sl] = np.asarray(r["lru_s_o"]).transpose(0, 3, 2, 1).reshape(NL, NSEQ, 512)
        lru_conv_p[:, c] = np.asarray(r["lconv_p_o"]).transpose(0, 3, 2, 1).reshape(NL, 3, 512)
        lru_conv_s[:, sl] = np.asarray(r["lconv_s_o"]).transpose(0, 3, 4, 2, 1).reshape(NL, NSEQ, 3, 512)
        s5_re_p[:, c] = np.asarray(r["s5r_p_o"]).transpose(0, 2, 1).reshape(NL, 32, 64)
        s5_im_p[:, c] = np.asarray(r["s5i_p_o"]).transpose(0, 2, 1).reshape(NL, 32, 64)
        s5_re_s[:, sl] = np.asarray(r["s5r_s_o"]).transpose(0, 3, 2, 1).reshape(NL, NSEQ, 32, 64)
        s5_im_s[:, sl] = np.asarray(r["s5i_s_o"]).transpose(0, 3, 2, 1).reshape(NL, NSEQ, 32, 64)
    return (y_prompt, y_sample, ssd_p, ssd_s, ssd_conv_p, ssd_conv_s, lru_p, lru_s,
            lru_conv_p, lru_conv_s, s5_re_p, s5_re_s, s5_im_p, s5_im_s)
```
